# Optimizing a Trainium2 kernel written in Bass

```python
import math
import jax, jax.numpy as jnp
from jax import lax
import numpy as np

D_MODEL = 1024
BATCH = 8
SEQ = 8192
DEPTH = 1

HEAD_DIM = 64
SWA_Q_HEADS = 8
SWA_KV_HEADS = 2
SWA_GROUP = SWA_Q_HEADS // SWA_KV_HEADS
SWA_WINDOW = 128
SB_HEADS = 8
BLOCK = 128
REL_BUCKETS = 32
REL_MAX_DIST = 128
D_FF = 2816
N_BRANCH = 2
RMS_EPS = 1e-6
NEG_BIG = -1e30

SWA_Q_W = SWA_Q_HEADS * HEAD_DIM
SWA_KV_W = SWA_KV_HEADS * HEAD_DIM
SB_W = SB_HEADS * HEAD_DIM
IN_SIZES = (SWA_Q_W, SWA_KV_W, SWA_KV_W, SB_W, SB_W, SB_W, D_MODEL, D_MODEL)
IN_W = sum(IN_SIZES)
IN_SPLITS = tuple(int(v) for v in np.cumsum(IN_SIZES)[:-1])

kernel_name = 'hybrid_swa_sink_stickbreaking_macaron'


def rmsnorm(x, g):
    xf = x.astype(jnp.float32)
    y = xf * lax.rsqrt(jnp.mean(xf * xf, axis=-1, keepdims=True) + RMS_EPS) * g.astype(jnp.float32)
    return y.astype(x.dtype)


def swiglu(h, w1, w3, w2):
    return (jax.nn.silu(h @ w1) * (h @ w3)) @ w2


def rel_bucket(dist):
    max_exact = REL_BUCKETS // 2
    d = jnp.maximum(dist, 1).astype(jnp.float32)
    large = max_exact + (jnp.log(d / max_exact) / math.log(REL_MAX_DIST / max_exact)
                         * (REL_BUCKETS - max_exact)).astype(jnp.int32)
    large = jnp.minimum(large, REL_BUCKETS - 1)
    return jnp.where(dist < max_exact, dist, large)


def sliding_window_attention(q, k, v, sinks, rel_table):
    B, S = q.shape[0], q.shape[1]
    nb = S // BLOCK
    qb = q.astype(jnp.float32).reshape(B, nb, BLOCK, SWA_KV_HEADS, SWA_GROUP, HEAD_DIM)
    kb = k.astype(jnp.float32).reshape(B, nb, BLOCK, SWA_KV_HEADS, HEAD_DIM)
    vb = v.astype(jnp.float32).reshape(B, nb, BLOCK, SWA_KV_HEADS, HEAD_DIM)
    pad = ((0, 0), (1, 0), (0, 0), (0, 0), (0, 0))
    kw = jnp.concatenate([jnp.pad(kb, pad)[:, :-1], kb], axis=2)
    vw = jnp.concatenate([jnp.pad(vb, pad)[:, :-1], vb], axis=2)
    logits = jnp.einsum('bnqhgd,bnkhd->bnhgqk', qb, kw) * (HEAD_DIM ** -0.5)
    qi = jnp.arange(BLOCK)[:, None] + BLOCK
    kj = jnp.arange(2 * BLOCK)[None, :]
    dist = qi - kj
    band = (dist >= 0) & (dist < SWA_WINDOW)
    bias = rel_table.astype(jnp.float32)[rel_bucket(jnp.maximum(dist, 0))]
    bias = bias.transpose(2, 0, 1).reshape(SWA_KV_HEADS, SWA_GROUP, BLOCK, 2 * BLOCK)
    key_pos = jnp.arange(nb)[:, None] * BLOCK + jnp.arange(2 * BLOCK)[None, :] - BLOCK
    valid = band[None] & (key_pos >= 0)[:, None, :]
    logits = jnp.where(valid[None, :, None, None], logits + bias, NEG_BIG)
    sink = sinks.astype(jnp.float32).reshape(SWA_KV_HEADS, SWA_GROUP)[None, None, :, :, None, None]
    m = jnp.maximum(jnp.max(logits, axis=-1, keepdims=True), sink)
    p = jnp.exp(logits - m)
    p = p / (jnp.sum(p, axis=-1, keepdims=True) + jnp.exp(sink - m))
    o = jnp.einsum('bnhgqk,bnkhd->bnqhgd', p, vw)
    return o.reshape(B, S, SWA_Q_W).astype(q.dtype)


def stick_breaking_attention(q, k, v):
    B, S = q.shape[0], q.shape[1]
    nb = S // BLOCK
    qf = q.astype(jnp.float32).transpose(0, 2, 1, 3) * (HEAD_DIM ** -0.5)
    kf = k.astype(jnp.float32).transpose(0, 2, 1, 3)
    vf = v.astype(jnp.float32).transpose(0, 2, 1, 3)
    qblocks = qf.reshape(B, SB_HEADS, nb, BLOCK, HEAD_DIM).transpose(2, 0, 1, 3, 4)
    key_pos = jnp.arange(S)

    def one_block(args):
        q_blk, start = args
        z = jnp.einsum('bhqd,bhkd->bhqk', q_blk, kf)
        q_pos = start + jnp.arange(BLOCK)
        causal = key_pos[None, :] < q_pos[:, None]
        log_keep = jnp.where(causal, jax.nn.log_sigmoid(-z), 0.0)
        rev = lax.cumsum(log_keep, axis=3, reverse=True)
        between = jnp.concatenate([rev[..., 1:], jnp.zeros_like(rev[..., :1])], axis=-1)
        a = jnp.where(causal, jnp.exp(jax.nn.log_sigmoid(z) + between), 0.0)
        return jnp.einsum('bhqk,bhkd->bhqd', a, vf)

    o = lax.map(one_block, (qblocks, jnp.arange(nb) * BLOCK))
    return o.transpose(1, 0, 3, 2, 4).reshape(B, S, SB_W).astype(q.dtype)


def setup_inputs(seed: int = 0) -> dict:
    key = jax.random.key(seed)
    ks = jax.random.split(key, 20)
    f32 = jnp.float32

    def w(k, shape, fan_in):
        return jax.random.normal(k, shape, f32) * (fan_in ** -0.5)

    def gain(k):
        return 1.0 + 0.02 * jax.random.normal(k, (DEPTH, D_MODEL), f32)

    return {
        'x': jax.random.normal(ks[0], (BATCH, SEQ, D_MODEL), f32),
        'norm_ffn1': gain(ks[1]),
        'ffn1_w1': w(ks[2], (DEPTH, D_MODEL, D_FF), D_MODEL),
        'ffn1_w3': w(ks[3], (DEPTH, D_MODEL, D_FF), D_MODEL),
        'ffn1_w2': w(ks[4], (DEPTH, D_FF, D_MODEL), D_FF),
        'norm_mix': gain(ks[5]),
        'w_in': w(ks[6], (DEPTH, D_MODEL, IN_W), D_MODEL),
        'swa_sinks': 0.5 * jax.random.normal(ks[7], (DEPTH, SWA_Q_HEADS), f32),
        'rel_bias': 0.5 * jax.random.normal(ks[8], (REL_BUCKETS, SWA_Q_HEADS), f32),
        'w_branch_swa': w(ks[9], (DEPTH, SWA_Q_W, D_MODEL), SWA_Q_W),
        'w_branch_sb': w(ks[10], (DEPTH, SB_W, D_MODEL), SB_W),
        'w_out': w(ks[11], (DEPTH, D_MODEL, D_MODEL), D_MODEL),
        'norm_ffn2': gain(ks[12]),
        'ffn2_w1': w(ks[13], (DEPTH, D_MODEL, D_FF), D_MODEL),
        'ffn2_w3': w(ks[14], (DEPTH, D_MODEL, D_FF), D_MODEL),
        'ffn2_w2': w(ks[15], (DEPTH, D_FF, D_MODEL), D_FF),
        'norm_final': 1.0 + 0.02 * jax.random.normal(ks[16], (D_MODEL,), f32),
    }


def reference(x, norm_ffn1, ffn1_w1, ffn1_w3, ffn1_w2, norm_mix, w_in, swa_sinks, rel_bias,
              w_branch_swa, w_branch_sb, w_out, norm_ffn2, ffn2_w1, ffn2_w3, ffn2_w2, norm_final):
    B, S = x.shape[0], x.shape[1]
    for layer in range(DEPTH):
        h = rmsnorm(x, norm_ffn1[layer])
        x = x + 0.5 * swiglu(h, ffn1_w1[layer], ffn1_w3[layer], ffn1_w2[layer])
        h = rmsnorm(x, norm_mix[layer])
        proj = h @ w_in[layer]
        q_a, k_a, v_a, q_b, k_b, v_b, g_a, g_b = jnp.split(proj, IN_SPLITS, axis=-1)
        o_a = sliding_window_attention(
            q_a.reshape(B, S, SWA_Q_HEADS, HEAD_DIM),
            k_a.reshape(B, S, SWA_KV_HEADS, HEAD_DIM),
            v_a.reshape(B, S, SWA_KV_HEADS, HEAD_DIM),
            swa_sinks[layer], rel_bias)
        o_b = stick_breaking_attention(
            q_b.reshape(B, S, SB_HEADS, HEAD_DIM),
            k_b.reshape(B, S, SB_HEADS, HEAD_DIM),
            v_b.reshape(B, S, SB_HEADS, HEAD_DIM))
        merged = (jax.nn.sigmoid(g_a) * (o_a @ w_branch_swa[layer])
                  + jax.nn.sigmoid(g_b) * (o_b @ w_branch_sb[layer]))
        x = x + merged @ w_out[layer]
        h = rmsnorm(x, norm_ffn2[layer])
        x = x + 0.5 * swiglu(h, ffn2_w1[layer], ffn2_w3[layer], ffn2_w2[layer])
    return rmsnorm(x, norm_final)
```

```python
from contextlib import ExitStack

import numpy as np
import ml_dtypes

import concourse.bass as bass
import concourse.mybir as mybir
from concourse.bass_utils import run_bass_kernel_spmd

F32 = mybir.dt.float32
BF16 = mybir.dt.bfloat16
AF = mybir.ActivationFunctionType
ALU = mybir.AluOpType
AX = mybir.AxisListType

D = 1024
DFF = 2816
NFF = DFF // 128
INW = 4352
EPS = 1e-6
NEG = -1e30
TT = 512
MASKB = -30000.0
STGW = 704

C_QA, C_KA, C_VA, C_QB, C_KB, C_VB, C_GA, C_GB = 0, 512, 640, 768, 1280, 1792, 2304, 3328


class Buf:
    __slots__ = ("name", "lw", "rd", "dmard")

    def __init__(self, name):
        self.name = name
        self.lw = None
        self.rd = {}
        self.dmard = []


class Op:
    __slots__ = ("eng", "fn", "deps", "signal", "sem", "count", "is_dma", "pairs")

    def __init__(self, eng, fn, is_dma=False):
        self.eng = eng
        self.fn = fn
        self.deps = []
        self.signal = False
        self.sem = None
        self.count = 0
        self.is_dma = is_dma
        self.pairs = None


SEM_LIMIT = 30000


class Sched:
    ENGS = ("pe", "act", "dve", "pool", "sp")

    def __init__(self, nc, es):
        self.nc = nc
        self.es = es
        self.ops = {e: [] for e in self.ENGS}
        self.dma_sems = {}
        self.barrier_deps = {e: None for e in self.ENGS}

    def _newsem(self, name):
        return self.es.enter_context(self.nc.semaphore(name))

    def _add(self, op, reads, writes):
        deps = []
        bd = self.barrier_deps[op.eng]
        if bd is not None:
            deps.extend(bd)
            self.barrier_deps[op.eng] = None
        for b in reads:
            w = b.lw
            if w is not None:
                if not (w.eng == op.eng == "pe" and not w.is_dma and not op.is_dma):
                    deps.append(w)
        for b in writes:
            w = b.lw
            if w is not None:
                if w.is_dma or op.is_dma or w.eng != op.eng:
                    deps.append(w)
            for e, r in b.rd.items():
                if op.is_dma or e != op.eng:
                    deps.append(r)
            deps.extend(b.dmard)
        for b in reads:
            if op.is_dma:
                b.dmard.append(op)
            else:
                b.rd[op.eng] = op
        for b in writes:
            b.lw = op
            b.rd = {}
            b.dmard = []
        seen = set()
        for d in deps:
            if id(d) not in seen and d is not op:
                seen.add(id(d))
                op.deps.append(d)
                d.signal = True
        self.ops[op.eng].append(op)
        return op

    def op(self, eng, fn, reads=(), writes=()):
        return self._add(Op(eng, fn), reads, writes)

    def dma(self, queue, pairs, reads=(), writes=(), key=None):
        op = Op(queue, None, is_dma=True)
        op.pairs = pairs
        op.signal = True
        if key not in self.dma_sems:
            self.dma_sems[key] = [self._newsem("d%d" % len(self.dma_sems)), 0, None]
        ent = self.dma_sems[key]
        ent[1] += 16 * len(pairs)
        ent[2] = op
        op.sem = ent[0]
        op.count = ent[1]
        return self._add(op, reads, writes)

    def barrier(self):
        prev = []
        for e in self.ENGS:
            for op in reversed(self.ops[e]):
                if not op.is_dma:
                    prev.append(op)
                    break
        for ent in self.dma_sems.values():
            if ent[2] is not None:
                prev.append(ent[2])
        for e in self.ENGS:
            self.barrier_deps[e] = list(prev)

    def emit(self):
        nc = self.nc
        eng_sems = {}
        for e in self.ENGS:
            n = 0
            for op in self.ops[e]:
                if op.is_dma or not op.signal:
                    continue
                k = n // SEM_LIMIT
                if (e, k) not in eng_sems:
                    eng_sems[(e, k)] = self._newsem("e_%s%d" % (e, k))
                op.sem = eng_sems[(e, k)]
                op.count = n % SEM_LIMIT + 1
                n += 1
        final_waits = [(ent[0], ent[1]) for ent in self.dma_sems.values()]

        def run(e, eng, final=False):
            waited = {}
            for op in self.ops[e]:
                for d in op.deps:
                    k = id(d.sem)
                    if waited.get(k, 0) < d.count:
                        eng.wait_ge(d.sem, d.count)
                        waited[k] = d.count
                if op.is_dma:
                    for (o, i) in op.pairs:
                        eng.dma_start(out=o, in_=i).then_inc(op.sem, 16)
                else:
                    ins = op.fn(eng)
                    if op.signal:
                        ins.then_inc(op.sem, 1)
            if final:
                for (h, c) in final_waits:
                    if waited.get(id(h), 0) < c:
                        eng.wait_ge(h, c)

        with nc.Block() as block:
            @block.sync
            def _(sync):
                run("sp", sync, final=True)

            @block.tensor
            def _(tensor):
                run("pe", tensor)

            @block.scalar
            def _(scalar):
                run("act", scalar)

            @block.vector
            def _(vector):
                run("dve", vector)

            @block.gpsimd
            def _(gpsimd):
                run("pool", gpsimd)


class Pool:
    uid = [0]

    def __init__(self, nc, es, name, n, shape, dtype, psum=False):
        self.tiles = []
        for i in range(n):
            Pool.uid[0] += 1
            nm = "%s%d_%d" % (name, i, Pool.uid[0])
            if psum:
                t = es.enter_context(nc.psum_tensor(nm, list(shape), dtype))
            else:
                t = es.enter_context(nc.sbuf_tensor(nm, list(shape), dtype))
            self.tiles.append((t, Buf(nm)))
        self.i = 0

    def next(self):
        t = self.tiles[self.i % len(self.tiles)]
        self.i += 1
        return t


def build_program(S, phases=("ffn1", "proj", "swa", "sb", "mix", "ffn2"), debug=False):
    NT = S // TT
    NB = S // 128
    nc = bass.Bass("TRN2", target_bir_lowering=False)
    dk = "ExternalOutput" if debug else "Internal"

    def din(name, shape, dt=F32):
        return nc.dram_tensor(name, list(shape), dt, kind="ExternalInput").ap()

    def dscr(name, shape, dt):
        return nc.dram_tensor(name, list(shape), dt, kind=dk).ap()

    x_d = din("x", [S, D])
    w1_d = [din("ffn1_w1", [D, DFF]), din("ffn2_w1", [D, DFF])]
    w3_d = [din("ffn1_w3", [D, DFF]), din("ffn2_w3", [D, DFF])]
    w2_d = [din("ffn1_w2", [DFF, D]), din("ffn2_w2", [DFF, D])]
    gains_d = din("gains", [128, 3, 8])
    gfin_d = din("norm_final", [D])
    win_d = din("w_in", [D, INW])
    sinks_d = din("swa_sinks", [8])
    bias_d = din("swa_bias", [128, 8, 256])
    maskc_d = din("swa_mask", [128, 256])
    wba_d = din("w_branch_swa", [512, D])
    wbb_d = din("w_branch_sb", [512, D])
    wout_d = din("w_out", [D, D])
    cst_d = din("consts", [128, 4 * 128], BF16)
    out_d = nc.dram_tensor("out", [S, D], F32, kind="ExternalOutput").ap()

    x1_d = dscr("x1", [S, D], F32)
    x2_d = dscr("x2", [S, D], F32)
    qta_d = dscr("qta", [4, 128, S], BF16)
    kta_d = dscr("kta", [128, S], BF16)
    va_d = dscr("va", [S, 128], BF16)
    qtb_d = dscr("qtb", [512, S], BF16)
    ktb_d = dscr("ktb", [512, S], BF16)
    vb_d = dscr("vb", [S, 512], BF16)
    ota_d = dscr("ota", [512, S], BF16)
    otb_d = dscr("otb", [512, S], BF16)

    B_x1 = [Buf("x1_%d" % i) for i in range(NT)]
    B_x2 = [Buf("x2_%d" % i) for i in range(NT)]
    B_qkv = [Buf("qkv_%d" % i) for i in range(NT)]
    B_ota = [Buf("ota_%d" % i) for i in range(NT)]
    B_otb = [Buf("otb_%d" % i) for i in range(NT)]

    with ExitStack() as es:
        sc = Sched(nc, es)

        def sbt(stack, name, shape, dt):
            Pool.uid[0] += 1
            return stack.enter_context(nc.sbuf_tensor("%s_%d" % (name, Pool.uid[0]), list(shape), dt))

        cst = sbt(es, "cst", [128, 512], BF16)
        B_cst = Buf("cst")
        sc.dma("sp", [(cst[:], cst_d)], writes=[B_cst], key="cst")
        ident = cst[:, 0:128]
        ntri = cst[:, 128:256]
        nones = cst[:, 256:384]
        dmask = cst[:, 384:512]

        gcol = sbt(es, "gcol", [128, 3, 8], F32)
        B_gcol = Buf("gcol")

        sc.dma("sp", [(gcol[:], gains_d)], writes=[B_gcol], key="gcol")

        def load_cast(stgp, dst_t, sel, B_dst, src_rows, ncols, gain=None):
            c0 = 0
            while c0 < ncols:
                c1 = min(ncols, c0 + STGW)
                st, B_st = stgp.next()
                w = c1 - c0
                sc.dma("sp", [(st[:, 0:w], src_rows[:, c0:c1])], writes=[B_st], key=("stg", id(B_st)))
                o = sel(c0, c1)
                if gain is None:
                    sc.op("dve", lambda e, o=o, i=st[:, 0:w]: e.tensor_copy(out=o, in_=i),
                          reads=[B_st], writes=[B_dst])
                else:
                    sc.op("dve", lambda e, o=o, i=st[:, 0:w], g=gain:
                          e.tensor_scalar(out=o, in0=i, scalar1=g, scalar2=None, op0=ALU.mult),
                          reads=[B_st, B_gcol], writes=[B_dst])
                c0 = c1

        def rstd_ops(st, B_st):
            sc.op("dve", lambda e, st=st: e.tensor_scalar(
                out=st[:, 1:2], in0=st[:, 0:1], scalar1=float(D * EPS), scalar2=None, op0=ALU.add),
                reads=[B_st], writes=[B_st])
            sc.op("act", lambda e, st=st: e.activation(out=st[:, 1:2], in_=st[:, 1:2], func=AF.Sqrt),
                  reads=[B_st], writes=[B_st])
            sc.op("dve", lambda e, st=st: e.reciprocal(out=st[:, 1:2], in_=st[:, 1:2]),
                  reads=[B_st], writes=[B_st])

        class NormT:
            def __init__(self, stack, nht=1):
                self.xin = Pool(nc, stack, "xin", 2, [128, D], F32)
                self.hrow = Pool(nc, stack, "hrow", 2, [128, D], BF16)
                self.sqj = sbt(stack, "sqj", [128, D], BF16)
                self.B_sqj = Buf("sqj")
                self.stat = Pool(nc, stack, "stat", 8, [128, 2], F32)
                self.hts = [(sbt(stack, "ht%d" % i, [128, 8, TT], BF16), [Buf("ht%d_%d" % (i, j)) for j in range(4)])
                            for i in range(nht)]
                self.hi = 0
                self.tpp = Pool(nc, stack, "tpp", 2, [128, 8, 128], BF16, psum=True)
                self.ev = 0

            def run(self, src_d, ti, B_src):
                ht_, B_ht_ = self.hts[self.hi % len(self.hts)]
                self.hi += 1
                for j in range(4):
                    xt, B_xt = self.xin.next()
                    r0 = ti * TT + j * 128
                    sc.dma("sp", [(xt[:], src_d[r0:r0 + 128, :])], reads=[B_src] if B_src else [],
                           writes=[B_xt], key=("xin", id(B_xt)))
                    st, B_st = self.stat.next()
                    sc.op("act", lambda e, xt=xt, st=st: e.activation(
                        out=self.sqj[:], in_=xt[:], func=AF.Square, accum_out=st[:, 0:1]),
                        reads=[B_xt], writes=[B_st, self.B_sqj])
                    rstd_ops(st, B_st)
                    hr, B_hr = self.hrow.next()
                    sc.op("dve", lambda e, hr=hr, xt=xt, st=st: e.tensor_scalar(
                        out=hr[:], in0=xt[:], scalar1=st[:, 1:2], scalar2=32.0, op0=ALU.mult, op1=ALU.mult),
                        reads=[B_xt, B_st], writes=[B_hr])
                    tp, B_tp = self.tpp.next()
                    for k in range(8):
                        sc.op("pe", lambda e, tp=tp, hr=hr, k=k: e.transpose(
                            out=tp[:, k, :], in_=hr[:, k * 128:(k + 1) * 128], identity=ident),
                            reads=[B_hr, B_cst], writes=[B_tp])
                    eng = ("dve", "act")[self.ev % 2]
                    self.ev += 1
                    if eng == "dve":
                        sc.op("dve", lambda e, tp=tp, j=j, ht_=ht_: e.tensor_copy(
                            out=ht_[:, :, j * 128:(j + 1) * 128], in_=tp[:]),
                            reads=[B_tp], writes=[B_ht_[j]])
                    else:
                        sc.op("act", lambda e, tp=tp, j=j, ht_=ht_: e.copy(
                            out=ht_[:, :, j * 128:(j + 1) * 128], in_=tp[:]),
                            reads=[B_tp], writes=[B_ht_[j]])
                return ht_, B_ht_

        def phase_ffn(layer, src_d, B_srcs, dst_d, B_dsts, final):
            with ExitStack() as pes:
                w1b = sbt(pes, "w1b", [128, 8, DFF], BF16)
                w3b = sbt(pes, "w3b", [128, 8, DFF], BF16)
                w2b = sbt(pes, "w2b", [128, NFF, D], BF16)
                B_w1 = [Buf("w1_%d" % k) for k in range(8)]
                B_w3 = [Buf("w3_%d" % k) for k in range(8)]
                B_w2 = [Buf("w2_%d" % c) for c in range(NFF)]
                stgp = Pool(nc, pes, "stg", 2, [128, STGW], F32)
                gi = 0 if layer == 0 else 2
                for k in range(8):
                    load_cast(stgp, w1b, lambda a, b, k=k: w1b[:, k, a:b], B_w1[k],
                              w1_d[layer][k * 128:(k + 1) * 128, :], DFF, gain=gcol[:, gi, k:k + 1])
                    load_cast(stgp, w3b, lambda a, b, k=k: w3b[:, k, a:b], B_w3[k],
                              w3_d[layer][k * 128:(k + 1) * 128, :], DFF, gain=gcol[:, gi, k:k + 1])
                for c in range(NFF):
                    load_cast(stgp, w2b, lambda a, b, c=c: w2b[:, c, a:b], B_w2[c],
                              w2_d[layer][c * 128:(c + 1) * 128, :], D)
                nt = NormT(pes)
                G = sbt(pes, "G", [128, NFF, TT], BF16)
                B_G = [Buf("G%d" % c) for c in range(NFF)]
                sgp = Pool(nc, pes, "sg", 2, [128, TT], F32)
                xres = Pool(nc, pes, "xres", 2, [128, D], F32)
                ps_up = Pool(nc, pes, "psu", 4, [128, 512], F32, psum=True)
                ps_dn = Pool(nc, pes, "psd", 2, [128, 512], F32, psum=True)
                if final:
                    gfb = sbt(pes, "gfb", [128, D], F32)
                    B_gfb = Buf("gfb")
                    sc.dma("sp", [(gfb[:], gfin_d.partition_broadcast(128))], writes=[B_gfb], key="gfb")
                    sc.op("dve", lambda e: e.tensor_scalar(out=gfb[:], in0=gfb[:], scalar1=32.0, scalar2=None,
                                                           op0=ALU.mult), reads=[B_gfb], writes=[B_gfb])
                    fstat = Pool(nc, pes, "fstat", 4, [128, 2], F32)

                nxt = nt.run(src_d, 0, B_srcs[0] if B_srcs else None)
                for ti in range(NT):
                    ht, B_ht = nxt
                    for c in range(NFF):
                        pu, B_pu = ps_up.next()
                        pv, B_pv = ps_up.next()
                        for k in range(8):
                            sc.op("pe", lambda e, pu=pu, k=k, c=c, ht=ht: e.matmul(
                                pu[:], lhsT=w1b[:, k, c * 128:(c + 1) * 128], rhs=ht[:, k, :],
                                start=(k == 0), stop=(k == 7)), reads=[B_w1[k]] + B_ht, writes=[B_pu])
                        for k in range(8):
                            sc.op("pe", lambda e, pv=pv, k=k, c=c, ht=ht: e.matmul(
                                pv[:], lhsT=w3b[:, k, c * 128:(c + 1) * 128], rhs=ht[:, k, :],
                                start=(k == 0), stop=(k == 7)), reads=[B_w3[k]] + B_ht, writes=[B_pv])
                        sg, B_sg = sgp.next()
                        sc.op("act", lambda e, sg=sg, pu=pu: e.activation(out=sg[:], in_=pu[:], func=AF.Silu),
                              reads=[B_pu], writes=[B_sg])
                        sc.op("dve", lambda e, sg=sg, pv=pv, c=c: e.tensor_tensor(
                            out=G[:, c, :], in0=sg[:], in1=pv[:], op=ALU.mult),
                            reads=[B_sg, B_pv], writes=[B_G[c]])
                    if ti + 1 < NT:
                        nxt = nt.run(src_d, ti + 1, B_srcs[ti + 1] if B_srcs else None)
                    for j in range(4):
                        xr, B_xr = xres.next()
                        r0 = ti * TT + j * 128
                        sc.dma("sp", [(xr[:], src_d[r0:r0 + 128, :])], reads=[B_srcs[ti]] if B_srcs else [],
                               writes=[B_xr], key=("xres", id(B_xr)))
                        for half in range(2):
                            pd, B_pd = ps_dn.next()
                            for c in range(NFF):
                                sc.op("pe", lambda e, pd=pd, c=c, j=j, half=half: e.matmul(
                                    pd[:], lhsT=G[:, c, j * 128:(j + 1) * 128],
                                    rhs=w2b[:, c, half * 512:(half + 1) * 512],
                                    start=(c == 0), stop=(c == NFF - 1)),
                                    reads=[B_G[c], B_w2[c]], writes=[B_pd])
                            sc.op("dve", lambda e, pd=pd, xr=xr, half=half: e.scalar_tensor_tensor(
                                out=xr[:, half * 512:(half + 1) * 512], in0=pd[:], scalar=0.5,
                                in1=xr[:, half * 512:(half + 1) * 512], op0=ALU.mult, op1=ALU.add),
                                reads=[B_pd, B_xr], writes=[B_xr])
                        if final:
                            fs, B_fs = fstat.next()
                            sc.op("act", lambda e, xr=xr, fs=fs: e.activation(
                                out=nt.sqj[:], in_=xr[:], func=AF.Square, accum_out=fs[:, 0:1]),
                                reads=[B_xr], writes=[B_fs, nt.B_sqj])
                            rstd_ops(fs, B_fs)
                            sc.op("dve", lambda e, xr=xr, fs=fs: e.scalar_tensor_tensor(
                                out=xr[:], in0=xr[:], scalar=fs[:, 1:2], in1=gfb[:], op0=ALU.mult, op1=ALU.mult),
                                reads=[B_xr, B_fs, B_gfb], writes=[B_xr])
                        sc.dma("sp", [(dst_d[r0:r0 + 128, :], xr[:])], reads=[B_xr],
                               writes=[B_dsts[ti]] if B_dsts else [], key=("xres_st", id(B_xr)))
            sc.barrier()

        def phase_proj():
            NQ = C_GA
            with ExitStack() as pes:
                wq = sbt(pes, "wq", [128, 8, NQ], BF16)
                B_wq = [Buf("wq%d" % k) for k in range(8)]
                stgp = Pool(nc, pes, "stg", 2, [128, STGW], F32)
                for k in range(8):
                    load_cast(stgp, wq, lambda a, b, k=k: wq[:, k, a:b], B_wq[k],
                              win_d[k * 128:(k + 1) * 128, 0:NQ], NQ, gain=gcol[:, 1, k:k + 1])
                nt = NormT(pes, nht=2)
                ps = Pool(nc, pes, "ps", 4, [128, 512], F32, psum=True)
                evp = Pool(nc, pes, "ev", 6, [128, 512], BF16)
                fm = []
                for g in range(4):
                    fm.append((C_QA + g * 128, lambda ti, g=g: qta_d[g, :, ti * TT:(ti + 1) * TT], 0.125))
                fm.append((C_KA, lambda ti: kta_d[:, ti * TT:(ti + 1) * TT], 1.0))
                for cc in range(4):
                    fm.append((C_QB + cc * 128, lambda ti, cc=cc: qtb_d[cc * 128:(cc + 1) * 128, ti * TT:(ti + 1) * TT], 0.125))
                for cc in range(4):
                    fm.append((C_KB + cc * 128, lambda ti, cc=cc: ktb_d[cc * 128:(cc + 1) * 128, ti * TT:(ti + 1) * TT], 1.0))
                evi = 0
                nxt = nt.run(x1_d, 0, B_x1[0])
                for ti in range(NT):
                    ht, B_ht = nxt
                    if ti + 1 < NT:
                        nxt = nt.run(x1_d, ti + 1, B_x1[ti + 1])
                    for (col, dst, scale) in fm:
                        pp, B_pp = ps.next()
                        for k in range(8):
                            sc.op("pe", lambda e, pp=pp, k=k, col=col, ht=ht: e.matmul(
                                pp[:], lhsT=wq[:, k, col:col + 128], rhs=ht[:, k, :],
                                start=(k == 0), stop=(k == 7)), reads=[B_wq[k]] + B_ht, writes=[B_pp])
                        ev, B_ev = evp.next()
                        if evi % 2 == 0:
                            sc.op("dve", lambda e, ev=ev, pp=pp, scale=scale: e.tensor_scalar(
                                out=ev[:], in0=pp[:], scalar1=float(scale), scalar2=None, op0=ALU.mult),
                                reads=[B_pp], writes=[B_ev])
                        else:
                            sc.op("act", lambda e, ev=ev, pp=pp, scale=scale: e.activation(
                                out=ev[:], in_=pp[:], func=AF.Copy, scale=float(scale)),
                                reads=[B_pp], writes=[B_ev])
                        evi += 1
                        sc.dma("sp", [(dst(ti), ev[:])], reads=[B_ev], writes=[B_qkv[ti]], key=("ev", id(B_ev)))
                    for j in range(4):
                        r0 = ti * TT + j * 128
                        pp, B_pp = ps.next()
                        for k in range(8):
                            sc.op("pe", lambda e, pp=pp, k=k, j=j, ht=ht: e.matmul(
                                pp[:], lhsT=ht[:, k, j * 128:(j + 1) * 128], rhs=wq[:, k, C_VB:C_VB + 512],
                                start=(k == 0), stop=(k == 7)), reads=[B_wq[k]] + B_ht, writes=[B_pp])
                        ev, B_ev = evp.next()
                        sc.op("dve", lambda e, ev=ev, pp=pp: e.tensor_copy(out=ev[:], in_=pp[:]),
                              reads=[B_pp], writes=[B_ev])
                        sc.dma("sp", [(vb_d[r0:r0 + 128, :], ev[:])], reads=[B_ev], writes=[B_qkv[ti]],
                               key=("ev", id(B_ev)))
                        pp, B_pp = ps.next()
                        for k in range(8):
                            sc.op("pe", lambda e, pp=pp, k=k, j=j, ht=ht: e.matmul(
                                pp[:, 0:128], lhsT=ht[:, k, j * 128:(j + 1) * 128], rhs=wq[:, k, C_VA:C_VA + 128],
                                start=(k == 0), stop=(k == 7)), reads=[B_wq[k]] + B_ht, writes=[B_pp])
                        ev, B_ev = evp.next()
                        sc.op("act", lambda e, ev=ev, pp=pp: e.copy(out=ev[:, 0:128], in_=pp[:, 0:128]),
                              reads=[B_pp], writes=[B_ev])
                        sc.dma("sp", [(va_d[r0:r0 + 128, :], ev[:, 0:128])], reads=[B_ev], writes=[B_qkv[ti]],
                               key=("ev", id(B_ev)))
            sc.barrier()

        def phase_swa():
            with ExitStack() as pes:
                kt = sbt(pes, "kta", [128, S], BF16)
                qt = sbt(pes, "qta", [128, 4, S], BF16)
                vv = sbt(pes, "vva", [128, NB, 128], BF16)
                B_in = Buf("swa_in")
                var = va_d.rearrange("(n p) c -> p n c", p=128)
                sc.dma("sp", [(kt[:], kta_d)] + [(qt[:, g, :], qta_d[g]) for g in range(4)]
                       + [(vv[:, n0:min(NB, n0 + 8), :], var[:, n0:min(NB, n0 + 8), :]) for n0 in range(0, NB, 8)],
                       reads=B_qkv, writes=[B_in], key="swa_in")
                bm = sbt(pes, "bm", [128, 8, 256], F32)
                mk = sbt(pes, "mk", [128, 256], F32)
                sk = sbt(pes, "sk", [128, 8], F32)
                B_bm = Buf("bm")
                B_sk = Buf("sk")
                sc.dma("sp", [(bm[:], bias_d), (mk[:], maskc_d)], writes=[B_bm], key="bm")
                sc.dma("sp", [(sk[:], sinks_d.partition_broadcast(128))], writes=[B_sk], key="sk")
                for h in range(8):
                    sc.op("dve", lambda e, h=h: e.tensor_tensor(out=bm[:, h, :], in0=bm[:, h, :], in1=mk[:], op=ALU.add),
                          reads=[B_bm], writes=[B_bm])
                psc = Pool(nc, pes, "psc", 2, [128, 4, 256], F32, psum=True)
                ppt = Pool(nc, pes, "ppt", 2, [128, 8, 128], BF16, psum=True)
                pso = Pool(nc, pes, "pso", 2, [128, 4, 128], F32, psum=True)
                scs = Pool(nc, pes, "scs", 2, [128, 8, 256], F32)
                pbf = Pool(nc, pes, "pbf", 2, [128, 8, 256], BF16)
                ptb = Pool(nc, pes, "ptb", 2, [128, 16, 128], BF16)
                sm = Pool(nc, pes, "sm", 4, [128, 5, 8], F32)
                osb = Pool(nc, pes, "osb", 3, [128, 4, 128], BF16)
                for n in range(NB):
                    k0 = 0 if n > 0 else 128
                    kw = 256 - k0
                    ks = (n - 1) * 128 + k0
                    ss, B_ss = scs.next()
                    pcs = [psc.next(), psc.next()]
                    for g in range(4):
                        for kv in range(2):
                            pc, B_pc = pcs[kv]
                            sc.op("pe", lambda e, pc=pc, g=g, kv=kv, n=n, ks=ks, kw=kw, k0=k0: e.matmul(
                                pc[:, g, k0:256], lhsT=qt[kv * 64:(kv + 1) * 64, g, n * 128:(n + 1) * 128],
                                rhs=kt[kv * 64:(kv + 1) * 64, ks:ks + kw], start=True, stop=True),
                                reads=[B_in], writes=[B_pc])
                    for kv in range(2):
                        pc, B_pc = pcs[kv]
                        sc.op("dve", lambda e, pc=pc, ss=ss, kv=kv, k0=k0: e.tensor_tensor(
                            out=ss[:, kv::2, k0:256], in0=pc[:, :, k0:256], in1=bm[:, kv::2, k0:256], op=ALU.add),
                            reads=[B_pc, B_bm], writes=[B_ss])
                    st, B_st = sm.next()
                    sc.op("dve", lambda e, ss=ss, st=st, k0=k0: e.tensor_reduce(
                        out=st[:, 0, :], in_=ss[:, :, k0:256], axis=AX.X, op=ALU.max),
                        reads=[B_ss], writes=[B_st])
                    sc.op("dve", lambda e, st=st: e.tensor_tensor(out=st[:, 0, :], in0=st[:, 0, :], in1=sk[:], op=ALU.max),
                          reads=[B_st, B_sk], writes=[B_st])
                    sc.op("dve", lambda e, st=st: e.tensor_scalar(out=st[:, 1, :], in0=st[:, 0, :], scalar1=-1.0,
                                                                   scalar2=None, op0=ALU.mult),
                          reads=[B_st], writes=[B_st])
                    sc.op("dve", lambda e, st=st: e.tensor_tensor(out=st[:, 2, :], in0=sk[:], in1=st[:, 1, :], op=ALU.add),
                          reads=[B_st, B_sk], writes=[B_st])
                    pb, B_pb = pbf.next()
                    for h in range(8):
                        sc.op("act", lambda e, pb=pb, ss=ss, st=st, h=h, k0=k0: e.activation(
                            out=pb[:, h, k0:256], in_=ss[:, h, k0:256], func=AF.Exp, bias=st[:, 1, h:h + 1],
                            accum_out=st[:, 3, h:h + 1]), reads=[B_ss, B_st], writes=[B_pb, B_st])
                    sc.op("act", lambda e, st=st: e.activation(out=st[:, 2, :], in_=st[:, 2, :], func=AF.Exp),
                          reads=[B_st], writes=[B_st])
                    sc.op("dve", lambda e, st=st: e.tensor_tensor(out=st[:, 4, :], in0=st[:, 3, :], in1=st[:, 2, :], op=ALU.add),
                          reads=[B_st], writes=[B_st])
                    sc.op("dve", lambda e, st=st: e.reciprocal(out=st[:, 4, :], in_=st[:, 4, :]),
                          reads=[B_st], writes=[B_st])
                    sc.op("dve", lambda e, pb=pb, st=st, k0=k0, kw=kw: e.tensor_tensor(
                        out=pb[:, :, k0:256], in0=pb[:, :, k0:256],
                        in1=st[:, 4, :].unsqueeze(2).to_broadcast([128, 8, kw]), op=ALU.mult),
                        reads=[B_pb, B_st], writes=[B_pb])
                    nkb = kw // 128
                    pt, B_pt = ptb.next()
                    for kb in range(nkb):
                        pp, B_pp = ppt.next()
                        for h in range(8):
                            sc.op("pe", lambda e, pp=pp, pb=pb, h=h, kb=kb, k0=k0: e.transpose(
                                out=pp[:, h, :], in_=pb[:, h, k0 + kb * 128:k0 + (kb + 1) * 128], identity=ident),
                                reads=[B_pb, B_cst], writes=[B_pp])
                        if kb == 0:
                            sc.op("dve", lambda e, pp=pp, pt=pt, kb=kb: e.tensor_copy(
                                out=pt[:, kb * 8:(kb + 1) * 8, :], in_=pp[:]), reads=[B_pp], writes=[B_pt])
                        else:
                            sc.op("act", lambda e, pp=pp, pt=pt, kb=kb: e.copy(
                                out=pt[:, kb * 8:(kb + 1) * 8, :], in_=pp[:]), reads=[B_pp], writes=[B_pt])
                    po, B_po = pso.next()
                    for h in range(8):
                        g, kv = h % 4, h // 4
                        hi = 2 * g + kv
                        c, half = h // 2, h % 2
                        for kb in range(nkb):
                            blk = n - (nkb - 1) + kb
                            sc.op("pe", lambda e, po=po, pt=pt, c=c, half=half, kv=kv, hi=hi, kb=kb, blk=blk, nkb=nkb: e.matmul(
                                po[half * 64:(half + 1) * 64, c, :], lhsT=vv[:, blk, kv * 64:(kv + 1) * 64],
                                rhs=pt[:, kb * 8 + hi, :], start=(kb == 0), stop=(kb == nkb - 1)),
                                reads=[B_pt, B_in], writes=[B_po])
                    ob, B_ob = osb.next()
                    sc.op("dve", lambda e, ob=ob, po=po: e.tensor_copy(out=ob[:], in_=po[:]), reads=[B_po], writes=[B_ob])
                    sc.dma("sp", [(ota_d.rearrange("(c p) s -> p c s", p=128)[:, :, n * 128:(n + 1) * 128], ob[:])],
                           reads=[B_ob], writes=[B_ota[n // 4]], key=("osb", id(B_ob)))
            sc.barrier()

        def phase_sb():
            with ExitStack() as pes:
                ktp = Pool(nc, pes, "ktb", 2, [128, S], BF16)
                qtp = Pool(nc, pes, "qtb", 2, [128, S], BF16)
                vvp = Pool(nc, pes, "vvb", 2, [128, NB, 128], BF16)
                zp = Pool(nc, pes, "zp", 3, [128, 2, 512], F32, psum=True)
                op_ = Pool(nc, pes, "op", 2, [128, 512], F32, psum=True)
                Ep = Pool(nc, pes, "E", 2, [128, 2, 512], F32)
                Lp = Pool(nc, pes, "L", 3, [128, 2, 512], BF16)
                Ap = Pool(nc, pes, "A", 2, [128, 2, 512], BF16)
                R32 = sbt(pes, "R32", [128, 2, 512], F32)
                B_R32 = Buf("R32")
                Rbp = Pool(nc, pes, "Rb", 2, [128, 2, 512], BF16)
                oev = Pool(nc, pes, "oev", 2, [128, 512], BF16)
                steps = []
                pair_in = {}
                for p in range(4):
                    for i in range(NT):
                        nk = 4 * i + 4
                        for si, kj in enumerate(range(nk - 1, -1, -1)):
                            steps.append(dict(p=p, i=i, kj=kj, first=(si == 0), last=(kj == 0),
                                              c0=max(0, (kj - 4 * i)) * 128, diag=(kj >= 4 * i)))

                def load_pair(p):
                    kt, B_kt = ktp.next()
                    qt, B_qt = qtp.next()
                    vv, B_vv = vvp.next()
                    sc.dma("sp", [(kt[:], ktb_d[p * 128:(p + 1) * 128, :])], reads=B_qkv, writes=[B_kt], key=("sbk", id(B_kt)))
                    sc.dma("sp", [(qt[:], qtb_d[p * 128:(p + 1) * 128, :])], reads=B_qkv, writes=[B_qt], key=("sbq", id(B_qt)))
                    vbr = vb_d[:, p * 128:(p + 1) * 128].rearrange("(n p) c -> p n c", p=128)
                    sc.dma("sp", [(vv[:, n0:min(NB, n0 + 8), :], vbr[:, n0:min(NB, n0 + 8), :]) for n0 in range(0, NB, 8)],
                           reads=B_qkv, writes=[B_vv], key=("sbv", id(B_vv)))
                    pair_in[p] = (kt, B_kt, qt, B_qt, vv, B_vv)

                load_pair(0)
                state = {}

                def stage0(s):
                    p, i, kj, c0 = s["p"], s["i"], s["kj"], s["c0"]
                    if s["first"] and i == 0 and p + 1 < 4:
                        load_pair(p + 1)
                    kt, B_kt, qt, B_qt, vv, B_vv = pair_in[p]
                    z, B_z = zp.next()
                    s["z"], s["B_z"] = z, B_z
                    for hh in range(2):
                        sc.op("pe", lambda e, z=z, hh=hh, kj=kj, i=i, c0=c0, kt=kt, qt=qt: e.matmul(
                            z[:, hh, c0:512], lhsT=kt[hh * 64:(hh + 1) * 64, kj * 128:(kj + 1) * 128],
                            rhs=qt[hh * 64:(hh + 1) * 64, i * 512 + c0:(i + 1) * 512],
                            start=True, stop=False, skip_group_check=True),
                            reads=[B_kt, B_qt], writes=[B_z])
                    if s["diag"]:
                        for hh in range(2):
                            sc.op("pe", lambda e, z=z, hh=hh, c0=c0: e.matmul(
                                z[:, hh, c0:c0 + 128], lhsT=ident, rhs=dmask, start=False, stop=False,
                                skip_group_check=True), reads=[B_cst], writes=[B_z])

                def stage1(s):
                    z, B_z, c0 = s["z"], s["B_z"], s["c0"]
                    E, B_E = Ep.next()
                    L, B_L = Lp.next()
                    s["L"], s["B_L"] = L, B_L
                    sc.op("act", lambda e, E=E, z=z, c0=c0: e.activation(out=E[:, :, c0:512], in_=z[:, :, c0:512], func=AF.Exp),
                          reads=[B_z], writes=[B_E])
                    sc.op("act", lambda e, E=E, L=L, c0=c0: e.activation(out=L[:, :, c0:512], in_=E[:, :, c0:512], func=AF.Ln, bias=1.0),
                          reads=[B_E], writes=[B_L])

                def stage2(s):
                    p, i, kj, c0 = s["p"], s["i"], s["kj"], s["c0"]
                    kt, B_kt, qt, B_qt, vv, B_vv = pair_in[p]
                    z, B_z, L, B_L = s["z"], s["B_z"], s["L"], s["B_L"]
                    for hh in range(2):
                        sc.op("pe", lambda e, z=z, hh=hh, c0=c0, L=L: e.matmul(
                            z[:, hh, c0:512], lhsT=ntri, rhs=L[:, hh, c0:512], start=False, stop=False,
                            skip_group_check=True), reads=[B_L, B_cst], writes=[B_z])
                    if not s["first"]:
                        Rb, B_Rb = state["Rb"]
                        for hh in range(2):
                            sc.op("pe", lambda e, z=z, hh=hh, c0=c0, Rb=Rb: e.matmul(
                                z[:, hh, c0:512], lhsT=nones, rhs=Rb[:, hh, c0:512], start=False, stop=False,
                                skip_group_check=True), reads=[B_Rb, B_cst], writes=[B_z])
                    A, B_A = Ap.next()
                    sc.op("act", lambda e, A=A, z=z, c0=c0: e.activation(out=A[:, :, c0:512], in_=z[:, :, c0:512], func=AF.Exp),
                          reads=[B_z], writes=[B_A])
                    if s["first"]:
                        state["o"] = op_.next()
                    o, B_o = state["o"]
                    for hh in range(2):
                        sc.op("pe", lambda e, o=o, hh=hh, c0=c0, A=A, vv=vv, kj=kj, first=s["first"]: e.matmul(
                            o[hh * 64:(hh + 1) * 64, c0:512], lhsT=vv[:, kj, hh * 64:(hh + 1) * 64],
                            rhs=A[:, hh, c0:512], start=first, stop=False, skip_group_check=True),
                            reads=[B_A, B_vv], writes=[B_o])
                    if not s["last"]:
                        if s["first"]:
                            sc.op("pool", lambda e: e.memset(R32[:], 0.0), writes=[B_R32])
                        sc.op("pool", lambda e, L=L, c0=c0: e.tensor_tensor(
                            out=R32[:, :, c0:512], in0=R32[:, :, c0:512], in1=L[:, :, c0:512], op=ALU.add),
                            reads=[B_L, B_R32], writes=[B_R32])
                        Rb, B_Rb = Rbp.next()
                        sc.op("dve", lambda e, Rb=Rb: e.tensor_copy(out=Rb[:], in_=R32[:]), reads=[B_R32], writes=[B_Rb])
                        state["Rb"] = (Rb, B_Rb)
                    else:
                        ob, B_ob = oev.next()
                        sc.op("dve", lambda e, ob=ob, o=o: e.tensor_copy(out=ob[:], in_=o[:]), reads=[B_o], writes=[B_ob])
                        sc.dma("sp", [(otb_d[p * 128:(p + 1) * 128, i * 512:(i + 1) * 512], ob[:])], reads=[B_ob],
                               writes=[B_otb[i]], key=("oev", id(B_ob)))

                n = len(steps)
                for t in range(n + 2):
                    if t < n:
                        stage0(steps[t])
                    if 0 <= t - 1 < n:
                        stage1(steps[t - 1])
                    if 0 <= t - 2 < n:
                        stage2(steps[t - 2])
            sc.barrier()

        def phase_mix():
            with ExitStack() as pes:
                wg = sbt(pes, "wg", [128, 8, 2048], BF16)
                wba = sbt(pes, "wba", [128, 4, D], BF16)
                wbb = sbt(pes, "wbb", [128, 4, D], BF16)
                wo = sbt(pes, "wo", [128, 8, D], BF16)
                B_wg = [Buf("wg%d" % k) for k in range(8)]
                B_wba = [Buf("wba%d" % k) for k in range(4)]
                B_wbb = [Buf("wbb%d" % k) for k in range(4)]
                B_wo = [Buf("wo%d" % k) for k in range(8)]
                stgp = Pool(nc, pes, "stg", 2, [128, STGW], F32)
                for k in range(8):
                    load_cast(stgp, wg, lambda a, b, k=k: wg[:, k, a:b], B_wg[k],
                              win_d[k * 128:(k + 1) * 128, C_GA:INW], 2048, gain=gcol[:, 1, k:k + 1])
                for k in range(4):
                    load_cast(stgp, wba, lambda a, b, k=k: wba[:, k, a:b], B_wba[k], wba_d[k * 128:(k + 1) * 128, :], D)
                    load_cast(stgp, wbb, lambda a, b, k=k: wbb[:, k, a:b], B_wbb[k], wbb_d[k * 128:(k + 1) * 128, :], D)
                for k in range(8):
                    load_cast(stgp, wo, lambda a, b, k=k: wo[:, k, a:b], B_wo[k], wout_d[k * 128:(k + 1) * 128, :], D)
                nt = NormT(pes, nht=2)
                ps = Pool(nc, pes, "ps", 6, [128, 512], F32, psum=True)
                otap = Pool(nc, pes, "ota", 2, [128, 4, TT], BF16)
                otbp = Pool(nc, pes, "otb", 2, [128, 4, TT], BF16)
                MT = sbt(pes, "MT", [128, 8, TT], BF16)
                B_MT = [Buf("MT%d" % c) for c in range(8)]
                sgp = Pool(nc, pes, "sgm", 4, [128, TT], F32)
                mp = Pool(nc, pes, "mm", 4, [128, TT], F32)
                xres = Pool(nc, pes, "xres", 3, [128, D], F32)
                nxt = nt.run(x1_d, 0, B_x1[0])
                for ti in range(NT):
                    ht, B_ht = nxt
                    if ti + 1 < NT:
                        nxt = nt.run(x1_d, ti + 1, B_x1[ti + 1])
                    oa, B_oa = otap.next()
                    ob, B_ob = otbp.next()
                    sc.dma("sp", [(oa[:], ota_d.rearrange("(c p) s -> p c s", p=128)[:, :, ti * TT:(ti + 1) * TT])],
                           reads=[B_ota[ti]], writes=[B_oa], key=("ota", id(B_oa)))
                    sc.dma("sp", [(ob[:], otb_d.rearrange("(c p) s -> p c s", p=128)[:, :, ti * TT:(ti + 1) * TT])],
                           reads=[B_otb[ti]], writes=[B_ob], key=("otb", id(B_ob)))
                    for cc in range(8):
                        ms = []
                        for br, (wbr, B_wbr, ot, B_ot, goff) in enumerate(((wba, B_wba, oa, B_oa, 0), (wbb, B_wbb, ob, B_ob, 1024))):
                            pg, B_pg = ps.next()
                            pb, B_pb = ps.next()
                            for k in range(8):
                                sc.op("pe", lambda e, pg=pg, k=k, cc=cc, goff=goff, ht=ht: e.matmul(
                                    pg[:], lhsT=wg[:, k, goff + cc * 128:goff + (cc + 1) * 128], rhs=ht[:, k, :],
                                    start=(k == 0), stop=(k == 7)), reads=[B_wg[k]] + B_ht, writes=[B_pg])
                            for k in range(4):
                                sc.op("pe", lambda e, pb=pb, k=k, cc=cc, wbr=wbr, ot=ot: e.matmul(
                                    pb[:], lhsT=wbr[:, k, cc * 128:(cc + 1) * 128], rhs=ot[:, k, :],
                                    start=(k == 0), stop=(k == 3)), reads=[B_wbr[k], B_ot], writes=[B_pb])
                            sg, B_sg = sgp.next()
                            sc.op("act", lambda e, sg=sg, pg=pg: e.activation(out=sg[:], in_=pg[:], func=AF.Sigmoid),
                                  reads=[B_pg], writes=[B_sg])
                            m, B_m = mp.next()
                            sc.op("dve", lambda e, m=m, sg=sg, pb=pb: e.tensor_tensor(out=m[:], in0=sg[:], in1=pb[:], op=ALU.mult),
                                  reads=[B_sg, B_pb], writes=[B_m])
                            ms.append((m, B_m))
                        sc.op("pool", lambda e, cc=cc, ms=ms: e.tensor_tensor(
                            out=MT[:, cc, :], in0=ms[0][0][:], in1=ms[1][0][:], op=ALU.add),
                            reads=[ms[0][1], ms[1][1]], writes=[B_MT[cc]])
                    for j in range(4):
                        xr, B_xr = xres.next()
                        r0 = ti * TT + j * 128
                        sc.dma("sp", [(xr[:], x1_d[r0:r0 + 128, :])], reads=[B_x1[ti]], writes=[B_xr], key=("xres", id(B_xr)))
                        for half in range(2):
                            po, B_po = ps.next()
                            for cc in range(8):
                                sc.op("pe", lambda e, po=po, cc=cc, j=j, half=half: e.matmul(
                                    po[:], lhsT=MT[:, cc, j * 128:(j + 1) * 128], rhs=wo[:, cc, half * 512:(half + 1) * 512],
                                    start=(cc == 0), stop=(cc == 7)), reads=[B_MT[cc], B_wo[cc]], writes=[B_po])
                            sc.op("dve", lambda e, po=po, xr=xr, half=half: e.tensor_tensor(
                                out=xr[:, half * 512:(half + 1) * 512], in0=po[:], in1=xr[:, half * 512:(half + 1) * 512], op=ALU.add),
                                reads=[B_po, B_xr], writes=[B_xr])
                        sc.dma("sp", [(x2_d[r0:r0 + 128, :], xr[:])], reads=[B_xr], writes=[B_x2[ti]], key=("xres_st", id(B_xr)))
            sc.barrier()

        sc.barrier()
        if "ffn1" in phases:
            phase_ffn(0, x_d, None, x1_d, B_x1, final=False)
        if "proj" in phases:
            phase_proj()
        if "swa" in phases:
            phase_swa()
        if "sb" in phases:
            phase_sb()
        if "mix" in phases:
            phase_mix()
        if "ffn2" in phases:
            phase_ffn(1, x2_d, B_x2, out_d, None, final=True)
        sc.emit()
    return nc


def _rel_bucket_np(dist):
    max_exact = 16
    d = np.maximum(dist, 1).astype(np.float32)
    large = max_exact + (np.log(d / max_exact) / np.float32(np.log(128 / max_exact)) * (32 - max_exact)).astype(np.int32)
    large = np.minimum(large, 31)
    return np.where(dist < max_exact, dist, large)


def _bucket_table():
    import jax
    import jax.numpy as jnp
    qi = np.arange(128)[:, None] + 128
    kj = np.arange(256)[None, :]
    dist = qi - kj
    band = (dist >= 0) & (dist < 128)
    with jax.default_device(jax.devices("cpu")[0]):
        dj = jnp.maximum(jnp.asarray(dist), 0)
        max_exact = 16
        d = jnp.maximum(dj, 1).astype(jnp.float32)
        large = max_exact + (jnp.log(d / max_exact) / np.log(128 / max_exact) * (32 - max_exact)).astype(jnp.int32)
        large = jnp.minimum(large, 31)
        bucket = np.asarray(jnp.where(dj < max_exact, dj, large))
    return bucket, band


def _consts():
    c = np.zeros((128, 512), np.float32)
    c[:, 0:128] = np.eye(128)
    j = np.arange(128)[:, None]
    s = np.arange(128)[None, :]
    c[:, 128:256] = np.where(j >= s, -1.0, 0.0)
    c[:, 256:384] = -1.0
    c[:, 384:512] = np.where(j < s, 0.0, MASKB)
    return c.astype(ml_dtypes.bfloat16)


_PROG_CACHE = {}


def _prepare_shared(inp, S):
    f = lambda a: np.ascontiguousarray(np.asarray(a, dtype=np.float32))
    bucket, band = _bucket_table()
    rb = f(inp["rel_bias"])
    bias = rb[bucket]
    order = [g + 4 * kv for g in range(4) for kv in range(2)]
    bias = np.ascontiguousarray(bias.transpose(0, 2, 1)[:, order, :])
    mask = np.where(band, 0.0, NEG).astype(np.float32)
    w_in = f(inp["w_in"])[0]
    qcols = []
    for g in range(4):
        for kv in range(2):
            h = g + 4 * kv
            qcols.extend(range(h * 64, (h + 1) * 64))
    w_in = np.ascontiguousarray(np.concatenate([w_in[:, qcols], w_in[:, 512:]], axis=1))
    sinks = f(inp["swa_sinks"])[0][order]
    shared = {
        "ffn1_w1": f(inp["ffn1_w1"])[0], "ffn1_w3": f(inp["ffn1_w3"])[0], "ffn1_w2": f(inp["ffn1_w2"])[0],
        "ffn2_w1": f(inp["ffn2_w1"])[0], "ffn2_w3": f(inp["ffn2_w3"])[0], "ffn2_w2": f(inp["ffn2_w2"])[0],
        "gains": np.ascontiguousarray(np.stack([f(inp["norm_ffn1"])[0].reshape(8, 128).T,
                                                f(inp["norm_mix"])[0].reshape(8, 128).T,
                                                f(inp["norm_ffn2"])[0].reshape(8, 128).T], axis=1)),
        "norm_final": f(inp["norm_final"]),
        "w_in": w_in, "swa_sinks": np.ascontiguousarray(sinks), "swa_bias": bias, "swa_mask": mask,
        "w_branch_swa": f(inp["w_branch_swa"])[0], "w_branch_sb": f(inp["w_branch_sb"])[0],
        "w_out": f(inp["w_out"])[0], "consts": _consts(),
    }
    return shared


def kernel(**inputs):
    x = np.asarray(inputs["x"], dtype=np.float32)
    B, S, _ = x.shape
    if S not in _PROG_CACHE:
        _PROG_CACHE[S] = build_program(S)
    nc = _PROG_CACHE[S]
    shared = _prepare_shared(inputs, S)
    in_maps = []
    for b in range(B):
        m = dict(shared)
        m["x"] = np.ascontiguousarray(x[b])
        in_maps.append(m)
    res = run_bass_kernel_spmd(nc, in_maps, core_ids=list(range(B)))
    return np.stack([np.asarray(r["out"], dtype=np.float32) for r in res.results], axis=0)
```

```python
from contextlib import ExitStack

import numpy as np
import ml_dtypes

import concourse.bass as bass
import concourse.mybir as mybir
from concourse.bass_utils import run_bass_kernel_spmd

F32 = mybir.dt.float32
BF16 = mybir.dt.bfloat16
AF = mybir.ActivationFunctionType
ALU = mybir.AluOpType
AX = mybir.AxisListType

D = 1024
DFF = 2816
NFF = DFF // 128
INW = 4352
EPS = 1e-6
NEG = -1e30
TT = 512
MASKB = -30000.0
STGW = 704

C_QA, C_KA, C_VA, C_QB, C_KB, C_VB, C_GA, C_GB = 0, 512, 640, 768, 1280, 1792, 2304, 3328


class Buf:
    __slots__ = ("name", "lw", "rd", "dmard")

    def __init__(self, name):
        self.name = name
        self.lw = None
        self.rd = {}
        self.dmard = []


class Op:
    __slots__ = ("eng", "fn", "deps", "signal", "sem", "count", "is_dma", "pairs")

    def __init__(self, eng, fn, is_dma=False):
        self.eng = eng
        self.fn = fn
        self.deps = []
        self.signal = False
        self.sem = None
        self.count = 0
        self.is_dma = is_dma
        self.pairs = None


SEM_LIMIT = 30000


class Sched:
    ENGS = ("pe", "act", "dve", "pool", "sp")

    def __init__(self, nc, es):
        self.nc = nc
        self.es = es
        self.ops = {e: [] for e in self.ENGS}
        self.dma_sems = {}
        self.barrier_deps = {e: None for e in self.ENGS}

    def _newsem(self, name):
        return self.es.enter_context(self.nc.semaphore(name))

    def _add(self, op, reads, writes):
        deps = []
        bd = self.barrier_deps[op.eng]
        if bd is not None:
            deps.extend(bd)
            self.barrier_deps[op.eng] = None
        for b in reads:
            w = b.lw
            if w is not None:
                if not (w.eng == op.eng == "pe" and not w.is_dma and not op.is_dma):
                    deps.append(w)
        for b in writes:
            w = b.lw
            if w is not None:
                if w.is_dma or op.is_dma or w.eng != op.eng:
                    deps.append(w)
            for e, r in b.rd.items():
                if op.is_dma or e != op.eng:
                    deps.append(r)
            deps.extend(b.dmard)
        for b in reads:
            if op.is_dma:
                b.dmard.append(op)
            else:
                b.rd[op.eng] = op
        for b in writes:
            b.lw = op
            b.rd = {}
            b.dmard = []
        seen = set()
        for d in deps:
            if id(d) not in seen and d is not op:
                seen.add(id(d))
                op.deps.append(d)
                d.signal = True
        self.ops[op.eng].append(op)
        return op

    def op(self, eng, fn, reads=(), writes=()):
        return self._add(Op(eng, fn), reads, writes)

    def dma(self, queue, pairs, reads=(), writes=(), key=None):
        op = Op(queue, None, is_dma=True)
        op.pairs = pairs
        op.signal = True
        if key not in self.dma_sems:
            self.dma_sems[key] = [self._newsem("d%d" % len(self.dma_sems)), 0, None]
        ent = self.dma_sems[key]
        ent[1] += 16 * len(pairs)
        ent[2] = op
        op.sem = ent[0]
        op.count = ent[1]
        return self._add(op, reads, writes)

    def barrier(self):
        prev = []
        for e in self.ENGS:
            for op in reversed(self.ops[e]):
                if not op.is_dma:
                    prev.append(op)
                    break
        for ent in self.dma_sems.values():
            if ent[2] is not None:
                prev.append(ent[2])
        for e in self.ENGS:
            self.barrier_deps[e] = list(prev)

    def emit(self):
        nc = self.nc
        eng_sems = {}
        for e in self.ENGS:
            n = 0
            for op in self.ops[e]:
                if op.is_dma or not op.signal:
                    continue
                k = n // SEM_LIMIT
                if (e, k) not in eng_sems:
                    eng_sems[(e, k)] = self._newsem("e_%s%d" % (e, k))
                op.sem = eng_sems[(e, k)]
                op.count = n % SEM_LIMIT + 1
                n += 1
        final_waits = [(ent[0], ent[1]) for ent in self.dma_sems.values()]

        def run(e, eng, final=False):
            waited = {}
            for op in self.ops[e]:
                for d in op.deps:
                    k = id(d.sem)
                    if waited.get(k, 0) < d.count:
                        eng.wait_ge(d.sem, d.count)
                        waited[k] = d.count
                if op.is_dma:
                    for (o, i) in op.pairs:
                        eng.dma_start(out=o, in_=i).then_inc(op.sem, 16)
                else:
                    ins = op.fn(eng)
                    if op.signal:
                        ins.then_inc(op.sem, 1)
            if final:
                for (h, c) in final_waits:
                    if waited.get(id(h), 0) < c:
                        eng.wait_ge(h, c)

        with nc.Block() as block:
            @block.sync
            def _(sync):
                run("sp", sync, final=True)

            @block.tensor
            def _(tensor):
                run("pe", tensor)

            @block.scalar
            def _(scalar):
                run("act", scalar)

            @block.vector
            def _(vector):
                run("dve", vector)

            @block.gpsimd
            def _(gpsimd):
                run("pool", gpsimd)


class Pool:
    uid = [0]

    def __init__(self, nc, es, name, n, shape, dtype, psum=False):
        self.tiles = []
        for i in range(n):
            Pool.uid[0] += 1
            nm = "%s%d_%d" % (name, i, Pool.uid[0])
            if psum:
                t = es.enter_context(nc.psum_tensor(nm, list(shape), dtype))
            else:
                t = es.enter_context(nc.sbuf_tensor(nm, list(shape), dtype))
            self.tiles.append((t, Buf(nm)))
        self.i = 0

    def next(self):
        t = self.tiles[self.i % len(self.tiles)]
        self.i += 1
        return t


def build_program(S, phases=("ffn1", "proj", "swa", "sb", "mix", "ffn2"), debug=False):
    NT = S // TT
    NB = S // 128
    nc = bass.Bass("TRN2", target_bir_lowering=False)
    dk = "ExternalOutput" if debug else "Internal"

    def din(name, shape, dt=F32):
        return nc.dram_tensor(name, list(shape), dt, kind="ExternalInput").ap()

    def dscr(name, shape, dt):
        return nc.dram_tensor(name, list(shape), dt, kind=dk).ap()

    x_d = din("x", [S, D])
    w1_d = [din("ffn1_w1", [D, DFF]), din("ffn2_w1", [D, DFF])]
    w3_d = [din("ffn1_w3", [D, DFF]), din("ffn2_w3", [D, DFF])]
    w2_d = [din("ffn1_w2", [DFF, D]), din("ffn2_w2", [DFF, D])]
    gains_d = din("gains", [128, 3, 8])
    gfin_d = din("norm_final", [D])
    win_d = din("w_in", [D, INW])
    sinks_d = din("swa_sinks", [8])
    bias_d = din("swa_bias", [128, 8, 256])
    maskc_d = din("swa_mask", [128, 256])
    wba_d = din("w_branch_swa", [512, D])
    wbb_d = din("w_branch_sb", [512, D])
    wout_d = din("w_out", [D, D])
    cst_d = din("consts", [128, 4 * 128], BF16)
    out_d = nc.dram_tensor("out", [S, D], F32, kind="ExternalOutput").ap()

    x1_d = dscr("x1", [S, D], F32)
    x2_d = dscr("x2", [S, D], F32)
    qta_d = dscr("qta", [4, 128, S], BF16)
    kta_d = dscr("kta", [128, S], BF16)
    va_d = dscr("va", [S, 128], BF16)
    qtb_d = dscr("qtb", [512, S], BF16)
    ktb_d = dscr("ktb", [512, S], BF16)
    vb_d = dscr("vb", [S, 512], BF16)
    ota_d = dscr("ota", [512, S], BF16)
    otb_d = dscr("otb", [512, S], BF16)

    B_x1 = [Buf("x1_%d" % i) for i in range(NT)]
    B_x2 = [Buf("x2_%d" % i) for i in range(NT)]
    B_qkv = [Buf("qkv_%d" % i) for i in range(NT)]
    B_ota = [Buf("ota_%d" % i) for i in range(NT)]
    B_otb = [Buf("otb_%d" % i) for i in range(NT)]

    with ExitStack() as es:
        sc = Sched(nc, es)

        def sbt(stack, name, shape, dt):
            Pool.uid[0] += 1
            return stack.enter_context(nc.sbuf_tensor("%s_%d" % (name, Pool.uid[0]), list(shape), dt))

        cst = sbt(es, "cst", [128, 512], BF16)
        B_cst = Buf("cst")
        sc.dma("sp", [(cst[:], cst_d)], writes=[B_cst], key="cst")
        ident = cst[:, 0:128]
        ntri = cst[:, 128:256]
        nones = cst[:, 256:384]
        dmask = cst[:, 384:512]

        gcol = sbt(es, "gcol", [128, 3, 8], F32)
        B_gcol = Buf("gcol")

        sc.dma("sp", [(gcol[:], gains_d)], writes=[B_gcol], key="gcol")

        def load_cast(stgp, dst_t, sel, B_dst, src_rows, ncols, gain=None):
            c0 = 0
            while c0 < ncols:
                c1 = min(ncols, c0 + STGW)
                st, B_st = stgp.next()
                w = c1 - c0
                sc.dma("sp", [(st[:, 0:w], src_rows[:, c0:c1])], writes=[B_st], key=("stg", id(B_st)))
                o = sel(c0, c1)
                if gain is None:
                    sc.op("dve", lambda e, o=o, i=st[:, 0:w]: e.tensor_copy(out=o, in_=i),
                          reads=[B_st], writes=[B_dst])
                else:
                    sc.op("dve", lambda e, o=o, i=st[:, 0:w], g=gain:
                          e.tensor_scalar(out=o, in0=i, scalar1=g, scalar2=None, op0=ALU.mult),
                          reads=[B_st, B_gcol], writes=[B_dst])
                c0 = c1

        def rstd_ops(st, B_st):
            sc.op("dve", lambda e, st=st: e.tensor_scalar(
                out=st[:, 1:2], in0=st[:, 0:1], scalar1=float(D * EPS), scalar2=None, op0=ALU.add),
                reads=[B_st], writes=[B_st])
            sc.op("act", lambda e, st=st: e.activation(out=st[:, 1:2], in_=st[:, 1:2], func=AF.Sqrt),
                  reads=[B_st], writes=[B_st])
            sc.op("dve", lambda e, st=st: e.reciprocal(out=st[:, 1:2], in_=st[:, 1:2]),
                  reads=[B_st], writes=[B_st])

        class NormT:
            def __init__(self, stack, nht=1, nxin=2):
                self.nxin = nxin
                self.xin = Pool(nc, stack, "xin", nxin, [128, D], F32)
                self.hrow = Pool(nc, stack, "hrow", 2, [128, D], BF16)
                self.sqj = sbt(stack, "sqj", [128, D], BF16)
                self.B_sqj = Buf("sqj")
                self.stat = Pool(nc, stack, "stat", 8, [128, 2], F32)
                self.hts = [(sbt(stack, "ht%d" % i, [128, 8, TT], BF16), [Buf("ht%d_%d" % (i, j)) for j in range(4)])
                            for i in range(nht)]
                self.hi = 0
                self.tpp = Pool(nc, stack, "tpp", 2, [128, 8, 128], BF16, psum=True)
                self.ev = 0

            def _load(self, J, j):
                xt, B_xt = self.xin.next()
                r0 = J["ti"] * TT + j * 128
                sc.dma("sp", [(xt[:], J["src"][r0:r0 + 128, :])], reads=[J["B_src"]] if J["B_src"] else [],
                       writes=[B_xt], key=("xin", id(B_xt)))
                J["x"][j] = (xt, B_xt)

            def begin(self, src_d, ti, B_src):
                ht_, B_ht_ = self.hts[self.hi % len(self.hts)]
                self.hi += 1
                J = dict(src=src_d, ti=ti, B_src=B_src, ht=ht_, B_ht=B_ht_, x={}, st={})
                if self.nxin >= 4:
                    for j in range(4):
                        self._load(J, j)
                return J

            def piece_a(self, J, j):
                if j not in J["x"]:
                    self._load(J, j)
                xt, B_xt = J["x"][j]
                st, B_st = self.stat.next()
                J["st"][j] = (st, B_st)
                sc.op("act", lambda e, xt=xt, st=st: e.activation(
                    out=self.sqj[:], in_=xt[:], func=AF.Square, accum_out=st[:, 0:1]),
                    reads=[B_xt], writes=[B_st, self.B_sqj])
                sc.op("dve", lambda e, st=st: e.tensor_scalar(
                    out=st[:, 1:2], in0=st[:, 0:1], scalar1=float(D * EPS), scalar2=None, op0=ALU.add),
                    reads=[B_st], writes=[B_st])

            def piece_b(self, J, j):
                st, B_st = J["st"][j]
                sc.op("act", lambda e, st=st: e.activation(out=st[:, 1:2], in_=st[:, 1:2], func=AF.Sqrt),
                      reads=[B_st], writes=[B_st])
                sc.op("dve", lambda e, st=st: e.reciprocal(out=st[:, 1:2], in_=st[:, 1:2]),
                      reads=[B_st], writes=[B_st])

            def piece_c(self, J, j):
                xt, B_xt = J["x"][j]
                st, B_st = J["st"][j]
                ht_, B_ht_ = J["ht"], J["B_ht"]
                hr, B_hr = self.hrow.next()
                sc.op("dve", lambda e, hr=hr, xt=xt, st=st: e.tensor_scalar(
                    out=hr[:], in0=xt[:], scalar1=st[:, 1:2], scalar2=32.0, op0=ALU.mult, op1=ALU.mult),
                    reads=[B_xt, B_st], writes=[B_hr])
                tp, B_tp = self.tpp.next()
                for k in range(8):
                    sc.op("pe", lambda e, tp=tp, hr=hr, k=k: e.transpose(
                        out=tp[:, k, :], in_=hr[:, k * 128:(k + 1) * 128], identity=ident),
                        reads=[B_hr, B_cst], writes=[B_tp])
                eng = ("dve", "act")[self.ev % 2]
                self.ev += 1
                if eng == "dve":
                    sc.op("dve", lambda e, tp=tp, j=j, ht_=ht_: e.tensor_copy(
                        out=ht_[:, :, j * 128:(j + 1) * 128], in_=tp[:]),
                        reads=[B_tp], writes=[B_ht_[j]])
                else:
                    sc.op("act", lambda e, tp=tp, j=j, ht_=ht_: e.copy(
                        out=ht_[:, :, j * 128:(j + 1) * 128], in_=tp[:]),
                        reads=[B_tp], writes=[B_ht_[j]])

            def pieces(self, J):
                order = [("a", 0), ("a", 1), ("b", 0), ("a", 2), ("b", 1), ("c", 0), ("a", 3), ("b", 2),
                         ("c", 1), ("b", 3), ("c", 2), ("c", 3)]
                fns = {"a": self.piece_a, "b": self.piece_b, "c": self.piece_c}
                return [(lambda f=fns[k], j=j: f(J, j)) for (k, j) in order]

            def run(self, src_d, ti, B_src):
                J = self.begin(src_d, ti, B_src)
                for j in range(4):
                    self.piece_a(J, j)
                    self.piece_b(J, j)
                    self.piece_c(J, j)
                return J["ht"], J["B_ht"]

        def wtoks(toks, col, w, blk=640):
            return [toks[b] for b in range(col // blk, (col + w - 1) // blk + 1)]

        def load_cast_cols(stgp, dst_t, toks, src_d, nrow_chunks, ncols, gain_i=None, blk=640):
            for b in range((ncols + blk - 1) // blk):
                c0, c1 = b * blk, min(ncols, (b + 1) * blk)
                for k in range(nrow_chunks):
                    load_cast(stgp, dst_t, lambda a, bb, k=k, c0=c0: dst_t[:, k, c0 + a:c0 + bb], toks[b],
                              src_d[k * 128:(k + 1) * 128, c0:c1], c1 - c0,
                              gain=None if gain_i is None else gcol[:, gain_i, k:k + 1])

        def phase_ffn(layer, src_d, B_srcs, dst_d, B_dsts, final):
            with ExitStack() as pes:
                w1b = sbt(pes, "w1b", [128, 8, DFF], BF16)
                w3b = sbt(pes, "w3b", [128, 8, DFF], BF16)
                w2b = sbt(pes, "w2b", [128, NFF, D], BF16)
                NBLK = (DFF + 639) // 640
                B_w1 = [Buf("w1_%d" % b) for b in range(NBLK)]
                B_w3 = [Buf("w3_%d" % b) for b in range(NBLK)]
                B_w2 = [Buf("w2_%d" % c) for c in range(NFF)]
                stgp = Pool(nc, pes, "stg", 2, [128, STGW], F32)
                gi = 0 if layer == 0 else 2
                for b in range(NBLK):
                    c0, c1 = b * 640, min(DFF, (b + 1) * 640)
                    for (wb, wd, toks) in ((w1b, w1_d[layer], B_w1), (w3b, w3_d[layer], B_w3)):
                        for k in range(8):
                            load_cast(stgp, wb, lambda a, bb, k=k, c0=c0, wb=wb: wb[:, k, c0 + a:c0 + bb], toks[b],
                                      wd[k * 128:(k + 1) * 128, c0:c1], c1 - c0, gain=gcol[:, gi, k:k + 1])
                for c in range(NFF):
                    load_cast(stgp, w2b, lambda a, b, c=c: w2b[:, c, a:b], B_w2[c],
                              w2_d[layer][c * 128:(c + 1) * 128, :], D)
                nt = NormT(pes)
                G = sbt(pes, "G", [128, NFF, TT], BF16)
                B_G = [Buf("G%d" % c) for c in range(NFF)]
                sgp = Pool(nc, pes, "sg", 2, [128, TT], F32)
                xres = Pool(nc, pes, "xres", 2, [128, D], F32)
                ps_up = Pool(nc, pes, "psu", 4, [128, 512], F32, psum=True)
                ps_dn = Pool(nc, pes, "psd", 2, [128, 512], F32, psum=True)
                if final:
                    gfb = sbt(pes, "gfb", [128, D], F32)
                    B_gfb = Buf("gfb")
                    sc.dma("sp", [(gfb[:], gfin_d.partition_broadcast(128))], writes=[B_gfb], key="gfb")
                    sc.op("dve", lambda e: e.tensor_scalar(out=gfb[:], in0=gfb[:], scalar1=32.0, scalar2=None,
                                                           op0=ALU.mult), reads=[B_gfb], writes=[B_gfb])
                    fstat = Pool(nc, pes, "fstat", 4, [128, 2], F32)

                nxt = nt.run(src_d, 0, B_srcs[0] if B_srcs else None)
                for ti in range(NT):
                    ht, B_ht = nxt
                    for c in range(NFF):
                        pu, B_pu = ps_up.next()
                        pv, B_pv = ps_up.next()
                        for k in range(8):
                            sc.op("pe", lambda e, pu=pu, k=k, c=c, ht=ht: e.matmul(
                                pu[:], lhsT=w1b[:, k, c * 128:(c + 1) * 128], rhs=ht[:, k, :],
                                start=(k == 0), stop=(k == 7)), reads=wtoks(B_w1, c * 128, 128) + B_ht, writes=[B_pu])
                        for k in range(8):
                            sc.op("pe", lambda e, pv=pv, k=k, c=c, ht=ht: e.matmul(
                                pv[:], lhsT=w3b[:, k, c * 128:(c + 1) * 128], rhs=ht[:, k, :],
                                start=(k == 0), stop=(k == 7)), reads=wtoks(B_w3, c * 128, 128) + B_ht, writes=[B_pv])
                        sg, B_sg = sgp.next()
                        sc.op("act", lambda e, sg=sg, pu=pu: e.activation(out=sg[:], in_=pu[:], func=AF.Silu),
                              reads=[B_pu], writes=[B_sg])
                        sc.op("dve", lambda e, sg=sg, pv=pv, c=c: e.tensor_tensor(
                            out=G[:, c, :], in0=sg[:], in1=pv[:], op=ALU.mult),
                            reads=[B_sg, B_pv], writes=[B_G[c]])
                    if ti + 1 < NT:
                        nxt = nt.run(src_d, ti + 1, B_srcs[ti + 1] if B_srcs else None)
                    for j in range(4):
                        xr, B_xr = xres.next()
                        r0 = ti * TT + j * 128
                        sc.dma("sp", [(xr[:], src_d[r0:r0 + 128, :])], reads=[B_srcs[ti]] if B_srcs else [],
                               writes=[B_xr], key=("xres", id(B_xr)))
                        for half in range(2):
                            pd, B_pd = ps_dn.next()
                            for c in range(NFF):
                                sc.op("pe", lambda e, pd=pd, c=c, j=j, half=half: e.matmul(
                                    pd[:], lhsT=G[:, c, j * 128:(j + 1) * 128],
                                    rhs=w2b[:, c, half * 512:(half + 1) * 512],
                                    start=(c == 0), stop=(c == NFF - 1)),
                                    reads=[B_G[c], B_w2[c]], writes=[B_pd])
                            sc.op("dve", lambda e, pd=pd, xr=xr, half=half: e.scalar_tensor_tensor(
                                out=xr[:, half * 512:(half + 1) * 512], in0=pd[:], scalar=0.5,
                                in1=xr[:, half * 512:(half + 1) * 512], op0=ALU.mult, op1=ALU.add),
                                reads=[B_pd, B_xr], writes=[B_xr])
                        if final:
                            fs, B_fs = fstat.next()
                            sc.op("act", lambda e, xr=xr, fs=fs: e.activation(
                                out=nt.sqj[:], in_=xr[:], func=AF.Square, accum_out=fs[:, 0:1]),
                                reads=[B_xr], writes=[B_fs, nt.B_sqj])
                            rstd_ops(fs, B_fs)
                            sc.op("dve", lambda e, xr=xr, fs=fs: e.scalar_tensor_tensor(
                                out=xr[:], in0=xr[:], scalar=fs[:, 1:2], in1=gfb[:], op0=ALU.mult, op1=ALU.mult),
                                reads=[B_xr, B_fs, B_gfb], writes=[B_xr])
                        sc.dma("sp", [(dst_d[r0:r0 + 128, :], xr[:])], reads=[B_xr],
                               writes=[B_dsts[ti]] if B_dsts else [], key=("xres_st", id(B_xr)))
            sc.barrier()

        def phase_proj():
            NQ = C_GA
            with ExitStack() as pes:
                wq = sbt(pes, "wq", [128, 8, NQ], BF16)
                B_wq = [Buf("wq%d" % b) for b in range((NQ + 639) // 640)]
                stgp = Pool(nc, pes, "stg", 2, [128, STGW], F32)
                load_cast_cols(stgp, wq, B_wq, win_d, 8, NQ, gain_i=1)
                nt = NormT(pes, nht=2, nxin=4)
                ps = Pool(nc, pes, "ps", 4, [128, 512], F32, psum=True)
                evp = Pool(nc, pes, "ev", 6, [128, 512], BF16)
                fm = []
                for g in range(4):
                    fm.append((C_QA + g * 128, lambda ti, g=g: qta_d[g, :, ti * TT:(ti + 1) * TT], 0.125))
                fm.append((C_KA, lambda ti: kta_d[:, ti * TT:(ti + 1) * TT], 1.0))
                for cc in range(4):
                    fm.append((C_QB + cc * 128, lambda ti, cc=cc: qtb_d[cc * 128:(cc + 1) * 128, ti * TT:(ti + 1) * TT], 0.125))
                for cc in range(4):
                    fm.append((C_KB + cc * 128, lambda ti, cc=cc: ktb_d[cc * 128:(cc + 1) * 128, ti * TT:(ti + 1) * TT], 1.0))
                evi = 0
                nxt = nt.run(x1_d, 0, B_x1[0])
                for ti in range(NT):
                    ht, B_ht = nxt
                    pcs = []
                    if ti + 1 < NT:
                        J = nt.begin(x1_d, ti + 1, B_x1[ti + 1])
                        nxt = (J["ht"], J["B_ht"])
                        pcs = nt.pieces(J)
                    for (col, dst, scale) in fm:
                        pp, B_pp = ps.next()
                        for k in range(8):
                            sc.op("pe", lambda e, pp=pp, k=k, col=col, ht=ht: e.matmul(
                                pp[:], lhsT=wq[:, k, col:col + 128], rhs=ht[:, k, :],
                                start=(k == 0), stop=(k == 7)), reads=wtoks(B_wq, col, 128) + B_ht, writes=[B_pp])
                        ev, B_ev = evp.next()
                        if evi % 2 == 0:
                            sc.op("dve", lambda e, ev=ev, pp=pp, scale=scale: e.tensor_scalar(
                                out=ev[:], in0=pp[:], scalar1=float(scale), scalar2=None, op0=ALU.mult),
                                reads=[B_pp], writes=[B_ev])
                        else:
                            sc.op("act", lambda e, ev=ev, pp=pp, scale=scale: e.activation(
                                out=ev[:], in_=pp[:], func=AF.Copy, scale=float(scale)),
                                reads=[B_pp], writes=[B_ev])
                        evi += 1
                        sc.dma("sp", [(dst(ti), ev[:])], reads=[B_ev], writes=[B_qkv[ti]], key=("ev", id(B_ev)))
                        if pcs:
                            pcs.pop(0)()
                    for j in range(4):
                        r0 = ti * TT + j * 128
                        pp, B_pp = ps.next()
                        for k in range(8):
                            sc.op("pe", lambda e, pp=pp, k=k, j=j, ht=ht: e.matmul(
                                pp[:], lhsT=ht[:, k, j * 128:(j + 1) * 128], rhs=wq[:, k, C_VB:C_VB + 512],
                                start=(k == 0), stop=(k == 7)), reads=wtoks(B_wq, C_VB, 512) + B_ht, writes=[B_pp])
                        ev, B_ev = evp.next()
                        sc.op("dve", lambda e, ev=ev, pp=pp: e.tensor_copy(out=ev[:], in_=pp[:]),
                              reads=[B_pp], writes=[B_ev])
                        sc.dma("sp", [(vb_d[r0:r0 + 128, :], ev[:])], reads=[B_ev], writes=[B_qkv[ti]],
                               key=("ev", id(B_ev)))
                        pp, B_pp = ps.next()
                        for k in range(8):
                            sc.op("pe", lambda e, pp=pp, k=k, j=j, ht=ht: e.matmul(
                                pp[:, 0:128], lhsT=ht[:, k, j * 128:(j + 1) * 128], rhs=wq[:, k, C_VA:C_VA + 128],
                                start=(k == 0), stop=(k == 7)), reads=wtoks(B_wq, C_VA, 128) + B_ht, writes=[B_pp])
                        ev, B_ev = evp.next()
                        sc.op("act", lambda e, ev=ev, pp=pp: e.copy(out=ev[:, 0:128], in_=pp[:, 0:128]),
                              reads=[B_pp], writes=[B_ev])
                        sc.dma("sp", [(va_d[r0:r0 + 128, :], ev[:, 0:128])], reads=[B_ev], writes=[B_qkv[ti]],
                               key=("ev", id(B_ev)))
                        if pcs:
                            pcs.pop(0)()
                    while pcs:
                        pcs.pop(0)()
            sc.barrier()

        def phase_swa():
            with ExitStack() as pes:
                kt = sbt(pes, "kta", [128, S], BF16)
                qt = sbt(pes, "qta", [128, 4, S], BF16)
                vv = sbt(pes, "vva", [128, NB, 128], BF16)
                B_in = Buf("swa_in")
                var = va_d.rearrange("(n p) c -> p n c", p=128)
                sc.dma("sp", [(kt[:], kta_d)] + [(qt[:, g, :], qta_d[g]) for g in range(4)]
                       + [(vv[:, n0:min(NB, n0 + 8), :], var[:, n0:min(NB, n0 + 8), :]) for n0 in range(0, NB, 8)],
                       reads=B_qkv, writes=[B_in], key="swa_in")
                bm = sbt(pes, "bm", [128, 8, 256], F32)
                mk = sbt(pes, "mk", [128, 256], F32)
                sk = sbt(pes, "sk", [128, 8], F32)
                B_bm = Buf("bm")
                B_sk = Buf("sk")
                sc.dma("sp", [(bm[:], bias_d), (mk[:], maskc_d)], writes=[B_bm], key="bm")
                sc.dma("sp", [(sk[:], sinks_d.partition_broadcast(128))], writes=[B_sk], key="sk")
                for h in range(8):
                    sc.op("dve", lambda e, h=h: e.tensor_tensor(out=bm[:, h, :], in0=bm[:, h, :], in1=mk[:], op=ALU.add),
                          reads=[B_bm], writes=[B_bm])
                psc = Pool(nc, pes, "psc", 2, [128, 4, 256], F32, psum=True)
                ppt = Pool(nc, pes, "ppt", 2, [128, 8, 128], BF16, psum=True)
                pso = Pool(nc, pes, "pso", 2, [128, 4, 128], F32, psum=True)
                scs = Pool(nc, pes, "scs", 2, [128, 8, 256], F32)
                pbf = Pool(nc, pes, "pbf", 2, [128, 8, 256], BF16)
                ptb = Pool(nc, pes, "ptb", 2, [128, 16, 128], BF16)
                sm = Pool(nc, pes, "sm", 4, [128, 5, 8], F32)
                osb = Pool(nc, pes, "osb", 3, [128, 4, 128], BF16)
                for n in range(NB):
                    k0 = 0 if n > 0 else 128
                    kw = 256 - k0
                    ks = (n - 1) * 128 + k0
                    ss, B_ss = scs.next()
                    pcs = [psc.next(), psc.next()]
                    for g in range(4):
                        for kv in range(2):
                            pc, B_pc = pcs[kv]
                            sc.op("pe", lambda e, pc=pc, g=g, kv=kv, n=n, ks=ks, kw=kw, k0=k0: e.matmul(
                                pc[:, g, k0:256], lhsT=qt[kv * 64:(kv + 1) * 64, g, n * 128:(n + 1) * 128],
                                rhs=kt[kv * 64:(kv + 1) * 64, ks:ks + kw], start=True, stop=True),
                                reads=[B_in], writes=[B_pc])
                    for kv in range(2):
                        pc, B_pc = pcs[kv]
                        sc.op("dve", lambda e, pc=pc, ss=ss, kv=kv, k0=k0: e.tensor_tensor(
                            out=ss[:, kv::2, k0:256], in0=pc[:, :, k0:256], in1=bm[:, kv::2, k0:256], op=ALU.add),
                            reads=[B_pc, B_bm], writes=[B_ss])
                    st, B_st = sm.next()
                    sc.op("dve", lambda e, ss=ss, st=st, k0=k0: e.tensor_reduce(
                        out=st[:, 0, :], in_=ss[:, :, k0:256], axis=AX.X, op=ALU.max),
                        reads=[B_ss], writes=[B_st])
                    sc.op("dve", lambda e, st=st: e.tensor_tensor(out=st[:, 0, :], in0=st[:, 0, :], in1=sk[:], op=ALU.max),
                          reads=[B_st, B_sk], writes=[B_st])
                    sc.op("dve", lambda e, st=st: e.tensor_scalar(out=st[:, 1, :], in0=st[:, 0, :], scalar1=-1.0,
                                                                   scalar2=None, op0=ALU.mult),
                          reads=[B_st], writes=[B_st])
                    sc.op("dve", lambda e, st=st: e.tensor_tensor(out=st[:, 2, :], in0=sk[:], in1=st[:, 1, :], op=ALU.add),
                          reads=[B_st, B_sk], writes=[B_st])
                    pb, B_pb = pbf.next()
                    for h in range(8):
                        sc.op("act", lambda e, pb=pb, ss=ss, st=st, h=h, k0=k0: e.activation(
                            out=pb[:, h, k0:256], in_=ss[:, h, k0:256], func=AF.Exp, bias=st[:, 1, h:h + 1],
                            accum_out=st[:, 3, h:h + 1]), reads=[B_ss, B_st], writes=[B_pb, B_st])
                    sc.op("act", lambda e, st=st: e.activation(out=st[:, 2, :], in_=st[:, 2, :], func=AF.Exp),
                          reads=[B_st], writes=[B_st])
                    sc.op("dve", lambda e, st=st: e.tensor_tensor(out=st[:, 4, :], in0=st[:, 3, :], in1=st[:, 2, :], op=ALU.add),
                          reads=[B_st], writes=[B_st])
                    sc.op("dve", lambda e, st=st: e.reciprocal(out=st[:, 4, :], in_=st[:, 4, :]),
                          reads=[B_st], writes=[B_st])
                    sc.op("dve", lambda e, pb=pb, st=st, k0=k0, kw=kw: e.tensor_tensor(
                        out=pb[:, :, k0:256], in0=pb[:, :, k0:256],
                        in1=st[:, 4, :].unsqueeze(2).to_broadcast([128, 8, kw]), op=ALU.mult),
                        reads=[B_pb, B_st], writes=[B_pb])
                    nkb = kw // 128
                    pt, B_pt = ptb.next()
                    for kb in range(nkb):
                        pp, B_pp = ppt.next()
                        for h in range(8):
                            sc.op("pe", lambda e, pp=pp, pb=pb, h=h, kb=kb, k0=k0: e.transpose(
                                out=pp[:, h, :], in_=pb[:, h, k0 + kb * 128:k0 + (kb + 1) * 128], identity=ident),
                                reads=[B_pb, B_cst], writes=[B_pp])
                        if kb == 0:
                            sc.op("dve", lambda e, pp=pp, pt=pt, kb=kb: e.tensor_copy(
                                out=pt[:, kb * 8:(kb + 1) * 8, :], in_=pp[:]), reads=[B_pp], writes=[B_pt])
                        else:
                            sc.op("act", lambda e, pp=pp, pt=pt, kb=kb: e.copy(
                                out=pt[:, kb * 8:(kb + 1) * 8, :], in_=pp[:]), reads=[B_pp], writes=[B_pt])
                    po, B_po = pso.next()
                    for h in range(8):
                        g, kv = h % 4, h // 4
                        hi = 2 * g + kv
                        c, half = h // 2, h % 2
                        for kb in range(nkb):
                            blk = n - (nkb - 1) + kb
                            sc.op("pe", lambda e, po=po, pt=pt, c=c, half=half, kv=kv, hi=hi, kb=kb, blk=blk, nkb=nkb: e.matmul(
                                po[half * 64:(half + 1) * 64, c, :], lhsT=vv[:, blk, kv * 64:(kv + 1) * 64],
                                rhs=pt[:, kb * 8 + hi, :], start=(kb == 0), stop=(kb == nkb - 1)),
                                reads=[B_pt, B_in], writes=[B_po])
                    ob, B_ob = osb.next()
                    sc.op("dve", lambda e, ob=ob, po=po: e.tensor_copy(out=ob[:], in_=po[:]), reads=[B_po], writes=[B_ob])
                    sc.dma("sp", [(ota_d.rearrange("(c p) s -> p c s", p=128)[:, :, n * 128:(n + 1) * 128], ob[:])],
                           reads=[B_ob], writes=[B_ota[n // 4]], key=("osb", id(B_ob)))
            sc.barrier()

        def phase_sb():
            with ExitStack() as pes:
                ktp = Pool(nc, pes, "ktb", 2, [128, S], BF16)
                qtp = Pool(nc, pes, "qtb", 2, [128, S], BF16)
                vvp = Pool(nc, pes, "vvb", 2, [128, NB, 128], BF16)
                zp = Pool(nc, pes, "zp", 3, [128, 2, 512], F32, psum=True)
                op_ = Pool(nc, pes, "op", 2, [128, 512], F32, psum=True)
                Ep = Pool(nc, pes, "E", 2, [128, 2, 512], F32)
                Lp = Pool(nc, pes, "L", 3, [128, 2, 512], BF16)
                Ap = Pool(nc, pes, "A", 2, [128, 2, 512], BF16)
                R32 = sbt(pes, "R32", [128, 2, 512], F32)
                B_R32 = Buf("R32")
                Rbp = Pool(nc, pes, "Rb", 2, [128, 2, 512], BF16)
                oev = Pool(nc, pes, "oev", 2, [128, 512], BF16)
                steps = []
                pair_in = {}
                for p in range(4):
                    for i in range(NT):
                        nk = 4 * i + 4
                        for si, kj in enumerate(range(nk - 1, -1, -1)):
                            steps.append(dict(p=p, i=i, kj=kj, first=(si == 0), last=(kj == 0),
                                              c0=max(0, (kj - 4 * i)) * 128, diag=(kj >= 4 * i)))

                def load_pair(p):
                    kt, B_kt = ktp.next()
                    qt, B_qt = qtp.next()
                    vv, B_vv = vvp.next()
                    sc.dma("sp", [(kt[:], ktb_d[p * 128:(p + 1) * 128, :])], reads=B_qkv, writes=[B_kt], key=("sbk", id(B_kt)))
                    sc.dma("sp", [(qt[:], qtb_d[p * 128:(p + 1) * 128, :])], reads=B_qkv, writes=[B_qt], key=("sbq", id(B_qt)))
                    vbr = vb_d[:, p * 128:(p + 1) * 128].rearrange("(n p) c -> p n c", p=128)
                    sc.dma("sp", [(vv[:, n0:min(NB, n0 + 8), :], vbr[:, n0:min(NB, n0 + 8), :]) for n0 in range(0, NB, 8)],
                           reads=B_qkv, writes=[B_vv], key=("sbv", id(B_vv)))
                    pair_in[p] = (kt, B_kt, qt, B_qt, vv, B_vv)

                load_pair(0)
                state = {}

                def stage0(s):
                    p, i, kj, c0 = s["p"], s["i"], s["kj"], s["c0"]
                    if s["first"] and i == 0 and p + 1 < 4:
                        load_pair(p + 1)
                    kt, B_kt, qt, B_qt, vv, B_vv = pair_in[p]
                    z, B_z = zp.next()
                    s["z"], s["B_z"] = z, B_z
                    for hh in range(2):
                        sc.op("pe", lambda e, z=z, hh=hh, kj=kj, i=i, c0=c0, kt=kt, qt=qt: e.matmul(
                            z[:, hh, c0:512], lhsT=kt[hh * 64:(hh + 1) * 64, kj * 128:(kj + 1) * 128],
                            rhs=qt[hh * 64:(hh + 1) * 64, i * 512 + c0:(i + 1) * 512],
                            start=True, stop=False, skip_group_check=True),
                            reads=[B_kt, B_qt], writes=[B_z])
                    if s["diag"]:
                        for hh in range(2):
                            sc.op("pe", lambda e, z=z, hh=hh, c0=c0: e.matmul(
                                z[:, hh, c0:c0 + 128], lhsT=ident, rhs=dmask, start=False, stop=False,
                                skip_group_check=True), reads=[B_cst], writes=[B_z])

                def stage1(s):
                    z, B_z, c0 = s["z"], s["B_z"], s["c0"]
                    E, B_E = Ep.next()
                    L, B_L = Lp.next()
                    s["L"], s["B_L"] = L, B_L
                    sc.op("act", lambda e, E=E, z=z, c0=c0: e.activation(out=E[:, :, c0:512], in_=z[:, :, c0:512], func=AF.Exp),
                          reads=[B_z], writes=[B_E])
                    sc.op("act", lambda e, E=E, L=L, c0=c0: e.activation(out=L[:, :, c0:512], in_=E[:, :, c0:512], func=AF.Ln, bias=1.0),
                          reads=[B_E], writes=[B_L])

                def stage2(s):
                    p, i, kj, c0 = s["p"], s["i"], s["kj"], s["c0"]
                    kt, B_kt, qt, B_qt, vv, B_vv = pair_in[p]
                    z, B_z, L, B_L = s["z"], s["B_z"], s["L"], s["B_L"]
                    for hh in range(2):
                        sc.op("pe", lambda e, z=z, hh=hh, c0=c0, L=L: e.matmul(
                            z[:, hh, c0:512], lhsT=ntri, rhs=L[:, hh, c0:512], start=False, stop=False,
                            skip_group_check=True), reads=[B_L, B_cst], writes=[B_z])
                    if not s["first"]:
                        Rb, B_Rb = state["Rb"]
                        for hh in range(2):
                            sc.op("pe", lambda e, z=z, hh=hh, c0=c0, Rb=Rb: e.matmul(
                                z[:, hh, c0:512], lhsT=nones, rhs=Rb[:, hh, c0:512], start=False, stop=False,
                                skip_group_check=True), reads=[B_Rb, B_cst], writes=[B_z])
                    A, B_A = Ap.next()
                    sc.op("act", lambda e, A=A, z=z, c0=c0: e.activation(out=A[:, :, c0:512], in_=z[:, :, c0:512], func=AF.Exp),
                          reads=[B_z], writes=[B_A])
                    if s["first"]:
                        state["o"] = op_.next()
                    o, B_o = state["o"]
                    for hh in range(2):
                        sc.op("pe", lambda e, o=o, hh=hh, c0=c0, A=A, vv=vv, kj=kj, first=s["first"]: e.matmul(
                            o[hh * 64:(hh + 1) * 64, c0:512], lhsT=vv[:, kj, hh * 64:(hh + 1) * 64],
                            rhs=A[:, hh, c0:512], start=first, stop=False, skip_group_check=True),
                            reads=[B_A, B_vv], writes=[B_o])
                    if not s["last"]:
                        if s["first"]:
                            sc.op("pool", lambda e: e.memset(R32[:], 0.0), writes=[B_R32])
                        sc.op("pool", lambda e, L=L, c0=c0: e.tensor_tensor(
                            out=R32[:, :, c0:512], in0=R32[:, :, c0:512], in1=L[:, :, c0:512], op=ALU.add),
                            reads=[B_L, B_R32], writes=[B_R32])
                        Rb, B_Rb = Rbp.next()
                        sc.op("dve", lambda e, Rb=Rb: e.tensor_copy(out=Rb[:], in_=R32[:]), reads=[B_R32], writes=[B_Rb])
                        state["Rb"] = (Rb, B_Rb)
                    else:
                        ob, B_ob = oev.next()
                        sc.op("dve", lambda e, ob=ob, o=o: e.tensor_copy(out=ob[:], in_=o[:]), reads=[B_o], writes=[B_ob])
                        sc.dma("sp", [(otb_d[p * 128:(p + 1) * 128, i * 512:(i + 1) * 512], ob[:])], reads=[B_ob],
                               writes=[B_otb[i]], key=("oev", id(B_ob)))

                n = len(steps)
                for t in range(n + 2):
                    if t < n:
                        stage0(steps[t])
                    if 0 <= t - 1 < n:
                        stage1(steps[t - 1])
                    if 0 <= t - 2 < n:
                        stage2(steps[t - 2])
            sc.barrier()

        def phase_mix():
            with ExitStack() as pes:
                wg = sbt(pes, "wg", [128, 8, 2048], BF16)
                wba = sbt(pes, "wba", [128, 4, D], BF16)
                wbb = sbt(pes, "wbb", [128, 4, D], BF16)
                wo = sbt(pes, "wo", [128, 8, D], BF16)
                B_wg = [Buf("wg%d" % b) for b in range(4)]
                B_wba = [Buf("wba%d" % k) for k in range(4)]
                B_wbb = [Buf("wbb%d" % k) for k in range(4)]
                B_wo = [Buf("wo%d" % k) for k in range(8)]
                stgp = Pool(nc, pes, "stg", 2, [128, STGW], F32)
                load_cast_cols(stgp, wg, B_wg, win_d[:, C_GA:INW], 8, 2048, gain_i=1)
                for k in range(4):
                    load_cast(stgp, wba, lambda a, b, k=k: wba[:, k, a:b], B_wba[k], wba_d[k * 128:(k + 1) * 128, :], D)
                    load_cast(stgp, wbb, lambda a, b, k=k: wbb[:, k, a:b], B_wbb[k], wbb_d[k * 128:(k + 1) * 128, :], D)
                for k in range(8):
                    load_cast(stgp, wo, lambda a, b, k=k: wo[:, k, a:b], B_wo[k], wout_d[k * 128:(k + 1) * 128, :], D)
                nt = NormT(pes, nht=2, nxin=4)
                ps = Pool(nc, pes, "ps", 6, [128, 512], F32, psum=True)
                otap = Pool(nc, pes, "ota", 2, [128, 4, TT], BF16)
                otbp = Pool(nc, pes, "otb", 2, [128, 4, TT], BF16)
                MT = sbt(pes, "MT", [128, 8, TT], BF16)
                B_MT = [Buf("MT%d" % c) for c in range(8)]
                sgp = Pool(nc, pes, "sgm", 4, [128, TT], F32)
                mp = Pool(nc, pes, "mm", 4, [128, TT], F32)
                xres = Pool(nc, pes, "xres", 3, [128, D], F32)
                nxt = nt.run(x1_d, 0, B_x1[0])
                for ti in range(NT):
                    ht, B_ht = nxt
                    pcs = []
                    if ti + 1 < NT:
                        J = nt.begin(x1_d, ti + 1, B_x1[ti + 1])
                        nxt = (J["ht"], J["B_ht"])
                        pcs = nt.pieces(J)
                    oa, B_oa = otap.next()
                    ob, B_ob = otbp.next()
                    sc.dma("sp", [(oa[:], ota_d.rearrange("(c p) s -> p c s", p=128)[:, :, ti * TT:(ti + 1) * TT])],
                           reads=[B_ota[ti]], writes=[B_oa], key=("ota", id(B_oa)))
                    sc.dma("sp", [(ob[:], otb_d.rearrange("(c p) s -> p c s", p=128)[:, :, ti * TT:(ti + 1) * TT])],
                           reads=[B_otb[ti]], writes=[B_ob], key=("otb", id(B_ob)))
                    for cc in range(8):
                        ms = []
                        for br, (wbr, B_wbr, ot, B_ot, goff) in enumerate(((wba, B_wba, oa, B_oa, 0), (wbb, B_wbb, ob, B_ob, 1024))):
                            pg, B_pg = ps.next()
                            pb, B_pb = ps.next()
                            for k in range(8):
                                sc.op("pe", lambda e, pg=pg, k=k, cc=cc, goff=goff, ht=ht: e.matmul(
                                    pg[:], lhsT=wg[:, k, goff + cc * 128:goff + (cc + 1) * 128], rhs=ht[:, k, :],
                                    start=(k == 0), stop=(k == 7)), reads=wtoks(B_wg, goff + cc * 128, 128) + B_ht, writes=[B_pg])
                            for k in range(4):
                                sc.op("pe", lambda e, pb=pb, k=k, cc=cc, wbr=wbr, ot=ot: e.matmul(
                                    pb[:], lhsT=wbr[:, k, cc * 128:(cc + 1) * 128], rhs=ot[:, k, :],
                                    start=(k == 0), stop=(k == 3)), reads=[B_wbr[k], B_ot], writes=[B_pb])
                            sg, B_sg = sgp.next()
                            sc.op("act", lambda e, sg=sg, pg=pg: e.activation(out=sg[:], in_=pg[:], func=AF.Sigmoid),
                                  reads=[B_pg], writes=[B_sg])
                            m, B_m = mp.next()
                            sc.op("dve", lambda e, m=m, sg=sg, pb=pb: e.tensor_tensor(out=m[:], in0=sg[:], in1=pb[:], op=ALU.mult),
                                  reads=[B_sg, B_pb], writes=[B_m])
                            ms.append((m, B_m))
                            if pcs:
                                pcs.pop(0)()
                        sc.op("pool", lambda e, cc=cc, ms=ms: e.tensor_tensor(
                            out=MT[:, cc, :], in0=ms[0][0][:], in1=ms[1][0][:], op=ALU.add),
                            reads=[ms[0][1], ms[1][1]], writes=[B_MT[cc]])
                    while pcs:
                        pcs.pop(0)()
                    for j in range(4):
                        xr, B_xr = xres.next()
                        r0 = ti * TT + j * 128
                        sc.dma("sp", [(xr[:], x1_d[r0:r0 + 128, :])], reads=[B_x1[ti]], writes=[B_xr], key=("xres", id(B_xr)))
                        for half in range(2):
                            po, B_po = ps.next()
                            for cc in range(8):
                                sc.op("pe", lambda e, po=po, cc=cc, j=j, half=half: e.matmul(
                                    po[:], lhsT=MT[:, cc, j * 128:(j + 1) * 128], rhs=wo[:, cc, half * 512:(half + 1) * 512],
                                    start=(cc == 0), stop=(cc == 7)), reads=[B_MT[cc], B_wo[cc]], writes=[B_po])
                            sc.op("dve", lambda e, po=po, xr=xr, half=half: e.tensor_tensor(
                                out=xr[:, half * 512:(half + 1) * 512], in0=po[:], in1=xr[:, half * 512:(half + 1) * 512], op=ALU.add),
                                reads=[B_po, B_xr], writes=[B_xr])
                        sc.dma("sp", [(x2_d[r0:r0 + 128, :], xr[:])], reads=[B_xr], writes=[B_x2[ti]], key=("xres_st", id(B_xr)))
            sc.barrier()

        sc.barrier()
        if "ffn1" in phases:
            phase_ffn(0, x_d, None, x1_d, B_x1, final=False)
        if "proj" in phases:
            phase_proj()
        if "swa" in phases:
            phase_swa()
        if "sb" in phases:
            phase_sb()
        if "mix" in phases:
            phase_mix()
        if "ffn2" in phases:
            phase_ffn(1, x2_d, B_x2, out_d, None, final=True)
        sc.emit()
    return nc


def _rel_bucket_np(dist):
    max_exact = 16
    d = np.maximum(dist, 1).astype(np.float32)
    large = max_exact + (np.log(d / max_exact) / np.float32(np.log(128 / max_exact)) * (32 - max_exact)).astype(np.int32)
    large = np.minimum(large, 31)
    return np.where(dist < max_exact, dist, large)


def _bucket_table():
    import jax
    import jax.numpy as jnp
    qi = np.arange(128)[:, None] + 128
    kj = np.arange(256)[None, :]
    dist = qi - kj
    band = (dist >= 0) & (dist < 128)
    with jax.default_device(jax.devices("cpu")[0]):
        dj = jnp.maximum(jnp.asarray(dist), 0)
        max_exact = 16
        d = jnp.maximum(dj, 1).astype(jnp.float32)
        large = max_exact + (jnp.log(d / max_exact) / np.log(128 / max_exact) * (32 - max_exact)).astype(jnp.int32)
        large = jnp.minimum(large, 31)
        bucket = np.asarray(jnp.where(dj < max_exact, dj, large))
    return bucket, band


def _consts():
    c = np.zeros((128, 512), np.float32)
    c[:, 0:128] = np.eye(128)
    j = np.arange(128)[:, None]
    s = np.arange(128)[None, :]
    c[:, 128:256] = np.where(j >= s, -1.0, 0.0)
    c[:, 256:384] = -1.0
    c[:, 384:512] = np.where(j < s, 0.0, MASKB)
    return c.astype(ml_dtypes.bfloat16)


_PROG_CACHE = {}


def _prepare_shared(inp, S):
    f = lambda a: np.ascontiguousarray(np.asarray(a, dtype=np.float32))
    bucket, band = _bucket_table()
    rb = f(inp["rel_bias"])
    bias = rb[bucket]
    order = [g + 4 * kv for g in range(4) for kv in range(2)]
    bias = np.ascontiguousarray(bias.transpose(0, 2, 1)[:, order, :])
    mask = np.where(band, 0.0, NEG).astype(np.float32)
    w_in = f(inp["w_in"])[0]
    qcols = []
    for g in range(4):
        for kv in range(2):
            h = g + 4 * kv
            qcols.extend(range(h * 64, (h + 1) * 64))
    w_in = np.ascontiguousarray(np.concatenate([w_in[:, qcols], w_in[:, 512:]], axis=1))
    sinks = f(inp["swa_sinks"])[0][order]
    shared = {
        "ffn1_w1": f(inp["ffn1_w1"])[0], "ffn1_w3": f(inp["ffn1_w3"])[0], "ffn1_w2": f(inp["ffn1_w2"])[0],
        "ffn2_w1": f(inp["ffn2_w1"])[0], "ffn2_w3": f(inp["ffn2_w3"])[0], "ffn2_w2": f(inp["ffn2_w2"])[0],
        "gains": np.ascontiguousarray(np.stack([f(inp["norm_ffn1"])[0].reshape(8, 128).T,
                                                f(inp["norm_mix"])[0].reshape(8, 128).T,
                                                f(inp["norm_ffn2"])[0].reshape(8, 128).T], axis=1)),
        "norm_final": f(inp["norm_final"]),
        "w_in": w_in, "swa_sinks": np.ascontiguousarray(sinks), "swa_bias": bias, "swa_mask": mask,
        "w_branch_swa": f(inp["w_branch_swa"])[0], "w_branch_sb": f(inp["w_branch_sb"])[0],
        "w_out": f(inp["w_out"])[0], "consts": _consts(),
    }
    return shared


def kernel(**inputs):
    x = np.asarray(inputs["x"], dtype=np.float32)
    B, S, _ = x.shape
    if S not in _PROG_CACHE:
        _PROG_CACHE[S] = build_program(S)
    nc = _PROG_CACHE[S]
    shared = _prepare_shared(inputs, S)
    in_maps = []
    for b in range(B):
        m = dict(shared)
        m["x"] = np.ascontiguousarray(x[b])
        in_maps.append(m)
    res = run_bass_kernel_spmd(nc, in_maps, core_ids=list(range(B)))
    return np.stack([np.asarray(r["out"], dtype=np.float32) for r in res.results], axis=0)
```

```python
from contextlib import ExitStack

import numpy as np
import ml_dtypes

import concourse.bass as bass
import concourse.mybir as mybir
from concourse.bass_utils import run_bass_kernel_spmd

F32 = mybir.dt.float32
BF16 = mybir.dt.bfloat16
AF = mybir.ActivationFunctionType
ALU = mybir.AluOpType
AX = mybir.AxisListType

D = 1024
DFF = 2816
NFF = DFF // 128
INW = 4352
EPS = 1e-6
NEG = -1e30
TT = 512
MASKB = -30000.0
STGW = 704

C_QA, C_KA, C_VA, C_QB, C_KB, C_VB, C_GA, C_GB = 0, 512, 640, 768, 1280, 1792, 2304, 3328


class Buf:
    __slots__ = ("name", "lw", "rd", "dmard")

    def __init__(self, name):
        self.name = name
        self.lw = None
        self.rd = {}
        self.dmard = []


class Op:
    __slots__ = ("eng", "fn", "deps", "signal", "sem", "count", "is_dma", "pairs")

    def __init__(self, eng, fn, is_dma=False):
        self.eng = eng
        self.fn = fn
        self.deps = []
        self.signal = False
        self.sem = None
        self.count = 0
        self.is_dma = is_dma
        self.pairs = None


SEM_LIMIT = 30000


class Sched:
    ENGS = ("pe", "act", "dve", "pool", "sp")

    def __init__(self, nc, es):
        self.nc = nc
        self.es = es
        self.ops = {e: [] for e in self.ENGS}
        self.dma_sems = {}
        self.barrier_deps = {e: None for e in self.ENGS}

    def _newsem(self, name):
        return self.es.enter_context(self.nc.semaphore(name))

    def _add(self, op, reads, writes):
        deps = []
        bd = self.barrier_deps[op.eng]
        if bd is not None:
            deps.extend(bd)
            self.barrier_deps[op.eng] = None
        for b in reads:
            w = b.lw
            if w is not None:
                if not (w.eng == op.eng == "pe" and not w.is_dma and not op.is_dma):
                    deps.append(w)
        for b in writes:
            w = b.lw
            if w is not None:
                if w.is_dma or op.is_dma or w.eng != op.eng:
                    deps.append(w)
            for e, r in b.rd.items():
                if op.is_dma or e != op.eng:
                    deps.append(r)
            deps.extend(b.dmard)
        for b in reads:
            if op.is_dma:
                b.dmard.append(op)
            else:
                b.rd[op.eng] = op
        for b in writes:
            b.lw = op
            b.rd = {}
            b.dmard = []
        seen = set()
        for d in deps:
            if id(d) not in seen and d is not op:
                seen.add(id(d))
                op.deps.append(d)
                d.signal = True
        self.ops[op.eng].append(op)
        return op

    def op(self, eng, fn, reads=(), writes=()):
        return self._add(Op(eng, fn), reads, writes)

    def dma(self, queue, pairs, reads=(), writes=(), key=None):
        op = Op(queue, None, is_dma=True)
        op.pairs = pairs
        op.signal = True
        if key not in self.dma_sems:
            self.dma_sems[key] = [self._newsem("d%d" % len(self.dma_sems)), 0, None]
        ent = self.dma_sems[key]
        ent[1] += 16 * len(pairs)
        ent[2] = op
        op.sem = ent[0]
        op.count = ent[1]
        return self._add(op, reads, writes)

    def barrier(self):
        prev = []
        for e in self.ENGS:
            for op in reversed(self.ops[e]):
                if not op.is_dma:
                    prev.append(op)
                    break
        for ent in self.dma_sems.values():
            if ent[2] is not None:
                prev.append(ent[2])
        for e in self.ENGS:
            self.barrier_deps[e] = list(prev)

    def emit(self):
        nc = self.nc
        eng_sems = {}
        for e in self.ENGS:
            n = 0
            for op in self.ops[e]:
                if op.is_dma or not op.signal:
                    continue
                k = n // SEM_LIMIT
                if (e, k) not in eng_sems:
                    eng_sems[(e, k)] = self._newsem("e_%s%d" % (e, k))
                op.sem = eng_sems[(e, k)]
                op.count = n % SEM_LIMIT + 1
                n += 1
        final_waits = [(ent[0], ent[1]) for ent in self.dma_sems.values()]

        def run(e, eng, final=False):
            waited = {}
            for op in self.ops[e]:
                for d in op.deps:
                    k = id(d.sem)
                    if waited.get(k, 0) < d.count:
                        eng.wait_ge(d.sem, d.count)
                        waited[k] = d.count
                if op.is_dma:
                    for (o, i) in op.pairs:
                        eng.dma_start(out=o, in_=i).then_inc(op.sem, 16)
                else:
                    ins = op.fn(eng)
                    if op.signal:
                        ins.then_inc(op.sem, 1)
            if final:
                for (h, c) in final_waits:
                    if waited.get(id(h), 0) < c:
                        eng.wait_ge(h, c)

        with nc.Block() as block:
            @block.sync
            def _(sync):
                run("sp", sync, final=True)

            @block.tensor
            def _(tensor):
                run("pe", tensor)

            @block.scalar
            def _(scalar):
                run("act", scalar)

            @block.vector
            def _(vector):
                run("dve", vector)

            @block.gpsimd
            def _(gpsimd):
                run("pool", gpsimd)


class Pool:
    uid = [0]

    def __init__(self, nc, es, name, n, shape, dtype, psum=False):
        self.tiles = []
        for i in range(n):
            Pool.uid[0] += 1
            nm = "%s%d_%d" % (name, i, Pool.uid[0])
            if psum:
                t = es.enter_context(nc.psum_tensor(nm, list(shape), dtype))
            else:
                t = es.enter_context(nc.sbuf_tensor(nm, list(shape), dtype))
            self.tiles.append((t, Buf(nm)))
        self.i = 0

    def next(self):
        t = self.tiles[self.i % len(self.tiles)]
        self.i += 1
        return t


def build_program(S, phases=("ffn1", "proj", "swa", "sb", "mix", "ffn2"), debug=False):
    NT = S // TT
    NB = S // 128
    nc = bass.Bass("TRN2", target_bir_lowering=False)
    dk = "ExternalOutput" if debug else "Internal"

    def din(name, shape, dt=F32):
        return nc.dram_tensor(name, list(shape), dt, kind="ExternalInput").ap()

    def dscr(name, shape, dt):
        return nc.dram_tensor(name, list(shape), dt, kind=dk).ap()

    x_d = din("x", [S, D])
    w1_d = [din("ffn1_w1", [D, DFF]), din("ffn2_w1", [D, DFF])]
    w3_d = [din("ffn1_w3", [D, DFF]), din("ffn2_w3", [D, DFF])]
    w2_d = [din("ffn1_w2", [DFF, D]), din("ffn2_w2", [DFF, D])]
    gains_d = din("gains", [128, 3, 8])
    gfin_d = din("norm_final", [D])
    win_d = din("w_in", [D, INW])
    sinks2_d = din("swa_sinks", [2, 4])
    bias_d = din("swa_bias", [128, 8, 256])
    maskc_d = din("swa_mask", [128, 256])
    wba_d = din("w_branch_swa", [512, D])
    wbb_d = din("w_branch_sb", [512, D])
    wout_d = din("w_out", [D, D])
    cst_d = din("consts", [128, 4 * 128], BF16)
    out_d = nc.dram_tensor("out", [S, D], F32, kind="ExternalOutput").ap()

    x1_d = dscr("x1", [S, D], F32)
    x2_d = dscr("x2", [S, D], F32)
    qta_d = dscr("qta", [4, 128, S], BF16)
    kta_d = dscr("kta", [128, S], BF16)
    va_d = dscr("va", [S, 128], BF16)
    qtb_d = dscr("qtb", [512, S], BF16)
    ktb_d = dscr("ktb", [512, S], BF16)
    vb_d = dscr("vb", [S, 512], BF16)
    ota_d = dscr("ota", [512, S], BF16)
    otb_d = dscr("otb", [512, S], BF16)

    B_x1 = [Buf("x1_%d" % i) for i in range(NT)]
    B_x2 = [Buf("x2_%d" % i) for i in range(NT)]
    B_qkv = [Buf("qkv_%d" % i) for i in range(NT)]
    B_ota = [Buf("ota_%d" % i) for i in range(NT)]
    B_otb = [Buf("otb_%d" % i) for i in range(NT)]

    with ExitStack() as es:
        sc = Sched(nc, es)

        def sbt(stack, name, shape, dt):
            Pool.uid[0] += 1
            return stack.enter_context(nc.sbuf_tensor("%s_%d" % (name, Pool.uid[0]), list(shape), dt))

        cst = sbt(es, "cst", [128, 512], BF16)
        B_cst = Buf("cst")
        sc.dma("sp", [(cst[:], cst_d)], writes=[B_cst], key="cst")
        ident = cst[:, 0:128]
        ntri = cst[:, 128:256]
        nones = cst[:, 256:384]
        dmask = cst[:, 384:512]

        gcol = sbt(es, "gcol", [128, 3, 8], F32)
        B_gcol = Buf("gcol")

        sc.dma("sp", [(gcol[:], gains_d)], writes=[B_gcol], key="gcol")

        def load_cast(stgp, dst_t, sel, B_dst, src_rows, ncols, gain=None):
            c0 = 0
            while c0 < ncols:
                c1 = min(ncols, c0 + STGW)
                st, B_st = stgp.next()
                w = c1 - c0
                sc.dma("sp", [(st[:, 0:w], src_rows[:, c0:c1])], writes=[B_st], key=("stg", id(B_st)))
                o = sel(c0, c1)
                if gain is None:
                    sc.op("dve", lambda e, o=o, i=st[:, 0:w]: e.tensor_copy(out=o, in_=i),
                          reads=[B_st], writes=[B_dst])
                else:
                    sc.op("dve", lambda e, o=o, i=st[:, 0:w], g=gain:
                          e.tensor_scalar(out=o, in0=i, scalar1=g, scalar2=None, op0=ALU.mult),
                          reads=[B_st, B_gcol], writes=[B_dst])
                c0 = c1

        def rstd_ops(st, B_st):
            sc.op("dve", lambda e, st=st: e.tensor_scalar(
                out=st[:, 1:2], in0=st[:, 0:1], scalar1=float(D * EPS), scalar2=None, op0=ALU.add),
                reads=[B_st], writes=[B_st])
            sc.op("act", lambda e, st=st: e.activation(out=st[:, 1:2], in_=st[:, 1:2], func=AF.Sqrt),
                  reads=[B_st], writes=[B_st])
            sc.op("dve", lambda e, st=st: e.reciprocal(out=st[:, 1:2], in_=st[:, 1:2]),
                  reads=[B_st], writes=[B_st])

        class NormT:
            def __init__(self, stack, nht=1, nxin=2):
                self.nxin = nxin
                self.xin = Pool(nc, stack, "xin", nxin, [128, D], F32)
                self.hrow = Pool(nc, stack, "hrow", 2, [128, D], BF16)
                self.sqj = sbt(stack, "sqj", [128, D], BF16)
                self.B_sqj = Buf("sqj")
                self.stat = Pool(nc, stack, "stat", 8, [128, 2], F32)
                self.hts = [(sbt(stack, "ht%d" % i, [128, 8, TT], BF16), [Buf("ht%d_%d" % (i, j)) for j in range(4)])
                            for i in range(nht)]
                self.hi = 0
                self.tpp = Pool(nc, stack, "tpp", 2, [128, 8, 128], BF16, psum=True)
                self.ev = 0

            def _load(self, J, j):
                xt, B_xt = self.xin.next()
                r0 = J["ti"] * TT + j * 128
                sc.dma("sp", [(xt[:], J["src"][r0:r0 + 128, :])], reads=[J["B_src"]] if J["B_src"] else [],
                       writes=[B_xt], key=("xin", id(B_xt)))
                J["x"][j] = (xt, B_xt)

            def begin(self, src_d, ti, B_src):
                ht_, B_ht_ = self.hts[self.hi % len(self.hts)]
                self.hi += 1
                J = dict(src=src_d, ti=ti, B_src=B_src, ht=ht_, B_ht=B_ht_, x={}, st={})
                if self.nxin >= 4:
                    for j in range(4):
                        self._load(J, j)
                return J

            def piece_a(self, J, j):
                if j not in J["x"]:
                    self._load(J, j)
                xt, B_xt = J["x"][j]
                st, B_st = self.stat.next()
                J["st"][j] = (st, B_st)
                sc.op("act", lambda e, xt=xt, st=st: e.activation(
                    out=self.sqj[:], in_=xt[:], func=AF.Square, accum_out=st[:, 0:1]),
                    reads=[B_xt], writes=[B_st, self.B_sqj])
                sc.op("dve", lambda e, st=st: e.tensor_scalar(
                    out=st[:, 1:2], in0=st[:, 0:1], scalar1=float(D * EPS), scalar2=None, op0=ALU.add),
                    reads=[B_st], writes=[B_st])

            def piece_b(self, J, j):
                st, B_st = J["st"][j]
                sc.op("act", lambda e, st=st: e.activation(out=st[:, 1:2], in_=st[:, 1:2], func=AF.Sqrt),
                      reads=[B_st], writes=[B_st])
                sc.op("dve", lambda e, st=st: e.reciprocal(out=st[:, 1:2], in_=st[:, 1:2]),
                      reads=[B_st], writes=[B_st])

            def piece_c(self, J, j):
                xt, B_xt = J["x"][j]
                st, B_st = J["st"][j]
                ht_, B_ht_ = J["ht"], J["B_ht"]
                hr, B_hr = self.hrow.next()
                sc.op("dve", lambda e, hr=hr, xt=xt, st=st: e.tensor_scalar(
                    out=hr[:], in0=xt[:], scalar1=st[:, 1:2], scalar2=32.0, op0=ALU.mult, op1=ALU.mult),
                    reads=[B_xt, B_st], writes=[B_hr])
                tp, B_tp = self.tpp.next()
                for k in range(8):
                    sc.op("pe", lambda e, tp=tp, hr=hr, k=k: e.transpose(
                        out=tp[:, k, :], in_=hr[:, k * 128:(k + 1) * 128], identity=ident),
                        reads=[B_hr, B_cst], writes=[B_tp])
                eng = ("dve", "act")[self.ev % 2]
                self.ev += 1
                if eng == "dve":
                    sc.op("dve", lambda e, tp=tp, j=j, ht_=ht_: e.tensor_copy(
                        out=ht_[:, :, j * 128:(j + 1) * 128], in_=tp[:]),
                        reads=[B_tp], writes=[B_ht_[j]])
                else:
                    sc.op("act", lambda e, tp=tp, j=j, ht_=ht_: e.copy(
                        out=ht_[:, :, j * 128:(j + 1) * 128], in_=tp[:]),
                        reads=[B_tp], writes=[B_ht_[j]])

            def pieces(self, J):
                order = [("a", 0), ("a", 1), ("b", 0), ("a", 2), ("b", 1), ("c", 0), ("a", 3), ("b", 2),
                         ("c", 1), ("b", 3), ("c", 2), ("c", 3)]
                fns = {"a": self.piece_a, "b": self.piece_b, "c": self.piece_c}
                return [(lambda f=fns[k], j=j: f(J, j)) for (k, j) in order]

            def run(self, src_d, ti, B_src):
                J = self.begin(src_d, ti, B_src)
                for j in range(4):
                    self.piece_a(J, j)
                    self.piece_b(J, j)
                    self.piece_c(J, j)
                return J["ht"], J["B_ht"]

        def wtoks(toks, col, w, blk=640):
            return [toks[b] for b in range(col // blk, (col + w - 1) // blk + 1)]

        def load_cast_cols(stgp, dst_t, toks, src_d, nrow_chunks, ncols, gain_i=None, blk=640):
            for b in range((ncols + blk - 1) // blk):
                c0, c1 = b * blk, min(ncols, (b + 1) * blk)
                for k in range(nrow_chunks):
                    load_cast(stgp, dst_t, lambda a, bb, k=k, c0=c0: dst_t[:, k, c0 + a:c0 + bb], toks[b],
                              src_d[k * 128:(k + 1) * 128, c0:c1], c1 - c0,
                              gain=None if gain_i is None else gcol[:, gain_i, k:k + 1])

        def phase_ffn(layer, src_d, B_srcs, dst_d, B_dsts, final):
            with ExitStack() as pes:
                w1b = sbt(pes, "w1b", [128, 8, DFF], BF16)
                w3b = sbt(pes, "w3b", [128, 8, DFF], BF16)
                w2b = sbt(pes, "w2b", [128, NFF, D], BF16)
                NBLK = (DFF + 639) // 640
                B_w1 = [Buf("w1_%d" % b) for b in range(NBLK)]
                B_w3 = [Buf("w3_%d" % b) for b in range(NBLK)]
                B_w2 = [Buf("w2_%d" % c) for c in range(NFF)]
                stgp = Pool(nc, pes, "stg", 2, [128, STGW], F32)
                gi = 0 if layer == 0 else 2
                for b in range(NBLK):
                    c0, c1 = b * 640, min(DFF, (b + 1) * 640)
                    for (wb, wd, toks) in ((w1b, w1_d[layer], B_w1), (w3b, w3_d[layer], B_w3)):
                        for k in range(8):
                            load_cast(stgp, wb, lambda a, bb, k=k, c0=c0, wb=wb: wb[:, k, c0 + a:c0 + bb], toks[b],
                                      wd[k * 128:(k + 1) * 128, c0:c1], c1 - c0, gain=gcol[:, gi, k:k + 1])
                for c in range(NFF):
                    load_cast(stgp, w2b, lambda a, b, c=c: w2b[:, c, a:b], B_w2[c],
                              w2_d[layer][c * 128:(c + 1) * 128, :], D)
                nt = NormT(pes)
                G = sbt(pes, "G", [128, NFF, TT], BF16)
                B_G = [Buf("G%d" % c) for c in range(NFF)]
                sgp = Pool(nc, pes, "sg", 2, [128, TT], F32)
                xres = Pool(nc, pes, "xres", 2, [128, D], F32)
                ps_up = Pool(nc, pes, "psu", 4, [128, 512], F32, psum=True)
                ps_dn = Pool(nc, pes, "psd", 2, [128, 512], F32, psum=True)
                if final:
                    gfb = sbt(pes, "gfb", [128, D], F32)
                    B_gfb = Buf("gfb")
                    sc.dma("sp", [(gfb[:], gfin_d.partition_broadcast(128))], writes=[B_gfb], key="gfb")
                    sc.op("dve", lambda e: e.tensor_scalar(out=gfb[:], in0=gfb[:], scalar1=32.0, scalar2=None,
                                                           op0=ALU.mult), reads=[B_gfb], writes=[B_gfb])
                    fstat = Pool(nc, pes, "fstat", 4, [128, 2], F32)

                nxt = nt.run(src_d, 0, B_srcs[0] if B_srcs else None)
                for ti in range(NT):
                    ht, B_ht = nxt
                    for c in range(NFF):
                        pu, B_pu = ps_up.next()
                        pv, B_pv = ps_up.next()
                        for k in range(8):
                            sc.op("pe", lambda e, pu=pu, k=k, c=c, ht=ht: e.matmul(
                                pu[:], lhsT=w1b[:, k, c * 128:(c + 1) * 128], rhs=ht[:, k, :],
                                start=(k == 0), stop=(k == 7)), reads=wtoks(B_w1, c * 128, 128) + B_ht, writes=[B_pu])
                        for k in range(8):
                            sc.op("pe", lambda e, pv=pv, k=k, c=c, ht=ht: e.matmul(
                                pv[:], lhsT=w3b[:, k, c * 128:(c + 1) * 128], rhs=ht[:, k, :],
                                start=(k == 0), stop=(k == 7)), reads=wtoks(B_w3, c * 128, 128) + B_ht, writes=[B_pv])
                        sg, B_sg = sgp.next()
                        sc.op("act", lambda e, sg=sg, pu=pu: e.activation(out=sg[:], in_=pu[:], func=AF.Silu),
                              reads=[B_pu], writes=[B_sg])
                        sc.op("dve", lambda e, sg=sg, pv=pv, c=c: e.tensor_tensor(
                            out=G[:, c, :], in0=sg[:], in1=pv[:], op=ALU.mult),
                            reads=[B_sg, B_pv], writes=[B_G[c]])
                    if ti + 1 < NT:
                        nxt = nt.run(src_d, ti + 1, B_srcs[ti + 1] if B_srcs else None)
                    for j in range(4):
                        xr, B_xr = xres.next()
                        r0 = ti * TT + j * 128
                        sc.dma("sp", [(xr[:], src_d[r0:r0 + 128, :])], reads=[B_srcs[ti]] if B_srcs else [],
                               writes=[B_xr], key=("xres", id(B_xr)))
                        for half in range(2):
                            pd, B_pd = ps_dn.next()
                            for c in range(NFF):
                                sc.op("pe", lambda e, pd=pd, c=c, j=j, half=half: e.matmul(
                                    pd[:], lhsT=G[:, c, j * 128:(j + 1) * 128],
                                    rhs=w2b[:, c, half * 512:(half + 1) * 512],
                                    start=(c == 0), stop=(c == NFF - 1)),
                                    reads=[B_G[c], B_w2[c]], writes=[B_pd])
                            sc.op("dve", lambda e, pd=pd, xr=xr, half=half: e.scalar_tensor_tensor(
                                out=xr[:, half * 512:(half + 1) * 512], in0=pd[:], scalar=0.5,
                                in1=xr[:, half * 512:(half + 1) * 512], op0=ALU.mult, op1=ALU.add),
                                reads=[B_pd, B_xr], writes=[B_xr])
                        if final:
                            fs, B_fs = fstat.next()
                            sc.op("act", lambda e, xr=xr, fs=fs: e.activation(
                                out=nt.sqj[:], in_=xr[:], func=AF.Square, accum_out=fs[:, 0:1]),
                                reads=[B_xr], writes=[B_fs, nt.B_sqj])
                            rstd_ops(fs, B_fs)
                            sc.op("dve", lambda e, xr=xr, fs=fs: e.scalar_tensor_tensor(
                                out=xr[:], in0=xr[:], scalar=fs[:, 1:2], in1=gfb[:], op0=ALU.mult, op1=ALU.mult),
                                reads=[B_xr, B_fs, B_gfb], writes=[B_xr])
                        sc.dma("sp", [(dst_d[r0:r0 + 128, :], xr[:])], reads=[B_xr],
                               writes=[B_dsts[ti]] if B_dsts else [], key=("xres_st", id(B_xr)))
            sc.barrier()

        def phase_proj():
            NQ = C_GA
            with ExitStack() as pes:
                wq = sbt(pes, "wq", [128, 8, NQ], BF16)
                B_wq = [Buf("wq%d" % b) for b in range((NQ + 639) // 640)]
                stgp = Pool(nc, pes, "stg", 2, [128, STGW], F32)
                load_cast_cols(stgp, wq, B_wq, win_d, 8, NQ, gain_i=1)
                nt = NormT(pes, nht=2, nxin=4)
                ps = Pool(nc, pes, "ps", 4, [128, 512], F32, psum=True)
                evp = Pool(nc, pes, "ev", 6, [128, 512], BF16)
                fm = []
                for g in range(4):
                    fm.append((C_QA + g * 128, lambda ti, g=g: qta_d[g, :, ti * TT:(ti + 1) * TT], 0.125))
                fm.append((C_KA, lambda ti: kta_d[:, ti * TT:(ti + 1) * TT], 1.0))
                for cc in range(4):
                    fm.append((C_QB + cc * 128, lambda ti, cc=cc: qtb_d[cc * 128:(cc + 1) * 128, ti * TT:(ti + 1) * TT], 0.125))
                for cc in range(4):
                    fm.append((C_KB + cc * 128, lambda ti, cc=cc: ktb_d[cc * 128:(cc + 1) * 128, ti * TT:(ti + 1) * TT], 1.0))
                evi = 0
                nxt = nt.run(x1_d, 0, B_x1[0])
                for ti in range(NT):
                    ht, B_ht = nxt
                    pcs = []
                    if ti + 1 < NT:
                        J = nt.begin(x1_d, ti + 1, B_x1[ti + 1])
                        nxt = (J["ht"], J["B_ht"])
                        pcs = nt.pieces(J)
                    for (col, dst, scale) in fm:
                        pp, B_pp = ps.next()
                        for k in range(8):
                            sc.op("pe", lambda e, pp=pp, k=k, col=col, ht=ht: e.matmul(
                                pp[:], lhsT=wq[:, k, col:col + 128], rhs=ht[:, k, :],
                                start=(k == 0), stop=(k == 7)), reads=wtoks(B_wq, col, 128) + B_ht, writes=[B_pp])
                        ev, B_ev = evp.next()
                        if evi % 2 == 0:
                            sc.op("dve", lambda e, ev=ev, pp=pp, scale=scale: e.tensor_scalar(
                                out=ev[:], in0=pp[:], scalar1=float(scale), scalar2=None, op0=ALU.mult),
                                reads=[B_pp], writes=[B_ev])
                        else:
                            sc.op("act", lambda e, ev=ev, pp=pp, scale=scale: e.activation(
                                out=ev[:], in_=pp[:], func=AF.Copy, scale=float(scale)),
                                reads=[B_pp], writes=[B_ev])
                        evi += 1
                        sc.dma("sp", [(dst(ti), ev[:])], reads=[B_ev], writes=[B_qkv[ti]], key=("ev", id(B_ev)))
                        if pcs:
                            pcs.pop(0)()
                    for j in range(4):
                        r0 = ti * TT + j * 128
                        pp, B_pp = ps.next()
                        for k in range(8):
                            sc.op("pe", lambda e, pp=pp, k=k, j=j, ht=ht: e.matmul(
                                pp[:], lhsT=ht[:, k, j * 128:(j + 1) * 128], rhs=wq[:, k, C_VB:C_VB + 512],
                                start=(k == 0), stop=(k == 7)), reads=wtoks(B_wq, C_VB, 512) + B_ht, writes=[B_pp])
                        ev, B_ev = evp.next()
                        sc.op("dve", lambda e, ev=ev, pp=pp: e.tensor_copy(out=ev[:], in_=pp[:]),
                              reads=[B_pp], writes=[B_ev])
                        sc.dma("sp", [(vb_d[r0:r0 + 128, :], ev[:])], reads=[B_ev], writes=[B_qkv[ti]],
                               key=("ev", id(B_ev)))
                        pp, B_pp = ps.next()
                        for k in range(8):
                            sc.op("pe", lambda e, pp=pp, k=k, j=j, ht=ht: e.matmul(
                                pp[:, 0:128], lhsT=ht[:, k, j * 128:(j + 1) * 128], rhs=wq[:, k, C_VA:C_VA + 128],
                                start=(k == 0), stop=(k == 7)), reads=wtoks(B_wq, C_VA, 128) + B_ht, writes=[B_pp])
                        ev, B_ev = evp.next()
                        sc.op("act", lambda e, ev=ev, pp=pp: e.copy(out=ev[:, 0:128], in_=pp[:, 0:128]),
                              reads=[B_pp], writes=[B_ev])
                        sc.dma("sp", [(va_d[r0:r0 + 128, :], ev[:, 0:128])], reads=[B_ev], writes=[B_qkv[ti]],
                               key=("ev", id(B_ev)))
                        if pcs:
                            pcs.pop(0)()
                    while pcs:
                        pcs.pop(0)()
            sc.barrier()

        def phase_swa():
            with ExitStack() as pes:
                kt = sbt(pes, "kta", [128, S], BF16)
                qt = sbt(pes, "qta", [128, 4, S], BF16)
                vv = sbt(pes, "vva", [128, NB, 128], BF16)
                B_in = Buf("swa_in")
                var = va_d.rearrange("(n p) c -> p n c", p=128)
                sc.dma("sp", [(kt[:], kta_d)] + [(qt[:, g, :], qta_d[g]) for g in range(4)]
                       + [(vv[:, n0:min(NB, n0 + 8), :], var[:, n0:min(NB, n0 + 8), :]) for n0 in range(0, NB, 8)],
                       reads=B_qkv, writes=[B_in], key="swa_in")
                bm = sbt(pes, "bm", [128, 2, 4, 256], F32)
                mk = sbt(pes, "mk", [128, 256], F32)
                sk = sbt(pes, "sk", [128, 2, 4], F32)
                B_bm = Buf("bm")
                B_sk = Buf("sk")
                bsrc = bias_d.rearrange("q (g kv) k -> q kv g k", kv=2)
                sc.dma("sp", [(bm[:, kv, :, :], bsrc[:, kv, :, :]) for kv in range(2)] + [(mk[:], maskc_d)],
                       writes=[B_bm], key="bm")
                sc.dma("sp", [(sk[:, kv, :], sinks2_d[kv].partition_broadcast(128)) for kv in range(2)],
                       writes=[B_sk], key="sk")
                for kv in range(2):
                    for g in range(4):
                        sc.op("dve", lambda e, kv=kv, g=g: e.tensor_tensor(
                            out=bm[:, kv, g, :], in0=bm[:, kv, g, :], in1=mk[:], op=ALU.add),
                            reads=[B_bm], writes=[B_bm])
                psc = Pool(nc, pes, "psc", 2, [128, 4, 256], F32, psum=True)
                ppt = Pool(nc, pes, "ppt", 2, [128, 8, 128], BF16, psum=True)
                pso = Pool(nc, pes, "pso", 2, [128, 4, 128], F32, psum=True)
                scs = Pool(nc, pes, "scs", 2, [128, 4, 256], F32)
                pbf = Pool(nc, pes, "pbf", 3, [128, 4, 256], BF16)
                ptb = Pool(nc, pes, "ptb", 2, [128, 8, 128], BF16)
                sm = Pool(nc, pes, "sm", 6, [128, 5, 4], F32)
                osb = Pool(nc, pes, "osb", 3, [128, 4, 128], BF16)
                units = [dict(n=n, kv=kv) for n in range(NB) for kv in range(2)]
                cur_ob = {}
                evc = [0]

                def stA(u):
                    n, kv = u["n"], u["kv"]
                    k0 = 0 if n > 0 else 128
                    kw = 256 - k0
                    ks = (n - 1) * 128 + k0
                    u.update(k0=k0, kw=kw)
                    pc, B_pc = psc.next()
                    u["pc"] = (pc, B_pc)
                    for g in range(4):
                        sc.op("pe", lambda e, pc=pc, g=g, kv=kv, n=n, ks=ks, kw=kw, k0=k0: e.matmul(
                            pc[:, g, k0:256], lhsT=qt[kv * 64:(kv + 1) * 64, g, n * 128:(n + 1) * 128],
                            rhs=kt[kv * 64:(kv + 1) * 64, ks:ks + kw], start=True, stop=True),
                            reads=[B_in], writes=[B_pc])

                def stB1(u):
                    kv, k0 = u["kv"], u["k0"]
                    pc, B_pc = u["pc"]
                    ss, B_ss = scs.next()
                    st, B_st = sm.next()
                    pb, B_pb = pbf.next()
                    u.update(st=(st, B_st), pb=(pb, B_pb))
                    sc.op("dve", lambda e, pc=pc, ss=ss, kv=kv, k0=k0: e.tensor_tensor(
                        out=ss[:, :, k0:256], in0=pc[:, :, k0:256], in1=bm[:, kv, :, k0:256], op=ALU.add),
                        reads=[B_pc, B_bm], writes=[B_ss])
                    sc.op("dve", lambda e, ss=ss, st=st, k0=k0: e.tensor_reduce(
                        out=st[:, 0, :], in_=ss[:, :, k0:256], axis=AX.X, op=ALU.max),
                        reads=[B_ss], writes=[B_st])
                    sc.op("dve", lambda e, st=st, kv=kv: e.tensor_tensor(
                        out=st[:, 0, :], in0=st[:, 0, :], in1=sk[:, kv, :], op=ALU.max),
                        reads=[B_st, B_sk], writes=[B_st])
                    sc.op("dve", lambda e, st=st: e.tensor_scalar(out=st[:, 1, :], in0=st[:, 0, :], scalar1=-1.0,
                                                                   scalar2=None, op0=ALU.mult),
                          reads=[B_st], writes=[B_st])
                    sc.op("dve", lambda e, st=st, kv=kv: e.tensor_tensor(
                        out=st[:, 2, :], in0=sk[:, kv, :], in1=st[:, 1, :], op=ALU.add),
                        reads=[B_st, B_sk], writes=[B_st])
                    for g in range(4):
                        sc.op("act", lambda e, pb=pb, ss=ss, st=st, g=g, k0=k0: e.activation(
                            out=pb[:, g, k0:256], in_=ss[:, g, k0:256], func=AF.Exp, bias=st[:, 1, g:g + 1],
                            accum_out=st[:, 3, g:g + 1]), reads=[B_ss, B_st], writes=[B_pb, B_st])
                    sc.op("act", lambda e, st=st: e.activation(out=st[:, 2, :], in_=st[:, 2, :], func=AF.Exp),
                          reads=[B_st], writes=[B_st])

                def stB2(u):
                    n, kv, k0, kw = u["n"], u["kv"], u["k0"], u["kw"]
                    st, B_st = u["st"]
                    pb, B_pb = u["pb"]
                    sc.op("dve", lambda e, st=st: e.tensor_tensor(out=st[:, 4, :], in0=st[:, 3, :], in1=st[:, 2, :], op=ALU.add),
                          reads=[B_st], writes=[B_st])
                    sc.op("dve", lambda e, st=st: e.reciprocal(out=st[:, 4, :], in_=st[:, 4, :]),
                          reads=[B_st], writes=[B_st])
                    sc.op("pool", lambda e, pb=pb, st=st, k0=k0, kw=kw: e.tensor_tensor(
                        out=pb[:, :, k0:256], in0=pb[:, :, k0:256],
                        in1=st[:, 4, :].unsqueeze(2).to_broadcast([128, 4, kw]), op=ALU.mult),
                        reads=[B_pb, B_st], writes=[B_pb])
                    nkb = kw // 128
                    pp, B_pp = ppt.next()
                    pt, B_pt = ptb.next()
                    for kb in range(nkb):
                        for g in range(4):
                            sc.op("pe", lambda e, pp=pp, pb=pb, g=g, kb=kb, k0=k0: e.transpose(
                                out=pp[:, kb * 4 + g, :], in_=pb[:, g, k0 + kb * 128:k0 + (kb + 1) * 128], identity=ident),
                                reads=[B_pb, B_cst], writes=[B_pp])
                    evc[0] += 1
                    if evc[0] % 2 == 0:
                        sc.op("dve", lambda e, pp=pp, pt=pt, nkb=nkb: e.tensor_copy(
                            out=pt[:, 0:4 * nkb, :], in_=pp[:, 0:4 * nkb, :]), reads=[B_pp], writes=[B_pt])
                    else:
                        sc.op("act", lambda e, pp=pp, pt=pt, nkb=nkb: e.copy(
                            out=pt[:, 0:4 * nkb, :], in_=pp[:, 0:4 * nkb, :]), reads=[B_pp], writes=[B_pt])
                    po, B_po = pso.next()
                    for g in range(4):
                        h = g + 4 * kv
                        cl, half = (h // 2) - 2 * kv, h % 2
                        for kb in range(nkb):
                            blk = n - (nkb - 1) + kb
                            sc.op("pe", lambda e, po=po, pt=pt, cl=cl, half=half, kv=kv, g=g, kb=kb, blk=blk, nkb=nkb: e.matmul(
                                po[half * 64:(half + 1) * 64, cl, :], lhsT=vv[:, blk, kv * 64:(kv + 1) * 64],
                                rhs=pt[:, kb * 4 + g, :], start=(kb == 0), stop=(kb == nkb - 1)),
                                reads=[B_pt, B_in], writes=[B_po])
                    if kv == 0:
                        cur_ob[n] = osb.next()
                    ob, B_ob = cur_ob[n]
                    sc.op("act", lambda e, ob=ob, po=po, kv=kv: e.copy(out=ob[:, 2 * kv:2 * kv + 2, :], in_=po[:, 0:2, :]),
                          reads=[B_po], writes=[B_ob])
                    if kv == 1:
                        sc.dma("sp", [(ota_d.rearrange("(c p) s -> p c s", p=128)[:, :, n * 128:(n + 1) * 128], ob[:])],
                               reads=[B_ob], writes=[B_ota[n // 4]], key=("osb", id(B_ob)))
                        del cur_ob[n]

                nu = len(units)
                for t in range(nu + 2):
                    if t < nu:
                        stA(units[t])
                    if 0 <= t - 1 < nu:
                        stB1(units[t - 1])
                    if 0 <= t - 2 < nu:
                        stB2(units[t - 2])
            sc.barrier()

        def phase_sb():
            with ExitStack() as pes:
                ktp = Pool(nc, pes, "ktb", 2, [128, S], BF16)
                qtp = Pool(nc, pes, "qtb", 2, [128, S], BF16)
                vvp = Pool(nc, pes, "vvb", 2, [128, NB, 128], BF16)
                zp = Pool(nc, pes, "zp", 3, [128, 2, 512], F32, psum=True)
                op_ = Pool(nc, pes, "op", 2, [128, 512], F32, psum=True)
                Ep = Pool(nc, pes, "E", 2, [128, 2, 512], F32)
                Lp = Pool(nc, pes, "L", 3, [128, 2, 512], BF16)
                Ap = Pool(nc, pes, "A", 2, [128, 2, 512], BF16)
                R32 = sbt(pes, "R32", [128, 2, 512], F32)
                B_R32 = Buf("R32")
                Rbp = Pool(nc, pes, "Rb", 2, [128, 2, 512], BF16)
                oev = Pool(nc, pes, "oev", 2, [128, 512], BF16)
                steps = []
                pair_in = {}
                for p in range(4):
                    for i in range(NT):
                        nk = 4 * i + 4
                        for si, kj in enumerate(range(nk - 1, -1, -1)):
                            steps.append(dict(p=p, i=i, kj=kj, first=(si == 0), last=(kj == 0),
                                              c0=max(0, (kj - 4 * i)) * 128, diag=(kj >= 4 * i)))

                def load_pair(p):
                    kt, B_kt = ktp.next()
                    qt, B_qt = qtp.next()
                    vv, B_vv = vvp.next()
                    sc.dma("sp", [(kt[:], ktb_d[p * 128:(p + 1) * 128, :])], reads=B_qkv, writes=[B_kt], key=("sbk", id(B_kt)))
                    sc.dma("sp", [(qt[:], qtb_d[p * 128:(p + 1) * 128, :])], reads=B_qkv, writes=[B_qt], key=("sbq", id(B_qt)))
                    vbr = vb_d[:, p * 128:(p + 1) * 128].rearrange("(n p) c -> p n c", p=128)
                    sc.dma("sp", [(vv[:, n0:min(NB, n0 + 8), :], vbr[:, n0:min(NB, n0 + 8), :]) for n0 in range(0, NB, 8)],
                           reads=B_qkv, writes=[B_vv], key=("sbv", id(B_vv)))
                    pair_in[p] = (kt, B_kt, qt, B_qt, vv, B_vv)

                load_pair(0)
                state = {}

                def stage0(s):
                    p, i, kj, c0 = s["p"], s["i"], s["kj"], s["c0"]
                    if s["first"] and i == 0 and p + 1 < 4:
                        load_pair(p + 1)
                    kt, B_kt, qt, B_qt, vv, B_vv = pair_in[p]
                    z, B_z = zp.next()
                    s["z"], s["B_z"] = z, B_z
                    for hh in range(2):
                        sc.op("pe", lambda e, z=z, hh=hh, kj=kj, i=i, c0=c0, kt=kt, qt=qt: e.matmul(
                            z[:, hh, c0:512], lhsT=kt[hh * 64:(hh + 1) * 64, kj * 128:(kj + 1) * 128],
                            rhs=qt[hh * 64:(hh + 1) * 64, i * 512 + c0:(i + 1) * 512],
                            start=True, stop=False, skip_group_check=True),
                            reads=[B_kt, B_qt], writes=[B_z])
                    if s["diag"]:
                        for hh in range(2):
                            sc.op("pe", lambda e, z=z, hh=hh, c0=c0: e.matmul(
                                z[:, hh, c0:c0 + 128], lhsT=ident, rhs=dmask, start=False, stop=False,
                                skip_group_check=True), reads=[B_cst], writes=[B_z])

                def stage1(s):
                    z, B_z, c0 = s["z"], s["B_z"], s["c0"]
                    E, B_E = Ep.next()
                    L, B_L = Lp.next()
                    s["L"], s["B_L"] = L, B_L
                    sc.op("act", lambda e, E=E, z=z, c0=c0: e.activation(out=E[:, :, c0:512], in_=z[:, :, c0:512], func=AF.Exp),
                          reads=[B_z], writes=[B_E])
                    sc.op("act", lambda e, E=E, L=L, c0=c0: e.activation(out=L[:, :, c0:512], in_=E[:, :, c0:512], func=AF.Ln, bias=1.0),
                          reads=[B_E], writes=[B_L])

                def stage2(s):
                    p, i, kj, c0 = s["p"], s["i"], s["kj"], s["c0"]
                    kt, B_kt, qt, B_qt, vv, B_vv = pair_in[p]
                    z, B_z, L, B_L = s["z"], s["B_z"], s["L"], s["B_L"]
                    for hh in range(2):
                        sc.op("pe", lambda e, z=z, hh=hh, c0=c0, L=L: e.matmul(
                            z[:, hh, c0:512], lhsT=ntri, rhs=L[:, hh, c0:512], start=False, stop=False,
                            skip_group_check=True), reads=[B_L, B_cst], writes=[B_z])
                    if not s["first"]:
                        Rb, B_Rb = state["Rb"]
                        for hh in range(2):
                            sc.op("pe", lambda e, z=z, hh=hh, c0=c0, Rb=Rb: e.matmul(
                                z[:, hh, c0:512], lhsT=nones, rhs=Rb[:, hh, c0:512], start=False, stop=False,
                                skip_group_check=True), reads=[B_Rb, B_cst], writes=[B_z])
                    A, B_A = Ap.next()
                    sc.op("act", lambda e, A=A, z=z, c0=c0: e.activation(out=A[:, :, c0:512], in_=z[:, :, c0:512], func=AF.Exp),
                          reads=[B_z], writes=[B_A])
                    if s["first"]:
                        state["o"] = op_.next()
                    o, B_o = state["o"]
                    for hh in range(2):
                        sc.op("pe", lambda e, o=o, hh=hh, c0=c0, A=A, vv=vv, kj=kj, first=s["first"]: e.matmul(
                            o[hh * 64:(hh + 1) * 64, c0:512], lhsT=vv[:, kj, hh * 64:(hh + 1) * 64],
                            rhs=A[:, hh, c0:512], start=first, stop=False, skip_group_check=True),
                            reads=[B_A, B_vv], writes=[B_o])
                    if not s["last"]:
                        if s["first"]:
                            sc.op("pool", lambda e: e.memset(R32[:], 0.0), writes=[B_R32])
                        sc.op("pool", lambda e, L=L, c0=c0: e.tensor_tensor(
                            out=R32[:, :, c0:512], in0=R32[:, :, c0:512], in1=L[:, :, c0:512], op=ALU.add),
                            reads=[B_L, B_R32], writes=[B_R32])
                        Rb, B_Rb = Rbp.next()
                        sc.op("dve", lambda e, Rb=Rb: e.tensor_copy(out=Rb[:], in_=R32[:]), reads=[B_R32], writes=[B_Rb])
                        state["Rb"] = (Rb, B_Rb)
                    else:
                        ob, B_ob = oev.next()
                        sc.op("dve", lambda e, ob=ob, o=o: e.tensor_copy(out=ob[:], in_=o[:]), reads=[B_o], writes=[B_ob])
                        sc.dma("sp", [(otb_d[p * 128:(p + 1) * 128, i * 512:(i + 1) * 512], ob[:])], reads=[B_ob],
                               writes=[B_otb[i]], key=("oev", id(B_ob)))

                n = len(steps)
                for t in range(n + 2):
                    if t < n:
                        stage0(steps[t])
                    if 0 <= t - 1 < n:
                        stage1(steps[t - 1])
                    if 0 <= t - 2 < n:
                        stage2(steps[t - 2])
            sc.barrier()

        def phase_mix():
            with ExitStack() as pes:
                wg = sbt(pes, "wg", [128, 8, 2048], BF16)
                wba = sbt(pes, "wba", [128, 4, D], BF16)
                wbb = sbt(pes, "wbb", [128, 4, D], BF16)
                wo = sbt(pes, "wo", [128, 8, D], BF16)
                B_wg = [Buf("wg%d" % b) for b in range(4)]
                B_wba = [Buf("wba%d" % k) for k in range(4)]
                B_wbb = [Buf("wbb%d" % k) for k in range(4)]
                B_wo = [Buf("wo%d" % k) for k in range(8)]
                stgp = Pool(nc, pes, "stg", 2, [128, STGW], F32)
                load_cast_cols(stgp, wg, B_wg, win_d[:, C_GA:INW], 8, 2048, gain_i=1)
                for k in range(4):
                    load_cast(stgp, wba, lambda a, b, k=k: wba[:, k, a:b], B_wba[k], wba_d[k * 128:(k + 1) * 128, :], D)
                    load_cast(stgp, wbb, lambda a, b, k=k: wbb[:, k, a:b], B_wbb[k], wbb_d[k * 128:(k + 1) * 128, :], D)
                for k in range(8):
                    load_cast(stgp, wo, lambda a, b, k=k: wo[:, k, a:b], B_wo[k], wout_d[k * 128:(k + 1) * 128, :], D)
                nt = NormT(pes, nht=2, nxin=4)
                ps = Pool(nc, pes, "ps", 6, [128, 512], F32, psum=True)
                otap = Pool(nc, pes, "ota", 2, [128, 4, TT], BF16)
                otbp = Pool(nc, pes, "otb", 2, [128, 4, TT], BF16)
                MT = sbt(pes, "MT", [128, 8, TT], BF16)
                B_MT = [Buf("MT%d" % c) for c in range(8)]
                sgp = Pool(nc, pes, "sgm", 4, [128, TT], F32)
                mp = Pool(nc, pes, "mm", 4, [128, TT], F32)
                xres = Pool(nc, pes, "xres", 3, [128, D], F32)
                nxt = nt.run(x1_d, 0, B_x1[0])
                for ti in range(NT):
                    ht, B_ht = nxt
                    pcs = []
                    if ti + 1 < NT:
                        J = nt.begin(x1_d, ti + 1, B_x1[ti + 1])
                        nxt = (J["ht"], J["B_ht"])
                        pcs = nt.pieces(J)
                    oa, B_oa = otap.next()
                    ob, B_ob = otbp.next()
                    sc.dma("sp", [(oa[:], ota_d.rearrange("(c p) s -> p c s", p=128)[:, :, ti * TT:(ti + 1) * TT])],
                           reads=[B_ota[ti]], writes=[B_oa], key=("ota", id(B_oa)))
                    sc.dma("sp", [(ob[:], otb_d.rearrange("(c p) s -> p c s", p=128)[:, :, ti * TT:(ti + 1) * TT])],
                           reads=[B_otb[ti]], writes=[B_ob], key=("otb", id(B_ob)))
                    for cc in range(8):
                        ms = []
                        for br, (wbr, B_wbr, ot, B_ot, goff) in enumerate(((wba, B_wba, oa, B_oa, 0), (wbb, B_wbb, ob, B_ob, 1024))):
                            pg, B_pg = ps.next()
                            pb, B_pb = ps.next()
                            for k in range(8):
                                sc.op("pe", lambda e, pg=pg, k=k, cc=cc, goff=goff, ht=ht: e.matmul(
                                    pg[:], lhsT=wg[:, k, goff + cc * 128:goff + (cc + 1) * 128], rhs=ht[:, k, :],
                                    start=(k == 0), stop=(k == 7)), reads=wtoks(B_wg, goff + cc * 128, 128) + B_ht, writes=[B_pg])
                            for k in range(4):
                                sc.op("pe", lambda e, pb=pb, k=k, cc=cc, wbr=wbr, ot=ot: e.matmul(
                                    pb[:], lhsT=wbr[:, k, cc * 128:(cc + 1) * 128], rhs=ot[:, k, :],
                                    start=(k == 0), stop=(k == 3)), reads=[B_wbr[k], B_ot], writes=[B_pb])
                            sg, B_sg = sgp.next()
                            sc.op("act", lambda e, sg=sg, pg=pg: e.activation(out=sg[:], in_=pg[:], func=AF.Sigmoid),
                                  reads=[B_pg], writes=[B_sg])
                            m, B_m = mp.next()
                            sc.op("dve", lambda e, m=m, sg=sg, pb=pb: e.tensor_tensor(out=m[:], in0=sg[:], in1=pb[:], op=ALU.mult),
                                  reads=[B_sg, B_pb], writes=[B_m])
                            ms.append((m, B_m))
                            if pcs:
                                pcs.pop(0)()
                        sc.op("pool", lambda e, cc=cc, ms=ms: e.tensor_tensor(
                            out=MT[:, cc, :], in0=ms[0][0][:], in1=ms[1][0][:], op=ALU.add),
                            reads=[ms[0][1], ms[1][1]], writes=[B_MT[cc]])
                    while pcs:
                        pcs.pop(0)()
                    for j in range(4):
                        xr, B_xr = xres.next()
                        r0 = ti * TT + j * 128
                        sc.dma("sp", [(xr[:], x1_d[r0:r0 + 128, :])], reads=[B_x1[ti]], writes=[B_xr], key=("xres", id(B_xr)))
                        for half in range(2):
                            po, B_po = ps.next()
                            for cc in range(8):
                                sc.op("pe", lambda e, po=po, cc=cc, j=j, half=half: e.matmul(
                                    po[:], lhsT=MT[:, cc, j * 128:(j + 1) * 128], rhs=wo[:, cc, half * 512:(half + 1) * 512],
                                    start=(cc == 0), stop=(cc == 7)), reads=[B_MT[cc], B_wo[cc]], writes=[B_po])
                            sc.op("dve", lambda e, po=po, xr=xr, half=half: e.tensor_tensor(
                                out=xr[:, half * 512:(half + 1) * 512], in0=po[:], in1=xr[:, half * 512:(half + 1) * 512], op=ALU.add),
                                reads=[B_po, B_xr], writes=[B_xr])
                        sc.dma("sp", [(x2_d[r0:r0 + 128, :], xr[:])], reads=[B_xr], writes=[B_x2[ti]], key=("xres_st", id(B_xr)))
            sc.barrier()

        sc.barrier()
        if "ffn1" in phases:
            phase_ffn(0, x_d, None, x1_d, B_x1, final=False)
        if "proj" in phases:
            phase_proj()
        if "swa" in phases:
            phase_swa()
        if "sb" in phases:
            phase_sb()
        if "mix" in phases:
            phase_mix()
        if "ffn2" in phases:
            phase_ffn(1, x2_d, B_x2, out_d, None, final=True)
        sc.emit()
    return nc


def _rel_bucket_np(dist):
    max_exact = 16
    d = np.maximum(dist, 1).astype(np.float32)
    large = max_exact + (np.log(d / max_exact) / np.float32(np.log(128 / max_exact)) * (32 - max_exact)).astype(np.int32)
    large = np.minimum(large, 31)
    return np.where(dist < max_exact, dist, large)


def _bucket_table():
    import jax
    import jax.numpy as jnp
    qi = np.arange(128)[:, None] + 128
    kj = np.arange(256)[None, :]
    dist = qi - kj
    band = (dist >= 0) & (dist < 128)
    with jax.default_device(jax.devices("cpu")[0]):
        dj = jnp.maximum(jnp.asarray(dist), 0)
        max_exact = 16
        d = jnp.maximum(dj, 1).astype(jnp.float32)
        large = max_exact + (jnp.log(d / max_exact) / np.log(128 / max_exact) * (32 - max_exact)).astype(jnp.int32)
        large = jnp.minimum(large, 31)
        bucket = np.asarray(jnp.where(dj < max_exact, dj, large))
    return bucket, band


def _consts():
    c = np.zeros((128, 512), np.float32)
    c[:, 0:128] = np.eye(128)
    j = np.arange(128)[:, None]
    s = np.arange(128)[None, :]
    c[:, 128:256] = np.where(j >= s, -1.0, 0.0)
    c[:, 256:384] = -1.0
    c[:, 384:512] = np.where(j < s, 0.0, MASKB)
    return c.astype(ml_dtypes.bfloat16)


_PROG_CACHE = {}


def _prepare_shared(inp, S):
    f = lambda a: np.ascontiguousarray(np.asarray(a, dtype=np.float32))
    bucket, band = _bucket_table()
    rb = f(inp["rel_bias"])
    bias = rb[bucket]
    order = [g + 4 * kv for g in range(4) for kv in range(2)]
    bias = np.ascontiguousarray(bias.transpose(0, 2, 1)[:, order, :])
    mask = np.where(band, 0.0, NEG).astype(np.float32)
    w_in = f(inp["w_in"])[0]
    qcols = []
    for g in range(4):
        for kv in range(2):
            h = g + 4 * kv
            qcols.extend(range(h * 64, (h + 1) * 64))
    w_in = np.ascontiguousarray(np.concatenate([w_in[:, qcols], w_in[:, 512:]], axis=1))
    sinks = f(inp["swa_sinks"])[0].reshape(2, 4)
    shared = {
        "ffn1_w1": f(inp["ffn1_w1"])[0], "ffn1_w3": f(inp["ffn1_w3"])[0], "ffn1_w2": f(inp["ffn1_w2"])[0],
        "ffn2_w1": f(inp["ffn2_w1"])[0], "ffn2_w3": f(inp["ffn2_w3"])[0], "ffn2_w2": f(inp["ffn2_w2"])[0],
        "gains": np.ascontiguousarray(np.stack([f(inp["norm_ffn1"])[0].reshape(8, 128).T,
                                                f(inp["norm_mix"])[0].reshape(8, 128).T,
                                                f(inp["norm_ffn2"])[0].reshape(8, 128).T], axis=1)),
        "norm_final": f(inp["norm_final"]),
        "w_in": w_in, "swa_sinks": np.ascontiguousarray(sinks), "swa_bias": bias, "swa_mask": mask,
        "w_branch_swa": f(inp["w_branch_swa"])[0], "w_branch_sb": f(inp["w_branch_sb"])[0],
        "w_out": f(inp["w_out"])[0], "consts": _consts(),
    }
    return shared


def kernel(**inputs):
    x = np.asarray(inputs["x"], dtype=np.float32)
    B, S, _ = x.shape
    if S not in _PROG_CACHE:
        _PROG_CACHE[S] = build_program(S)
    nc = _PROG_CACHE[S]
    shared = _prepare_shared(inputs, S)
    in_maps = []
    for b in range(B):
        m = dict(shared)
        m["x"] = np.ascontiguousarray(x[b])
        in_maps.append(m)
    res = run_bass_kernel_spmd(nc, in_maps, core_ids=list(range(B)))
    return np.stack([np.asarray(r["out"], dtype=np.float32) for r in res.results], axis=0)
```

```python
from contextlib import ExitStack

import numpy as np
import ml_dtypes

import concourse.bass as bass
import concourse.mybir as mybir
from concourse.bass_utils import run_bass_kernel_spmd

F32 = mybir.dt.float32
BF16 = mybir.dt.bfloat16
AF = mybir.ActivationFunctionType
ALU = mybir.AluOpType
AX = mybir.AxisListType

D = 1024
DFF = 2816
NFF = DFF // 128
INW = 4352
EPS = 1e-6
NEG = -1e30
TT = 512
MASKB = -30000.0
STGW = 704

C_QA, C_KA, C_VA, C_QB, C_KB, C_VB, C_GA, C_GB = 0, 512, 640, 768, 1280, 1792, 2304, 3328


class Buf:
    __slots__ = ("name", "lw", "rd", "dmard")

    def __init__(self, name):
        self.name = name
        self.lw = None
        self.rd = {}
        self.dmard = []


class Op:
    __slots__ = ("eng", "fn", "deps", "signal", "sem", "count", "is_dma", "pairs")

    def __init__(self, eng, fn, is_dma=False):
        self.eng = eng
        self.fn = fn
        self.deps = []
        self.signal = False
        self.sem = None
        self.count = 0
        self.is_dma = is_dma
        self.pairs = None


SEM_LIMIT = 30000


class Sched:
    ENGS = ("pe", "act", "dve", "pool", "sp")

    def __init__(self, nc, es):
        self.nc = nc
        self.es = es
        self.ops = {e: [] for e in self.ENGS}
        self.dma_sems = {}
        self.barrier_deps = {e: None for e in self.ENGS}

    def _newsem(self, name):
        return self.es.enter_context(self.nc.semaphore(name))

    def _add(self, op, reads, writes):
        deps = []
        bd = self.barrier_deps[op.eng]
        if bd is not None:
            deps.extend(bd)
            self.barrier_deps[op.eng] = None
        for b in reads:
            w = b.lw
            if w is not None:
                if not (w.eng == op.eng == "pe" and not w.is_dma and not op.is_dma):
                    deps.append(w)
        for b in writes:
            w = b.lw
            if w is not None:
                if w.is_dma or op.is_dma or w.eng != op.eng:
                    deps.append(w)
            for e, r in b.rd.items():
                if op.is_dma or e != op.eng:
                    deps.append(r)
            deps.extend(b.dmard)
        for b in reads:
            if op.is_dma:
                b.dmard.append(op)
            else:
                b.rd[op.eng] = op
        for b in writes:
            b.lw = op
            b.rd = {}
            b.dmard = []
        seen = set()
        for d in deps:
            if id(d) not in seen and d is not op:
                seen.add(id(d))
                op.deps.append(d)
                d.signal = True
        self.ops[op.eng].append(op)
        return op

    def op(self, eng, fn, reads=(), writes=()):
        return self._add(Op(eng, fn), reads, writes)

    def dma(self, queue, pairs, reads=(), writes=(), key=None):
        op = Op(queue, None, is_dma=True)
        op.pairs = pairs
        op.signal = True
        if key not in self.dma_sems:
            self.dma_sems[key] = [self._newsem("d%d" % len(self.dma_sems)), 0, None]
        ent = self.dma_sems[key]
        ent[1] += 16 * len(pairs)
        ent[2] = op
        op.sem = ent[0]
        op.count = ent[1]
        return self._add(op, reads, writes)

    def barrier(self):
        prev = []
        for e in self.ENGS:
            for op in reversed(self.ops[e]):
                if not op.is_dma:
                    prev.append(op)
                    break
        for ent in self.dma_sems.values():
            if ent[2] is not None:
                prev.append(ent[2])
        for e in self.ENGS:
            self.barrier_deps[e] = list(prev)

    def emit(self):
        nc = self.nc
        eng_sems = {}
        for e in self.ENGS:
            n = 0
            for op in self.ops[e]:
                if op.is_dma or not op.signal:
                    continue
                k = n // SEM_LIMIT
                if (e, k) not in eng_sems:
                    eng_sems[(e, k)] = self._newsem("e_%s%d" % (e, k))
                op.sem = eng_sems[(e, k)]
                op.count = n % SEM_LIMIT + 1
                n += 1
        final_waits = [(ent[0], ent[1]) for ent in self.dma_sems.values()]

        def run(e, eng, final=False):
            waited = {}
            for op in self.ops[e]:
                for d in op.deps:
                    k = id(d.sem)
                    if waited.get(k, 0) < d.count:
                        eng.wait_ge(d.sem, d.count)
                        waited[k] = d.count
                if op.is_dma:
                    for (o, i) in op.pairs:
                        eng.dma_start(out=o, in_=i).then_inc(op.sem, 16)
                else:
                    ins = op.fn(eng)
                    if op.signal:
                        ins.then_inc(op.sem, 1)
            if final:
                for (h, c) in final_waits:
                    if waited.get(id(h), 0) < c:
                        eng.wait_ge(h, c)

        with nc.Block() as block:
            @block.sync
            def _(sync):
                run("sp", sync, final=True)

            @block.tensor
            def _(tensor):
                run("pe", tensor)

            @block.scalar
            def _(scalar):
                run("act", scalar)

            @block.vector
            def _(vector):
                run("dve", vector)

            @block.gpsimd
            def _(gpsimd):
                run("pool", gpsimd)


class Pool:
    uid = [0]

    def __init__(self, nc, es, name, n, shape, dtype, psum=False):
        self.tiles = []
        for i in range(n):
            Pool.uid[0] += 1
            nm = "%s%d_%d" % (name, i, Pool.uid[0])
            if psum:
                t = es.enter_context(nc.psum_tensor(nm, list(shape), dtype))
            else:
                t = es.enter_context(nc.sbuf_tensor(nm, list(shape), dtype))
            self.tiles.append((t, Buf(nm)))
        self.i = 0

    def next(self):
        t = self.tiles[self.i % len(self.tiles)]
        self.i += 1
        return t


def build_program(S, phases=("ffn1", "proj", "swa", "sb", "mix", "ffn2"), debug=False):
    NT = S // TT
    NB = S // 128
    nc = bass.Bass("TRN2", target_bir_lowering=False)
    dk = "ExternalOutput" if debug else "Internal"

    def din(name, shape, dt=F32):
        return nc.dram_tensor(name, list(shape), dt, kind="ExternalInput").ap()

    def dscr(name, shape, dt):
        return nc.dram_tensor(name, list(shape), dt, kind=dk).ap()

    x_d = din("x", [S, D])
    w1_d = [din("ffn1_w1", [D, DFF]), din("ffn2_w1", [D, DFF])]
    w3_d = [din("ffn1_w3", [D, DFF]), din("ffn2_w3", [D, DFF])]
    w2_d = [din("ffn1_w2", [DFF, D]), din("ffn2_w2", [DFF, D])]
    gains_d = din("gains", [128, 3, 8])
    gfin_d = din("norm_final", [D])
    win_d = din("w_in", [D, INW])
    sinks2_d = din("swa_sinks", [2, 4])
    bias_d = din("swa_bias", [128, 8, 256])
    maskc_d = din("swa_mask", [128, 256])
    wba_d = din("w_branch_swa", [512, D])
    wbb_d = din("w_branch_sb", [512, D])
    wout_d = din("w_out", [D, D])
    cst_d = din("consts", [128, 4 * 128], BF16)
    out_d = nc.dram_tensor("out", [S, D], F32, kind="ExternalOutput").ap()

    x1_d = dscr("x1", [S, D], F32)
    x2_d = dscr("x2", [S, D], F32)
    qta_d = dscr("qta", [4, 128, S], BF16)
    kta_d = dscr("kta", [128, S], BF16)
    va_d = dscr("va", [S, 128], BF16)
    qtb_d = dscr("qtb", [512, S], BF16)
    ktb_d = dscr("ktb", [512, S], BF16)
    vb_d = dscr("vb", [S, 512], BF16)
    ota_d = dscr("ota", [512, S], BF16)
    otb_d = dscr("otb", [512, S], BF16)

    B_x1 = [Buf("x1_%d" % i) for i in range(NT)]
    B_x2 = [Buf("x2_%d" % i) for i in range(NT)]
    B_qkv = [Buf("qkv_%d" % i) for i in range(NT)]
    B_ota = [Buf("ota_%d" % i) for i in range(NT)]
    B_otb = [Buf("otb_%d" % i) for i in range(NT)]

    with ExitStack() as es:
        sc = Sched(nc, es)

        def sbt(stack, name, shape, dt):
            Pool.uid[0] += 1
            return stack.enter_context(nc.sbuf_tensor("%s_%d" % (name, Pool.uid[0]), list(shape), dt))

        cst = sbt(es, "cst", [128, 512], BF16)
        B_cst = Buf("cst")
        sc.dma("sp", [(cst[:], cst_d)], writes=[B_cst], key="cst")
        ident = cst[:, 0:128]
        ntri = cst[:, 128:256]
        nones = cst[:, 256:384]
        dmask = cst[:, 384:512]

        gcol = sbt(es, "gcol", [128, 3, 8], F32)
        B_gcol = Buf("gcol")

        sc.dma("sp", [(gcol[:], gains_d)], writes=[B_gcol], key="gcol")

        def load_cast(stgp, dst_t, sel, B_dst, src_rows, ncols, gain=None):
            c0 = 0
            while c0 < ncols:
                c1 = min(ncols, c0 + STGW)
                st, B_st = stgp.next()
                w = c1 - c0
                sc.dma("sp", [(st[:, 0:w], src_rows[:, c0:c1])], writes=[B_st], key=("stg", id(B_st)))
                o = sel(c0, c1)
                if gain is None:
                    sc.op("dve", lambda e, o=o, i=st[:, 0:w]: e.tensor_copy(out=o, in_=i),
                          reads=[B_st], writes=[B_dst])
                else:
                    sc.op("dve", lambda e, o=o, i=st[:, 0:w], g=gain:
                          e.tensor_scalar(out=o, in0=i, scalar1=g, scalar2=None, op0=ALU.mult),
                          reads=[B_st, B_gcol], writes=[B_dst])
                c0 = c1

        def rstd_ops(st, B_st):
            sc.op("dve", lambda e, st=st: e.tensor_scalar(
                out=st[:, 1:2], in0=st[:, 0:1], scalar1=float(D * EPS), scalar2=None, op0=ALU.add),
                reads=[B_st], writes=[B_st])
            sc.op("act", lambda e, st=st: e.activation(out=st[:, 1:2], in_=st[:, 1:2], func=AF.Sqrt),
                  reads=[B_st], writes=[B_st])
            sc.op("dve", lambda e, st=st: e.reciprocal(out=st[:, 1:2], in_=st[:, 1:2]),
                  reads=[B_st], writes=[B_st])

        class NormT:
            def __init__(self, stack, nht=1, nxin=2):
                self.nxin = nxin
                self.xin = Pool(nc, stack, "xin", nxin, [128, D], F32)
                self.hrow = Pool(nc, stack, "hrow", 2, [128, D], BF16)
                self.sqj = sbt(stack, "sqj", [128, D], BF16)
                self.B_sqj = Buf("sqj")
                self.stat = Pool(nc, stack, "stat", 8, [128, 2], F32)
                self.hts = [(sbt(stack, "ht%d" % i, [128, 8, TT], BF16), [Buf("ht%d_%d" % (i, j)) for j in range(4)])
                            for i in range(nht)]
                self.hi = 0
                self.tpp = Pool(nc, stack, "tpp", 2, [128, 8, 128], BF16, psum=True)
                self.ev = 0

            def _load(self, J, j):
                xt, B_xt = self.xin.next()
                r0 = J["ti"] * TT + j * 128
                sc.dma("sp", [(xt[:], J["src"][r0:r0 + 128, :])], reads=[J["B_src"]] if J["B_src"] else [],
                       writes=[B_xt], key=("xin", id(B_xt)))
                J["x"][j] = (xt, B_xt)

            def begin(self, src_d, ti, B_src):
                ht_, B_ht_ = self.hts[self.hi % len(self.hts)]
                self.hi += 1
                J = dict(src=src_d, ti=ti, B_src=B_src, ht=ht_, B_ht=B_ht_, x={}, st={})
                if self.nxin >= 4:
                    for j in range(4):
                        self._load(J, j)
                return J

            def piece_a(self, J, j):
                if j not in J["x"]:
                    self._load(J, j)
                xt, B_xt = J["x"][j]
                st, B_st = self.stat.next()
                J["st"][j] = (st, B_st)
                sc.op("act", lambda e, xt=xt, st=st: e.activation(
                    out=self.sqj[:], in_=xt[:], func=AF.Square, accum_out=st[:, 0:1]),
                    reads=[B_xt], writes=[B_st, self.B_sqj])
                sc.op("dve", lambda e, st=st: e.tensor_scalar(
                    out=st[:, 1:2], in0=st[:, 0:1], scalar1=float(D * EPS), scalar2=None, op0=ALU.add),
                    reads=[B_st], writes=[B_st])

            def piece_b(self, J, j):
                st, B_st = J["st"][j]
                sc.op("act", lambda e, st=st: e.activation(out=st[:, 1:2], in_=st[:, 1:2], func=AF.Sqrt),
                      reads=[B_st], writes=[B_st])
                sc.op("dve", lambda e, st=st: e.reciprocal(out=st[:, 1:2], in_=st[:, 1:2]),
                      reads=[B_st], writes=[B_st])

            def piece_c(self, J, j):
                xt, B_xt = J["x"][j]
                st, B_st = J["st"][j]
                ht_, B_ht_ = J["ht"], J["B_ht"]
                hr, B_hr = self.hrow.next()
                sc.op("dve", lambda e, hr=hr, xt=xt, st=st: e.tensor_scalar(
                    out=hr[:], in0=xt[:], scalar1=st[:, 1:2], scalar2=32.0, op0=ALU.mult, op1=ALU.mult),
                    reads=[B_xt, B_st], writes=[B_hr])
                tp, B_tp = self.tpp.next()
                for k in range(8):
                    sc.op("pe", lambda e, tp=tp, hr=hr, k=k: e.transpose(
                        out=tp[:, k, :], in_=hr[:, k * 128:(k + 1) * 128], identity=ident),
                        reads=[B_hr, B_cst], writes=[B_tp])
                eng = ("dve", "act")[self.ev % 2]
                self.ev += 1
                if eng == "dve":
                    sc.op("dve", lambda e, tp=tp, j=j, ht_=ht_: e.tensor_copy(
                        out=ht_[:, :, j * 128:(j + 1) * 128], in_=tp[:]),
                        reads=[B_tp], writes=[B_ht_[j]])
                else:
                    sc.op("act", lambda e, tp=tp, j=j, ht_=ht_: e.copy(
                        out=ht_[:, :, j * 128:(j + 1) * 128], in_=tp[:]),
                        reads=[B_tp], writes=[B_ht_[j]])

            def pieces(self, J):
                order = [("a", 0), ("a", 1), ("b", 0), ("a", 2), ("b", 1), ("c", 0), ("a", 3), ("b", 2),
                         ("c", 1), ("b", 3), ("c", 2), ("c", 3)]
                fns = {"a": self.piece_a, "b": self.piece_b, "c": self.piece_c}
                return [(lambda f=fns[k], j=j: f(J, j)) for (k, j) in order]

            def run(self, src_d, ti, B_src):
                J = self.begin(src_d, ti, B_src)
                for j in range(4):
                    self.piece_a(J, j)
                    self.piece_b(J, j)
                    self.piece_c(J, j)
                return J["ht"], J["B_ht"]

        def wtoks(toks, col, w, blk=640):
            return [toks[b] for b in range(col // blk, (col + w - 1) // blk + 1)]

        def load_cast_cols(stgp, dst_t, toks, src_d, nrow_chunks, ncols, gain_i=None, blk=640):
            for b in range((ncols + blk - 1) // blk):
                c0, c1 = b * blk, min(ncols, (b + 1) * blk)
                for k in range(nrow_chunks):
                    load_cast(stgp, dst_t, lambda a, bb, k=k, c0=c0: dst_t[:, k, c0 + a:c0 + bb], toks[b],
                              src_d[k * 128:(k + 1) * 128, c0:c1], c1 - c0,
                              gain=None if gain_i is None else gcol[:, gain_i, k:k + 1])

        def phase_ffn(layer, src_d, B_srcs, dst_d, B_dsts, final):
            with ExitStack() as pes:
                w1b = sbt(pes, "w1b", [128, 8, DFF], BF16)
                w3b = sbt(pes, "w3b", [128, 8, DFF], BF16)
                w2b = sbt(pes, "w2b", [128, NFF, D], BF16)
                NBLK = (DFF + 639) // 640
                B_w1 = [Buf("w1_%d" % b) for b in range(NBLK)]
                B_w3 = [Buf("w3_%d" % b) for b in range(NBLK)]
                B_w2 = [Buf("w2_%d" % c) for c in range(NFF)]
                stgp = Pool(nc, pes, "stg", 2, [128, STGW], F32)
                gi = 0 if layer == 0 else 2
                for b in range(NBLK):
                    c0, c1 = b * 640, min(DFF, (b + 1) * 640)
                    for (wb, wd, toks) in ((w1b, w1_d[layer], B_w1), (w3b, w3_d[layer], B_w3)):
                        for k in range(8):
                            load_cast(stgp, wb, lambda a, bb, k=k, c0=c0, wb=wb: wb[:, k, c0 + a:c0 + bb], toks[b],
                                      wd[k * 128:(k + 1) * 128, c0:c1], c1 - c0, gain=gcol[:, gi, k:k + 1])
                for c in range(NFF):
                    load_cast(stgp, w2b, lambda a, b, c=c: w2b[:, c, a:b], B_w2[c],
                              w2_d[layer][c * 128:(c + 1) * 128, :], D)
                nt = NormT(pes)
                G = sbt(pes, "G", [128, NFF, TT], BF16)
                B_G = [Buf("G%d" % c) for c in range(NFF)]
                sgp = Pool(nc, pes, "sg", 2, [128, TT], F32)
                xres = Pool(nc, pes, "xres", 2, [128, D], F32)
                ps_up = Pool(nc, pes, "psu", 4, [128, 512], F32, psum=True)
                ps_dn = Pool(nc, pes, "psd", 2, [128, 512], F32, psum=True)
                if final:
                    gfb = sbt(pes, "gfb", [128, D], F32)
                    B_gfb = Buf("gfb")
                    sc.dma("sp", [(gfb[:], gfin_d.partition_broadcast(128))], writes=[B_gfb], key="gfb")
                    sc.op("dve", lambda e: e.tensor_scalar(out=gfb[:], in0=gfb[:], scalar1=32.0, scalar2=None,
                                                           op0=ALU.mult), reads=[B_gfb], writes=[B_gfb])
                    fstat = Pool(nc, pes, "fstat", 4, [128, 2], F32)

                nxt = nt.run(src_d, 0, B_srcs[0] if B_srcs else None)
                for ti in range(NT):
                    ht, B_ht = nxt
                    for c in range(NFF):
                        pu, B_pu = ps_up.next()
                        pv, B_pv = ps_up.next()
                        for k in range(8):
                            sc.op("pe", lambda e, pu=pu, k=k, c=c, ht=ht: e.matmul(
                                pu[:], lhsT=w1b[:, k, c * 128:(c + 1) * 128], rhs=ht[:, k, :],
                                start=(k == 0), stop=(k == 7)), reads=wtoks(B_w1, c * 128, 128) + B_ht, writes=[B_pu])
                        for k in range(8):
                            sc.op("pe", lambda e, pv=pv, k=k, c=c, ht=ht: e.matmul(
                                pv[:], lhsT=w3b[:, k, c * 128:(c + 1) * 128], rhs=ht[:, k, :],
                                start=(k == 0), stop=(k == 7)), reads=wtoks(B_w3, c * 128, 128) + B_ht, writes=[B_pv])
                        sg, B_sg = sgp.next()
                        sc.op("act", lambda e, sg=sg, pu=pu: e.activation(out=sg[:], in_=pu[:], func=AF.Silu),
                              reads=[B_pu], writes=[B_sg])
                        sc.op("dve", lambda e, sg=sg, pv=pv, c=c: e.tensor_tensor(
                            out=G[:, c, :], in0=sg[:], in1=pv[:], op=ALU.mult),
                            reads=[B_sg, B_pv], writes=[B_G[c]])
                    if ti + 1 < NT:
                        nxt = nt.run(src_d, ti + 1, B_srcs[ti + 1] if B_srcs else None)
                    for j in range(4):
                        xr, B_xr = xres.next()
                        r0 = ti * TT + j * 128
                        sc.dma("sp", [(xr[:], src_d[r0:r0 + 128, :])], reads=[B_srcs[ti]] if B_srcs else [],
                               writes=[B_xr], key=("xres", id(B_xr)))
                        for half in range(2):
                            pd, B_pd = ps_dn.next()
                            for c in range(NFF):
                                sc.op("pe", lambda e, pd=pd, c=c, j=j, half=half: e.matmul(
                                    pd[:], lhsT=G[:, c, j * 128:(j + 1) * 128],
                                    rhs=w2b[:, c, half * 512:(half + 1) * 512],
                                    start=(c == 0), stop=(c == NFF - 1)),
                                    reads=[B_G[c], B_w2[c]], writes=[B_pd])
                            sc.op("dve", lambda e, pd=pd, xr=xr, half=half: e.scalar_tensor_tensor(
                                out=xr[:, half * 512:(half + 1) * 512], in0=pd[:], scalar=0.5,
                                in1=xr[:, half * 512:(half + 1) * 512], op0=ALU.mult, op1=ALU.add),
                                reads=[B_pd, B_xr], writes=[B_xr])
                        if final:
                            fs, B_fs = fstat.next()
                            sc.op("act", lambda e, xr=xr, fs=fs: e.activation(
                                out=nt.sqj[:], in_=xr[:], func=AF.Square, accum_out=fs[:, 0:1]),
                                reads=[B_xr], writes=[B_fs, nt.B_sqj])
                            rstd_ops(fs, B_fs)
                            sc.op("dve", lambda e, xr=xr, fs=fs: e.scalar_tensor_tensor(
                                out=xr[:], in0=xr[:], scalar=fs[:, 1:2], in1=gfb[:], op0=ALU.mult, op1=ALU.mult),
                                reads=[B_xr, B_fs, B_gfb], writes=[B_xr])
                        sc.dma("sp", [(dst_d[r0:r0 + 128, :], xr[:])], reads=[B_xr],
                               writes=[B_dsts[ti]] if B_dsts else [], key=("xres_st", id(B_xr)))
            sc.barrier()

        def phase_proj():
            NQ = C_GA
            with ExitStack() as pes:
                wq = sbt(pes, "wq", [128, 8, NQ], BF16)
                B_wq = [Buf("wq%d" % b) for b in range((NQ + 639) // 640)]
                stgp = Pool(nc, pes, "stg", 2, [128, STGW], F32)
                load_cast_cols(stgp, wq, B_wq, win_d, 8, NQ, gain_i=1)
                nt = NormT(pes, nht=2, nxin=4)
                ps = Pool(nc, pes, "ps", 4, [128, 512], F32, psum=True)
                evp = Pool(nc, pes, "ev", 6, [128, 512], BF16)
                fm = []
                for g in range(4):
                    fm.append((C_QA + g * 128, lambda ti, g=g: qta_d[g, :, ti * TT:(ti + 1) * TT], 0.125))
                fm.append((C_KA, lambda ti: kta_d[:, ti * TT:(ti + 1) * TT], 1.0))
                for cc in range(4):
                    fm.append((C_QB + cc * 128, lambda ti, cc=cc: qtb_d[cc * 128:(cc + 1) * 128, ti * TT:(ti + 1) * TT], 0.125))
                for cc in range(4):
                    fm.append((C_KB + cc * 128, lambda ti, cc=cc: ktb_d[cc * 128:(cc + 1) * 128, ti * TT:(ti + 1) * TT], 1.0))
                evi = 0
                nxt = nt.run(x1_d, 0, B_x1[0])
                for ti in range(NT):
                    ht, B_ht = nxt
                    pcs = []
                    if ti + 1 < NT:
                        J = nt.begin(x1_d, ti + 1, B_x1[ti + 1])
                        nxt = (J["ht"], J["B_ht"])
                        pcs = nt.pieces(J)
                    for (col, dst, scale) in fm:
                        pp, B_pp = ps.next()
                        for k in range(8):
                            sc.op("pe", lambda e, pp=pp, k=k, col=col, ht=ht: e.matmul(
                                pp[:], lhsT=wq[:, k, col:col + 128], rhs=ht[:, k, :],
                                start=(k == 0), stop=(k == 7)), reads=wtoks(B_wq, col, 128) + B_ht, writes=[B_pp])
                        ev, B_ev = evp.next()
                        if evi % 2 == 0:
                            sc.op("dve", lambda e, ev=ev, pp=pp, scale=scale: e.tensor_scalar(
                                out=ev[:], in0=pp[:], scalar1=float(scale), scalar2=None, op0=ALU.mult),
                                reads=[B_pp], writes=[B_ev])
                        else:
                            sc.op("act", lambda e, ev=ev, pp=pp, scale=scale: e.activation(
                                out=ev[:], in_=pp[:], func=AF.Copy, scale=float(scale)),
                                reads=[B_pp], writes=[B_ev])
                        evi += 1
                        sc.dma("sp", [(dst(ti), ev[:])], reads=[B_ev], writes=[B_qkv[ti]], key=("ev", id(B_ev)))
                        if pcs:
                            pcs.pop(0)()
                    for j in range(4):
                        r0 = ti * TT + j * 128
                        pp, B_pp = ps.next()
                        for k in range(8):
                            sc.op("pe", lambda e, pp=pp, k=k, j=j, ht=ht: e.matmul(
                                pp[:], lhsT=ht[:, k, j * 128:(j + 1) * 128], rhs=wq[:, k, C_VB:C_VB + 512],
                                start=(k == 0), stop=(k == 7)), reads=wtoks(B_wq, C_VB, 512) + B_ht, writes=[B_pp])
                        ev, B_ev = evp.next()
                        sc.op("dve", lambda e, ev=ev, pp=pp: e.tensor_copy(out=ev[:], in_=pp[:]),
                              reads=[B_pp], writes=[B_ev])
                        sc.dma("sp", [(vb_d[r0:r0 + 128, :], ev[:])], reads=[B_ev], writes=[B_qkv[ti]],
                               key=("ev", id(B_ev)))
                        pp, B_pp = ps.next()
                        for k in range(8):
                            sc.op("pe", lambda e, pp=pp, k=k, j=j, ht=ht: e.matmul(
                                pp[:, 0:128], lhsT=ht[:, k, j * 128:(j + 1) * 128], rhs=wq[:, k, C_VA:C_VA + 128],
                                start=(k == 0), stop=(k == 7)), reads=wtoks(B_wq, C_VA, 128) + B_ht, writes=[B_pp])
                        ev, B_ev = evp.next()
                        sc.op("act", lambda e, ev=ev, pp=pp: e.copy(out=ev[:, 0:128], in_=pp[:, 0:128]),
                              reads=[B_pp], writes=[B_ev])
                        sc.dma("sp", [(va_d[r0:r0 + 128, :], ev[:, 0:128])], reads=[B_ev], writes=[B_qkv[ti]],
                               key=("ev", id(B_ev)))
                        if pcs:
                            pcs.pop(0)()
                    while pcs:
                        pcs.pop(0)()
            sc.barrier()

        def phase_swa():
            with ExitStack() as pes:
                kt = sbt(pes, "kta", [128, S], BF16)
                qt = sbt(pes, "qta", [128, 4, S], BF16)
                vv = sbt(pes, "vva", [128, NB, 128], BF16)
                B_in = Buf("swa_in")
                var = va_d.rearrange("(n p) c -> p n c", p=128)
                sc.dma("sp", [(kt[:], kta_d)] + [(qt[:, g, :], qta_d[g]) for g in range(4)]
                       + [(vv[:, n0:min(NB, n0 + 8), :], var[:, n0:min(NB, n0 + 8), :]) for n0 in range(0, NB, 8)],
                       reads=B_qkv, writes=[B_in], key="swa_in")
                bm = sbt(pes, "bm", [128, 2, 4, 256], F32)
                mk = sbt(pes, "mk", [128, 256], F32)
                sk = sbt(pes, "sk", [128, 2, 4], F32)
                B_bm = Buf("bm")
                B_sk = Buf("sk")
                bsrc = bias_d.rearrange("q (g kv) k -> q kv g k", kv=2)
                sc.dma("sp", [(bm[:, kv, :, :], bsrc[:, kv, :, :]) for kv in range(2)] + [(mk[:], maskc_d)],
                       writes=[B_bm], key="bm")
                sc.dma("sp", [(sk[:, kv, :], sinks2_d[kv].partition_broadcast(128)) for kv in range(2)],
                       writes=[B_sk], key="sk")
                for kv in range(2):
                    for g in range(4):
                        sc.op("dve", lambda e, kv=kv, g=g: e.tensor_tensor(
                            out=bm[:, kv, g, :], in0=bm[:, kv, g, :], in1=mk[:], op=ALU.add),
                            reads=[B_bm], writes=[B_bm])
                psc = Pool(nc, pes, "psc", 2, [128, 4, 256], F32, psum=True)
                ppt = Pool(nc, pes, "ppt", 2, [128, 8, 128], BF16, psum=True)
                pso = Pool(nc, pes, "pso", 2, [128, 4, 128], F32, psum=True)
                scs = Pool(nc, pes, "scs", 2, [128, 4, 256], F32)
                pbf = Pool(nc, pes, "pbf", 3, [128, 4, 256], BF16)
                ptb = Pool(nc, pes, "ptb", 2, [128, 8, 128], BF16)
                sm = Pool(nc, pes, "sm", 6, [128, 5, 4], F32)
                osb = Pool(nc, pes, "osb", 3, [128, 4, 128], BF16)
                units = [dict(n=n, kv=kv) for n in range(NB) for kv in range(2)]
                cur_ob = {}
                evc = [0]

                def stA(u):
                    n, kv = u["n"], u["kv"]
                    k0 = 0 if n > 0 else 128
                    kw = 256 - k0
                    ks = (n - 1) * 128 + k0
                    u.update(k0=k0, kw=kw)
                    pc, B_pc = psc.next()
                    u["pc"] = (pc, B_pc)
                    for g in range(4):
                        sc.op("pe", lambda e, pc=pc, g=g, kv=kv, n=n, ks=ks, kw=kw, k0=k0: e.matmul(
                            pc[:, g, k0:256], lhsT=qt[kv * 64:(kv + 1) * 64, g, n * 128:(n + 1) * 128],
                            rhs=kt[kv * 64:(kv + 1) * 64, ks:ks + kw], start=True, stop=True),
                            reads=[B_in], writes=[B_pc])

                def stB1(u):
                    kv, k0 = u["kv"], u["k0"]
                    pc, B_pc = u["pc"]
                    ss, B_ss = scs.next()
                    st, B_st = sm.next()
                    pb, B_pb = pbf.next()
                    u.update(st=(st, B_st), pb=(pb, B_pb))
                    sc.op("dve", lambda e, pc=pc, ss=ss, kv=kv, k0=k0: e.tensor_tensor(
                        out=ss[:, :, k0:256], in0=pc[:, :, k0:256], in1=bm[:, kv, :, k0:256], op=ALU.add),
                        reads=[B_pc, B_bm], writes=[B_ss])
                    sc.op("dve", lambda e, ss=ss, st=st, k0=k0: e.tensor_reduce(
                        out=st[:, 0, :], in_=ss[:, :, k0:256], axis=AX.X, op=ALU.max),
                        reads=[B_ss], writes=[B_st])
                    sc.op("dve", lambda e, st=st, kv=kv: e.tensor_tensor(
                        out=st[:, 0, :], in0=st[:, 0, :], in1=sk[:, kv, :], op=ALU.max),
                        reads=[B_st, B_sk], writes=[B_st])
                    sc.op("dve", lambda e, st=st: e.tensor_scalar(out=st[:, 1, :], in0=st[:, 0, :], scalar1=-1.0,
                                                                   scalar2=None, op0=ALU.mult),
                          reads=[B_st], writes=[B_st])
                    sc.op("dve", lambda e, st=st, kv=kv: e.tensor_tensor(
                        out=st[:, 2, :], in0=sk[:, kv, :], in1=st[:, 1, :], op=ALU.add),
                        reads=[B_st, B_sk], writes=[B_st])
                    for g in range(4):
                        sc.op("act", lambda e, pb=pb, ss=ss, st=st, g=g, k0=k0: e.activation(
                            out=pb[:, g, k0:256], in_=ss[:, g, k0:256], func=AF.Exp, bias=st[:, 1, g:g + 1],
                            accum_out=st[:, 3, g:g + 1]), reads=[B_ss, B_st], writes=[B_pb, B_st])
                    sc.op("act", lambda e, st=st: e.activation(out=st[:, 2, :], in_=st[:, 2, :], func=AF.Exp),
                          reads=[B_st], writes=[B_st])

                def stB2(u):
                    n, kv, k0, kw = u["n"], u["kv"], u["k0"], u["kw"]
                    st, B_st = u["st"]
                    pb, B_pb = u["pb"]
                    sc.op("dve", lambda e, st=st: e.tensor_tensor(out=st[:, 4, :], in0=st[:, 3, :], in1=st[:, 2, :], op=ALU.add),
                          reads=[B_st], writes=[B_st])
                    sc.op("dve", lambda e, st=st: e.reciprocal(out=st[:, 4, :], in_=st[:, 4, :]),
                          reads=[B_st], writes=[B_st])
                    sc.op("pool", lambda e, pb=pb, st=st, k0=k0, kw=kw: e.tensor_tensor(
                        out=pb[:, :, k0:256], in0=pb[:, :, k0:256],
                        in1=st[:, 4, :].unsqueeze(2).to_broadcast([128, 4, kw]), op=ALU.mult),
                        reads=[B_pb, B_st], writes=[B_pb])
                    nkb = kw // 128
                    pp, B_pp = ppt.next()
                    pt, B_pt = ptb.next()
                    for kb in range(nkb):
                        for g in range(4):
                            sc.op("pe", lambda e, pp=pp, pb=pb, g=g, kb=kb, k0=k0: e.transpose(
                                out=pp[:, kb * 4 + g, :], in_=pb[:, g, k0 + kb * 128:k0 + (kb + 1) * 128], identity=ident),
                                reads=[B_pb, B_cst], writes=[B_pp])
                    evc[0] += 1
                    if evc[0] % 2 == 0:
                        sc.op("dve", lambda e, pp=pp, pt=pt, nkb=nkb: e.tensor_copy(
                            out=pt[:, 0:4 * nkb, :], in_=pp[:, 0:4 * nkb, :]), reads=[B_pp], writes=[B_pt])
                    else:
                        sc.op("act", lambda e, pp=pp, pt=pt, nkb=nkb: e.copy(
                            out=pt[:, 0:4 * nkb, :], in_=pp[:, 0:4 * nkb, :]), reads=[B_pp], writes=[B_pt])
                    po, B_po = pso.next()
                    for g in range(4):
                        h = g + 4 * kv
                        cl, half = (h // 2) - 2 * kv, h % 2
                        for kb in range(nkb):
                            blk = n - (nkb - 1) + kb
                            sc.op("pe", lambda e, po=po, pt=pt, cl=cl, half=half, kv=kv, g=g, kb=kb, blk=blk, nkb=nkb: e.matmul(
                                po[half * 64:(half + 1) * 64, cl, :], lhsT=vv[:, blk, kv * 64:(kv + 1) * 64],
                                rhs=pt[:, kb * 4 + g, :], start=(kb == 0), stop=(kb == nkb - 1)),
                                reads=[B_pt, B_in], writes=[B_po])
                    if kv == 0:
                        cur_ob[n] = osb.next()
                    ob, B_ob = cur_ob[n]
                    sc.op("act", lambda e, ob=ob, po=po, kv=kv: e.copy(out=ob[:, 2 * kv:2 * kv + 2, :], in_=po[:, 0:2, :]),
                          reads=[B_po], writes=[B_ob])
                    if kv == 1:
                        sc.dma("sp", [(ota_d.rearrange("(c p) s -> p c s", p=128)[:, :, n * 128:(n + 1) * 128], ob[:])],
                               reads=[B_ob], writes=[B_ota[n // 4]], key=("osb", id(B_ob)))
                        del cur_ob[n]

                nu = len(units)
                for t in range(nu + 2):
                    if t < nu:
                        stA(units[t])
                    if 0 <= t - 1 < nu:
                        stB1(units[t - 1])
                    if 0 <= t - 2 < nu:
                        stB2(units[t - 2])
            sc.barrier()

        def phase_sb():
            with ExitStack() as pes:
                ktp = Pool(nc, pes, "ktb", 2, [128, S], BF16)
                qtp = Pool(nc, pes, "qtb", 2, [128, S], BF16)
                vvp = Pool(nc, pes, "vvb", 2, [128, NB, 128], BF16)
                zp = Pool(nc, pes, "zp", 3, [128, 2, 512], F32, psum=True)
                op_ = Pool(nc, pes, "op", 2, [128, 512], F32, psum=True)
                Ep = Pool(nc, pes, "E", 2, [128, 2, 512], F32)
                Lp = Pool(nc, pes, "L", 3, [128, 2, 512], BF16)
                Ap = Pool(nc, pes, "A", 2, [128, 2, 512], BF16)
                R32 = sbt(pes, "R32", [128, 2, 512], F32)
                B_R32 = Buf("R32")
                Rbp = Pool(nc, pes, "Rb", 2, [128, 2, 512], BF16)
                oev = Pool(nc, pes, "oev", 2, [128, 512], BF16)
                steps = []
                pair_in = {}
                for p in range(4):
                    for i in range(NT):
                        nk = 4 * i + 4
                        for si, kj in enumerate(range(nk - 1, -1, -1)):
                            steps.append(dict(p=p, i=i, kj=kj, first=(si == 0), last=(kj == 0),
                                              c0=max(0, (kj - 4 * i)) * 128, diag=(kj >= 4 * i)))

                def load_pair(p):
                    kt, B_kt = ktp.next()
                    qt, B_qt = qtp.next()
                    vv, B_vv = vvp.next()
                    sc.dma("sp", [(kt[:], ktb_d[p * 128:(p + 1) * 128, :])], reads=B_qkv, writes=[B_kt], key=("sbk", id(B_kt)))
                    sc.dma("sp", [(qt[:], qtb_d[p * 128:(p + 1) * 128, :])], reads=B_qkv, writes=[B_qt], key=("sbq", id(B_qt)))
                    vbr = vb_d[:, p * 128:(p + 1) * 128].rearrange("(n p) c -> p n c", p=128)
                    sc.dma("sp", [(vv[:, n0:min(NB, n0 + 8), :], vbr[:, n0:min(NB, n0 + 8), :]) for n0 in range(0, NB, 8)],
                           reads=B_qkv, writes=[B_vv], key=("sbv", id(B_vv)))
                    pair_in[p] = (kt, B_kt, qt, B_qt, vv, B_vv)

                load_pair(0)
                state = {}

                def stage0(s):
                    p, i, kj, c0 = s["p"], s["i"], s["kj"], s["c0"]
                    if s["first"] and i == 0 and p + 1 < 4:
                        load_pair(p + 1)
                    kt, B_kt, qt, B_qt, vv, B_vv = pair_in[p]
                    z, B_z = zp.next()
                    s["z"], s["B_z"] = z, B_z
                    for hh in range(2):
                        sc.op("pe", lambda e, z=z, hh=hh, kj=kj, i=i, c0=c0, kt=kt, qt=qt: e.matmul(
                            z[:, hh, c0:512], lhsT=kt[hh * 64:(hh + 1) * 64, kj * 128:(kj + 1) * 128],
                            rhs=qt[hh * 64:(hh + 1) * 64, i * 512 + c0:(i + 1) * 512],
                            start=True, stop=False, skip_group_check=True),
                            reads=[B_kt, B_qt], writes=[B_z])
                    if s["diag"]:
                        for hh in range(2):
                            sc.op("pe", lambda e, z=z, hh=hh, c0=c0: e.matmul(
                                z[:, hh, c0:c0 + 128], lhsT=ident, rhs=dmask, start=False, stop=False,
                                skip_group_check=True), reads=[B_cst], writes=[B_z])

                def stage1(s):
                    z, B_z, c0 = s["z"], s["B_z"], s["c0"]
                    E, B_E = Ep.next()
                    L, B_L = Lp.next()
                    s["L"], s["B_L"] = L, B_L
                    sc.op("act", lambda e, E=E, z=z, c0=c0: e.activation(out=E[:, :, c0:512], in_=z[:, :, c0:512], func=AF.Exp),
                          reads=[B_z], writes=[B_E])
                    sc.op("act", lambda e, E=E, L=L, c0=c0: e.activation(out=L[:, :, c0:512], in_=E[:, :, c0:512], func=AF.Ln, bias=1.0),
                          reads=[B_E], writes=[B_L])

                def stage2(s):
                    p, i, kj, c0 = s["p"], s["i"], s["kj"], s["c0"]
                    kt, B_kt, qt, B_qt, vv, B_vv = pair_in[p]
                    z, B_z, L, B_L = s["z"], s["B_z"], s["L"], s["B_L"]
                    for hh in range(2):
                        sc.op("pe", lambda e, z=z, hh=hh, c0=c0, L=L: e.matmul(
                            z[:, hh, c0:512], lhsT=ntri, rhs=L[:, hh, c0:512], start=False, stop=False,
                            skip_group_check=True), reads=[B_L, B_cst], writes=[B_z])
                    if not s["first"]:
                        Rb, B_Rb = state["Rb"]
                        for hh in range(2):
                            sc.op("pe", lambda e, z=z, hh=hh, c0=c0, Rb=Rb: e.matmul(
                                z[:, hh, c0:512], lhsT=nones, rhs=Rb[:, hh, c0:512], start=False, stop=False,
                                skip_group_check=True), reads=[B_Rb, B_cst], writes=[B_z])
                    A, B_A = Ap.next()
                    sc.op("act", lambda e, A=A, z=z, c0=c0: e.activation(out=A[:, :, c0:512], in_=z[:, :, c0:512], func=AF.Exp),
                          reads=[B_z], writes=[B_A])
                    if s["first"]:
                        state["o"] = op_.next()
                    o, B_o = state["o"]
                    for hh in range(2):
                        sc.op("pe", lambda e, o=o, hh=hh, c0=c0, A=A, vv=vv, kj=kj, first=s["first"]: e.matmul(
                            o[hh * 64:(hh + 1) * 64, c0:512], lhsT=vv[:, kj, hh * 64:(hh + 1) * 64],
                            rhs=A[:, hh, c0:512], start=first, stop=False, skip_group_check=True),
                            reads=[B_A, B_vv], writes=[B_o])
                    if not s["last"]:
                        if s["first"]:
                            sc.op("pool", lambda e: e.memset(R32[:], 0.0), writes=[B_R32])
                        sc.op("pool", lambda e, L=L, c0=c0: e.tensor_tensor(
                            out=R32[:, :, c0:512], in0=R32[:, :, c0:512], in1=L[:, :, c0:512], op=ALU.add),
                            reads=[B_L, B_R32], writes=[B_R32])
                        Rb, B_Rb = Rbp.next()
                        sc.op("dve", lambda e, Rb=Rb: e.tensor_copy(out=Rb[:], in_=R32[:]), reads=[B_R32], writes=[B_Rb])
                        state["Rb"] = (Rb, B_Rb)
                    else:
                        ob, B_ob = oev.next()
                        sc.op("dve", lambda e, ob=ob, o=o: e.tensor_copy(out=ob[:], in_=o[:]), reads=[B_o], writes=[B_ob])
                        sc.dma("sp", [(otb_d[p * 128:(p + 1) * 128, i * 512:(i + 1) * 512], ob[:])], reads=[B_ob],
                               writes=[B_otb[i]], key=("oev", id(B_ob)))

                n = len(steps)
                for t in range(n + 2):
                    if t < n:
                        stage0(steps[t])
                    if 0 <= t - 1 < n:
                        stage1(steps[t - 1])
                    if 0 <= t - 2 < n:
                        stage2(steps[t - 2])
            sc.barrier()

        def phase_sb2(K=8):
            with ExitStack() as pes:
                per_pair = sum(4 * i + 4 for i in range(NT))
                nsets = 2 if per_pair >= 4 * K else 4
                ktp = Pool(nc, pes, "ktb", nsets, [128, S], BF16)
                qtp = Pool(nc, pes, "qtb", nsets, [128, S], BF16)
                vvp = Pool(nc, pes, "vvb", nsets, [128, NB, 128], BF16)
                zp = Pool(nc, pes, "zp", 3, [128, 2, 512], F32, psum=True)
                op_ = Pool(nc, pes, "op", 2, [128, 512], F32, psum=True)
                Lp = Pool(nc, pes, "L", 2 * K + 2, [128, 2, 512], BF16)
                Rbp = Pool(nc, pes, "Rb", 2 * K + 2, [128, 2, 512], BF16)
                Ap = Pool(nc, pes, "A", K + 3, [128, 2, 512], BF16)
                R32 = sbt(pes, "R32", [128, 2, 512], F32)
                B_R32 = Buf("R32")
                oev = Pool(nc, pes, "oev", 2, [128, 512], BF16)
                steps = []
                pair_in = {}
                for p in range(4):
                    for i in range(NT):
                        nk = 4 * i + 4
                        for si, kj in enumerate(range(nk - 1, -1, -1)):
                            steps.append(dict(p=p, i=i, kj=kj, first=(si == 0), last=(kj == 0),
                                              c0=max(0, (kj - 4 * i)) * 128, diag=(kj >= 4 * i)))
                for a, b in zip(steps[:-1], steps[1:]):
                    b["prev"] = a

                def load_pair(p):
                    kt, B_kt = ktp.next()
                    qt, B_qt = qtp.next()
                    vv, B_vv = vvp.next()
                    sc.dma("sp", [(kt[:], ktb_d[p * 128:(p + 1) * 128, :])], reads=B_qkv, writes=[B_kt], key=("sbk", id(B_kt)))
                    sc.dma("sp", [(qt[:], qtb_d[p * 128:(p + 1) * 128, :])], reads=B_qkv, writes=[B_qt], key=("sbq", id(B_qt)))
                    vbr = vb_d[:, p * 128:(p + 1) * 128].rearrange("(n p) c -> p n c", p=128)
                    sc.dma("sp", [(vv[:, n0:min(NB, n0 + 8), :], vbr[:, n0:min(NB, n0 + 8), :]) for n0 in range(0, NB, 8)],
                           reads=B_qkv, writes=[B_vv], key=("sbv", id(B_vv)))
                    pair_in[p] = (kt, B_kt, qt, B_qt, vv, B_vv)

                for p_ in range(nsets):
                    load_pair(p_)
                state = {}

                def zmm(s_, z, B_z):
                    p, i, kj, c0 = s_["p"], s_["i"], s_["kj"], s_["c0"]
                    kt, B_kt, qt, B_qt, vv, B_vv = pair_in[p]
                    for hh in range(2):
                        sc.op("pe", lambda e, z=z, hh=hh, kj=kj, i=i, c0=c0, kt=kt, qt=qt: e.matmul(
                            z[:, hh, c0:512], lhsT=kt[hh * 64:(hh + 1) * 64, kj * 128:(kj + 1) * 128],
                            rhs=qt[hh * 64:(hh + 1) * 64, i * 512 + c0:(i + 1) * 512],
                            start=True, stop=False, skip_group_check=True),
                            reads=[B_kt, B_qt], writes=[B_z])
                    if s_["diag"]:
                        for hh in range(2):
                            sc.op("pe", lambda e, z=z, hh=hh, c0=c0: e.matmul(
                                z[:, hh, c0:c0 + 128], lhsT=ident, rhs=dmask, start=False, stop=False,
                                skip_group_check=True), reads=[B_cst], writes=[B_z])

                def Xstep(s_):
                    p, i, c0 = s_["p"], s_["i"], s_["c0"]
                    z, B_z = zp.next()
                    zmm(s_, z, B_z)
                    L, B_L = Lp.next()
                    s_["L"], s_["B_L"] = L, B_L
                    sc.op("act", lambda e, L=L, z=z, c0=c0: e.activation(
                        out=L[:, :, c0:512], in_=z[:, :, c0:512], func=AF.Softplus), reads=[B_z], writes=[B_L])
                    if not s_["last"]:
                        if s_["first"]:
                            sc.op("pool", lambda e: e.memset(R32[:], 0.0), writes=[B_R32])
                        sc.op("dve", lambda e, L=L, c0=c0: e.tensor_tensor(
                            out=R32[:, :, c0:512], in0=R32[:, :, c0:512], in1=L[:, :, c0:512], op=ALU.add),
                            reads=[B_L, B_R32], writes=[B_R32])
                        Rb, B_Rb = Rbp.next()
                        sc.op("dve", lambda e, Rb=Rb: e.tensor_copy(out=Rb[:], in_=R32[:]), reads=[B_R32], writes=[B_Rb])
                        s_["Rb"] = (Rb, B_Rb)

                def Ystep(s_):
                    c0 = s_["c0"]
                    L, B_L = s_["L"], s_["B_L"]
                    z, B_z = zp.next()
                    zmm(s_, z, B_z)
                    for hh in range(2):
                        sc.op("pe", lambda e, z=z, hh=hh, c0=c0, L=L: e.matmul(
                            z[:, hh, c0:512], lhsT=ntri, rhs=L[:, hh, c0:512], start=False, stop=False,
                            skip_group_check=True), reads=[B_L, B_cst], writes=[B_z])
                    if not s_["first"]:
                        Rb, B_Rb = s_["prev"]["Rb"]
                        for hh in range(2):
                            sc.op("pe", lambda e, z=z, hh=hh, c0=c0, Rb=Rb: e.matmul(
                                z[:, hh, c0:512], lhsT=nones, rhs=Rb[:, hh, c0:512], start=False, stop=False,
                                skip_group_check=True), reads=[B_Rb, B_cst], writes=[B_z])
                    A, B_A = Ap.next()
                    s_["A"] = (A, B_A)
                    sc.op("act", lambda e, A=A, z=z, c0=c0: e.activation(out=A[:, :, c0:512], in_=z[:, :, c0:512], func=AF.Exp),
                          reads=[B_z], writes=[B_A])

                def AVstep(s_):
                    p, i, kj, c0 = s_["p"], s_["i"], s_["kj"], s_["c0"]
                    kt, B_kt, qt, B_qt, vv, B_vv = pair_in[p]
                    A, B_A = s_["A"]
                    if s_["first"]:
                        state["o"] = op_.next()
                    o, B_o = state["o"]
                    for hh in range(2):
                        sc.op("pe", lambda e, o=o, hh=hh, c0=c0, A=A, vv=vv, kj=kj, first=s_["first"]: e.matmul(
                            o[hh * 64:(hh + 1) * 64, c0:512], lhsT=vv[:, kj, hh * 64:(hh + 1) * 64],
                            rhs=A[:, hh, c0:512], start=first, stop=False, skip_group_check=True),
                            reads=[B_A, B_vv], writes=[B_o])
                    if s_["last"]:
                        ob, B_ob = oev.next()
                        sc.op("dve", lambda e, ob=ob, o=o: e.tensor_copy(out=ob[:], in_=o[:]), reads=[B_o], writes=[B_ob])
                        sc.dma("sp", [(otb_d[p * 128:(p + 1) * 128, i * 512:(i + 1) * 512], ob[:])], reads=[B_ob],
                               writes=[B_otb[i]], key=("oev", id(B_ob)))
                        if i == NT - 1 and p + 2 < 4 and nsets == 2:
                            load_pair(p + 2)

                batches = [steps[a:a + K] for a in range(0, len(steps), K)]
                nbt = len(batches)
                for b in range(-1, nbt + 1):
                    xs = batches[b + 1] if 0 <= b + 1 < nbt else []
                    avs = batches[b - 1] if 0 <= b - 1 < nbt else []
                    for idx in range(max(len(xs), len(avs))):
                        if idx < len(xs):
                            Xstep(xs[idx])
                        if idx < len(avs):
                            AVstep(avs[idx])
                    if 0 <= b < nbt:
                        for s_ in batches[b]:
                            Ystep(s_)
            sc.barrier()

        def phase_mix():
            with ExitStack() as pes:
                wg = sbt(pes, "wg", [128, 8, 2048], BF16)
                wba = sbt(pes, "wba", [128, 4, D], BF16)
                wbb = sbt(pes, "wbb", [128, 4, D], BF16)
                wo = sbt(pes, "wo", [128, 8, D], BF16)
                B_wg = [Buf("wg%d" % b) for b in range(4)]
                B_wba = [Buf("wba%d" % k) for k in range(4)]
                B_wbb = [Buf("wbb%d" % k) for k in range(4)]
                B_wo = [Buf("wo%d" % k) for k in range(8)]
                stgp = Pool(nc, pes, "stg", 2, [128, STGW], F32)
                load_cast_cols(stgp, wg, B_wg, win_d[:, C_GA:INW], 8, 2048, gain_i=1)
                for k in range(4):
                    load_cast(stgp, wba, lambda a, b, k=k: wba[:, k, a:b], B_wba[k], wba_d[k * 128:(k + 1) * 128, :], D)
                    load_cast(stgp, wbb, lambda a, b, k=k: wbb[:, k, a:b], B_wbb[k], wbb_d[k * 128:(k + 1) * 128, :], D)
                for k in range(8):
                    load_cast(stgp, wo, lambda a, b, k=k: wo[:, k, a:b], B_wo[k], wout_d[k * 128:(k + 1) * 128, :], D)
                nt = NormT(pes, nht=2, nxin=4)
                ps = Pool(nc, pes, "ps", 6, [128, 512], F32, psum=True)
                otap = Pool(nc, pes, "ota", 2, [128, 4, TT], BF16)
                otbp = Pool(nc, pes, "otb", 2, [128, 4, TT], BF16)
                MT = sbt(pes, "MT", [128, 8, TT], BF16)
                B_MT = [Buf("MT%d" % c) for c in range(8)]
                sgp = Pool(nc, pes, "sgm", 4, [128, TT], F32)
                mp = Pool(nc, pes, "mm", 4, [128, TT], F32)
                xres = Pool(nc, pes, "xres", 3, [128, D], F32)
                nxt = nt.run(x1_d, 0, B_x1[0])
                for ti in range(NT):
                    ht, B_ht = nxt
                    pcs = []
                    if ti + 1 < NT:
                        J = nt.begin(x1_d, ti + 1, B_x1[ti + 1])
                        nxt = (J["ht"], J["B_ht"])
                        pcs = nt.pieces(J)
                    oa, B_oa = otap.next()
                    ob, B_ob = otbp.next()
                    sc.dma("sp", [(oa[:], ota_d.rearrange("(c p) s -> p c s", p=128)[:, :, ti * TT:(ti + 1) * TT])],
                           reads=[B_ota[ti]], writes=[B_oa], key=("ota", id(B_oa)))
                    sc.dma("sp", [(ob[:], otb_d.rearrange("(c p) s -> p c s", p=128)[:, :, ti * TT:(ti + 1) * TT])],
                           reads=[B_otb[ti]], writes=[B_ob], key=("otb", id(B_ob)))
                    for cc in range(8):
                        ms = []
                        for br, (wbr, B_wbr, ot, B_ot, goff) in enumerate(((wba, B_wba, oa, B_oa, 0), (wbb, B_wbb, ob, B_ob, 1024))):
                            pg, B_pg = ps.next()
                            pb, B_pb = ps.next()
                            for k in range(8):
                                sc.op("pe", lambda e, pg=pg, k=k, cc=cc, goff=goff, ht=ht: e.matmul(
                                    pg[:], lhsT=wg[:, k, goff + cc * 128:goff + (cc + 1) * 128], rhs=ht[:, k, :],
                                    start=(k == 0), stop=(k == 7)), reads=wtoks(B_wg, goff + cc * 128, 128) + B_ht, writes=[B_pg])
                            for k in range(4):
                                sc.op("pe", lambda e, pb=pb, k=k, cc=cc, wbr=wbr, ot=ot: e.matmul(
                                    pb[:], lhsT=wbr[:, k, cc * 128:(cc + 1) * 128], rhs=ot[:, k, :],
                                    start=(k == 0), stop=(k == 3)), reads=[B_wbr[k], B_ot], writes=[B_pb])
                            sg, B_sg = sgp.next()
                            sc.op("act", lambda e, sg=sg, pg=pg: e.activation(out=sg[:], in_=pg[:], func=AF.Sigmoid),
                                  reads=[B_pg], writes=[B_sg])
                            m, B_m = mp.next()
                            sc.op("dve", lambda e, m=m, sg=sg, pb=pb: e.tensor_tensor(out=m[:], in0=sg[:], in1=pb[:], op=ALU.mult),
                                  reads=[B_sg, B_pb], writes=[B_m])
                            ms.append((m, B_m))
                            if pcs:
                                pcs.pop(0)()
                        sc.op("pool", lambda e, cc=cc, ms=ms: e.tensor_tensor(
                            out=MT[:, cc, :], in0=ms[0][0][:], in1=ms[1][0][:], op=ALU.add),
                            reads=[ms[0][1], ms[1][1]], writes=[B_MT[cc]])
                    while pcs:
                        pcs.pop(0)()
                    for j in range(4):
                        xr, B_xr = xres.next()
                        r0 = ti * TT + j * 128
                        sc.dma("sp", [(xr[:], x1_d[r0:r0 + 128, :])], reads=[B_x1[ti]], writes=[B_xr], key=("xres", id(B_xr)))
                        for half in range(2):
                            po, B_po = ps.next()
                            for cc in range(8):
                                sc.op("pe", lambda e, po=po, cc=cc, j=j, half=half: e.matmul(
                                    po[:], lhsT=MT[:, cc, j * 128:(j + 1) * 128], rhs=wo[:, cc, half * 512:(half + 1) * 512],
                                    start=(cc == 0), stop=(cc == 7)), reads=[B_MT[cc], B_wo[cc]], writes=[B_po])
                            sc.op("dve", lambda e, po=po, xr=xr, half=half: e.tensor_tensor(
                                out=xr[:, half * 512:(half + 1) * 512], in0=po[:], in1=xr[:, half * 512:(half + 1) * 512], op=ALU.add),
                                reads=[B_po, B_xr], writes=[B_xr])
                        sc.dma("sp", [(x2_d[r0:r0 + 128, :], xr[:])], reads=[B_xr], writes=[B_x2[ti]], key=("xres_st", id(B_xr)))
            sc.barrier()

        sc.barrier()
        if "ffn1" in phases:
            phase_ffn(0, x_d, None, x1_d, B_x1, final=False)
        if "proj" in phases:
            phase_proj()
        if "swa" in phases:
            phase_swa()
        if "sb" in phases:
            phase_sb2()
        if "sb_old" in phases:
            phase_sb()
        if "mix" in phases:
            phase_mix()
        if "ffn2" in phases:
            phase_ffn(1, x2_d, B_x2, out_d, None, final=True)
        sc.emit()
    return nc


def _rel_bucket_np(dist):
    max_exact = 16
    d = np.maximum(dist, 1).astype(np.float32)
    large = max_exact + (np.log(d / max_exact) / np.float32(np.log(128 / max_exact)) * (32 - max_exact)).astype(np.int32)
    large = np.minimum(large, 31)
    return np.where(dist < max_exact, dist, large)


def _bucket_table():
    import jax
    import jax.numpy as jnp
    qi = np.arange(128)[:, None] + 128
    kj = np.arange(256)[None, :]
    dist = qi - kj
    band = (dist >= 0) & (dist < 128)
    with jax.default_device(jax.devices("cpu")[0]):
        dj = jnp.maximum(jnp.asarray(dist), 0)
        max_exact = 16
        d = jnp.maximum(dj, 1).astype(jnp.float32)
        large = max_exact + (jnp.log(d / max_exact) / np.log(128 / max_exact) * (32 - max_exact)).astype(jnp.int32)
        large = jnp.minimum(large, 31)
        bucket = np.asarray(jnp.where(dj < max_exact, dj, large))
    return bucket, band


def _consts():
    c = np.zeros((128, 512), np.float32)
    c[:, 0:128] = np.eye(128)
    j = np.arange(128)[:, None]
    s = np.arange(128)[None, :]
    c[:, 128:256] = np.where(j >= s, -1.0, 0.0)
    c[:, 256:384] = -1.0
    c[:, 384:512] = np.where(j < s, 0.0, MASKB)
    return c.astype(ml_dtypes.bfloat16)


_PROG_CACHE = {}


def _prepare_shared(inp, S):
    f = lambda a: np.ascontiguousarray(np.asarray(a, dtype=np.float32))
    bucket, band = _bucket_table()
    rb = f(inp["rel_bias"])
    bias = rb[bucket]
    order = [g + 4 * kv for g in range(4) for kv in range(2)]
    bias = np.ascontiguousarray(bias.transpose(0, 2, 1)[:, order, :])
    mask = np.where(band, 0.0, NEG).astype(np.float32)
    w_in = f(inp["w_in"])[0]
    qcols = []
    for g in range(4):
        for kv in range(2):
            h = g + 4 * kv
            qcols.extend(range(h * 64, (h + 1) * 64))
    w_in = np.ascontiguousarray(np.concatenate([w_in[:, qcols], w_in[:, 512:]], axis=1))
    sinks = f(inp["swa_sinks"])[0].reshape(2, 4)
    shared = {
        "ffn1_w1": f(inp["ffn1_w1"])[0], "ffn1_w3": f(inp["ffn1_w3"])[0], "ffn1_w2": f(inp["ffn1_w2"])[0],
        "ffn2_w1": f(inp["ffn2_w1"])[0], "ffn2_w3": f(inp["ffn2_w3"])[0], "ffn2_w2": f(inp["ffn2_w2"])[0],
        "gains": np.ascontiguousarray(np.stack([f(inp["norm_ffn1"])[0].reshape(8, 128).T,
                                                f(inp["norm_mix"])[0].reshape(8, 128).T,
                                                f(inp["norm_ffn2"])[0].reshape(8, 128).T], axis=1)),
        "norm_final": f(inp["norm_final"]),
        "w_in": w_in, "swa_sinks": np.ascontiguousarray(sinks), "swa_bias": bias, "swa_mask": mask,
        "w_branch_swa": f(inp["w_branch_swa"])[0], "w_branch_sb": f(inp["w_branch_sb"])[0],
        "w_out": f(inp["w_out"])[0], "consts": _consts(),
    }
    return shared


def kernel(**inputs):
    x = np.asarray(inputs["x"], dtype=np.float32)
    B, S, _ = x.shape
    if S not in _PROG_CACHE:
        _PROG_CACHE[S] = build_program(S)
    nc = _PROG_CACHE[S]
    shared = _prepare_shared(inputs, S)
    in_maps = []
    for b in range(B):
        m = dict(shared)
        m["x"] = np.ascontiguousarray(x[b])
        in_maps.append(m)
    res = run_bass_kernel_spmd(nc, in_maps, core_ids=list(range(B)))
    return np.stack([np.asarray(r["out"], dtype=np.float32) for r in res.results], axis=0)
```

```python
from contextlib import ExitStack

import numpy as np
import ml_dtypes

import concourse.bass as bass
import concourse.mybir as mybir
from concourse.bass_utils import run_bass_kernel_spmd

F32 = mybir.dt.float32
BF16 = mybir.dt.bfloat16
AF = mybir.ActivationFunctionType
ALU = mybir.AluOpType
AX = mybir.AxisListType

D = 1024
DFF = 2816
NFF = DFF // 128
INW = 4352
EPS = 1e-6
NEG = -1e30
TT = 512
MASKB = -30000.0
STGW = 320

C_QA, C_KA, C_VA, C_QB, C_KB, C_VB, C_GA, C_GB = 0, 512, 640, 768, 1280, 1792, 2304, 3328


class Buf:
    __slots__ = ("name", "lw", "rd", "dmard")

    def __init__(self, name):
        self.name = name
        self.lw = None
        self.rd = {}
        self.dmard = []


class Op:
    __slots__ = ("eng", "fn", "deps", "signal", "sem", "count", "is_dma", "pairs")

    def __init__(self, eng, fn, is_dma=False):
        self.eng = eng
        self.fn = fn
        self.deps = []
        self.signal = False
        self.sem = None
        self.count = 0
        self.is_dma = is_dma
        self.pairs = None


SEM_LIMIT = 30000


class Sched:
    ENGS = ("pe", "act", "dve", "pool", "sp")

    def __init__(self, nc, es):
        self.nc = nc
        self.es = es
        self.ops = {e: [] for e in self.ENGS}
        self.dma_sems = {}
        self.barrier_deps = {e: None for e in self.ENGS}

    def _newsem(self, name):
        return self.es.enter_context(self.nc.semaphore(name))

    def _add(self, op, reads, writes):
        deps = []
        bd = self.barrier_deps[op.eng]
        if bd is not None:
            deps.extend(bd)
            self.barrier_deps[op.eng] = None
        for b in reads:
            w = b.lw
            if w is not None:
                if not (w.eng == op.eng == "pe" and not w.is_dma and not op.is_dma):
                    deps.append(w)
        for b in writes:
            w = b.lw
            if w is not None:
                if w.is_dma or op.is_dma or w.eng != op.eng:
                    deps.append(w)
            for e, r in b.rd.items():
                if op.is_dma or e != op.eng:
                    deps.append(r)
            deps.extend(b.dmard)
        for b in reads:
            if op.is_dma:
                b.dmard.append(op)
            else:
                b.rd[op.eng] = op
        for b in writes:
            b.lw = op
            b.rd = {}
            b.dmard = []
        seen = set()
        for d in deps:
            if id(d) not in seen and d is not op:
                seen.add(id(d))
                op.deps.append(d)
                d.signal = True
        self.ops[op.eng].append(op)
        return op

    def op(self, eng, fn, reads=(), writes=()):
        return self._add(Op(eng, fn), reads, writes)

    def dma(self, queue, pairs, reads=(), writes=(), key=None):
        op = Op(queue, None, is_dma=True)
        op.pairs = pairs
        op.signal = True
        if key not in self.dma_sems:
            self.dma_sems[key] = [self._newsem("d%d" % len(self.dma_sems)), 0, None]
        ent = self.dma_sems[key]
        ent[1] += 16 * len(pairs)
        ent[2] = op
        op.sem = ent[0]
        op.count = ent[1]
        return self._add(op, reads, writes)

    def barrier(self):
        prev = []
        for e in self.ENGS:
            for op in reversed(self.ops[e]):
                if not op.is_dma:
                    prev.append(op)
                    break
        for ent in self.dma_sems.values():
            if ent[2] is not None:
                prev.append(ent[2])
        for e in self.ENGS:
            self.barrier_deps[e] = list(prev)

    def emit(self):
        nc = self.nc
        eng_sems = {}
        for e in self.ENGS:
            n = 0
            for op in self.ops[e]:
                if op.is_dma or not op.signal:
                    continue
                k = n // SEM_LIMIT
                if (e, k) not in eng_sems:
                    eng_sems[(e, k)] = self._newsem("e_%s%d" % (e, k))
                op.sem = eng_sems[(e, k)]
                op.count = n % SEM_LIMIT + 1
                n += 1
        final_waits = [(ent[0], ent[1]) for ent in self.dma_sems.values()]

        def run(e, eng, final=False):
            waited = {}
            for op in self.ops[e]:
                for d in op.deps:
                    k = id(d.sem)
                    if waited.get(k, 0) < d.count:
                        eng.wait_ge(d.sem, d.count)
                        waited[k] = d.count
                if op.is_dma:
                    for (o, i) in op.pairs:
                        eng.dma_start(out=o, in_=i).then_inc(op.sem, 16)
                else:
                    ins = op.fn(eng)
                    if op.signal:
                        ins.then_inc(op.sem, 1)
            if final:
                for (h, c) in final_waits:
                    if waited.get(id(h), 0) < c:
                        eng.wait_ge(h, c)

        with nc.Block() as block:
            @block.sync
            def _(sync):
                run("sp", sync, final=True)

            @block.tensor
            def _(tensor):
                run("pe", tensor)

            @block.scalar
            def _(scalar):
                run("act", scalar)

            @block.vector
            def _(vector):
                run("dve", vector)

            @block.gpsimd
            def _(gpsimd):
                run("pool", gpsimd)


class Pool:
    uid = [0]

    def __init__(self, nc, es, name, n, shape, dtype, psum=False):
        self.tiles = []
        for i in range(n):
            Pool.uid[0] += 1
            nm = "%s%d_%d" % (name, i, Pool.uid[0])
            if psum:
                t = es.enter_context(nc.psum_tensor(nm, list(shape), dtype))
            else:
                t = es.enter_context(nc.sbuf_tensor(nm, list(shape), dtype))
            self.tiles.append((t, Buf(nm)))
        self.i = 0

    def next(self):
        t = self.tiles[self.i % len(self.tiles)]
        self.i += 1
        return t


def build_program(S, phases=("ffn1", "proj", "swa", "sb", "mix", "ffn2"), debug=False):
    NT = S // TT
    NB = S // 128
    nc = bass.Bass("TRN2", target_bir_lowering=False)
    dk = "ExternalOutput" if debug else "Internal"

    def din(name, shape, dt=F32):
        return nc.dram_tensor(name, list(shape), dt, kind="ExternalInput").ap()

    def dscr(name, shape, dt):
        return nc.dram_tensor(name, list(shape), dt, kind=dk).ap()

    x_d = din("x", [S, D])
    w1_d = [din("ffn1_w1", [D, DFF]), din("ffn2_w1", [D, DFF])]
    w3_d = [din("ffn1_w3", [D, DFF]), din("ffn2_w3", [D, DFF])]
    w2_d = [din("ffn1_w2", [DFF, D]), din("ffn2_w2", [DFF, D])]
    gains_d = din("gains", [128, 3, 8])
    gfin_d = din("norm_final", [D])
    win_d = din("w_in", [D, INW])
    sinks2_d = din("swa_sinks", [2, 4])
    bias_d = din("swa_bias", [128, 8, 256])
    maskc_d = din("swa_mask", [128, 256])
    wba_d = din("w_branch_swa", [512, D])
    wbb_d = din("w_branch_sb", [512, D])
    wout_d = din("w_out", [D, D])
    cst_d = din("consts", [128, 4 * 128], BF16)
    out_d = nc.dram_tensor("out", [S, D], F32, kind="ExternalOutput").ap()

    x1_d = dscr("x1", [S, D], F32)
    x2_d = dscr("x2", [S, D], F32)
    qta_d = dscr("qta", [4, 128, S], BF16)
    kta_d = dscr("kta", [128, S], BF16)
    va_d = dscr("va", [S, 128], BF16)
    qtb_d = dscr("qtb", [512, S], BF16)
    ktb_d = dscr("ktb", [512, S], BF16)
    vb_d = dscr("vb", [S, 512], BF16)
    ota_d = dscr("ota", [512, S], BF16)
    otb_d = dscr("otb", [512, S], BF16)

    B_x1 = [Buf("x1_%d" % i) for i in range(NT)]
    B_x2 = [Buf("x2_%d" % i) for i in range(NT)]
    B_qkv = [Buf("qkv_%d" % i) for i in range(NT)]
    B_ota = [Buf("ota_%d" % i) for i in range(NT)]
    B_otb = [Buf("otb_%d" % i) for i in range(NT)]

    with ExitStack() as es:
        sc = Sched(nc, es)

        def sbt(stack, name, shape, dt):
            Pool.uid[0] += 1
            return stack.enter_context(nc.sbuf_tensor("%s_%d" % (name, Pool.uid[0]), list(shape), dt))

        cst = sbt(es, "cst", [128, 512], BF16)
        B_cst = Buf("cst")
        sc.dma("sp", [(cst[:], cst_d)], writes=[B_cst], key="cst")
        ident = cst[:, 0:128]
        ntri = cst[:, 128:256]
        nones = cst[:, 256:384]
        dmask = cst[:, 384:512]

        gcol = sbt(es, "gcol", [128, 3, 8], F32)
        B_gcol = Buf("gcol")

        sc.dma("sp", [(gcol[:], gains_d)], writes=[B_gcol], key="gcol")

        def load_cast(stgp, dst_t, sel, B_dst, src_rows, ncols, gain=None):
            c0 = 0
            while c0 < ncols:
                c1 = min(ncols, c0 + STGW)
                st, B_st = stgp.next()
                w = c1 - c0
                sc.dma("sp", [(st[:, 0:w], src_rows[:, c0:c1])], writes=[B_st], key=("stg", id(B_st)))
                o = sel(c0, c1)
                if gain is None:
                    sc.op("dve", lambda e, o=o, i=st[:, 0:w]: e.tensor_copy(out=o, in_=i),
                          reads=[B_st], writes=[B_dst])
                else:
                    sc.op("dve", lambda e, o=o, i=st[:, 0:w], g=gain:
                          e.tensor_scalar(out=o, in0=i, scalar1=g, scalar2=None, op0=ALU.mult),
                          reads=[B_st, B_gcol], writes=[B_dst])
                c0 = c1

        def rstd_ops(st, B_st):
            sc.op("dve", lambda e, st=st: e.tensor_scalar(
                out=st[:, 1:2], in0=st[:, 0:1], scalar1=float(D * EPS), scalar2=None, op0=ALU.add),
                reads=[B_st], writes=[B_st])
            sc.op("act", lambda e, st=st: e.activation(out=st[:, 1:2], in_=st[:, 1:2], func=AF.Sqrt),
                  reads=[B_st], writes=[B_st])
            sc.op("dve", lambda e, st=st: e.reciprocal(out=st[:, 1:2], in_=st[:, 1:2]),
                  reads=[B_st], writes=[B_st])

        class NormT:
            def __init__(self, stack, nht=1, nxin=2):
                self.nxin = nxin
                self.xin = Pool(nc, stack, "xin", nxin, [128, D], F32)
                self.hrow = Pool(nc, stack, "hrow", 2, [128, D], BF16)
                self.sqj = sbt(stack, "sqj", [128, D], BF16)
                self.B_sqj = Buf("sqj")
                self.stat = Pool(nc, stack, "stat", 8, [128, 2], F32)
                self.hts = [(sbt(stack, "ht%d" % i, [128, 8, TT], BF16), [Buf("ht%d_%d" % (i, j)) for j in range(4)])
                            for i in range(nht)]
                self.hi = 0
                self.tpp = Pool(nc, stack, "tpp", 2, [128, 8, 128], BF16, psum=True)
                self.ev = 0

            def _load(self, J, j):
                xt, B_xt = self.xin.next()
                r0 = J["ti"] * TT + j * 128
                sc.dma("sp", [(xt[:], J["src"][r0:r0 + 128, :])], reads=[J["B_src"]] if J["B_src"] else [],
                       writes=[B_xt], key=("xin", id(B_xt)))
                J["x"][j] = (xt, B_xt)

            def begin(self, src_d, ti, B_src):
                ht_, B_ht_ = self.hts[self.hi % len(self.hts)]
                self.hi += 1
                J = dict(src=src_d, ti=ti, B_src=B_src, ht=ht_, B_ht=B_ht_, x={}, st={})
                if self.nxin >= 4:
                    for j in range(4):
                        self._load(J, j)
                return J

            def piece_a(self, J, j):
                if j not in J["x"]:
                    self._load(J, j)
                xt, B_xt = J["x"][j]
                st, B_st = self.stat.next()
                J["st"][j] = (st, B_st)
                sc.op("act", lambda e, xt=xt, st=st: e.activation(
                    out=self.sqj[:], in_=xt[:], func=AF.Square, accum_out=st[:, 0:1]),
                    reads=[B_xt], writes=[B_st, self.B_sqj])
                sc.op("dve", lambda e, st=st: e.tensor_scalar(
                    out=st[:, 1:2], in0=st[:, 0:1], scalar1=float(D * EPS), scalar2=None, op0=ALU.add),
                    reads=[B_st], writes=[B_st])

            def piece_b(self, J, j):
                st, B_st = J["st"][j]
                sc.op("act", lambda e, st=st: e.activation(out=st[:, 1:2], in_=st[:, 1:2], func=AF.Sqrt),
                      reads=[B_st], writes=[B_st])
                sc.op("dve", lambda e, st=st: e.reciprocal(out=st[:, 1:2], in_=st[:, 1:2]),
                      reads=[B_st], writes=[B_st])

            def piece_c(self, J, j):
                xt, B_xt = J["x"][j]
                st, B_st = J["st"][j]
                ht_, B_ht_ = J["ht"], J["B_ht"]
                hr, B_hr = self.hrow.next()
                sc.op("dve", lambda e, hr=hr, xt=xt, st=st: e.tensor_scalar(
                    out=hr[:], in0=xt[:], scalar1=st[:, 1:2], scalar2=32.0, op0=ALU.mult, op1=ALU.mult),
                    reads=[B_xt, B_st], writes=[B_hr])
                tp, B_tp = self.tpp.next()
                for k in range(8):
                    sc.op("pe", lambda e, tp=tp, hr=hr, k=k: e.transpose(
                        out=tp[:, k, :], in_=hr[:, k * 128:(k + 1) * 128], identity=ident),
                        reads=[B_hr, B_cst], writes=[B_tp])
                eng = ("dve", "act")[self.ev % 2]
                self.ev += 1
                if eng == "dve":
                    sc.op("dve", lambda e, tp=tp, j=j, ht_=ht_: e.tensor_copy(
                        out=ht_[:, :, j * 128:(j + 1) * 128], in_=tp[:]),
                        reads=[B_tp], writes=[B_ht_[j]])
                else:
                    sc.op("act", lambda e, tp=tp, j=j, ht_=ht_: e.copy(
                        out=ht_[:, :, j * 128:(j + 1) * 128], in_=tp[:]),
                        reads=[B_tp], writes=[B_ht_[j]])

            def pieces(self, J):
                order = [("a", 0), ("a", 1), ("b", 0), ("a", 2), ("b", 1), ("c", 0), ("a", 3), ("b", 2),
                         ("c", 1), ("b", 3), ("c", 2), ("c", 3)]
                fns = {"a": self.piece_a, "b": self.piece_b, "c": self.piece_c}
                return [(lambda f=fns[k], j=j: f(J, j)) for (k, j) in order]

            def run(self, src_d, ti, B_src):
                J = self.begin(src_d, ti, B_src)
                for j in range(4):
                    self.piece_a(J, j)
                    self.piece_b(J, j)
                    self.piece_c(J, j)
                return J["ht"], J["B_ht"]

        def wtoks(toks, col, w, blk=640):
            return [toks[b] for b in range(col // blk, (col + w - 1) // blk + 1)]

        def load_cast_cols(stgp, dst_t, toks, src_d, nrow_chunks, ncols, gain_i=None, blk=640):
            for b in range((ncols + blk - 1) // blk):
                c0, c1 = b * blk, min(ncols, (b + 1) * blk)
                for k in range(nrow_chunks):
                    load_cast(stgp, dst_t, lambda a, bb, k=k, c0=c0: dst_t[:, k, c0 + a:c0 + bb], toks[b],
                              src_d[k * 128:(k + 1) * 128, c0:c1], c1 - c0,
                              gain=None if gain_i is None else gcol[:, gain_i, k:k + 1])

        def phase_ffn(layer, src_d, B_srcs, dst_d, B_dsts, final):
            with ExitStack() as pes:
                w1b = sbt(pes, "w1b", [128, 8, DFF], BF16)
                w3b = sbt(pes, "w3b", [128, 8, DFF], BF16)
                w2b = sbt(pes, "w2b", [128, NFF, D], BF16)
                NBLK = (DFF + 639) // 640
                B_w1 = [Buf("w1_%d" % b) for b in range(NBLK)]
                B_w3 = [Buf("w3_%d" % b) for b in range(NBLK)]
                B_w2 = [Buf("w2_%d" % c) for c in range(NFF)]
                nt = NormT(pes)
                nxt = nt.run(src_d, 0, B_srcs[0] if B_srcs else None)
                stgp = Pool(nc, pes, "stg", 4, [128, STGW], F32)
                gi = 0 if layer == 0 else 2
                for b in range(NBLK):
                    c0, c1 = b * 640, min(DFF, (b + 1) * 640)
                    for (wb, wd, toks) in ((w1b, w1_d[layer], B_w1), (w3b, w3_d[layer], B_w3)):
                        for k in range(8):
                            load_cast(stgp, wb, lambda a, bb, k=k, c0=c0, wb=wb: wb[:, k, c0 + a:c0 + bb], toks[b],
                                      wd[k * 128:(k + 1) * 128, c0:c1], c1 - c0, gain=gcol[:, gi, k:k + 1])
                for c in range(NFF):
                    load_cast(stgp, w2b, lambda a, b, c=c: w2b[:, c, a:b], B_w2[c],
                              w2_d[layer][c * 128:(c + 1) * 128, :], D)
                G = sbt(pes, "G", [128, NFF, TT], BF16)
                B_G = [Buf("G%d" % c) for c in range(NFF)]
                sgp = Pool(nc, pes, "sg", 2, [128, TT], F32)
                xres = Pool(nc, pes, "xres", 2, [128, D], F32)
                ps_up = Pool(nc, pes, "psu", 4, [128, 512], F32, psum=True)
                ps_dn = Pool(nc, pes, "psd", 2, [128, 512], F32, psum=True)
                if final:
                    gfb = sbt(pes, "gfb", [128, D], F32)
                    B_gfb = Buf("gfb")
                    sc.dma("sp", [(gfb[:], gfin_d.partition_broadcast(128))], writes=[B_gfb], key="gfb")
                    sc.op("dve", lambda e: e.tensor_scalar(out=gfb[:], in0=gfb[:], scalar1=32.0, scalar2=None,
                                                           op0=ALU.mult), reads=[B_gfb], writes=[B_gfb])
                    fstat = Pool(nc, pes, "fstat", 4, [128, 2], F32)

                for ti in range(NT):
                    ht, B_ht = nxt
                    for c in range(NFF):
                        pu, B_pu = ps_up.next()
                        pv, B_pv = ps_up.next()
                        for k in range(8):
                            sc.op("pe", lambda e, pu=pu, k=k, c=c, ht=ht: e.matmul(
                                pu[:], lhsT=w1b[:, k, c * 128:(c + 1) * 128], rhs=ht[:, k, :],
                                start=(k == 0), stop=(k == 7)), reads=wtoks(B_w1, c * 128, 128) + B_ht, writes=[B_pu])
                        for k in range(8):
                            sc.op("pe", lambda e, pv=pv, k=k, c=c, ht=ht: e.matmul(
                                pv[:], lhsT=w3b[:, k, c * 128:(c + 1) * 128], rhs=ht[:, k, :],
                                start=(k == 0), stop=(k == 7)), reads=wtoks(B_w3, c * 128, 128) + B_ht, writes=[B_pv])
                        sg, B_sg = sgp.next()
                        sc.op("act", lambda e, sg=sg, pu=pu: e.activation(out=sg[:], in_=pu[:], func=AF.Silu),
                              reads=[B_pu], writes=[B_sg])
                        sc.op("dve", lambda e, sg=sg, pv=pv, c=c: e.tensor_tensor(
                            out=G[:, c, :], in0=sg[:], in1=pv[:], op=ALU.mult),
                            reads=[B_sg, B_pv], writes=[B_G[c]])
                    if ti + 1 < NT:
                        nxt = nt.run(src_d, ti + 1, B_srcs[ti + 1] if B_srcs else None)
                    for j in range(4):
                        xr, B_xr = xres.next()
                        r0 = ti * TT + j * 128
                        sc.dma("sp", [(xr[:], src_d[r0:r0 + 128, :])], reads=[B_srcs[ti]] if B_srcs else [],
                               writes=[B_xr], key=("xres", id(B_xr)))
                        for half in range(2):
                            pd, B_pd = ps_dn.next()
                            for c in range(NFF):
                                sc.op("pe", lambda e, pd=pd, c=c, j=j, half=half: e.matmul(
                                    pd[:], lhsT=G[:, c, j * 128:(j + 1) * 128],
                                    rhs=w2b[:, c, half * 512:(half + 1) * 512],
                                    start=(c == 0), stop=(c == NFF - 1)),
                                    reads=[B_G[c], B_w2[c]], writes=[B_pd])
                            sc.op("dve", lambda e, pd=pd, xr=xr, half=half: e.scalar_tensor_tensor(
                                out=xr[:, half * 512:(half + 1) * 512], in0=pd[:], scalar=0.5,
                                in1=xr[:, half * 512:(half + 1) * 512], op0=ALU.mult, op1=ALU.add),
                                reads=[B_pd, B_xr], writes=[B_xr])
                        if final:
                            fs, B_fs = fstat.next()
                            sc.op("act", lambda e, xr=xr, fs=fs: e.activation(
                                out=nt.sqj[:], in_=xr[:], func=AF.Square, accum_out=fs[:, 0:1]),
                                reads=[B_xr], writes=[B_fs, nt.B_sqj])
                            rstd_ops(fs, B_fs)
                            sc.op("dve", lambda e, xr=xr, fs=fs: e.scalar_tensor_tensor(
                                out=xr[:], in0=xr[:], scalar=fs[:, 1:2], in1=gfb[:], op0=ALU.mult, op1=ALU.mult),
                                reads=[B_xr, B_fs, B_gfb], writes=[B_xr])
                        sc.dma("sp", [(dst_d[r0:r0 + 128, :], xr[:])], reads=[B_xr],
                               writes=[B_dsts[ti]] if B_dsts else [], key=("xres_st", id(B_xr)))
            sc.barrier()

        def phase_proj():
            NQ = C_GA
            with ExitStack() as pes:
                wq = sbt(pes, "wq", [128, 8, NQ], BF16)
                B_wq = [Buf("wq%d" % b) for b in range((NQ + 639) // 640)]
                nt = NormT(pes, nht=2, nxin=4)
                nxt = nt.run(x1_d, 0, B_x1[0])
                stgp = Pool(nc, pes, "stg", 4, [128, STGW], F32)
                load_cast_cols(stgp, wq, B_wq, win_d, 8, NQ, gain_i=1)
                ps = Pool(nc, pes, "ps", 4, [128, 512], F32, psum=True)
                evp = Pool(nc, pes, "ev", 6, [128, 512], BF16)
                fm = []
                for g in range(4):
                    fm.append((C_QA + g * 128, lambda ti, g=g: qta_d[g, :, ti * TT:(ti + 1) * TT], 0.125))
                fm.append((C_KA, lambda ti: kta_d[:, ti * TT:(ti + 1) * TT], 1.0))
                for cc in range(4):
                    fm.append((C_QB + cc * 128, lambda ti, cc=cc: qtb_d[cc * 128:(cc + 1) * 128, ti * TT:(ti + 1) * TT], 0.125))
                for cc in range(4):
                    fm.append((C_KB + cc * 128, lambda ti, cc=cc: ktb_d[cc * 128:(cc + 1) * 128, ti * TT:(ti + 1) * TT], 1.0))
                evi = 0
                for ti in range(NT):
                    ht, B_ht = nxt
                    pcs = []
                    if ti + 1 < NT:
                        J = nt.begin(x1_d, ti + 1, B_x1[ti + 1])
                        nxt = (J["ht"], J["B_ht"])
                        pcs = nt.pieces(J)
                    for (col, dst, scale) in fm:
                        pp, B_pp = ps.next()
                        for k in range(8):
                            sc.op("pe", lambda e, pp=pp, k=k, col=col, ht=ht: e.matmul(
                                pp[:], lhsT=wq[:, k, col:col + 128], rhs=ht[:, k, :],
                                start=(k == 0), stop=(k == 7)), reads=wtoks(B_wq, col, 128) + B_ht, writes=[B_pp])
                        ev, B_ev = evp.next()
                        if evi % 2 == 0:
                            sc.op("dve", lambda e, ev=ev, pp=pp, scale=scale: e.tensor_scalar(
                                out=ev[:], in0=pp[:], scalar1=float(scale), scalar2=None, op0=ALU.mult),
                                reads=[B_pp], writes=[B_ev])
                        else:
                            sc.op("act", lambda e, ev=ev, pp=pp, scale=scale: e.activation(
                                out=ev[:], in_=pp[:], func=AF.Copy, scale=float(scale)),
                                reads=[B_pp], writes=[B_ev])
                        evi += 1
                        sc.dma("sp", [(dst(ti), ev[:])], reads=[B_ev], writes=[B_qkv[ti]], key=("ev", id(B_ev)))
                        if pcs:
                            pcs.pop(0)()
                    for j in range(4):
                        r0 = ti * TT + j * 128
                        pp, B_pp = ps.next()
                        for k in range(8):
                            sc.op("pe", lambda e, pp=pp, k=k, j=j, ht=ht: e.matmul(
                                pp[:], lhsT=ht[:, k, j * 128:(j + 1) * 128], rhs=wq[:, k, C_VB:C_VB + 512],
                                start=(k == 0), stop=(k == 7)), reads=wtoks(B_wq, C_VB, 512) + B_ht, writes=[B_pp])
                        ev, B_ev = evp.next()
                        sc.op("dve", lambda e, ev=ev, pp=pp: e.tensor_copy(out=ev[:], in_=pp[:]),
                              reads=[B_pp], writes=[B_ev])
                        sc.dma("sp", [(vb_d[r0:r0 + 128, :], ev[:])], reads=[B_ev], writes=[B_qkv[ti]],
                               key=("ev", id(B_ev)))
                        pp, B_pp = ps.next()
                        for k in range(8):
                            sc.op("pe", lambda e, pp=pp, k=k, j=j, ht=ht: e.matmul(
                                pp[:, 0:128], lhsT=ht[:, k, j * 128:(j + 1) * 128], rhs=wq[:, k, C_VA:C_VA + 128],
                                start=(k == 0), stop=(k == 7)), reads=wtoks(B_wq, C_VA, 128) + B_ht, writes=[B_pp])
                        ev, B_ev = evp.next()
                        sc.op("act", lambda e, ev=ev, pp=pp: e.copy(out=ev[:, 0:128], in_=pp[:, 0:128]),
                              reads=[B_pp], writes=[B_ev])
                        sc.dma("sp", [(va_d[r0:r0 + 128, :], ev[:, 0:128])], reads=[B_ev], writes=[B_qkv[ti]],
                               key=("ev", id(B_ev)))
                        if pcs:
                            pcs.pop(0)()
                    while pcs:
                        pcs.pop(0)()
            sc.barrier()

        def phase_swa():
            with ExitStack() as pes:
                kt = sbt(pes, "kta", [128, S], BF16)
                qt = sbt(pes, "qta", [128, 4, S], BF16)
                vv = sbt(pes, "vva", [128, NB, 128], BF16)
                B_in = Buf("swa_in")
                var = va_d.rearrange("(n p) c -> p n c", p=128)
                sc.dma("sp", [(kt[:], kta_d)] + [(qt[:, g, :], qta_d[g]) for g in range(4)]
                       + [(vv[:, n0:min(NB, n0 + 8), :], var[:, n0:min(NB, n0 + 8), :]) for n0 in range(0, NB, 8)],
                       reads=B_qkv, writes=[B_in], key="swa_in")
                bm = sbt(pes, "bm", [128, 2, 4, 256], F32)
                mk = sbt(pes, "mk", [128, 256], F32)
                sk = sbt(pes, "sk", [128, 2, 4], F32)
                B_bm = Buf("bm")
                B_sk = Buf("sk")
                bsrc = bias_d.rearrange("q (g kv) k -> q kv g k", kv=2)
                sc.dma("sp", [(bm[:, kv, :, :], bsrc[:, kv, :, :]) for kv in range(2)] + [(mk[:], maskc_d)],
                       writes=[B_bm], key="bm")
                sc.dma("sp", [(sk[:, kv, :], sinks2_d[kv].partition_broadcast(128)) for kv in range(2)],
                       writes=[B_sk], key="sk")
                for kv in range(2):
                    for g in range(4):
                        sc.op("dve", lambda e, kv=kv, g=g: e.tensor_tensor(
                            out=bm[:, kv, g, :], in0=bm[:, kv, g, :], in1=mk[:], op=ALU.add),
                            reads=[B_bm], writes=[B_bm])
                psc = Pool(nc, pes, "psc", 2, [128, 4, 256], F32, psum=True)
                ppt = Pool(nc, pes, "ppt", 2, [128, 8, 128], BF16, psum=True)
                pso = Pool(nc, pes, "pso", 2, [128, 4, 128], F32, psum=True)
                scs = Pool(nc, pes, "scs", 2, [128, 4, 256], F32)
                pbf = Pool(nc, pes, "pbf", 3, [128, 4, 256], BF16)
                ptb = Pool(nc, pes, "ptb", 2, [128, 8, 128], BF16)
                sm = Pool(nc, pes, "sm", 6, [128, 5, 4], F32)
                osb = Pool(nc, pes, "osb", 3, [128, 4, 128], BF16)
                units = [dict(n=n, kv=kv) for n in range(NB) for kv in range(2)]
                cur_ob = {}
                evc = [0]

                def stA(u):
                    n, kv = u["n"], u["kv"]
                    k0 = 0 if n > 0 else 128
                    kw = 256 - k0
                    ks = (n - 1) * 128 + k0
                    u.update(k0=k0, kw=kw)
                    pc, B_pc = psc.next()
                    u["pc"] = (pc, B_pc)
                    for g in range(4):
                        sc.op("pe", lambda e, pc=pc, g=g, kv=kv, n=n, ks=ks, kw=kw, k0=k0: e.matmul(
                            pc[:, g, k0:256], lhsT=qt[kv * 64:(kv + 1) * 64, g, n * 128:(n + 1) * 128],
                            rhs=kt[kv * 64:(kv + 1) * 64, ks:ks + kw], start=True, stop=True),
                            reads=[B_in], writes=[B_pc])

                def stB1(u):
                    kv, k0 = u["kv"], u["k0"]
                    pc, B_pc = u["pc"]
                    ss, B_ss = scs.next()
                    st, B_st = sm.next()
                    pb, B_pb = pbf.next()
                    u.update(st=(st, B_st), pb=(pb, B_pb))
                    sc.op("dve", lambda e, pc=pc, ss=ss, kv=kv, k0=k0: e.tensor_tensor(
                        out=ss[:, :, k0:256], in0=pc[:, :, k0:256], in1=bm[:, kv, :, k0:256], op=ALU.add),
                        reads=[B_pc, B_bm], writes=[B_ss])
                    sc.op("dve", lambda e, ss=ss, st=st, k0=k0: e.tensor_reduce(
                        out=st[:, 0, :], in_=ss[:, :, k0:256], axis=AX.X, op=ALU.max),
                        reads=[B_ss], writes=[B_st])
                    sc.op("dve", lambda e, st=st, kv=kv: e.tensor_tensor(
                        out=st[:, 0, :], in0=st[:, 0, :], in1=sk[:, kv, :], op=ALU.max),
                        reads=[B_st, B_sk], writes=[B_st])
                    sc.op("dve", lambda e, st=st: e.tensor_scalar(out=st[:, 1, :], in0=st[:, 0, :], scalar1=-1.0,
                                                                   scalar2=None, op0=ALU.mult),
                          reads=[B_st], writes=[B_st])
                    sc.op("dve", lambda e, st=st, kv=kv: e.tensor_tensor(
                        out=st[:, 2, :], in0=sk[:, kv, :], in1=st[:, 1, :], op=ALU.add),
                        reads=[B_st, B_sk], writes=[B_st])
                    for g in range(4):
                        sc.op("act", lambda e, pb=pb, ss=ss, st=st, g=g, k0=k0: e.activation(
                            out=pb[:, g, k0:256], in_=ss[:, g, k0:256], func=AF.Exp, bias=st[:, 1, g:g + 1],
                            accum_out=st[:, 3, g:g + 1]), reads=[B_ss, B_st], writes=[B_pb, B_st])
                    sc.op("act", lambda e, st=st: e.activation(out=st[:, 2, :], in_=st[:, 2, :], func=AF.Exp),
                          reads=[B_st], writes=[B_st])

                def stB2(u):
                    n, kv, k0, kw = u["n"], u["kv"], u["k0"], u["kw"]
                    st, B_st = u["st"]
                    pb, B_pb = u["pb"]
                    sc.op("dve", lambda e, st=st: e.tensor_tensor(out=st[:, 4, :], in0=st[:, 3, :], in1=st[:, 2, :], op=ALU.add),
                          reads=[B_st], writes=[B_st])
                    sc.op("dve", lambda e, st=st: e.reciprocal(out=st[:, 4, :], in_=st[:, 4, :]),
                          reads=[B_st], writes=[B_st])
                    sc.op("pool", lambda e, pb=pb, st=st, k0=k0, kw=kw: e.tensor_tensor(
                        out=pb[:, :, k0:256], in0=pb[:, :, k0:256],
                        in1=st[:, 4, :].unsqueeze(2).to_broadcast([128, 4, kw]), op=ALU.mult),
                        reads=[B_pb, B_st], writes=[B_pb])
                    nkb = kw // 128
                    pp, B_pp = ppt.next()
                    pt, B_pt = ptb.next()
                    for kb in range(nkb):
                        for g in range(4):
                            sc.op("pe", lambda e, pp=pp, pb=pb, g=g, kb=kb, k0=k0: e.transpose(
                                out=pp[:, kb * 4 + g, :], in_=pb[:, g, k0 + kb * 128:k0 + (kb + 1) * 128], identity=ident),
                                reads=[B_pb, B_cst], writes=[B_pp])
                    evc[0] += 1
                    if evc[0] % 2 == 0:
                        sc.op("dve", lambda e, pp=pp, pt=pt, nkb=nkb: e.tensor_copy(
                            out=pt[:, 0:4 * nkb, :], in_=pp[:, 0:4 * nkb, :]), reads=[B_pp], writes=[B_pt])
                    else:
                        sc.op("act", lambda e, pp=pp, pt=pt, nkb=nkb: e.copy(
                            out=pt[:, 0:4 * nkb, :], in_=pp[:, 0:4 * nkb, :]), reads=[B_pp], writes=[B_pt])
                    po, B_po = pso.next()
                    for g in range(4):
                        h = g + 4 * kv
                        cl, half = (h // 2) - 2 * kv, h % 2
                        for kb in range(nkb):
                            blk = n - (nkb - 1) + kb
                            sc.op("pe", lambda e, po=po, pt=pt, cl=cl, half=half, kv=kv, g=g, kb=kb, blk=blk, nkb=nkb: e.matmul(
                                po[half * 64:(half + 1) * 64, cl, :], lhsT=vv[:, blk, kv * 64:(kv + 1) * 64],
                                rhs=pt[:, kb * 4 + g, :], start=(kb == 0), stop=(kb == nkb - 1)),
                                reads=[B_pt, B_in], writes=[B_po])
                    if kv == 0:
                        cur_ob[n] = osb.next()
                    ob, B_ob = cur_ob[n]
                    sc.op("act", lambda e, ob=ob, po=po, kv=kv: e.copy(out=ob[:, 2 * kv:2 * kv + 2, :], in_=po[:, 0:2, :]),
                          reads=[B_po], writes=[B_ob])
                    if kv == 1:
                        sc.dma("sp", [(ota_d.rearrange("(c p) s -> p c s", p=128)[:, :, n * 128:(n + 1) * 128], ob[:])],
                               reads=[B_ob], writes=[B_ota[n // 4]], key=("osb", id(B_ob)))
                        del cur_ob[n]

                nu = len(units)
                for t in range(nu + 2):
                    if t < nu:
                        stA(units[t])
                    if 0 <= t - 1 < nu:
                        stB1(units[t - 1])
                    if 0 <= t - 2 < nu:
                        stB2(units[t - 2])
            sc.barrier()

        def phase_sb():
            with ExitStack() as pes:
                ktp = Pool(nc, pes, "ktb", 2, [128, S], BF16)
                qtp = Pool(nc, pes, "qtb", 2, [128, S], BF16)
                vvp = Pool(nc, pes, "vvb", 2, [128, NB, 128], BF16)
                zp = Pool(nc, pes, "zp", 3, [128, 2, 512], F32, psum=True)
                op_ = Pool(nc, pes, "op", 2, [128, 512], F32, psum=True)
                Ep = Pool(nc, pes, "E", 2, [128, 2, 512], F32)
                Lp = Pool(nc, pes, "L", 3, [128, 2, 512], BF16)
                Ap = Pool(nc, pes, "A", 2, [128, 2, 512], BF16)
                R32 = sbt(pes, "R32", [128, 2, 512], F32)
                B_R32 = Buf("R32")
                Rbp = Pool(nc, pes, "Rb", 2, [128, 2, 512], BF16)
                oev = Pool(nc, pes, "oev", 2, [128, 512], BF16)
                steps = []
                pair_in = {}
                for p in range(4):
                    for i in range(NT):
                        nk = 4 * i + 4
                        for si, kj in enumerate(range(nk - 1, -1, -1)):
                            steps.append(dict(p=p, i=i, kj=kj, first=(si == 0), last=(kj == 0),
                                              c0=max(0, (kj - 4 * i)) * 128, diag=(kj >= 4 * i)))

                def load_pair(p):
                    kt, B_kt = ktp.next()
                    qt, B_qt = qtp.next()
                    vv, B_vv = vvp.next()
                    sc.dma("sp", [(kt[:], ktb_d[p * 128:(p + 1) * 128, :])], reads=B_qkv, writes=[B_kt], key=("sbk", id(B_kt)))
                    sc.dma("sp", [(qt[:], qtb_d[p * 128:(p + 1) * 128, :])], reads=B_qkv, writes=[B_qt], key=("sbq", id(B_qt)))
                    vbr = vb_d[:, p * 128:(p + 1) * 128].rearrange("(n p) c -> p n c", p=128)
                    sc.dma("sp", [(vv[:, n0:min(NB, n0 + 8), :], vbr[:, n0:min(NB, n0 + 8), :]) for n0 in range(0, NB, 8)],
                           reads=B_qkv, writes=[B_vv], key=("sbv", id(B_vv)))
                    pair_in[p] = (kt, B_kt, qt, B_qt, vv, B_vv)

                load_pair(0)
                state = {}

                def stage0(s):
                    p, i, kj, c0 = s["p"], s["i"], s["kj"], s["c0"]
                    if s["first"] and i == 0 and p + 1 < 4:
                        load_pair(p + 1)
                    kt, B_kt, qt, B_qt, vv, B_vv = pair_in[p]
                    z, B_z = zp.next()
                    s["z"], s["B_z"] = z, B_z
                    for hh in range(2):
                        sc.op("pe", lambda e, z=z, hh=hh, kj=kj, i=i, c0=c0, kt=kt, qt=qt: e.matmul(
                            z[:, hh, c0:512], lhsT=kt[hh * 64:(hh + 1) * 64, kj * 128:(kj + 1) * 128],
                            rhs=qt[hh * 64:(hh + 1) * 64, i * 512 + c0:(i + 1) * 512],
                            start=True, stop=False, skip_group_check=True),
                            reads=[B_kt, B_qt], writes=[B_z])
                    if s["diag"]:
                        for hh in range(2):
                            sc.op("pe", lambda e, z=z, hh=hh, c0=c0: e.matmul(
                                z[:, hh, c0:c0 + 128], lhsT=ident, rhs=dmask, start=False, stop=False,
                                skip_group_check=True), reads=[B_cst], writes=[B_z])

                def stage1(s):
                    z, B_z, c0 = s["z"], s["B_z"], s["c0"]
                    E, B_E = Ep.next()
                    L, B_L = Lp.next()
                    s["L"], s["B_L"] = L, B_L
                    sc.op("act", lambda e, E=E, z=z, c0=c0: e.activation(out=E[:, :, c0:512], in_=z[:, :, c0:512], func=AF.Exp),
                          reads=[B_z], writes=[B_E])
                    sc.op("act", lambda e, E=E, L=L, c0=c0: e.activation(out=L[:, :, c0:512], in_=E[:, :, c0:512], func=AF.Ln, bias=1.0),
                          reads=[B_E], writes=[B_L])

                def stage2(s):
                    p, i, kj, c0 = s["p"], s["i"], s["kj"], s["c0"]
                    kt, B_kt, qt, B_qt, vv, B_vv = pair_in[p]
                    z, B_z, L, B_L = s["z"], s["B_z"], s["L"], s["B_L"]
                    for hh in range(2):
                        sc.op("pe", lambda e, z=z, hh=hh, c0=c0, L=L: e.matmul(
                            z[:, hh, c0:512], lhsT=ntri, rhs=L[:, hh, c0:512], start=False, stop=False,
                            skip_group_check=True), reads=[B_L, B_cst], writes=[B_z])
                    if not s["first"]:
                        Rb, B_Rb = state["Rb"]
                        for hh in range(2):
                            sc.op("pe", lambda e, z=z, hh=hh, c0=c0, Rb=Rb: e.matmul(
                                z[:, hh, c0:512], lhsT=nones, rhs=Rb[:, hh, c0:512], start=False, stop=False,
                                skip_group_check=True), reads=[B_Rb, B_cst], writes=[B_z])
                    A, B_A = Ap.next()
                    sc.op("act", lambda e, A=A, z=z, c0=c0: e.activation(out=A[:, :, c0:512], in_=z[:, :, c0:512], func=AF.Exp),
                          reads=[B_z], writes=[B_A])
                    if s["first"]:
                        state["o"] = op_.next()
                    o, B_o = state["o"]
                    for hh in range(2):
                        sc.op("pe", lambda e, o=o, hh=hh, c0=c0, A=A, vv=vv, kj=kj, first=s["first"]: e.matmul(
                            o[hh * 64:(hh + 1) * 64, c0:512], lhsT=vv[:, kj, hh * 64:(hh + 1) * 64],
                            rhs=A[:, hh, c0:512], start=first, stop=False, skip_group_check=True),
                            reads=[B_A, B_vv], writes=[B_o])
                    if not s["last"]:
                        if s["first"]:
                            sc.op("pool", lambda e: e.memset(R32[:], 0.0), writes=[B_R32])
                        sc.op("pool", lambda e, L=L, c0=c0: e.tensor_tensor(
                            out=R32[:, :, c0:512], in0=R32[:, :, c0:512], in1=L[:, :, c0:512], op=ALU.add),
                            reads=[B_L, B_R32], writes=[B_R32])
                        Rb, B_Rb = Rbp.next()
                        sc.op("dve", lambda e, Rb=Rb: e.tensor_copy(out=Rb[:], in_=R32[:]), reads=[B_R32], writes=[B_Rb])
                        state["Rb"] = (Rb, B_Rb)
                    else:
                        ob, B_ob = oev.next()
                        sc.op("dve", lambda e, ob=ob, o=o: e.tensor_copy(out=ob[:], in_=o[:]), reads=[B_o], writes=[B_ob])
                        sc.dma("sp", [(otb_d[p * 128:(p + 1) * 128, i * 512:(i + 1) * 512], ob[:])], reads=[B_ob],
                               writes=[B_otb[i]], key=("oev", id(B_ob)))

                n = len(steps)
                for t in range(n + 2):
                    if t < n:
                        stage0(steps[t])
                    if 0 <= t - 1 < n:
                        stage1(steps[t - 1])
                    if 0 <= t - 2 < n:
                        stage2(steps[t - 2])
            sc.barrier()

        def phase_sb2(K=8):
            with ExitStack() as pes:
                per_pair = sum(4 * i + 4 for i in range(NT))
                nsets = 2 if per_pair >= 4 * K else 4
                ktp = Pool(nc, pes, "ktb", nsets, [128, S], BF16)
                qtp = Pool(nc, pes, "qtb", nsets, [128, S], BF16)
                vvp = Pool(nc, pes, "vvb", nsets, [128, NB, 128], BF16)
                zp = Pool(nc, pes, "zp", 3, [128, 2, 512], F32, psum=True)
                op_ = Pool(nc, pes, "op", 2, [128, 512], F32, psum=True)
                Lp = Pool(nc, pes, "L", 2 * K + 2, [128, 2, 512], BF16)
                Rbp = Pool(nc, pes, "Rb", 2 * K + 2, [128, 2, 512], BF16)
                Ap = Pool(nc, pes, "A", K + 3, [128, 2, 512], BF16)
                R32 = sbt(pes, "R32", [128, 2, 512], F32)
                B_R32 = Buf("R32")
                oev = Pool(nc, pes, "oev", 2, [128, 512], BF16)
                steps = []
                pair_in = {}
                for p in range(4):
                    for i in range(NT):
                        nk = 4 * i + 4
                        for si, kj in enumerate(range(nk - 1, -1, -1)):
                            steps.append(dict(p=p, i=i, kj=kj, first=(si == 0), last=(kj == 0),
                                              c0=max(0, (kj - 4 * i)) * 128, diag=(kj >= 4 * i)))
                for a, b in zip(steps[:-1], steps[1:]):
                    b["prev"] = a

                def load_pair(p):
                    kt, B_kt = ktp.next()
                    qt, B_qt = qtp.next()
                    vv, B_vv = vvp.next()
                    sc.dma("sp", [(kt[:], ktb_d[p * 128:(p + 1) * 128, :])], reads=B_qkv, writes=[B_kt], key=("sbk", id(B_kt)))
                    sc.dma("sp", [(qt[:], qtb_d[p * 128:(p + 1) * 128, :])], reads=B_qkv, writes=[B_qt], key=("sbq", id(B_qt)))
                    vbr = vb_d[:, p * 128:(p + 1) * 128].rearrange("(n p) c -> p n c", p=128)
                    sc.dma("sp", [(vv[:, n0:min(NB, n0 + 8), :], vbr[:, n0:min(NB, n0 + 8), :]) for n0 in range(0, NB, 8)],
                           reads=B_qkv, writes=[B_vv], key=("sbv", id(B_vv)))
                    pair_in[p] = (kt, B_kt, qt, B_qt, vv, B_vv)

                for p_ in range(nsets):
                    load_pair(p_)
                state = {}

                def zmm(s_, z, B_z):
                    p, i, kj, c0 = s_["p"], s_["i"], s_["kj"], s_["c0"]
                    kt, B_kt, qt, B_qt, vv, B_vv = pair_in[p]
                    for hh in range(2):
                        sc.op("pe", lambda e, z=z, hh=hh, kj=kj, i=i, c0=c0, kt=kt, qt=qt: e.matmul(
                            z[:, hh, c0:512], lhsT=kt[hh * 64:(hh + 1) * 64, kj * 128:(kj + 1) * 128],
                            rhs=qt[hh * 64:(hh + 1) * 64, i * 512 + c0:(i + 1) * 512],
                            start=True, stop=False, skip_group_check=True),
                            reads=[B_kt, B_qt], writes=[B_z])
                    if s_["diag"]:
                        for hh in range(2):
                            sc.op("pe", lambda e, z=z, hh=hh, c0=c0: e.matmul(
                                z[:, hh, c0:c0 + 128], lhsT=ident, rhs=dmask, start=False, stop=False,
                                skip_group_check=True), reads=[B_cst], writes=[B_z])

                def Xstep(s_):
                    p, i, c0 = s_["p"], s_["i"], s_["c0"]
                    z, B_z = zp.next()
                    zmm(s_, z, B_z)
                    L, B_L = Lp.next()
                    s_["L"], s_["B_L"] = L, B_L
                    sc.op("act", lambda e, L=L, z=z, c0=c0: e.activation(
                        out=L[:, :, c0:512], in_=z[:, :, c0:512], func=AF.Softplus), reads=[B_z], writes=[B_L])
                    if not s_["last"]:
                        if s_["first"]:
                            sc.op("pool", lambda e: e.memset(R32[:], 0.0), writes=[B_R32])
                        sc.op("dve", lambda e, L=L, c0=c0: e.tensor_tensor(
                            out=R32[:, :, c0:512], in0=R32[:, :, c0:512], in1=L[:, :, c0:512], op=ALU.add),
                            reads=[B_L, B_R32], writes=[B_R32])
                        Rb, B_Rb = Rbp.next()
                        sc.op("dve", lambda e, Rb=Rb: e.tensor_copy(out=Rb[:], in_=R32[:]), reads=[B_R32], writes=[B_Rb])
                        s_["Rb"] = (Rb, B_Rb)

                def Ystep(s_):
                    c0 = s_["c0"]
                    L, B_L = s_["L"], s_["B_L"]
                    z, B_z = zp.next()
                    zmm(s_, z, B_z)
                    for hh in range(2):
                        sc.op("pe", lambda e, z=z, hh=hh, c0=c0, L=L: e.matmul(
                            z[:, hh, c0:512], lhsT=ntri, rhs=L[:, hh, c0:512], start=False, stop=False,
                            skip_group_check=True), reads=[B_L, B_cst], writes=[B_z])
                    if not s_["first"]:
                        Rb, B_Rb = s_["prev"]["Rb"]
                        for hh in range(2):
                            sc.op("pe", lambda e, z=z, hh=hh, c0=c0, Rb=Rb: e.matmul(
                                z[:, hh, c0:512], lhsT=nones, rhs=Rb[:, hh, c0:512], start=False, stop=False,
                                skip_group_check=True), reads=[B_Rb, B_cst], writes=[B_z])
                    A, B_A = Ap.next()
                    s_["A"] = (A, B_A)
                    sc.op("act", lambda e, A=A, z=z, c0=c0: e.activation(out=A[:, :, c0:512], in_=z[:, :, c0:512], func=AF.Exp),
                          reads=[B_z], writes=[B_A])

                def AVstep(s_):
                    p, i, kj, c0 = s_["p"], s_["i"], s_["kj"], s_["c0"]
                    kt, B_kt, qt, B_qt, vv, B_vv = pair_in[p]
                    A, B_A = s_["A"]
                    if s_["first"]:
                        state["o"] = op_.next()
                    o, B_o = state["o"]
                    for hh in range(2):
                        sc.op("pe", lambda e, o=o, hh=hh, c0=c0, A=A, vv=vv, kj=kj, first=s_["first"]: e.matmul(
                            o[hh * 64:(hh + 1) * 64, c0:512], lhsT=vv[:, kj, hh * 64:(hh + 1) * 64],
                            rhs=A[:, hh, c0:512], start=first, stop=False, skip_group_check=True),
                            reads=[B_A, B_vv], writes=[B_o])
                    if s_["last"]:
                        ob, B_ob = oev.next()
                        sc.op("dve", lambda e, ob=ob, o=o: e.tensor_copy(out=ob[:], in_=o[:]), reads=[B_o], writes=[B_ob])
                        sc.dma("sp", [(otb_d[p * 128:(p + 1) * 128, i * 512:(i + 1) * 512], ob[:])], reads=[B_ob],
                               writes=[B_otb[i]], key=("oev", id(B_ob)))
                        if i == NT - 1 and p + 2 < 4 and nsets == 2:
                            load_pair(p + 2)

                batches = [steps[a:a + K] for a in range(0, len(steps), K)]
                nbt = len(batches)
                for b in range(-1, nbt + 1):
                    xs = batches[b + 1] if 0 <= b + 1 < nbt else []
                    avs = batches[b - 1] if 0 <= b - 1 < nbt else []
                    for idx in range(max(len(xs), len(avs))):
                        if idx < len(xs):
                            Xstep(xs[idx])
                        if idx < len(avs):
                            AVstep(avs[idx])
                    if 0 <= b < nbt:
                        for s_ in batches[b]:
                            Ystep(s_)
            sc.barrier()

        def phase_mix():
            with ExitStack() as pes:
                wg = sbt(pes, "wg", [128, 8, 2048], BF16)
                wba = sbt(pes, "wba", [128, 4, D], BF16)
                wbb = sbt(pes, "wbb", [128, 4, D], BF16)
                wo = sbt(pes, "wo", [128, 8, D], BF16)
                B_wg = [Buf("wg%d" % b) for b in range(4)]
                B_wba = [Buf("wba%d" % k) for k in range(4)]
                B_wbb = [Buf("wbb%d" % k) for k in range(4)]
                B_wo = [Buf("wo%d" % k) for k in range(8)]
                nt = NormT(pes, nht=2, nxin=4)
                nxt = nt.run(x1_d, 0, B_x1[0])
                stgp = Pool(nc, pes, "stg", 4, [128, STGW], F32)
                load_cast_cols(stgp, wg, B_wg, win_d[:, C_GA:INW], 8, 2048, gain_i=1)
                for k in range(4):
                    load_cast(stgp, wba, lambda a, b, k=k: wba[:, k, a:b], B_wba[k], wba_d[k * 128:(k + 1) * 128, :], D)
                    load_cast(stgp, wbb, lambda a, b, k=k: wbb[:, k, a:b], B_wbb[k], wbb_d[k * 128:(k + 1) * 128, :], D)
                for k in range(8):
                    load_cast(stgp, wo, lambda a, b, k=k: wo[:, k, a:b], B_wo[k], wout_d[k * 128:(k + 1) * 128, :], D)
                ps = Pool(nc, pes, "ps", 6, [128, 512], F32, psum=True)
                otap = Pool(nc, pes, "ota", 2, [128, 4, TT], BF16)
                otbp = Pool(nc, pes, "otb", 2, [128, 4, TT], BF16)
                MT = sbt(pes, "MT", [128, 8, TT], BF16)
                B_MT = [Buf("MT%d" % c) for c in range(8)]
                sgp = Pool(nc, pes, "sgm", 4, [128, TT], F32)
                mp = Pool(nc, pes, "mm", 4, [128, TT], F32)
                xres = Pool(nc, pes, "xres", 4, [128, D], F32)
                for ti in range(NT):
                    ht, B_ht = nxt
                    pcs = []
                    if ti + 1 < NT:
                        J = nt.begin(x1_d, ti + 1, B_x1[ti + 1])
                        nxt = (J["ht"], J["B_ht"])
                        pcs = nt.pieces(J)
                    oa, B_oa = otap.next()
                    ob, B_ob = otbp.next()
                    sc.dma("sp", [(oa[:], ota_d.rearrange("(c p) s -> p c s", p=128)[:, :, ti * TT:(ti + 1) * TT])],
                           reads=[B_ota[ti]], writes=[B_oa], key=("ota", id(B_oa)))
                    sc.dma("sp", [(ob[:], otb_d.rearrange("(c p) s -> p c s", p=128)[:, :, ti * TT:(ti + 1) * TT])],
                           reads=[B_otb[ti]], writes=[B_ob], key=("otb", id(B_ob)))
                    xrs = []
                    for j in range(4):
                        xr, B_xr = xres.next()
                        r0 = ti * TT + j * 128
                        sc.dma("sp", [(xr[:], x1_d[r0:r0 + 128, :])], reads=[B_x1[ti]], writes=[B_xr], key=("xres", id(B_xr)))
                        xrs.append((xr, B_xr))
                    for cc in range(8):
                        ms = []
                        for br, (wbr, B_wbr, ot, B_ot, goff) in enumerate(((wba, B_wba, oa, B_oa, 0), (wbb, B_wbb, ob, B_ob, 1024))):
                            pg, B_pg = ps.next()
                            pb, B_pb = ps.next()
                            for k in range(8):
                                sc.op("pe", lambda e, pg=pg, k=k, cc=cc, goff=goff, ht=ht: e.matmul(
                                    pg[:], lhsT=wg[:, k, goff + cc * 128:goff + (cc + 1) * 128], rhs=ht[:, k, :],
                                    start=(k == 0), stop=(k == 7)), reads=wtoks(B_wg, goff + cc * 128, 128) + B_ht, writes=[B_pg])
                            for k in range(4):
                                sc.op("pe", lambda e, pb=pb, k=k, cc=cc, wbr=wbr, ot=ot: e.matmul(
                                    pb[:], lhsT=wbr[:, k, cc * 128:(cc + 1) * 128], rhs=ot[:, k, :],
                                    start=(k == 0), stop=(k == 3)), reads=[B_wbr[k], B_ot], writes=[B_pb])
                            sg, B_sg = sgp.next()
                            sc.op("act", lambda e, sg=sg, pg=pg: e.activation(out=sg[:], in_=pg[:], func=AF.Sigmoid),
                                  reads=[B_pg], writes=[B_sg])
                            m, B_m = mp.next()
                            sc.op("dve", lambda e, m=m, sg=sg, pb=pb: e.tensor_tensor(out=m[:], in0=sg[:], in1=pb[:], op=ALU.mult),
                                  reads=[B_sg, B_pb], writes=[B_m])
                            ms.append((m, B_m))
                            if pcs:
                                pcs.pop(0)()
                        sc.op("pool", lambda e, cc=cc, ms=ms: e.tensor_tensor(
                            out=MT[:, cc, :], in0=ms[0][0][:], in1=ms[1][0][:], op=ALU.add),
                            reads=[ms[0][1], ms[1][1]], writes=[B_MT[cc]])
                    while pcs:
                        pcs.pop(0)()
                    for j in range(4):
                        xr, B_xr = xrs[j]
                        r0 = ti * TT + j * 128
                        for half in range(2):
                            po, B_po = ps.next()
                            for cc in range(8):
                                sc.op("pe", lambda e, po=po, cc=cc, j=j, half=half: e.matmul(
                                    po[:], lhsT=MT[:, cc, j * 128:(j + 1) * 128], rhs=wo[:, cc, half * 512:(half + 1) * 512],
                                    start=(cc == 0), stop=(cc == 7)), reads=[B_MT[cc], B_wo[cc]], writes=[B_po])
                            sc.op("dve", lambda e, po=po, xr=xr, half=half: e.tensor_tensor(
                                out=xr[:, half * 512:(half + 1) * 512], in0=po[:], in1=xr[:, half * 512:(half + 1) * 512], op=ALU.add),
                                reads=[B_po, B_xr], writes=[B_xr])
                        sc.dma("sp", [(x2_d[r0:r0 + 128, :], xr[:])], reads=[B_xr], writes=[B_x2[ti]], key=("xres_st", id(B_xr)))
            sc.barrier()

        sc.barrier()
        if "ffn1" in phases:
            phase_ffn(0, x_d, None, x1_d, B_x1, final=False)
        if "proj" in phases:
            phase_proj()
        if "swa" in phases:
            phase_swa()
        if "sb" in phases:
            phase_sb2()
        if "sb_old" in phases:
            phase_sb()
        if "mix" in phases:
            phase_mix()
        if "ffn2" in phases:
            phase_ffn(1, x2_d, B_x2, out_d, None, final=True)
        sc.emit()
    return nc


def _rel_bucket_np(dist):
    max_exact = 16
    d = np.maximum(dist, 1).astype(np.float32)
    large = max_exact + (np.log(d / max_exact) / np.float32(np.log(128 / max_exact)) * (32 - max_exact)).astype(np.int32)
    large = np.minimum(large, 31)
    return np.where(dist < max_exact, dist, large)


def _bucket_table():
    import jax
    import jax.numpy as jnp
    qi = np.arange(128)[:, None] + 128
    kj = np.arange(256)[None, :]
    dist = qi - kj
    band = (dist >= 0) & (dist < 128)
    with jax.default_device(jax.devices("cpu")[0]):
        dj = jnp.maximum(jnp.asarray(dist), 0)
        max_exact = 16
        d = jnp.maximum(dj, 1).astype(jnp.float32)
        large = max_exact + (jnp.log(d / max_exact) / np.log(128 / max_exact) * (32 - max_exact)).astype(jnp.int32)
        large = jnp.minimum(large, 31)
        bucket = np.asarray(jnp.where(dj < max_exact, dj, large))
    return bucket, band


def _consts():
    c = np.zeros((128, 512), np.float32)
    c[:, 0:128] = np.eye(128)
    j = np.arange(128)[:, None]
    s = np.arange(128)[None, :]
    c[:, 128:256] = np.where(j >= s, -1.0, 0.0)
    c[:, 256:384] = -1.0
    c[:, 384:512] = np.where(j < s, 0.0, MASKB)
    return c.astype(ml_dtypes.bfloat16)


_PROG_CACHE = {}


def _prepare_shared(inp, S):
    f = lambda a: np.ascontiguousarray(np.asarray(a, dtype=np.float32))
    bucket, band = _bucket_table()
    rb = f(inp["rel_bias"])
    bias = rb[bucket]
    order = [g + 4 * kv for g in range(4) for kv in range(2)]
    bias = np.ascontiguousarray(bias.transpose(0, 2, 1)[:, order, :])
    mask = np.where(band, 0.0, NEG).astype(np.float32)
    w_in = f(inp["w_in"])[0]
    qcols = []
    for g in range(4):
        for kv in range(2):
            h = g + 4 * kv
            qcols.extend(range(h * 64, (h + 1) * 64))
    w_in = np.ascontiguousarray(np.concatenate([w_in[:, qcols], w_in[:, 512:]], axis=1))
    sinks = f(inp["swa_sinks"])[0].reshape(2, 4)
    shared = {
        "ffn1_w1": f(inp["ffn1_w1"])[0], "ffn1_w3": f(inp["ffn1_w3"])[0], "ffn1_w2": f(inp["ffn1_w2"])[0],
        "ffn2_w1": f(inp["ffn2_w1"])[0], "ffn2_w3": f(inp["ffn2_w3"])[0], "ffn2_w2": f(inp["ffn2_w2"])[0],
        "gains": np.ascontiguousarray(np.stack([f(inp["norm_ffn1"])[0].reshape(8, 128).T,
                                                f(inp["norm_mix"])[0].reshape(8, 128).T,
                                                f(inp["norm_ffn2"])[0].reshape(8, 128).T], axis=1)),
        "norm_final": f(inp["norm_final"]),
        "w_in": w_in, "swa_sinks": np.ascontiguousarray(sinks), "swa_bias": bias, "swa_mask": mask,
        "w_branch_swa": f(inp["w_branch_swa"])[0], "w_branch_sb": f(inp["w_branch_sb"])[0],
        "w_out": f(inp["w_out"])[0], "consts": _consts(),
    }
    return shared


def kernel(**inputs):
    x = np.asarray(inputs["x"], dtype=np.float32)
    B, S, _ = x.shape
    if S not in _PROG_CACHE:
        _PROG_CACHE[S] = build_program(S)
    nc = _PROG_CACHE[S]
    shared = _prepare_shared(inputs, S)
    in_maps = []
    for b in range(B):
        m = dict(shared)
        m["x"] = np.ascontiguousarray(x[b])
        in_maps.append(m)
    res = run_bass_kernel_spmd(nc, in_maps, core_ids=list(range(B)))
    return np.stack([np.asarray(r["out"], dtype=np.float32) for r in res.results], axis=0)
```

```python
from contextlib import ExitStack

import numpy as np
import ml_dtypes

import concourse.bass as bass
import concourse.mybir as mybir
from concourse.bass_utils import run_bass_kernel_spmd

F32 = mybir.dt.float32
BF16 = mybir.dt.bfloat16
AF = mybir.ActivationFunctionType
ALU = mybir.AluOpType
AX = mybir.AxisListType

D = 1024
DFF = 2816
NFF = DFF // 128
INW = 4352
EPS = 1e-6
NEG = -1e30
TT = 512
MASKB = -30000.0
STGW = 320

C_QA, C_KA, C_VA, C_QB, C_KB, C_VB, C_GA, C_GB = 0, 512, 640, 768, 1280, 1792, 2304, 3328


class Buf:
    __slots__ = ("name", "lw", "rd", "dmard")

    def __init__(self, name):
        self.name = name
        self.lw = None
        self.rd = {}
        self.dmard = []


class Op:
    __slots__ = ("eng", "fn", "deps", "signal", "sem", "count", "is_dma", "pairs")

    def __init__(self, eng, fn, is_dma=False):
        self.eng = eng
        self.fn = fn
        self.deps = []
        self.signal = False
        self.sem = None
        self.count = 0
        self.is_dma = is_dma
        self.pairs = None


SEM_LIMIT = 30000


class Sched:
    ENGS = ("pe", "act", "dve", "pool", "sp")

    def __init__(self, nc, es):
        self.nc = nc
        self.es = es
        self.ops = {e: [] for e in self.ENGS}
        self.dma_sems = {}
        self.barrier_deps = {e: None for e in self.ENGS}

    def _newsem(self, name):
        return self.es.enter_context(self.nc.semaphore(name))

    def _add(self, op, reads, writes):
        deps = []
        bd = self.barrier_deps[op.eng]
        if bd is not None:
            deps.extend(bd)
            self.barrier_deps[op.eng] = None
        for b in reads:
            w = b.lw
            if w is not None:
                if not (w.eng == op.eng == "pe" and not w.is_dma and not op.is_dma):
                    deps.append(w)
        for b in writes:
            w = b.lw
            if w is not None:
                if w.is_dma or op.is_dma or w.eng != op.eng:
                    deps.append(w)
            for e, r in b.rd.items():
                if op.is_dma or e != op.eng:
                    deps.append(r)
            deps.extend(b.dmard)
        for b in reads:
            if op.is_dma:
                b.dmard.append(op)
            else:
                b.rd[op.eng] = op
        for b in writes:
            b.lw = op
            b.rd = {}
            b.dmard = []
        seen = set()
        for d in deps:
            if id(d) not in seen and d is not op:
                seen.add(id(d))
                op.deps.append(d)
                d.signal = True
        self.ops[op.eng].append(op)
        return op

    def op(self, eng, fn, reads=(), writes=()):
        return self._add(Op(eng, fn), reads, writes)

    def dma(self, queue, pairs, reads=(), writes=(), key=None):
        op = Op(queue, None, is_dma=True)
        op.pairs = pairs
        op.signal = True
        if key not in self.dma_sems:
            self.dma_sems[key] = [self._newsem("d%d" % len(self.dma_sems)), 0, None]
        ent = self.dma_sems[key]
        ent[1] += 16 * len(pairs)
        ent[2] = op
        op.sem = ent[0]
        op.count = ent[1]
        return self._add(op, reads, writes)

    def barrier(self):
        prev = []
        for e in self.ENGS:
            for op in reversed(self.ops[e]):
                if not op.is_dma:
                    prev.append(op)
                    break
        for ent in self.dma_sems.values():
            if ent[2] is not None:
                prev.append(ent[2])
        for e in self.ENGS:
            self.barrier_deps[e] = list(prev)

    def emit(self):
        nc = self.nc
        eng_sems = {}
        for e in self.ENGS:
            n = 0
            for op in self.ops[e]:
                if op.is_dma or not op.signal:
                    continue
                k = n // SEM_LIMIT
                if (e, k) not in eng_sems:
                    eng_sems[(e, k)] = self._newsem("e_%s%d" % (e, k))
                op.sem = eng_sems[(e, k)]
                op.count = n % SEM_LIMIT + 1
                n += 1
        final_waits = [(ent[0], ent[1]) for ent in self.dma_sems.values()]

        def run(e, eng, final=False):
            waited = {}
            for op in self.ops[e]:
                for d in op.deps:
                    k = id(d.sem)
                    if waited.get(k, 0) < d.count:
                        eng.wait_ge(d.sem, d.count)
                        waited[k] = d.count
                if op.is_dma:
                    for (o, i) in op.pairs:
                        eng.dma_start(out=o, in_=i).then_inc(op.sem, 16)
                else:
                    ins = op.fn(eng)
                    if op.signal:
                        ins.then_inc(op.sem, 1)
            if final:
                for (h, c) in final_waits:
                    if waited.get(id(h), 0) < c:
                        eng.wait_ge(h, c)

        with nc.Block() as block:
            @block.sync
            def _(sync):
                run("sp", sync, final=True)

            @block.tensor
            def _(tensor):
                run("pe", tensor)

            @block.scalar
            def _(scalar):
                run("act", scalar)

            @block.vector
            def _(vector):
                run("dve", vector)

            @block.gpsimd
            def _(gpsimd):
                run("pool", gpsimd)


class Pool:
    uid = [0]

    def __init__(self, nc, es, name, n, shape, dtype, psum=False):
        self.tiles = []
        for i in range(n):
            Pool.uid[0] += 1
            nm = "%s%d_%d" % (name, i, Pool.uid[0])
            if psum:
                t = es.enter_context(nc.psum_tensor(nm, list(shape), dtype))
            else:
                t = es.enter_context(nc.sbuf_tensor(nm, list(shape), dtype))
            self.tiles.append((t, Buf(nm)))
        self.i = 0

    def next(self):
        t = self.tiles[self.i % len(self.tiles)]
        self.i += 1
        return t


def build_program(S, phases=("ffn1", "proj", "swa", "sb", "mix", "ffn2"), debug=False):
    NT = S // TT
    NB = S // 128
    nc = bass.Bass("TRN2", target_bir_lowering=False)
    dk = "ExternalOutput" if debug else "Internal"

    def din(name, shape, dt=F32):
        return nc.dram_tensor(name, list(shape), dt, kind="ExternalInput").ap()

    def dscr(name, shape, dt):
        return nc.dram_tensor(name, list(shape), dt, kind=dk).ap()

    x_d = din("x", [S, D])
    w1_d = [din("ffn1_w1", [D, DFF]), din("ffn2_w1", [D, DFF])]
    w3_d = [din("ffn1_w3", [D, DFF]), din("ffn2_w3", [D, DFF])]
    w2_d = [din("ffn1_w2", [DFF, D]), din("ffn2_w2", [DFF, D])]
    gains_d = din("gains", [128, 3, 8])
    gfin_d = din("norm_final", [D])
    win_d = din("w_in", [D, INW])
    sinks2_d = din("swa_sinks", [2, 4])
    bias_d = din("swa_bias", [128, 8, 256])
    maskc_d = din("swa_mask", [128, 256])
    wba_d = din("w_branch_swa", [512, D])
    wbb_d = din("w_branch_sb", [512, D])
    wout_d = din("w_out", [D, D])
    cst_d = din("consts", [128, 4 * 128], BF16)
    out_d = nc.dram_tensor("out", [S, D], F32, kind="ExternalOutput").ap()

    x1_d = dscr("x1", [S, D], F32)
    x2_d = dscr("x2", [S, D], F32)
    qta_d = dscr("qta", [4, 128, S], BF16)
    kta_d = dscr("kta", [128, S], BF16)
    va_d = dscr("va", [S, 128], BF16)
    qtb_d = dscr("qtb", [512, S], BF16)
    ktb_d = dscr("ktb", [512, S], BF16)
    vb_d = dscr("vb", [S, 512], BF16)
    ota_d = dscr("ota", [512, S], BF16)
    otb_d = dscr("otb", [512, S], BF16)

    B_x1 = [Buf("x1_%d" % i) for i in range(NT)]
    B_x2 = [Buf("x2_%d" % i) for i in range(NT)]
    B_qkv = [Buf("qkv_%d" % i) for i in range(NT)]
    B_ota = [Buf("ota_%d" % i) for i in range(NT)]
    B_otb = [Buf("otb_%d" % i) for i in range(NT)]

    with ExitStack() as es:
        sc = Sched(nc, es)

        def sbt(stack, name, shape, dt):
            Pool.uid[0] += 1
            return stack.enter_context(nc.sbuf_tensor("%s_%d" % (name, Pool.uid[0]), list(shape), dt))

        cst = sbt(es, "cst", [128, 512], BF16)
        B_cst = Buf("cst")
        sc.dma("sp", [(cst[:], cst_d)], writes=[B_cst], key="cst")
        ident = cst[:, 0:128]
        ntri = cst[:, 128:256]
        nones = cst[:, 256:384]
        dmask = cst[:, 384:512]

        gcol = sbt(es, "gcol", [128, 3, 8], F32)
        B_gcol = Buf("gcol")

        sc.dma("sp", [(gcol[:], gains_d)], writes=[B_gcol], key="gcol")

        def load_cast(stgp, dst_t, sel, B_dst, src_rows, ncols, gain=None):
            c0 = 0
            while c0 < ncols:
                c1 = min(ncols, c0 + STGW)
                st, B_st = stgp.next()
                w = c1 - c0
                sc.dma("sp", [(st[:, 0:w], src_rows[:, c0:c1])], writes=[B_st], key=("stg", id(B_st)))
                o = sel(c0, c1)
                if gain is None:
                    sc.op("dve", lambda e, o=o, i=st[:, 0:w]: e.tensor_copy(out=o, in_=i),
                          reads=[B_st], writes=[B_dst])
                else:
                    sc.op("dve", lambda e, o=o, i=st[:, 0:w], g=gain:
                          e.tensor_scalar(out=o, in0=i, scalar1=g, scalar2=None, op0=ALU.mult),
                          reads=[B_st, B_gcol], writes=[B_dst])
                c0 = c1

        def rstd_ops(st, B_st):
            sc.op("dve", lambda e, st=st: e.tensor_scalar(
                out=st[:, 1:2], in0=st[:, 0:1], scalar1=float(D * EPS), scalar2=None, op0=ALU.add),
                reads=[B_st], writes=[B_st])
            sc.op("act", lambda e, st=st: e.activation(out=st[:, 1:2], in_=st[:, 1:2], func=AF.Sqrt),
                  reads=[B_st], writes=[B_st])
            sc.op("dve", lambda e, st=st: e.reciprocal(out=st[:, 1:2], in_=st[:, 1:2]),
                  reads=[B_st], writes=[B_st])

        class NormT:
            def __init__(self, stack, nht=1, nxin=2):
                self.nxin = nxin
                self.xin = Pool(nc, stack, "xin", nxin, [128, D], F32)
                self.hrow = Pool(nc, stack, "hrow", 2, [128, D], BF16)
                self.sqj = sbt(stack, "sqj", [128, D], BF16)
                self.B_sqj = Buf("sqj")
                self.stat = Pool(nc, stack, "stat", 8, [128, 2], F32)
                self.hts = [(sbt(stack, "ht%d" % i, [128, 8, TT], BF16), [Buf("ht%d_%d" % (i, j)) for j in range(4)])
                            for i in range(nht)]
                self.hi = 0
                self.tpp = Pool(nc, stack, "tpp", 2, [128, 8, 128], BF16, psum=True)
                self.ev = 0

            def _load(self, J, j):
                xt, B_xt = self.xin.next()
                r0 = J["ti"] * TT + j * 128
                sc.dma("sp", [(xt[:], J["src"][r0:r0 + 128, :])], reads=[J["B_src"]] if J["B_src"] else [],
                       writes=[B_xt], key=("xin", id(B_xt)))
                J["x"][j] = (xt, B_xt)

            def begin(self, src_d, ti, B_src):
                ht_, B_ht_ = self.hts[self.hi % len(self.hts)]
                self.hi += 1
                J = dict(src=src_d, ti=ti, B_src=B_src, ht=ht_, B_ht=B_ht_, x={}, st={})
                if self.nxin >= 4:
                    for j in range(4):
                        self._load(J, j)
                return J

            def piece_a(self, J, j):
                if j not in J["x"]:
                    self._load(J, j)
                xt, B_xt = J["x"][j]
                st, B_st = self.stat.next()
                J["st"][j] = (st, B_st)
                sc.op("act", lambda e, xt=xt, st=st: e.activation(
                    out=self.sqj[:], in_=xt[:], func=AF.Square, accum_out=st[:, 0:1]),
                    reads=[B_xt], writes=[B_st, self.B_sqj])
                sc.op("dve", lambda e, st=st: e.tensor_scalar(
                    out=st[:, 1:2], in0=st[:, 0:1], scalar1=float(D * EPS), scalar2=None, op0=ALU.add),
                    reads=[B_st], writes=[B_st])

            def piece_b(self, J, j):
                st, B_st = J["st"][j]
                sc.op("act", lambda e, st=st: e.activation(out=st[:, 1:2], in_=st[:, 1:2], func=AF.Sqrt),
                      reads=[B_st], writes=[B_st])
                sc.op("dve", lambda e, st=st: e.reciprocal(out=st[:, 1:2], in_=st[:, 1:2]),
                      reads=[B_st], writes=[B_st])

            def piece_c(self, J, j):
                xt, B_xt = J["x"][j]
                st, B_st = J["st"][j]
                ht_, B_ht_ = J["ht"], J["B_ht"]
                hr, B_hr = self.hrow.next()
                sc.op("dve", lambda e, hr=hr, xt=xt, st=st: e.tensor_scalar(
                    out=hr[:], in0=xt[:], scalar1=st[:, 1:2], scalar2=32.0, op0=ALU.mult, op1=ALU.mult),
                    reads=[B_xt, B_st], writes=[B_hr])
                tp, B_tp = self.tpp.next()
                for k in range(8):
                    sc.op("pe", lambda e, tp=tp, hr=hr, k=k: e.transpose(
                        out=tp[:, k, :], in_=hr[:, k * 128:(k + 1) * 128], identity=ident),
                        reads=[B_hr, B_cst], writes=[B_tp])
                eng = ("dve", "act")[self.ev % 2]
                self.ev += 1
                if eng == "dve":
                    sc.op("dve", lambda e, tp=tp, j=j, ht_=ht_: e.tensor_copy(
                        out=ht_[:, :, j * 128:(j + 1) * 128], in_=tp[:]),
                        reads=[B_tp], writes=[B_ht_[j]])
                else:
                    sc.op("act", lambda e, tp=tp, j=j, ht_=ht_: e.copy(
                        out=ht_[:, :, j * 128:(j + 1) * 128], in_=tp[:]),
                        reads=[B_tp], writes=[B_ht_[j]])

            def pieces(self, J):
                order = [("a", 0), ("a", 1), ("b", 0), ("a", 2), ("b", 1), ("c", 0), ("a", 3), ("b", 2),
                         ("c", 1), ("b", 3), ("c", 2), ("c", 3)]
                fns = {"a": self.piece_a, "b": self.piece_b, "c": self.piece_c}
                return [(lambda f=fns[k], j=j: f(J, j)) for (k, j) in order]

            def run(self, src_d, ti, B_src):
                J = self.begin(src_d, ti, B_src)
                for j in range(4):
                    self.piece_a(J, j)
                    self.piece_b(J, j)
                    self.piece_c(J, j)
                return J["ht"], J["B_ht"]

        def wtoks(toks, col, w, blk=640):
            return [toks[b] for b in range(col // blk, (col + w - 1) // blk + 1)]

        def load_cast_cols(stgp, dst_t, toks, src_d, nrow_chunks, ncols, gain_i=None, blk=640):
            for b in range((ncols + blk - 1) // blk):
                c0, c1 = b * blk, min(ncols, (b + 1) * blk)
                for k in range(nrow_chunks):
                    load_cast(stgp, dst_t, lambda a, bb, k=k, c0=c0: dst_t[:, k, c0 + a:c0 + bb], toks[b],
                              src_d[k * 128:(k + 1) * 128, c0:c1], c1 - c0,
                              gain=None if gain_i is None else gcol[:, gain_i, k:k + 1])

        def phase_ffn(layer, src_d, B_srcs, dst_d, B_dsts, final):
            with ExitStack() as pes:
                w1b = sbt(pes, "w1b", [128, 8, DFF], BF16)
                w3b = sbt(pes, "w3b", [128, 8, DFF], BF16)
                w2b = sbt(pes, "w2b", [128, NFF, D], BF16)
                NBLK = (DFF + 639) // 640
                B_w1 = [Buf("w1_%d" % b) for b in range(NBLK)]
                B_w3 = [Buf("w3_%d" % b) for b in range(NBLK)]
                B_w2 = [Buf("w2_%d" % c) for c in range(NFF)]
                nt = NormT(pes)
                nxt = nt.run(src_d, 0, B_srcs[0] if B_srcs else None)
                stgp = Pool(nc, pes, "stg", 8, [128, STGW], F32)
                gi = 0 if layer == 0 else 2
                for b in range(NBLK):
                    c0, c1 = b * 640, min(DFF, (b + 1) * 640)
                    for (wb, wd, toks) in ((w1b, w1_d[layer], B_w1), (w3b, w3_d[layer], B_w3)):
                        for k in range(8):
                            load_cast(stgp, wb, lambda a, bb, k=k, c0=c0, wb=wb: wb[:, k, c0 + a:c0 + bb], toks[b],
                                      wd[k * 128:(k + 1) * 128, c0:c1], c1 - c0, gain=gcol[:, gi, k:k + 1])
                for c in range(NFF):
                    load_cast(stgp, w2b, lambda a, b, c=c: w2b[:, c, a:b], B_w2[c],
                              w2_d[layer][c * 128:(c + 1) * 128, :], D)
                G = sbt(pes, "G", [128, NFF, TT], BF16)
                B_G = [Buf("G%d" % c) for c in range(NFF)]
                sgp = Pool(nc, pes, "sg", 2, [128, TT], F32)
                xres = Pool(nc, pes, "xres", 2, [128, D], F32)
                ps_up = Pool(nc, pes, "psu", 4, [128, 512], F32, psum=True)
                ps_dn = Pool(nc, pes, "psd", 2, [128, 512], F32, psum=True)
                if final:
                    gfb = sbt(pes, "gfb", [128, D], F32)
                    B_gfb = Buf("gfb")
                    sc.dma("sp", [(gfb[:], gfin_d.partition_broadcast(128))], writes=[B_gfb], key="gfb")
                    sc.op("dve", lambda e: e.tensor_scalar(out=gfb[:], in0=gfb[:], scalar1=32.0, scalar2=None,
                                                           op0=ALU.mult), reads=[B_gfb], writes=[B_gfb])
                    fstat = Pool(nc, pes, "fstat", 4, [128, 2], F32)

                for ti in range(NT):
                    ht, B_ht = nxt
                    for c in range(NFF):
                        pu, B_pu = ps_up.next()
                        pv, B_pv = ps_up.next()
                        for k in range(8):
                            sc.op("pe", lambda e, pu=pu, k=k, c=c, ht=ht: e.matmul(
                                pu[:], lhsT=w1b[:, k, c * 128:(c + 1) * 128], rhs=ht[:, k, :],
                                start=(k == 0), stop=(k == 7)), reads=wtoks(B_w1, c * 128, 128) + B_ht, writes=[B_pu])
                        for k in range(8):
                            sc.op("pe", lambda e, pv=pv, k=k, c=c, ht=ht: e.matmul(
                                pv[:], lhsT=w3b[:, k, c * 128:(c + 1) * 128], rhs=ht[:, k, :],
                                start=(k == 0), stop=(k == 7)), reads=wtoks(B_w3, c * 128, 128) + B_ht, writes=[B_pv])
                        sg, B_sg = sgp.next()
                        sc.op("act", lambda e, sg=sg, pu=pu: e.activation(out=sg[:], in_=pu[:], func=AF.Silu),
                              reads=[B_pu], writes=[B_sg])
                        sc.op("dve", lambda e, sg=sg, pv=pv, c=c: e.tensor_tensor(
                            out=G[:, c, :], in0=sg[:], in1=pv[:], op=ALU.mult),
                            reads=[B_sg, B_pv], writes=[B_G[c]])
                    if ti + 1 < NT:
                        nxt = nt.run(src_d, ti + 1, B_srcs[ti + 1] if B_srcs else None)
                    for j in range(4):
                        xr, B_xr = xres.next()
                        r0 = ti * TT + j * 128
                        sc.dma("sp", [(xr[:], src_d[r0:r0 + 128, :])], reads=[B_srcs[ti]] if B_srcs else [],
                               writes=[B_xr], key=("xres", id(B_xr)))
                        for half in range(2):
                            pd, B_pd = ps_dn.next()
                            for c in range(NFF):
                                sc.op("pe", lambda e, pd=pd, c=c, j=j, half=half: e.matmul(
                                    pd[:], lhsT=G[:, c, j * 128:(j + 1) * 128],
                                    rhs=w2b[:, c, half * 512:(half + 1) * 512],
                                    start=(c == 0), stop=(c == NFF - 1)),
                                    reads=[B_G[c], B_w2[c]], writes=[B_pd])
                            sc.op("dve", lambda e, pd=pd, xr=xr, half=half: e.scalar_tensor_tensor(
                                out=xr[:, half * 512:(half + 1) * 512], in0=pd[:], scalar=0.5,
                                in1=xr[:, half * 512:(half + 1) * 512], op0=ALU.mult, op1=ALU.add),
                                reads=[B_pd, B_xr], writes=[B_xr])
                        if final:
                            fs, B_fs = fstat.next()
                            sc.op("act", lambda e, xr=xr, fs=fs: e.activation(
                                out=nt.sqj[:], in_=xr[:], func=AF.Square, accum_out=fs[:, 0:1]),
                                reads=[B_xr], writes=[B_fs, nt.B_sqj])
                            rstd_ops(fs, B_fs)
                            sc.op("dve", lambda e, xr=xr, fs=fs: e.scalar_tensor_tensor(
                                out=xr[:], in0=xr[:], scalar=fs[:, 1:2], in1=gfb[:], op0=ALU.mult, op1=ALU.mult),
                                reads=[B_xr, B_fs, B_gfb], writes=[B_xr])
                        sc.dma("sp", [(dst_d[r0:r0 + 128, :], xr[:])], reads=[B_xr],
                               writes=[B_dsts[ti]] if B_dsts else [], key=("xres_st", id(B_xr)))
            sc.barrier()

        def phase_proj():
            NQ = C_GA
            with ExitStack() as pes:
                wq = sbt(pes, "wq", [128, 8, NQ], BF16)
                B_wq = [Buf("wq%d" % b) for b in range((NQ + 639) // 640)]
                nt = NormT(pes, nht=2, nxin=4)
                nxt = nt.run(x1_d, 0, B_x1[0])
                stgp = Pool(nc, pes, "stg", 8, [128, STGW], F32)
                load_cast_cols(stgp, wq, B_wq, win_d, 8, NQ, gain_i=1)
                ps = Pool(nc, pes, "ps", 4, [128, 512], F32, psum=True)
                evp = Pool(nc, pes, "ev", 6, [128, 512], BF16)
                fm = []
                for g in range(4):
                    fm.append((C_QA + g * 128, lambda ti, g=g: qta_d[g, :, ti * TT:(ti + 1) * TT], 0.125))
                fm.append((C_KA, lambda ti: kta_d[:, ti * TT:(ti + 1) * TT], 1.0))
                for cc in range(4):
                    fm.append((C_QB + cc * 128, lambda ti, cc=cc: qtb_d[cc * 128:(cc + 1) * 128, ti * TT:(ti + 1) * TT], 0.125))
                for cc in range(4):
                    fm.append((C_KB + cc * 128, lambda ti, cc=cc: ktb_d[cc * 128:(cc + 1) * 128, ti * TT:(ti + 1) * TT], 1.0))
                evi = 0
                for ti in range(NT):
                    ht, B_ht = nxt
                    pcs = []
                    if ti + 1 < NT:
                        J = nt.begin(x1_d, ti + 1, B_x1[ti + 1])
                        nxt = (J["ht"], J["B_ht"])
                        pcs = nt.pieces(J)
                    for (col, dst, scale) in fm:
                        pp, B_pp = ps.next()
                        for k in range(8):
                            sc.op("pe", lambda e, pp=pp, k=k, col=col, ht=ht: e.matmul(
                                pp[:], lhsT=wq[:, k, col:col + 128], rhs=ht[:, k, :],
                                start=(k == 0), stop=(k == 7)), reads=wtoks(B_wq, col, 128) + B_ht, writes=[B_pp])
                        ev, B_ev = evp.next()
                        if evi % 2 == 0:
                            sc.op("dve", lambda e, ev=ev, pp=pp, scale=scale: e.tensor_scalar(
                                out=ev[:], in0=pp[:], scalar1=float(scale), scalar2=None, op0=ALU.mult),
                                reads=[B_pp], writes=[B_ev])
                        else:
                            sc.op("act", lambda e, ev=ev, pp=pp, scale=scale: e.activation(
                                out=ev[:], in_=pp[:], func=AF.Copy, scale=float(scale)),
                                reads=[B_pp], writes=[B_ev])
                        evi += 1
                        sc.dma("sp", [(dst(ti), ev[:])], reads=[B_ev], writes=[B_qkv[ti]], key=("ev", id(B_ev)))
                        if pcs:
                            pcs.pop(0)()
                    for j in range(4):
                        r0 = ti * TT + j * 128
                        pp, B_pp = ps.next()
                        for k in range(8):
                            sc.op("pe", lambda e, pp=pp, k=k, j=j, ht=ht: e.matmul(
                                pp[:], lhsT=ht[:, k, j * 128:(j + 1) * 128], rhs=wq[:, k, C_VB:C_VB + 512],
                                start=(k == 0), stop=(k == 7)), reads=wtoks(B_wq, C_VB, 512) + B_ht, writes=[B_pp])
                        ev, B_ev = evp.next()
                        sc.op("dve", lambda e, ev=ev, pp=pp: e.tensor_copy(out=ev[:], in_=pp[:]),
                              reads=[B_pp], writes=[B_ev])
                        sc.dma("sp", [(vb_d[r0:r0 + 128, :], ev[:])], reads=[B_ev], writes=[B_qkv[ti]],
                               key=("ev", id(B_ev)))
                        pp, B_pp = ps.next()
                        for k in range(8):
                            sc.op("pe", lambda e, pp=pp, k=k, j=j, ht=ht: e.matmul(
                                pp[:, 0:128], lhsT=ht[:, k, j * 128:(j + 1) * 128], rhs=wq[:, k, C_VA:C_VA + 128],
                                start=(k == 0), stop=(k == 7)), reads=wtoks(B_wq, C_VA, 128) + B_ht, writes=[B_pp])
                        ev, B_ev = evp.next()
                        sc.op("act", lambda e, ev=ev, pp=pp: e.copy(out=ev[:, 0:128], in_=pp[:, 0:128]),
                              reads=[B_pp], writes=[B_ev])
                        sc.dma("sp", [(va_d[r0:r0 + 128, :], ev[:, 0:128])], reads=[B_ev], writes=[B_qkv[ti]],
                               key=("ev", id(B_ev)))
                        if pcs:
                            pcs.pop(0)()
                    while pcs:
                        pcs.pop(0)()
            sc.barrier()

        def phase_swa():
            with ExitStack() as pes:
                kt = sbt(pes, "kta", [128, S], BF16)
                qt = sbt(pes, "qta", [128, 4, S], BF16)
                vv = sbt(pes, "vva", [128, NB, 128], BF16)
                B_in = Buf("swa_in")
                var = va_d.rearrange("(n p) c -> p n c", p=128)
                sc.dma("sp", [(kt[:], kta_d)] + [(qt[:, g, :], qta_d[g]) for g in range(4)]
                       + [(vv[:, n0:min(NB, n0 + 8), :], var[:, n0:min(NB, n0 + 8), :]) for n0 in range(0, NB, 8)],
                       reads=B_qkv, writes=[B_in], key="swa_in")
                bm = sbt(pes, "bm", [128, 2, 4, 256], F32)
                mk = sbt(pes, "mk", [128, 256], F32)
                sk = sbt(pes, "sk", [128, 2, 4], F32)
                B_bm = Buf("bm")
                B_sk = Buf("sk")
                bsrc = bias_d.rearrange("q (g kv) k -> q kv g k", kv=2)
                sc.dma("sp", [(bm[:, kv, :, :], bsrc[:, kv, :, :]) for kv in range(2)] + [(mk[:], maskc_d)],
                       writes=[B_bm], key="bm")
                sc.dma("sp", [(sk[:, kv, :], sinks2_d[kv].partition_broadcast(128)) for kv in range(2)],
                       writes=[B_sk], key="sk")
                for kv in range(2):
                    for g in range(4):
                        sc.op("dve", lambda e, kv=kv, g=g: e.tensor_tensor(
                            out=bm[:, kv, g, :], in0=bm[:, kv, g, :], in1=mk[:], op=ALU.add),
                            reads=[B_bm], writes=[B_bm])
                psc = Pool(nc, pes, "psc", 2, [128, 4, 256], F32, psum=True)
                ppt = Pool(nc, pes, "ppt", 2, [128, 8, 128], BF16, psum=True)
                pso = Pool(nc, pes, "pso", 2, [128, 4, 128], F32, psum=True)
                scs = Pool(nc, pes, "scs", 2, [128, 4, 256], F32)
                pbf = Pool(nc, pes, "pbf", 3, [128, 4, 256], BF16)
                ptb = Pool(nc, pes, "ptb", 2, [128, 8, 128], BF16)
                sm = Pool(nc, pes, "sm", 6, [128, 5, 4], F32)
                osb = Pool(nc, pes, "osb", 3, [128, 4, 128], BF16)
                units = [dict(n=n, kv=kv) for n in range(NB) for kv in range(2)]
                cur_ob = {}
                evc = [0]

                def stA(u):
                    n, kv = u["n"], u["kv"]
                    k0 = 0 if n > 0 else 128
                    kw = 256 - k0
                    ks = (n - 1) * 128 + k0
                    u.update(k0=k0, kw=kw)
                    pc, B_pc = psc.next()
                    u["pc"] = (pc, B_pc)
                    for g in range(4):
                        sc.op("pe", lambda e, pc=pc, g=g, kv=kv, n=n, ks=ks, kw=kw, k0=k0: e.matmul(
                            pc[:, g, k0:256], lhsT=qt[kv * 64:(kv + 1) * 64, g, n * 128:(n + 1) * 128],
                            rhs=kt[kv * 64:(kv + 1) * 64, ks:ks + kw], start=True, stop=True),
                            reads=[B_in], writes=[B_pc])

                def stB1(u):
                    kv, k0 = u["kv"], u["k0"]
                    pc, B_pc = u["pc"]
                    ss, B_ss = scs.next()
                    st, B_st = sm.next()
                    pb, B_pb = pbf.next()
                    u.update(st=(st, B_st), pb=(pb, B_pb))
                    sc.op("dve", lambda e, pc=pc, ss=ss, kv=kv, k0=k0: e.tensor_tensor(
                        out=ss[:, :, k0:256], in0=pc[:, :, k0:256], in1=bm[:, kv, :, k0:256], op=ALU.add),
                        reads=[B_pc, B_bm], writes=[B_ss])
                    sc.op("dve", lambda e, ss=ss, st=st, k0=k0: e.tensor_reduce(
                        out=st[:, 0, :], in_=ss[:, :, k0:256], axis=AX.X, op=ALU.max),
                        reads=[B_ss], writes=[B_st])
                    sc.op("dve", lambda e, st=st, kv=kv: e.tensor_tensor(
                        out=st[:, 0, :], in0=st[:, 0, :], in1=sk[:, kv, :], op=ALU.max),
                        reads=[B_st, B_sk], writes=[B_st])
                    sc.op("dve", lambda e, st=st: e.tensor_scalar(out=st[:, 1, :], in0=st[:, 0, :], scalar1=-1.0,
                                                                   scalar2=None, op0=ALU.mult),
                          reads=[B_st], writes=[B_st])
                    sc.op("dve", lambda e, st=st, kv=kv: e.tensor_tensor(
                        out=st[:, 2, :], in0=sk[:, kv, :], in1=st[:, 1, :], op=ALU.add),
                        reads=[B_st, B_sk], writes=[B_st])
                    for g in range(4):
                        sc.op("act", lambda e, pb=pb, ss=ss, st=st, g=g, k0=k0: e.activation(
                            out=pb[:, g, k0:256], in_=ss[:, g, k0:256], func=AF.Exp, bias=st[:, 1, g:g + 1],
                            accum_out=st[:, 3, g:g + 1]), reads=[B_ss, B_st], writes=[B_pb, B_st])
                    sc.op("act", lambda e, st=st: e.activation(out=st[:, 2, :], in_=st[:, 2, :], func=AF.Exp),
                          reads=[B_st], writes=[B_st])

                def stB2(u):
                    n, kv, k0, kw = u["n"], u["kv"], u["k0"], u["kw"]
                    st, B_st = u["st"]
                    pb, B_pb = u["pb"]
                    sc.op("dve", lambda e, st=st: e.tensor_tensor(out=st[:, 4, :], in0=st[:, 3, :], in1=st[:, 2, :], op=ALU.add),
                          reads=[B_st], writes=[B_st])
                    sc.op("dve", lambda e, st=st: e.reciprocal(out=st[:, 4, :], in_=st[:, 4, :]),
                          reads=[B_st], writes=[B_st])
                    sc.op("pool", lambda e, pb=pb, st=st, k0=k0, kw=kw: e.tensor_tensor(
                        out=pb[:, :, k0:256], in0=pb[:, :, k0:256],
                        in1=st[:, 4, :].unsqueeze(2).to_broadcast([128, 4, kw]), op=ALU.mult),
                        reads=[B_pb, B_st], writes=[B_pb])
                    nkb = kw // 128
                    pp, B_pp = ppt.next()
                    pt, B_pt = ptb.next()
                    for kb in range(nkb):
                        for g in range(4):
                            sc.op("pe", lambda e, pp=pp, pb=pb, g=g, kb=kb, k0=k0: e.transpose(
                                out=pp[:, kb * 4 + g, :], in_=pb[:, g, k0 + kb * 128:k0 + (kb + 1) * 128], identity=ident),
                                reads=[B_pb, B_cst], writes=[B_pp])
                    evc[0] += 1
                    if evc[0] % 2 == 0:
                        sc.op("dve", lambda e, pp=pp, pt=pt, nkb=nkb: e.tensor_copy(
                            out=pt[:, 0:4 * nkb, :], in_=pp[:, 0:4 * nkb, :]), reads=[B_pp], writes=[B_pt])
                    else:
                        sc.op("act", lambda e, pp=pp, pt=pt, nkb=nkb: e.copy(
                            out=pt[:, 0:4 * nkb, :], in_=pp[:, 0:4 * nkb, :]), reads=[B_pp], writes=[B_pt])
                    po, B_po = pso.next()
                    for g in range(4):
                        h = g + 4 * kv
                        cl, half = (h // 2) - 2 * kv, h % 2
                        for kb in range(nkb):
                            blk = n - (nkb - 1) + kb
                            sc.op("pe", lambda e, po=po, pt=pt, cl=cl, half=half, kv=kv, g=g, kb=kb, blk=blk, nkb=nkb: e.matmul(
                                po[half * 64:(half + 1) * 64, cl, :], lhsT=vv[:, blk, kv * 64:(kv + 1) * 64],
                                rhs=pt[:, kb * 4 + g, :], start=(kb == 0), stop=(kb == nkb - 1)),
                                reads=[B_pt, B_in], writes=[B_po])
                    if kv == 0:
                        cur_ob[n] = osb.next()
                    ob, B_ob = cur_ob[n]
                    sc.op("act", lambda e, ob=ob, po=po, kv=kv: e.copy(out=ob[:, 2 * kv:2 * kv + 2, :], in_=po[:, 0:2, :]),
                          reads=[B_po], writes=[B_ob])
                    if kv == 1:
                        sc.dma("sp", [(ota_d.rearrange("(c p) s -> p c s", p=128)[:, :, n * 128:(n + 1) * 128], ob[:])],
                               reads=[B_ob], writes=[B_ota[n // 4]], key=("osb", id(B_ob)))
                        del cur_ob[n]

                nu = len(units)
                for t in range(nu + 2):
                    if t < nu:
                        stA(units[t])
                    if 0 <= t - 1 < nu:
                        stB1(units[t - 1])
                    if 0 <= t - 2 < nu:
                        stB2(units[t - 2])
            sc.barrier()

        def phase_sb():
            with ExitStack() as pes:
                ktp = Pool(nc, pes, "ktb", 2, [128, S], BF16)
                qtp = Pool(nc, pes, "qtb", 2, [128, S], BF16)
                vvp = Pool(nc, pes, "vvb", 2, [128, NB, 128], BF16)
                zp = Pool(nc, pes, "zp", 3, [128, 2, 512], F32, psum=True)
                op_ = Pool(nc, pes, "op", 2, [128, 512], F32, psum=True)
                Ep = Pool(nc, pes, "E", 2, [128, 2, 512], F32)
                Lp = Pool(nc, pes, "L", 3, [128, 2, 512], BF16)
                Ap = Pool(nc, pes, "A", 2, [128, 2, 512], BF16)
                R32 = sbt(pes, "R32", [128, 2, 512], F32)
                B_R32 = Buf("R32")
                Rbp = Pool(nc, pes, "Rb", 2, [128, 2, 512], BF16)
                oev = Pool(nc, pes, "oev", 2, [128, 512], BF16)
                steps = []
                pair_in = {}
                for p in range(4):
                    for i in range(NT):
                        nk = 4 * i + 4
                        for si, kj in enumerate(range(nk - 1, -1, -1)):
                            steps.append(dict(p=p, i=i, kj=kj, first=(si == 0), last=(kj == 0),
                                              c0=max(0, (kj - 4 * i)) * 128, diag=(kj >= 4 * i)))

                def load_pair(p):
                    kt, B_kt = ktp.next()
                    qt, B_qt = qtp.next()
                    vv, B_vv = vvp.next()
                    sc.dma("sp", [(kt[:], ktb_d[p * 128:(p + 1) * 128, :])], reads=B_qkv, writes=[B_kt], key=("sbk", id(B_kt)))
                    sc.dma("sp", [(qt[:], qtb_d[p * 128:(p + 1) * 128, :])], reads=B_qkv, writes=[B_qt], key=("sbq", id(B_qt)))
                    vbr = vb_d[:, p * 128:(p + 1) * 128].rearrange("(n p) c -> p n c", p=128)
                    sc.dma("sp", [(vv[:, n0:min(NB, n0 + 8), :], vbr[:, n0:min(NB, n0 + 8), :]) for n0 in range(0, NB, 8)],
                           reads=B_qkv, writes=[B_vv], key=("sbv", id(B_vv)))
                    pair_in[p] = (kt, B_kt, qt, B_qt, vv, B_vv)

                load_pair(0)
                state = {}

                def stage0(s):
                    p, i, kj, c0 = s["p"], s["i"], s["kj"], s["c0"]
                    if s["first"] and i == 0 and p + 1 < 4:
                        load_pair(p + 1)
                    kt, B_kt, qt, B_qt, vv, B_vv = pair_in[p]
                    z, B_z = zp.next()
                    s["z"], s["B_z"] = z, B_z
                    for hh in range(2):
                        sc.op("pe", lambda e, z=z, hh=hh, kj=kj, i=i, c0=c0, kt=kt, qt=qt: e.matmul(
                            z[:, hh, c0:512], lhsT=kt[hh * 64:(hh + 1) * 64, kj * 128:(kj + 1) * 128],
                            rhs=qt[hh * 64:(hh + 1) * 64, i * 512 + c0:(i + 1) * 512],
                            start=True, stop=False, skip_group_check=True),
                            reads=[B_kt, B_qt], writes=[B_z])
                    if s["diag"]:
                        for hh in range(2):
                            sc.op("pe", lambda e, z=z, hh=hh, c0=c0: e.matmul(
                                z[:, hh, c0:c0 + 128], lhsT=ident, rhs=dmask, start=False, stop=False,
                                skip_group_check=True), reads=[B_cst], writes=[B_z])

                def stage1(s):
                    z, B_z, c0 = s["z"], s["B_z"], s["c0"]
                    E, B_E = Ep.next()
                    L, B_L = Lp.next()
                    s["L"], s["B_L"] = L, B_L
                    sc.op("act", lambda e, E=E, z=z, c0=c0: e.activation(out=E[:, :, c0:512], in_=z[:, :, c0:512], func=AF.Exp),
                          reads=[B_z], writes=[B_E])
                    sc.op("act", lambda e, E=E, L=L, c0=c0: e.activation(out=L[:, :, c0:512], in_=E[:, :, c0:512], func=AF.Ln, bias=1.0),
                          reads=[B_E], writes=[B_L])

                def stage2(s):
                    p, i, kj, c0 = s["p"], s["i"], s["kj"], s["c0"]
                    kt, B_kt, qt, B_qt, vv, B_vv = pair_in[p]
                    z, B_z, L, B_L = s["z"], s["B_z"], s["L"], s["B_L"]
                    for hh in range(2):
                        sc.op("pe", lambda e, z=z, hh=hh, c0=c0, L=L: e.matmul(
                            z[:, hh, c0:512], lhsT=ntri, rhs=L[:, hh, c0:512], start=False, stop=False,
                            skip_group_check=True), reads=[B_L, B_cst], writes=[B_z])
                    if not s["first"]:
                        Rb, B_Rb = state["Rb"]
                        for hh in range(2):
                            sc.op("pe", lambda e, z=z, hh=hh, c0=c0, Rb=Rb: e.matmul(
                                z[:, hh, c0:512], lhsT=nones, rhs=Rb[:, hh, c0:512], start=False, stop=False,
                                skip_group_check=True), reads=[B_Rb, B_cst], writes=[B_z])
                    A, B_A = Ap.next()
                    sc.op("act", lambda e, A=A, z=z, c0=c0: e.activation(out=A[:, :, c0:512], in_=z[:, :, c0:512], func=AF.Exp),
                          reads=[B_z], writes=[B_A])
                    if s["first"]:
                        state["o"] = op_.next()
                    o, B_o = state["o"]
                    for hh in range(2):
                        sc.op("pe", lambda e, o=o, hh=hh, c0=c0, A=A, vv=vv, kj=kj, first=s["first"]: e.matmul(
                            o[hh * 64:(hh + 1) * 64, c0:512], lhsT=vv[:, kj, hh * 64:(hh + 1) * 64],
                            rhs=A[:, hh, c0:512], start=first, stop=False, skip_group_check=True),
                            reads=[B_A, B_vv], writes=[B_o])
                    if not s["last"]:
                        if s["first"]:
                            sc.op("pool", lambda e: e.memset(R32[:], 0.0), writes=[B_R32])
                        sc.op("pool", lambda e, L=L, c0=c0: e.tensor_tensor(
                            out=R32[:, :, c0:512], in0=R32[:, :, c0:512], in1=L[:, :, c0:512], op=ALU.add),
                            reads=[B_L, B_R32], writes=[B_R32])
                        Rb, B_Rb = Rbp.next()
                        sc.op("dve", lambda e, Rb=Rb: e.tensor_copy(out=Rb[:], in_=R32[:]), reads=[B_R32], writes=[B_Rb])
                        state["Rb"] = (Rb, B_Rb)
                    else:
                        ob, B_ob = oev.next()
                        sc.op("dve", lambda e, ob=ob, o=o: e.tensor_copy(out=ob[:], in_=o[:]), reads=[B_o], writes=[B_ob])
                        sc.dma("sp", [(otb_d[p * 128:(p + 1) * 128, i * 512:(i + 1) * 512], ob[:])], reads=[B_ob],
                               writes=[B_otb[i]], key=("oev", id(B_ob)))

                n = len(steps)
                for t in range(n + 2):
                    if t < n:
                        stage0(steps[t])
                    if 0 <= t - 1 < n:
                        stage1(steps[t - 1])
                    if 0 <= t - 2 < n:
                        stage2(steps[t - 2])
            sc.barrier()

        def phase_sb2(K=8):
            with ExitStack() as pes:
                per_pair = sum(4 * i + 4 for i in range(NT))
                nsets = 2 if per_pair >= 4 * K else 4
                ktp = Pool(nc, pes, "ktb", nsets, [128, S], BF16)
                qtp = Pool(nc, pes, "qtb", nsets, [128, S], BF16)
                vvp = Pool(nc, pes, "vvb", nsets, [128, NB, 128], BF16)
                zp = Pool(nc, pes, "zp", 3, [128, 2, 512], F32, psum=True)
                op_ = Pool(nc, pes, "op", 2, [128, 512], F32, psum=True)
                Lp = Pool(nc, pes, "L", 2 * K + 2, [128, 2, 512], BF16)
                Rbp = Pool(nc, pes, "Rb", 2 * K + 2, [128, 2, 512], BF16)
                Ap = Pool(nc, pes, "A", K + 3, [128, 2, 512], BF16)
                R32 = sbt(pes, "R32", [128, 2, 512], F32)
                B_R32 = Buf("R32")
                oev = Pool(nc, pes, "oev", 2, [128, 512], BF16)
                steps = []
                pair_in = {}
                for p in range(4):
                    for i in range(NT):
                        nk = 4 * i + 4
                        for si, kj in enumerate(range(nk - 1, -1, -1)):
                            steps.append(dict(p=p, i=i, kj=kj, first=(si == 0), last=(kj == 0),
                                              c0=max(0, (kj - 4 * i)) * 128, diag=(kj >= 4 * i)))
                for a, b in zip(steps[:-1], steps[1:]):
                    b["prev"] = a

                def load_pair(p):
                    kt, B_kt = ktp.next()
                    qt, B_qt = qtp.next()
                    vv, B_vv = vvp.next()
                    sc.dma("sp", [(kt[:], ktb_d[p * 128:(p + 1) * 128, :])], reads=B_qkv, writes=[B_kt], key=("sbk", id(B_kt)))
                    sc.dma("sp", [(qt[:], qtb_d[p * 128:(p + 1) * 128, :])], reads=B_qkv, writes=[B_qt], key=("sbq", id(B_qt)))
                    vbr = vb_d[:, p * 128:(p + 1) * 128].rearrange("(n p) c -> p n c", p=128)
                    sc.dma("sp", [(vv[:, n0:min(NB, n0 + 8), :], vbr[:, n0:min(NB, n0 + 8), :]) for n0 in range(0, NB, 8)],
                           reads=B_qkv, writes=[B_vv], key=("sbv", id(B_vv)))
                    pair_in[p] = (kt, B_kt, qt, B_qt, vv, B_vv)

                for p_ in range(nsets):
                    load_pair(p_)
                state = {}

                def zmm(s_, z, B_z):
                    p, i, kj, c0 = s_["p"], s_["i"], s_["kj"], s_["c0"]
                    kt, B_kt, qt, B_qt, vv, B_vv = pair_in[p]
                    for hh in range(2):
                        sc.op("pe", lambda e, z=z, hh=hh, kj=kj, i=i, c0=c0, kt=kt, qt=qt: e.matmul(
                            z[:, hh, c0:512], lhsT=kt[hh * 64:(hh + 1) * 64, kj * 128:(kj + 1) * 128],
                            rhs=qt[hh * 64:(hh + 1) * 64, i * 512 + c0:(i + 1) * 512],
                            start=True, stop=False, skip_group_check=True),
                            reads=[B_kt, B_qt], writes=[B_z])
                    if s_["diag"]:
                        for hh in range(2):
                            sc.op("pe", lambda e, z=z, hh=hh, c0=c0: e.matmul(
                                z[:, hh, c0:c0 + 128], lhsT=ident, rhs=dmask, start=False, stop=False,
                                skip_group_check=True), reads=[B_cst], writes=[B_z])

                def Xstep(s_):
                    p, i, c0 = s_["p"], s_["i"], s_["c0"]
                    z, B_z = zp.next()
                    zmm(s_, z, B_z)
                    L, B_L = Lp.next()
                    s_["L"], s_["B_L"] = L, B_L
                    sc.op("act", lambda e, L=L, z=z, c0=c0: e.activation(
                        out=L[:, :, c0:512], in_=z[:, :, c0:512], func=AF.Softplus), reads=[B_z], writes=[B_L])
                    if not s_["last"]:
                        if s_["first"]:
                            sc.op("pool", lambda e: e.memset(R32[:], 0.0), writes=[B_R32])
                        sc.op("dve", lambda e, L=L, c0=c0: e.tensor_tensor(
                            out=R32[:, :, c0:512], in0=R32[:, :, c0:512], in1=L[:, :, c0:512], op=ALU.add),
                            reads=[B_L, B_R32], writes=[B_R32])
                        Rb, B_Rb = Rbp.next()
                        sc.op("dve", lambda e, Rb=Rb: e.tensor_copy(out=Rb[:], in_=R32[:]), reads=[B_R32], writes=[B_Rb])
                        s_["Rb"] = (Rb, B_Rb)

                def Ystep(s_):
                    c0 = s_["c0"]
                    L, B_L = s_["L"], s_["B_L"]
                    z, B_z = zp.next()
                    zmm(s_, z, B_z)
                    for hh in range(2):
                        sc.op("pe", lambda e, z=z, hh=hh, c0=c0, L=L: e.matmul(
                            z[:, hh, c0:512], lhsT=ntri, rhs=L[:, hh, c0:512], start=False, stop=False,
                            skip_group_check=True), reads=[B_L, B_cst], writes=[B_z])
                    if not s_["first"]:
                        Rb, B_Rb = s_["prev"]["Rb"]
                        for hh in range(2):
                            sc.op("pe", lambda e, z=z, hh=hh, c0=c0, Rb=Rb: e.matmul(
                                z[:, hh, c0:512], lhsT=nones, rhs=Rb[:, hh, c0:512], start=False, stop=False,
                                skip_group_check=True), reads=[B_Rb, B_cst], writes=[B_z])
                    A, B_A = Ap.next()
                    s_["A"] = (A, B_A)
                    sc.op("act", lambda e, A=A, z=z, c0=c0: e.activation(out=A[:, :, c0:512], in_=z[:, :, c0:512], func=AF.Exp),
                          reads=[B_z], writes=[B_A])

                def AVstep(s_):
                    p, i, kj, c0 = s_["p"], s_["i"], s_["kj"], s_["c0"]
                    kt, B_kt, qt, B_qt, vv, B_vv = pair_in[p]
                    A, B_A = s_["A"]
                    if s_["first"]:
                        state["o"] = op_.next()
                    o, B_o = state["o"]
                    for hh in range(2):
                        sc.op("pe", lambda e, o=o, hh=hh, c0=c0, A=A, vv=vv, kj=kj, first=s_["first"]: e.matmul(
                            o[hh * 64:(hh + 1) * 64, c0:512], lhsT=vv[:, kj, hh * 64:(hh + 1) * 64],
                            rhs=A[:, hh, c0:512], start=first, stop=False, skip_group_check=True),
                            reads=[B_A, B_vv], writes=[B_o])
                    if s_["last"]:
                        ob, B_ob = oev.next()
                        sc.op("dve", lambda e, ob=ob, o=o: e.tensor_copy(out=ob[:], in_=o[:]), reads=[B_o], writes=[B_ob])
                        sc.dma("sp", [(otb_d[p * 128:(p + 1) * 128, i * 512:(i + 1) * 512], ob[:])], reads=[B_ob],
                               writes=[B_otb[i]], key=("oev", id(B_ob)))
                        if i == NT - 1 and p + 2 < 4 and nsets == 2:
                            load_pair(p + 2)

                batches = [steps[a:a + K] for a in range(0, len(steps), K)]
                nbt = len(batches)
                for b in range(-1, nbt + 1):
                    xs = batches[b + 1] if 0 <= b + 1 < nbt else []
                    avs = batches[b - 1] if 0 <= b - 1 < nbt else []
                    for idx in range(max(len(xs), len(avs))):
                        if idx < len(xs):
                            Xstep(xs[idx])
                        if idx < len(avs):
                            AVstep(avs[idx])
                    if 0 <= b < nbt:
                        for s_ in batches[b]:
                            Ystep(s_)
            sc.barrier()

        def phase_mix():
            with ExitStack() as pes:
                wg = sbt(pes, "wg", [128, 8, 2048], BF16)
                wba = sbt(pes, "wba", [128, 4, D], BF16)
                wbb = sbt(pes, "wbb", [128, 4, D], BF16)
                wo = sbt(pes, "wo", [128, 8, D], BF16)
                B_wg = [Buf("wg%d" % b) for b in range(4)]
                B_wba = [Buf("wba%d" % k) for k in range(4)]
                B_wbb = [Buf("wbb%d" % k) for k in range(4)]
                B_wo = [Buf("wo%d" % k) for k in range(8)]
                nt = NormT(pes, nht=2, nxin=4)
                nxt = nt.run(x1_d, 0, B_x1[0])
                stgp = Pool(nc, pes, "stg", 8, [128, STGW], F32)
                load_cast_cols(stgp, wg, B_wg, win_d[:, C_GA:INW], 8, 2048, gain_i=1)
                for k in range(4):
                    load_cast(stgp, wba, lambda a, b, k=k: wba[:, k, a:b], B_wba[k], wba_d[k * 128:(k + 1) * 128, :], D)
                    load_cast(stgp, wbb, lambda a, b, k=k: wbb[:, k, a:b], B_wbb[k], wbb_d[k * 128:(k + 1) * 128, :], D)
                for k in range(8):
                    load_cast(stgp, wo, lambda a, b, k=k: wo[:, k, a:b], B_wo[k], wout_d[k * 128:(k + 1) * 128, :], D)
                ps = Pool(nc, pes, "ps", 6, [128, 512], F32, psum=True)
                otap = Pool(nc, pes, "ota", 2, [128, 4, TT], BF16)
                otbp = Pool(nc, pes, "otb", 2, [128, 4, TT], BF16)
                MT = sbt(pes, "MT", [128, 8, TT], BF16)
                B_MT = [Buf("MT%d" % c) for c in range(8)]
                sgp = Pool(nc, pes, "sgm", 4, [128, TT], F32)
                mp = Pool(nc, pes, "mm", 4, [128, TT], F32)
                xres = Pool(nc, pes, "xres", 4, [128, D], F32)

                def load_o(tj):
                    oa, B_oa = otap.next()
                    ob, B_ob = otbp.next()
                    sc.dma("sp", [(oa[:], ota_d.rearrange("(c p) s -> p c s", p=128)[:, :, tj * TT:(tj + 1) * TT])],
                           reads=[B_ota[tj]], writes=[B_oa], key=("ota", id(B_oa)))
                    sc.dma("sp", [(ob[:], otb_d.rearrange("(c p) s -> p c s", p=128)[:, :, tj * TT:(tj + 1) * TT])],
                           reads=[B_otb[tj]], writes=[B_ob], key=("otb", id(B_ob)))
                    return oa, B_oa, ob, B_ob

                for ti in range(NT):
                    ht, B_ht = nxt
                    pcs = []
                    if ti + 1 < NT:
                        J = nt.begin(x1_d, ti + 1, B_x1[ti + 1])
                        nxt = (J["ht"], J["B_ht"])
                        pcs = nt.pieces(J)
                    if ti == 0:
                        nxt_o = load_o(0)
                    oa, B_oa, ob, B_ob = nxt_o
                    if ti + 1 < NT:
                        nxt_o = load_o(ti + 1)
                    xrs = []
                    for j in range(4):
                        xr, B_xr = xres.next()
                        r0 = ti * TT + j * 128
                        sc.dma("sp", [(xr[:], x1_d[r0:r0 + 128, :])], reads=[B_x1[ti]], writes=[B_xr], key=("xres", id(B_xr)))
                        xrs.append((xr, B_xr))
                    for cc in range(8):
                        ms = []
                        for br, (wbr, B_wbr, ot, B_ot, goff) in enumerate(((wba, B_wba, oa, B_oa, 0), (wbb, B_wbb, ob, B_ob, 1024))):
                            pg, B_pg = ps.next()
                            pb, B_pb = ps.next()
                            for k in range(8):
                                sc.op("pe", lambda e, pg=pg, k=k, cc=cc, goff=goff, ht=ht: e.matmul(
                                    pg[:], lhsT=wg[:, k, goff + cc * 128:goff + (cc + 1) * 128], rhs=ht[:, k, :],
                                    start=(k == 0), stop=(k == 7)), reads=wtoks(B_wg, goff + cc * 128, 128) + B_ht, writes=[B_pg])
                            for k in range(4):
                                sc.op("pe", lambda e, pb=pb, k=k, cc=cc, wbr=wbr, ot=ot: e.matmul(
                                    pb[:], lhsT=wbr[:, k, cc * 128:(cc + 1) * 128], rhs=ot[:, k, :],
                                    start=(k == 0), stop=(k == 3)), reads=[B_wbr[k], B_ot], writes=[B_pb])
                            sg, B_sg = sgp.next()
                            sc.op("act", lambda e, sg=sg, pg=pg: e.activation(out=sg[:], in_=pg[:], func=AF.Sigmoid),
                                  reads=[B_pg], writes=[B_sg])
                            m, B_m = mp.next()
                            sc.op("dve", lambda e, m=m, sg=sg, pb=pb: e.tensor_tensor(out=m[:], in0=sg[:], in1=pb[:], op=ALU.mult),
                                  reads=[B_sg, B_pb], writes=[B_m])
                            ms.append((m, B_m))
                            if pcs:
                                pcs.pop(0)()
                        sc.op("pool", lambda e, cc=cc, ms=ms: e.tensor_tensor(
                            out=MT[:, cc, :], in0=ms[0][0][:], in1=ms[1][0][:], op=ALU.add),
                            reads=[ms[0][1], ms[1][1]], writes=[B_MT[cc]])
                    while pcs:
                        pcs.pop(0)()
                    for j in range(4):
                        xr, B_xr = xrs[j]
                        r0 = ti * TT + j * 128
                        for half in range(2):
                            po, B_po = ps.next()
                            for cc in range(8):
                                sc.op("pe", lambda e, po=po, cc=cc, j=j, half=half: e.matmul(
                                    po[:], lhsT=MT[:, cc, j * 128:(j + 1) * 128], rhs=wo[:, cc, half * 512:(half + 1) * 512],
                                    start=(cc == 0), stop=(cc == 7)), reads=[B_MT[cc], B_wo[cc]], writes=[B_po])
                            sc.op("dve", lambda e, po=po, xr=xr, half=half: e.tensor_tensor(
                                out=xr[:, half * 512:(half + 1) * 512], in0=po[:], in1=xr[:, half * 512:(half + 1) * 512], op=ALU.add),
                                reads=[B_po, B_xr], writes=[B_xr])
                        sc.dma("sp", [(x2_d[r0:r0 + 128, :], xr[:])], reads=[B_xr], writes=[B_x2[ti]], key=("xres_st", id(B_xr)))
            sc.barrier()

        sc.barrier()
        if "ffn1" in phases:
            phase_ffn(0, x_d, None, x1_d, B_x1, final=False)
        if "proj" in phases:
            phase_proj()
        if "swa" in phases:
            phase_swa()
        if "sb" in phases:
            phase_sb2()
        if "sb_old" in phases:
            phase_sb()
        if "mix" in phases:
            phase_mix()
        if "ffn2" in phases:
            phase_ffn(1, x2_d, B_x2, out_d, None, final=True)
        sc.emit()
    return nc


def _rel_bucket_np(dist):
    max_exact = 16
    d = np.maximum(dist, 1).astype(np.float32)
    large = max_exact + (np.log(d / max_exact) / np.float32(np.log(128 / max_exact)) * (32 - max_exact)).astype(np.int32)
    large = np.minimum(large, 31)
    return np.where(dist < max_exact, dist, large)


def _bucket_table():
    import jax
    import jax.numpy as jnp
    qi = np.arange(128)[:, None] + 128
    kj = np.arange(256)[None, :]
    dist = qi - kj
    band = (dist >= 0) & (dist < 128)
    with jax.default_device(jax.devices("cpu")[0]):
        dj = jnp.maximum(jnp.asarray(dist), 0)
        max_exact = 16
        d = jnp.maximum(dj, 1).astype(jnp.float32)
        large = max_exact + (jnp.log(d / max_exact) / np.log(128 / max_exact) * (32 - max_exact)).astype(jnp.int32)
        large = jnp.minimum(large, 31)
        bucket = np.asarray(jnp.where(dj < max_exact, dj, large))
    return bucket, band


def _consts():
    c = np.zeros((128, 512), np.float32)
    c[:, 0:128] = np.eye(128)
    j = np.arange(128)[:, None]
    s = np.arange(128)[None, :]
    c[:, 128:256] = np.where(j >= s, -1.0, 0.0)
    c[:, 256:384] = -1.0
    c[:, 384:512] = np.where(j < s, 0.0, MASKB)
    return c.astype(ml_dtypes.bfloat16)


_PROG_CACHE = {}


def _prepare_shared(inp, S):
    f = lambda a: np.ascontiguousarray(np.asarray(a, dtype=np.float32))
    bucket, band = _bucket_table()
    rb = f(inp["rel_bias"])
    bias = rb[bucket]
    order = [g + 4 * kv for g in range(4) for kv in range(2)]
    bias = np.ascontiguousarray(bias.transpose(0, 2, 1)[:, order, :])
    mask = np.where(band, 0.0, NEG).astype(np.float32)
    w_in = f(inp["w_in"])[0]
    qcols = []
    for g in range(4):
        for kv in range(2):
            h = g + 4 * kv
            qcols.extend(range(h * 64, (h + 1) * 64))
    w_in = np.ascontiguousarray(np.concatenate([w_in[:, qcols], w_in[:, 512:]], axis=1))
    sinks = f(inp["swa_sinks"])[0].reshape(2, 4)
    shared = {
        "ffn1_w1": f(inp["ffn1_w1"])[0], "ffn1_w3": f(inp["ffn1_w3"])[0], "ffn1_w2": f(inp["ffn1_w2"])[0],
        "ffn2_w1": f(inp["ffn2_w1"])[0], "ffn2_w3": f(inp["ffn2_w3"])[0], "ffn2_w2": f(inp["ffn2_w2"])[0],
        "gains": np.ascontiguousarray(np.stack([f(inp["norm_ffn1"])[0].reshape(8, 128).T,
                                                f(inp["norm_mix"])[0].reshape(8, 128).T,
                                                f(inp["norm_ffn2"])[0].reshape(8, 128).T], axis=1)),
        "norm_final": f(inp["norm_final"]),
        "w_in": w_in, "swa_sinks": np.ascontiguousarray(sinks), "swa_bias": bias, "swa_mask": mask,
        "w_branch_swa": f(inp["w_branch_swa"])[0], "w_branch_sb": f(inp["w_branch_sb"])[0],
        "w_out": f(inp["w_out"])[0], "consts": _consts(),
    }
    return shared


def kernel(**inputs):
    x = np.asarray(inputs["x"], dtype=np.float32)
    B, S, _ = x.shape
    if S not in _PROG_CACHE:
        _PROG_CACHE[S] = build_program(S)
    nc = _PROG_CACHE[S]
    shared = _prepare_shared(inputs, S)
    in_maps = []
    for b in range(B):
        m = dict(shared)
        m["x"] = np.ascontiguousarray(x[b])
        in_maps.append(m)
    res = run_bass_kernel_spmd(nc, in_maps, core_ids=list(range(B)))
    return np.stack([np.asarray(r["out"], dtype=np.float32) for r in res.results], axis=0)
```

```python
from contextlib import ExitStack

import numpy as np
import ml_dtypes

import concourse.bass as bass
import concourse.mybir as mybir
from concourse.bass_utils import run_bass_kernel_spmd

F32 = mybir.dt.float32
BF16 = mybir.dt.bfloat16
AF = mybir.ActivationFunctionType
ALU = mybir.AluOpType
AX = mybir.AxisListType

D = 1024
DFF = 2816
NFF = DFF // 128
INW = 4352
EPS = 1e-6
NEG = -1e30
TT = 512
MASKB = -30000.0
STGW = 320

C_QA, C_KA, C_VA, C_QB, C_KB, C_VB, C_GA, C_GB = 0, 512, 640, 768, 1280, 1792, 2304, 3328


class Buf:
    __slots__ = ("name", "lw", "rd", "dmard")

    def __init__(self, name):
        self.name = name
        self.lw = None
        self.rd = {}
        self.dmard = []


class Op:
    __slots__ = ("eng", "fn", "deps", "signal", "sem", "count", "is_dma", "pairs")

    def __init__(self, eng, fn, is_dma=False):
        self.eng = eng
        self.fn = fn
        self.deps = []
        self.signal = False
        self.sem = None
        self.count = 0
        self.is_dma = is_dma
        self.pairs = None


SEM_LIMIT = 30000


class Sched:
    ENGS = ("pe", "act", "dve", "pool", "sp")

    def __init__(self, nc, es):
        self.nc = nc
        self.es = es
        self.ops = {e: [] for e in self.ENGS}
        self.dma_sems = {}
        self.barrier_deps = {e: None for e in self.ENGS}

    def _newsem(self, name):
        return self.es.enter_context(self.nc.semaphore(name))

    def _add(self, op, reads, writes):
        deps = []
        bd = self.barrier_deps[op.eng]
        if bd is not None:
            deps.extend(bd)
            self.barrier_deps[op.eng] = None
        for b in reads:
            w = b.lw
            if w is not None:
                if not (w.eng == op.eng == "pe" and not w.is_dma and not op.is_dma):
                    deps.append(w)
        for b in writes:
            w = b.lw
            if w is not None:
                if w.is_dma or op.is_dma or w.eng != op.eng:
                    deps.append(w)
            for e, r in b.rd.items():
                if op.is_dma or e != op.eng:
                    deps.append(r)
            deps.extend(b.dmard)
        for b in reads:
            if op.is_dma:
                b.dmard.append(op)
            else:
                b.rd[op.eng] = op
        for b in writes:
            b.lw = op
            b.rd = {}
            b.dmard = []
        seen = set()
        for d in deps:
            if id(d) not in seen and d is not op:
                seen.add(id(d))
                op.deps.append(d)
                d.signal = True
        self.ops[op.eng].append(op)
        return op

    def op(self, eng, fn, reads=(), writes=()):
        return self._add(Op(eng, fn), reads, writes)

    def dma(self, queue, pairs, reads=(), writes=(), key=None):
        op = Op(queue, None, is_dma=True)
        op.pairs = pairs
        op.signal = True
        if key not in self.dma_sems:
            self.dma_sems[key] = [self._newsem("d%d" % len(self.dma_sems)), 0, None]
        ent = self.dma_sems[key]
        ent[1] += 16 * len(pairs)
        ent[2] = op
        op.sem = ent[0]
        op.count = ent[1]
        return self._add(op, reads, writes)

    def barrier(self):
        prev = []
        for e in self.ENGS:
            for op in reversed(self.ops[e]):
                if not op.is_dma:
                    prev.append(op)
                    break
        for ent in self.dma_sems.values():
            if ent[2] is not None:
                prev.append(ent[2])
        for e in self.ENGS:
            self.barrier_deps[e] = list(prev)

    def emit(self):
        nc = self.nc
        eng_sems = {}
        for e in self.ENGS:
            n = 0
            for op in self.ops[e]:
                if op.is_dma or not op.signal:
                    continue
                k = n // SEM_LIMIT
                if (e, k) not in eng_sems:
                    eng_sems[(e, k)] = self._newsem("e_%s%d" % (e, k))
                op.sem = eng_sems[(e, k)]
                op.count = n % SEM_LIMIT + 1
                n += 1
        final_waits = [(ent[0], ent[1]) for ent in self.dma_sems.values()]

        def run(e, eng, final=False):
            waited = {}
            for op in self.ops[e]:
                for d in op.deps:
                    k = id(d.sem)
                    if waited.get(k, 0) < d.count:
                        eng.wait_ge(d.sem, d.count)
                        waited[k] = d.count
                if op.is_dma:
                    for (o, i) in op.pairs:
                        eng.dma_start(out=o, in_=i).then_inc(op.sem, 16)
                else:
                    ins = op.fn(eng)
                    if op.signal:
                        ins.then_inc(op.sem, 1)
            if final:
                for (h, c) in final_waits:
                    if waited.get(id(h), 0) < c:
                        eng.wait_ge(h, c)

        with nc.Block() as block:
            @block.sync
            def _(sync):
                run("sp", sync, final=True)

            @block.tensor
            def _(tensor):
                run("pe", tensor)

            @block.scalar
            def _(scalar):
                run("act", scalar)

            @block.vector
            def _(vector):
                run("dve", vector)

            @block.gpsimd
            def _(gpsimd):
                run("pool", gpsimd)


class Pool:
    uid = [0]

    def __init__(self, nc, es, name, n, shape, dtype, psum=False):
        self.tiles = []
        for i in range(n):
            Pool.uid[0] += 1
            nm = "%s%d_%d" % (name, i, Pool.uid[0])
            if psum:
                t = es.enter_context(nc.psum_tensor(nm, list(shape), dtype))
            else:
                t = es.enter_context(nc.sbuf_tensor(nm, list(shape), dtype))
            self.tiles.append((t, Buf(nm)))
        self.i = 0

    def next(self):
        t = self.tiles[self.i % len(self.tiles)]
        self.i += 1
        return t


def build_program(S, phases=("ffn1", "proj", "swa", "sb", "mix", "ffn2"), debug=False):
    NT = S // TT
    NB = S // 128
    nc = bass.Bass("TRN2", target_bir_lowering=False)
    dk = "ExternalOutput" if debug else "Internal"

    def din(name, shape, dt=F32):
        return nc.dram_tensor(name, list(shape), dt, kind="ExternalInput").ap()

    def dscr(name, shape, dt):
        return nc.dram_tensor(name, list(shape), dt, kind=dk).ap()

    x_d = din("x", [S, D])
    w1_d = [din("ffn1_w1", [D, DFF]), din("ffn2_w1", [D, DFF])]
    w3_d = [din("ffn1_w3", [D, DFF]), din("ffn2_w3", [D, DFF])]
    w2_d = [din("ffn1_w2", [DFF, D]), din("ffn2_w2", [DFF, D])]
    gains_d = din("gains", [128, 3, 8])
    gfin_d = din("norm_final", [D])
    win_d = din("w_in", [D, INW])
    sinks2_d = din("swa_sinks", [2, 4])
    bias_d = din("swa_bias", [128, 8, 256])
    maskc_d = din("swa_mask", [128, 256])
    wba_d = din("w_branch_swa", [512, D])
    wbb_d = din("w_branch_sb", [512, D])
    wout_d = din("w_out", [D, D])
    cst_d = din("consts", [128, 4 * 128], BF16)
    out_d = nc.dram_tensor("out", [S, D], F32, kind="ExternalOutput").ap()

    x1_d = dscr("x1", [S, D], F32)
    x2_d = dscr("x2", [S, D], F32)
    qta_d = dscr("qta", [4, 128, S], BF16)
    kta_d = dscr("kta", [128, S], BF16)
    va_d = dscr("va", [S, 128], BF16)
    qtb_d = dscr("qtb", [512, S], BF16)
    ktb_d = dscr("ktb", [512, S], BF16)
    vb_d = dscr("vb", [S, 512], BF16)
    ota_d = dscr("ota", [512, S], BF16)
    otb_d = dscr("otb", [512, S], BF16)

    B_x1 = [Buf("x1_%d" % i) for i in range(NT)]
    B_x2 = [Buf("x2_%d" % i) for i in range(NT)]
    B_qkv = [Buf("qkv_%d" % i) for i in range(NT)]
    B_ota = [Buf("ota_%d" % i) for i in range(NT)]
    B_otb = [Buf("otb_%d" % i) for i in range(NT)]

    with ExitStack() as es:
        sc = Sched(nc, es)

        def sbt(stack, name, shape, dt):
            Pool.uid[0] += 1
            return stack.enter_context(nc.sbuf_tensor("%s_%d" % (name, Pool.uid[0]), list(shape), dt))

        cst = sbt(es, "cst", [128, 512], BF16)
        B_cst = Buf("cst")
        sc.dma("sp", [(cst[:], cst_d)], writes=[B_cst], key="cst")
        ident = cst[:, 0:128]
        ntri = cst[:, 128:256]
        nones = cst[:, 256:384]
        dmask = cst[:, 384:512]

        gcol = sbt(es, "gcol", [128, 3, 8], F32)
        B_gcol = Buf("gcol")

        sc.dma("sp", [(gcol[:], gains_d)], writes=[B_gcol], key="gcol")

        def load_cast(stgp, dst_t, sel, B_dst, src_rows, ncols, gain=None):
            c0 = 0
            while c0 < ncols:
                c1 = min(ncols, c0 + STGW)
                st, B_st = stgp.next()
                w = c1 - c0
                sc.dma("sp", [(st[:, 0:w], src_rows[:, c0:c1])], writes=[B_st], key=("stg", id(B_st)))
                o = sel(c0, c1)
                if gain is None:
                    sc.op("dve", lambda e, o=o, i=st[:, 0:w]: e.tensor_copy(out=o, in_=i),
                          reads=[B_st], writes=[B_dst])
                else:
                    sc.op("dve", lambda e, o=o, i=st[:, 0:w], g=gain:
                          e.tensor_scalar(out=o, in0=i, scalar1=g, scalar2=None, op0=ALU.mult),
                          reads=[B_st, B_gcol], writes=[B_dst])
                c0 = c1

        def rstd_ops(st, B_st):
            sc.op("dve", lambda e, st=st: e.tensor_scalar(
                out=st[:, 1:2], in0=st[:, 0:1], scalar1=float(D * EPS), scalar2=None, op0=ALU.add),
                reads=[B_st], writes=[B_st])
            sc.op("act", lambda e, st=st: e.activation(out=st[:, 1:2], in_=st[:, 1:2], func=AF.Sqrt),
                  reads=[B_st], writes=[B_st])
            sc.op("dve", lambda e, st=st: e.reciprocal(out=st[:, 1:2], in_=st[:, 1:2]),
                  reads=[B_st], writes=[B_st])

        class NormT:
            def __init__(self, stack, nht=1, nxin=2, nhrow=2):
                self.nxin = nxin
                self.xin = Pool(nc, stack, "xin", nxin, [128, D], F32)
                self.hrow = Pool(nc, stack, "hrow", nhrow, [128, D], BF16)
                self.sqj = sbt(stack, "sqj", [128, D], BF16)
                self.B_sqj = Buf("sqj")
                self.stat = Pool(nc, stack, "stat", 8, [128, 2], F32)
                self.hts = [(sbt(stack, "ht%d" % i, [128, 8, TT], BF16), [Buf("ht%d_%d" % (i, j)) for j in range(4)])
                            for i in range(nht)]
                self.hi = 0
                self.tpp = Pool(nc, stack, "tpp", 2, [128, 8, 128], BF16, psum=True)
                self.ev = 0

            def _load(self, J, j):
                xt, B_xt = self.xin.next()
                r0 = J["ti"] * TT + j * 128
                sc.dma("sp", [(xt[:], J["src"][r0:r0 + 128, :])], reads=[J["B_src"]] if J["B_src"] else [],
                       writes=[B_xt], key=("xin", id(B_xt)))
                J["x"][j] = (xt, B_xt)

            def begin(self, src_d, ti, B_src):
                ht_, B_ht_ = self.hts[self.hi % len(self.hts)]
                self.hi += 1
                J = dict(src=src_d, ti=ti, B_src=B_src, ht=ht_, B_ht=B_ht_, x={}, st={})
                if self.nxin >= 4:
                    for j in range(4):
                        self._load(J, j)
                return J

            def piece_a(self, J, j):
                if j not in J["x"]:
                    self._load(J, j)
                xt, B_xt = J["x"][j]
                st, B_st = self.stat.next()
                J["st"][j] = (st, B_st)
                sc.op("act", lambda e, xt=xt, st=st: e.activation(
                    out=self.sqj[:], in_=xt[:], func=AF.Square, accum_out=st[:, 0:1]),
                    reads=[B_xt], writes=[B_st, self.B_sqj])
                sc.op("dve", lambda e, st=st: e.tensor_scalar(
                    out=st[:, 1:2], in0=st[:, 0:1], scalar1=float(D * EPS), scalar2=None, op0=ALU.add),
                    reads=[B_st], writes=[B_st])

            def piece_b(self, J, j):
                st, B_st = J["st"][j]
                sc.op("act", lambda e, st=st: e.activation(out=st[:, 1:2], in_=st[:, 1:2], func=AF.Sqrt),
                      reads=[B_st], writes=[B_st])
                sc.op("dve", lambda e, st=st: e.reciprocal(out=st[:, 1:2], in_=st[:, 1:2]),
                      reads=[B_st], writes=[B_st])

            def piece_cH(self, J, j):
                xt, B_xt = J["x"][j]
                st, B_st = J["st"][j]
                hr, B_hr = self.hrow.next()
                J.setdefault("h", {})[j] = (hr, B_hr)
                sc.op("dve", lambda e, hr=hr, xt=xt, st=st: e.tensor_scalar(
                    out=hr[:], in0=xt[:], scalar1=st[:, 1:2], scalar2=32.0, op0=ALU.mult, op1=ALU.mult),
                    reads=[B_xt, B_st], writes=[B_hr])

            def piece_cT(self, J, j):
                hr, B_hr = J["h"][j]
                ht_, B_ht_ = J["ht"], J["B_ht"]
                tp, B_tp = self.tpp.next()
                for k in range(8):
                    sc.op("pe", lambda e, tp=tp, hr=hr, k=k: e.transpose(
                        out=tp[:, k, :], in_=hr[:, k * 128:(k + 1) * 128], identity=ident),
                        reads=[B_hr, B_cst], writes=[B_tp])
                eng = ("dve", "act")[self.ev % 2]
                self.ev += 1
                if eng == "dve":
                    sc.op("dve", lambda e, tp=tp, j=j, ht_=ht_: e.tensor_copy(
                        out=ht_[:, :, j * 128:(j + 1) * 128], in_=tp[:]),
                        reads=[B_tp], writes=[B_ht_[j]])
                else:
                    sc.op("act", lambda e, tp=tp, j=j, ht_=ht_: e.copy(
                        out=ht_[:, :, j * 128:(j + 1) * 128], in_=tp[:]),
                        reads=[B_tp], writes=[B_ht_[j]])

            def piece_c(self, J, j):
                self.piece_cH(J, j)
                self.piece_cT(J, j)

            def pieces(self, J):
                order = [("a", 0), ("a", 1), ("b", 0), ("a", 2), ("b", 1), ("c", 0), ("a", 3), ("b", 2),
                         ("c", 1), ("b", 3), ("c", 2), ("c", 3)]
                fns = {"a": self.piece_a, "b": self.piece_b, "c": self.piece_c}
                return [(lambda f=fns[k], j=j: f(J, j)) for (k, j) in order]

            def run(self, src_d, ti, B_src):
                J = self.begin(src_d, ti, B_src)
                for j in range(4):
                    self.piece_a(J, j)
                    self.piece_b(J, j)
                    self.piece_c(J, j)
                return J["ht"], J["B_ht"]

        def wtoks(toks, col, w, blk=640):
            return [toks[b] for b in range(col // blk, (col + w - 1) // blk + 1)]

        def load_cast_cols(stgp, dst_t, toks, src_d, nrow_chunks, ncols, gain_i=None, blk=640):
            for b in range((ncols + blk - 1) // blk):
                c0, c1 = b * blk, min(ncols, (b + 1) * blk)
                for k in range(nrow_chunks):
                    load_cast(stgp, dst_t, lambda a, bb, k=k, c0=c0: dst_t[:, k, c0 + a:c0 + bb], toks[b],
                              src_d[k * 128:(k + 1) * 128, c0:c1], c1 - c0,
                              gain=None if gain_i is None else gcol[:, gain_i, k:k + 1])

        def phase_ffn(layer, src_d, B_srcs, dst_d, B_dsts, final):
            with ExitStack() as pes:
                w1b = sbt(pes, "w1b", [128, 8, DFF], BF16)
                w3b = sbt(pes, "w3b", [128, 8, DFF], BF16)
                w2b = sbt(pes, "w2b", [128, NFF, D], BF16)
                NBLK = (DFF + 639) // 640
                B_w1 = [Buf("w1_%d" % b) for b in range(NBLK)]
                B_w3 = [Buf("w3_%d" % b) for b in range(NBLK)]
                B_w2 = [Buf("w2_%d" % c) for c in range(NFF)]
                nt = NormT(pes, nhrow=4)
                nxt = nt.run(src_d, 0, B_srcs[0] if B_srcs else None)
                stgp = Pool(nc, pes, "stg", 4, [128, STGW], F32)
                gi = 0 if layer == 0 else 2
                for b in range(NBLK):
                    c0, c1 = b * 640, min(DFF, (b + 1) * 640)
                    for (wb, wd, toks) in ((w1b, w1_d[layer], B_w1), (w3b, w3_d[layer], B_w3)):
                        for k in range(8):
                            load_cast(stgp, wb, lambda a, bb, k=k, c0=c0, wb=wb: wb[:, k, c0 + a:c0 + bb], toks[b],
                                      wd[k * 128:(k + 1) * 128, c0:c1], c1 - c0, gain=gcol[:, gi, k:k + 1])
                for c in range(NFF):
                    load_cast(stgp, w2b, lambda a, b, c=c: w2b[:, c, a:b], B_w2[c],
                              w2_d[layer][c * 128:(c + 1) * 128, :], D)
                G = sbt(pes, "G", [128, NFF, TT], BF16)
                B_G = [Buf("G%d" % c) for c in range(NFF)]
                sgp = Pool(nc, pes, "sg", 2, [128, TT], F32)
                xres = Pool(nc, pes, "xres", 2, [128, D], F32)
                ps_up = Pool(nc, pes, "psu", 4, [128, 512], F32, psum=True)
                ps_dn = Pool(nc, pes, "psd", 2, [128, 512], F32, psum=True)
                if final:
                    gfb = sbt(pes, "gfb", [128, D], F32)
                    B_gfb = Buf("gfb")
                    sc.dma("sp", [(gfb[:], gfin_d.partition_broadcast(128))], writes=[B_gfb], key="gfb")
                    sc.op("dve", lambda e: e.tensor_scalar(out=gfb[:], in0=gfb[:], scalar1=32.0, scalar2=None,
                                                           op0=ALU.mult), reads=[B_gfb], writes=[B_gfb])
                    fstat = Pool(nc, pes, "fstat", 4, [128, 2], F32)

                SPR = {2: ("a", 0), 4: ("a", 1), 5: ("b", 0), 7: ("b", 1), 8: ("cH", 0), 9: ("a", 2), 10: ("cH", 1),
                       11: ("a", 3), 12: ("b", 2), 14: ("b", 3), 15: ("cH", 2), 17: ("cH", 3)}
                for ti in range(NT):
                    ht, B_ht = nxt
                    Jn = nt.begin(src_d, ti + 1, B_srcs[ti + 1] if B_srcs else None) if ti + 1 < NT else None
                    for c in range(NFF):
                        if Jn is not None and c in SPR:
                            kind, jj = SPR[c]
                            {"a": nt.piece_a, "b": nt.piece_b, "cH": nt.piece_cH}[kind](Jn, jj)
                        pu, B_pu = ps_up.next()
                        pv, B_pv = ps_up.next()
                        for k in range(8):
                            sc.op("pe", lambda e, pu=pu, k=k, c=c, ht=ht: e.matmul(
                                pu[:], lhsT=w1b[:, k, c * 128:(c + 1) * 128], rhs=ht[:, k, :],
                                start=(k == 0), stop=(k == 7)), reads=wtoks(B_w1, c * 128, 128) + B_ht, writes=[B_pu])
                        for k in range(8):
                            sc.op("pe", lambda e, pv=pv, k=k, c=c, ht=ht: e.matmul(
                                pv[:], lhsT=w3b[:, k, c * 128:(c + 1) * 128], rhs=ht[:, k, :],
                                start=(k == 0), stop=(k == 7)), reads=wtoks(B_w3, c * 128, 128) + B_ht, writes=[B_pv])
                        sg, B_sg = sgp.next()
                        sc.op("act", lambda e, sg=sg, pu=pu: e.activation(out=sg[:], in_=pu[:], func=AF.Silu),
                              reads=[B_pu], writes=[B_sg])
                        sc.op("dve", lambda e, sg=sg, pv=pv, c=c: e.tensor_tensor(
                            out=G[:, c, :], in0=sg[:], in1=pv[:], op=ALU.mult),
                            reads=[B_sg, B_pv], writes=[B_G[c]])
                    if Jn is not None:
                        for jj in range(4):
                            nt.piece_cT(Jn, jj)
                        nxt = (Jn["ht"], Jn["B_ht"])
                    for j in range(4):
                        xr, B_xr = xres.next()
                        r0 = ti * TT + j * 128
                        sc.dma("sp", [(xr[:], src_d[r0:r0 + 128, :])], reads=[B_srcs[ti]] if B_srcs else [],
                               writes=[B_xr], key=("xres", id(B_xr)))
                        for half in range(2):
                            pd, B_pd = ps_dn.next()
                            for c in range(NFF):
                                sc.op("pe", lambda e, pd=pd, c=c, j=j, half=half: e.matmul(
                                    pd[:], lhsT=G[:, c, j * 128:(j + 1) * 128],
                                    rhs=w2b[:, c, half * 512:(half + 1) * 512],
                                    start=(c == 0), stop=(c == NFF - 1)),
                                    reads=[B_G[c], B_w2[c]], writes=[B_pd])
                            sc.op("dve", lambda e, pd=pd, xr=xr, half=half: e.scalar_tensor_tensor(
                                out=xr[:, half * 512:(half + 1) * 512], in0=pd[:], scalar=0.5,
                                in1=xr[:, half * 512:(half + 1) * 512], op0=ALU.mult, op1=ALU.add),
                                reads=[B_pd, B_xr], writes=[B_xr])
                        if final:
                            fs, B_fs = fstat.next()
                            sc.op("act", lambda e, xr=xr, fs=fs: e.activation(
                                out=nt.sqj[:], in_=xr[:], func=AF.Square, accum_out=fs[:, 0:1]),
                                reads=[B_xr], writes=[B_fs, nt.B_sqj])
                            rstd_ops(fs, B_fs)
                            sc.op("dve", lambda e, xr=xr, fs=fs: e.scalar_tensor_tensor(
                                out=xr[:], in0=xr[:], scalar=fs[:, 1:2], in1=gfb[:], op0=ALU.mult, op1=ALU.mult),
                                reads=[B_xr, B_fs, B_gfb], writes=[B_xr])
                        sc.dma("sp", [(dst_d[r0:r0 + 128, :], xr[:])], reads=[B_xr],
                               writes=[B_dsts[ti]] if B_dsts else [], key=("xres_st", id(B_xr)))
            sc.barrier()

        def phase_proj():
            NQ = C_GA
            with ExitStack() as pes:
                wq = sbt(pes, "wq", [128, 8, NQ], BF16)
                B_wq = [Buf("wq%d" % b) for b in range((NQ + 639) // 640)]
                nt = NormT(pes, nht=2, nxin=4)
                nxt = nt.run(x1_d, 0, B_x1[0])
                stgp = Pool(nc, pes, "stg", 8, [128, STGW], F32)
                load_cast_cols(stgp, wq, B_wq, win_d, 8, NQ, gain_i=1)
                ps = Pool(nc, pes, "ps", 4, [128, 512], F32, psum=True)
                evp = Pool(nc, pes, "ev", 6, [128, 512], BF16)
                fm = []
                for g in range(4):
                    fm.append((C_QA + g * 128, lambda ti, g=g: qta_d[g, :, ti * TT:(ti + 1) * TT], 0.125))
                fm.append((C_KA, lambda ti: kta_d[:, ti * TT:(ti + 1) * TT], 1.0))
                for cc in range(4):
                    fm.append((C_QB + cc * 128, lambda ti, cc=cc: qtb_d[cc * 128:(cc + 1) * 128, ti * TT:(ti + 1) * TT], 0.125))
                for cc in range(4):
                    fm.append((C_KB + cc * 128, lambda ti, cc=cc: ktb_d[cc * 128:(cc + 1) * 128, ti * TT:(ti + 1) * TT], 1.0))
                evi = 0
                for ti in range(NT):
                    ht, B_ht = nxt
                    pcs = []
                    if ti + 1 < NT:
                        J = nt.begin(x1_d, ti + 1, B_x1[ti + 1])
                        nxt = (J["ht"], J["B_ht"])
                        pcs = nt.pieces(J)
                    for (col, dst, scale) in fm:
                        pp, B_pp = ps.next()
                        for k in range(8):
                            sc.op("pe", lambda e, pp=pp, k=k, col=col, ht=ht: e.matmul(
                                pp[:], lhsT=wq[:, k, col:col + 128], rhs=ht[:, k, :],
                                start=(k == 0), stop=(k == 7)), reads=wtoks(B_wq, col, 128) + B_ht, writes=[B_pp])
                        ev, B_ev = evp.next()
                        if evi % 2 == 0:
                            sc.op("dve", lambda e, ev=ev, pp=pp, scale=scale: e.tensor_scalar(
                                out=ev[:], in0=pp[:], scalar1=float(scale), scalar2=None, op0=ALU.mult),
                                reads=[B_pp], writes=[B_ev])
                        else:
                            sc.op("act", lambda e, ev=ev, pp=pp, scale=scale: e.activation(
                                out=ev[:], in_=pp[:], func=AF.Copy, scale=float(scale)),
                                reads=[B_pp], writes=[B_ev])
                        evi += 1
                        sc.dma("sp", [(dst(ti), ev[:])], reads=[B_ev], writes=[B_qkv[ti]], key=("ev", id(B_ev)))
                        if pcs:
                            pcs.pop(0)()
                    for j in range(4):
                        r0 = ti * TT + j * 128
                        pp, B_pp = ps.next()
                        for k in range(8):
                            sc.op("pe", lambda e, pp=pp, k=k, j=j, ht=ht: e.matmul(
                                pp[:], lhsT=ht[:, k, j * 128:(j + 1) * 128], rhs=wq[:, k, C_VB:C_VB + 512],
                                start=(k == 0), stop=(k == 7)), reads=wtoks(B_wq, C_VB, 512) + B_ht, writes=[B_pp])
                        ev, B_ev = evp.next()
                        sc.op("dve", lambda e, ev=ev, pp=pp: e.tensor_copy(out=ev[:], in_=pp[:]),
                              reads=[B_pp], writes=[B_ev])
                        sc.dma("sp", [(vb_d[r0:r0 + 128, :], ev[:])], reads=[B_ev], writes=[B_qkv[ti]],
                               key=("ev", id(B_ev)))
                        pp, B_pp = ps.next()
                        for k in range(8):
                            sc.op("pe", lambda e, pp=pp, k=k, j=j, ht=ht: e.matmul(
                                pp[:, 0:128], lhsT=ht[:, k, j * 128:(j + 1) * 128], rhs=wq[:, k, C_VA:C_VA + 128],
                                start=(k == 0), stop=(k == 7)), reads=wtoks(B_wq, C_VA, 128) + B_ht, writes=[B_pp])
                        ev, B_ev = evp.next()
                        sc.op("act", lambda e, ev=ev, pp=pp: e.copy(out=ev[:, 0:128], in_=pp[:, 0:128]),
                              reads=[B_pp], writes=[B_ev])
                        sc.dma("sp", [(va_d[r0:r0 + 128, :], ev[:, 0:128])], reads=[B_ev], writes=[B_qkv[ti]],
                               key=("ev", id(B_ev)))
                        if pcs:
                            pcs.pop(0)()
                    while pcs:
                        pcs.pop(0)()
            sc.barrier()

        def phase_swa():
            with ExitStack() as pes:
                kt = sbt(pes, "kta", [128, S], BF16)
                qt = sbt(pes, "qta", [128, 4, S], BF16)
                vv = sbt(pes, "vva", [128, NB, 128], BF16)
                B_in = Buf("swa_in")
                var = va_d.rearrange("(n p) c -> p n c", p=128)
                sc.dma("sp", [(kt[:], kta_d)] + [(qt[:, g, :], qta_d[g]) for g in range(4)]
                       + [(vv[:, n0:min(NB, n0 + 8), :], var[:, n0:min(NB, n0 + 8), :]) for n0 in range(0, NB, 8)],
                       reads=B_qkv, writes=[B_in], key="swa_in")
                bm = sbt(pes, "bm", [128, 2, 4, 256], F32)
                mk = sbt(pes, "mk", [128, 256], F32)
                sk = sbt(pes, "sk", [128, 2, 4], F32)
                B_bm = Buf("bm")
                B_sk = Buf("sk")
                bsrc = bias_d.rearrange("q (g kv) k -> q kv g k", kv=2)
                sc.dma("sp", [(bm[:, kv, :, :], bsrc[:, kv, :, :]) for kv in range(2)] + [(mk[:], maskc_d)],
                       writes=[B_bm], key="bm")
                sc.dma("sp", [(sk[:, kv, :], sinks2_d[kv].partition_broadcast(128)) for kv in range(2)],
                       writes=[B_sk], key="sk")
                for kv in range(2):
                    for g in range(4):
                        sc.op("dve", lambda e, kv=kv, g=g: e.tensor_tensor(
                            out=bm[:, kv, g, :], in0=bm[:, kv, g, :], in1=mk[:], op=ALU.add),
                            reads=[B_bm], writes=[B_bm])
                psc = Pool(nc, pes, "psc", 2, [128, 4, 256], F32, psum=True)
                ppt = Pool(nc, pes, "ppt", 2, [128, 8, 128], BF16, psum=True)
                pso = Pool(nc, pes, "pso", 2, [128, 4, 128], F32, psum=True)
                scs = Pool(nc, pes, "scs", 2, [128, 4, 256], F32)
                pbf = Pool(nc, pes, "pbf", 3, [128, 4, 256], BF16)
                ptb = Pool(nc, pes, "ptb", 2, [128, 8, 128], BF16)
                sm = Pool(nc, pes, "sm", 6, [128, 5, 4], F32)
                osb = Pool(nc, pes, "osb", 3, [128, 4, 128], BF16)
                units = [dict(n=n, kv=kv) for n in range(NB) for kv in range(2)]
                cur_ob = {}
                evc = [0]

                def stA(u):
                    n, kv = u["n"], u["kv"]
                    k0 = 0 if n > 0 else 128
                    kw = 256 - k0
                    ks = (n - 1) * 128 + k0
                    u.update(k0=k0, kw=kw)
                    pc, B_pc = psc.next()
                    u["pc"] = (pc, B_pc)
                    for g in range(4):
                        sc.op("pe", lambda e, pc=pc, g=g, kv=kv, n=n, ks=ks, kw=kw, k0=k0: e.matmul(
                            pc[:, g, k0:256], lhsT=qt[kv * 64:(kv + 1) * 64, g, n * 128:(n + 1) * 128],
                            rhs=kt[kv * 64:(kv + 1) * 64, ks:ks + kw], start=True, stop=True),
                            reads=[B_in], writes=[B_pc])

                def stB1(u):
                    kv, k0 = u["kv"], u["k0"]
                    pc, B_pc = u["pc"]
                    ss, B_ss = scs.next()
                    st, B_st = sm.next()
                    pb, B_pb = pbf.next()
                    u.update(st=(st, B_st), pb=(pb, B_pb))
                    sc.op("dve", lambda e, pc=pc, ss=ss, kv=kv, k0=k0: e.tensor_tensor(
                        out=ss[:, :, k0:256], in0=pc[:, :, k0:256], in1=bm[:, kv, :, k0:256], op=ALU.add),
                        reads=[B_pc, B_bm], writes=[B_ss])
                    sc.op("dve", lambda e, ss=ss, st=st, k0=k0: e.tensor_reduce(
                        out=st[:, 0, :], in_=ss[:, :, k0:256], axis=AX.X, op=ALU.max),
                        reads=[B_ss], writes=[B_st])
                    sc.op("dve", lambda e, st=st, kv=kv: e.tensor_tensor(
                        out=st[:, 0, :], in0=st[:, 0, :], in1=sk[:, kv, :], op=ALU.max),
                        reads=[B_st, B_sk], writes=[B_st])
                    sc.op("dve", lambda e, st=st: e.tensor_scalar(out=st[:, 1, :], in0=st[:, 0, :], scalar1=-1.0,
                                                                   scalar2=None, op0=ALU.mult),
                          reads=[B_st], writes=[B_st])
                    sc.op("dve", lambda e, st=st, kv=kv: e.tensor_tensor(
                        out=st[:, 2, :], in0=sk[:, kv, :], in1=st[:, 1, :], op=ALU.add),
                        reads=[B_st, B_sk], writes=[B_st])
                    for g in range(4):
                        sc.op("act", lambda e, pb=pb, ss=ss, st=st, g=g, k0=k0: e.activation(
                            out=pb[:, g, k0:256], in_=ss[:, g, k0:256], func=AF.Exp, bias=st[:, 1, g:g + 1],
                            accum_out=st[:, 3, g:g + 1]), reads=[B_ss, B_st], writes=[B_pb, B_st])
                    sc.op("act", lambda e, st=st: e.activation(out=st[:, 2, :], in_=st[:, 2, :], func=AF.Exp),
                          reads=[B_st], writes=[B_st])

                def stB2(u):
                    n, kv, k0, kw = u["n"], u["kv"], u["k0"], u["kw"]
                    st, B_st = u["st"]
                    pb, B_pb = u["pb"]
                    sc.op("dve", lambda e, st=st: e.tensor_tensor(out=st[:, 4, :], in0=st[:, 3, :], in1=st[:, 2, :], op=ALU.add),
                          reads=[B_st], writes=[B_st])
                    sc.op("dve", lambda e, st=st: e.reciprocal(out=st[:, 4, :], in_=st[:, 4, :]),
                          reads=[B_st], writes=[B_st])
                    sc.op("pool", lambda e, pb=pb, st=st, k0=k0, kw=kw: e.tensor_tensor(
                        out=pb[:, :, k0:256], in0=pb[:, :, k0:256],
                        in1=st[:, 4, :].unsqueeze(2).to_broadcast([128, 4, kw]), op=ALU.mult),
                        reads=[B_pb, B_st], writes=[B_pb])
                    nkb = kw // 128
                    pp, B_pp = ppt.next()
                    pt, B_pt = ptb.next()
                    for kb in range(nkb):
                        for g in range(4):
                            sc.op("pe", lambda e, pp=pp, pb=pb, g=g, kb=kb, k0=k0: e.transpose(
                                out=pp[:, kb * 4 + g, :], in_=pb[:, g, k0 + kb * 128:k0 + (kb + 1) * 128], identity=ident),
                                reads=[B_pb, B_cst], writes=[B_pp])
                    evc[0] += 1
                    if evc[0] % 2 == 0:
                        sc.op("dve", lambda e, pp=pp, pt=pt, nkb=nkb: e.tensor_copy(
                            out=pt[:, 0:4 * nkb, :], in_=pp[:, 0:4 * nkb, :]), reads=[B_pp], writes=[B_pt])
                    else:
                        sc.op("act", lambda e, pp=pp, pt=pt, nkb=nkb: e.copy(
                            out=pt[:, 0:4 * nkb, :], in_=pp[:, 0:4 * nkb, :]), reads=[B_pp], writes=[B_pt])
                    po, B_po = pso.next()
                    for g in range(4):
                        h = g + 4 * kv
                        cl, half = (h // 2) - 2 * kv, h % 2
                        for kb in range(nkb):
                            blk = n - (nkb - 1) + kb
                            sc.op("pe", lambda e, po=po, pt=pt, cl=cl, half=half, kv=kv, g=g, kb=kb, blk=blk, nkb=nkb: e.matmul(
                                po[half * 64:(half + 1) * 64, cl, :], lhsT=vv[:, blk, kv * 64:(kv + 1) * 64],
                                rhs=pt[:, kb * 4 + g, :], start=(kb == 0), stop=(kb == nkb - 1)),
                                reads=[B_pt, B_in], writes=[B_po])
                    if kv == 0:
                        cur_ob[n] = osb.next()
                    ob, B_ob = cur_ob[n]
                    sc.op("act", lambda e, ob=ob, po=po, kv=kv: e.copy(out=ob[:, 2 * kv:2 * kv + 2, :], in_=po[:, 0:2, :]),
                          reads=[B_po], writes=[B_ob])
                    if kv == 1:
                        sc.dma("sp", [(ota_d.rearrange("(c p) s -> p c s", p=128)[:, :, n * 128:(n + 1) * 128], ob[:])],
                               reads=[B_ob], writes=[B_ota[n // 4]], key=("osb", id(B_ob)))
                        del cur_ob[n]

                nu = len(units)
                for t in range(nu + 2):
                    if t < nu:
                        stA(units[t])
                    if 0 <= t - 1 < nu:
                        stB1(units[t - 1])
                    if 0 <= t - 2 < nu:
                        stB2(units[t - 2])
            sc.barrier()

        def phase_sb():
            with ExitStack() as pes:
                ktp = Pool(nc, pes, "ktb", 2, [128, S], BF16)
                qtp = Pool(nc, pes, "qtb", 2, [128, S], BF16)
                vvp = Pool(nc, pes, "vvb", 2, [128, NB, 128], BF16)
                zp = Pool(nc, pes, "zp", 3, [128, 2, 512], F32, psum=True)
                op_ = Pool(nc, pes, "op", 2, [128, 512], F32, psum=True)
                Ep = Pool(nc, pes, "E", 2, [128, 2, 512], F32)
                Lp = Pool(nc, pes, "L", 3, [128, 2, 512], BF16)
                Ap = Pool(nc, pes, "A", 2, [128, 2, 512], BF16)
                R32 = sbt(pes, "R32", [128, 2, 512], F32)
                B_R32 = Buf("R32")
                Rbp = Pool(nc, pes, "Rb", 2, [128, 2, 512], BF16)
                oev = Pool(nc, pes, "oev", 2, [128, 512], BF16)
                steps = []
                pair_in = {}
                for p in range(4):
                    for i in range(NT):
                        nk = 4 * i + 4
                        for si, kj in enumerate(range(nk - 1, -1, -1)):
                            steps.append(dict(p=p, i=i, kj=kj, first=(si == 0), last=(kj == 0),
                                              c0=max(0, (kj - 4 * i)) * 128, diag=(kj >= 4 * i)))

                def load_pair(p):
                    kt, B_kt = ktp.next()
                    qt, B_qt = qtp.next()
                    vv, B_vv = vvp.next()
                    sc.dma("sp", [(kt[:], ktb_d[p * 128:(p + 1) * 128, :])], reads=B_qkv, writes=[B_kt], key=("sbk", id(B_kt)))
                    sc.dma("sp", [(qt[:], qtb_d[p * 128:(p + 1) * 128, :])], reads=B_qkv, writes=[B_qt], key=("sbq", id(B_qt)))
                    vbr = vb_d[:, p * 128:(p + 1) * 128].rearrange("(n p) c -> p n c", p=128)
                    sc.dma("sp", [(vv[:, n0:min(NB, n0 + 8), :], vbr[:, n0:min(NB, n0 + 8), :]) for n0 in range(0, NB, 8)],
                           reads=B_qkv, writes=[B_vv], key=("sbv", id(B_vv)))
                    pair_in[p] = (kt, B_kt, qt, B_qt, vv, B_vv)

                load_pair(0)
                state = {}

                def stage0(s):
                    p, i, kj, c0 = s["p"], s["i"], s["kj"], s["c0"]
                    if s["first"] and i == 0 and p + 1 < 4:
                        load_pair(p + 1)
                    kt, B_kt, qt, B_qt, vv, B_vv = pair_in[p]
                    z, B_z = zp.next()
                    s["z"], s["B_z"] = z, B_z
                    for hh in range(2):
                        sc.op("pe", lambda e, z=z, hh=hh, kj=kj, i=i, c0=c0, kt=kt, qt=qt: e.matmul(
                            z[:, hh, c0:512], lhsT=kt[hh * 64:(hh + 1) * 64, kj * 128:(kj + 1) * 128],
                            rhs=qt[hh * 64:(hh + 1) * 64, i * 512 + c0:(i + 1) * 512],
                            start=True, stop=False, skip_group_check=True),
                            reads=[B_kt, B_qt], writes=[B_z])
                    if s["diag"]:
                        for hh in range(2):
                            sc.op("pe", lambda e, z=z, hh=hh, c0=c0: e.matmul(
                                z[:, hh, c0:c0 + 128], lhsT=ident, rhs=dmask, start=False, stop=False,
                                skip_group_check=True), reads=[B_cst], writes=[B_z])

                def stage1(s):
                    z, B_z, c0 = s["z"], s["B_z"], s["c0"]
                    E, B_E = Ep.next()
                    L, B_L = Lp.next()
                    s["L"], s["B_L"] = L, B_L
                    sc.op("act", lambda e, E=E, z=z, c0=c0: e.activation(out=E[:, :, c0:512], in_=z[:, :, c0:512], func=AF.Exp),
                          reads=[B_z], writes=[B_E])
                    sc.op("act", lambda e, E=E, L=L, c0=c0: e.activation(out=L[:, :, c0:512], in_=E[:, :, c0:512], func=AF.Ln, bias=1.0),
                          reads=[B_E], writes=[B_L])

                def stage2(s):
                    p, i, kj, c0 = s["p"], s["i"], s["kj"], s["c0"]
                    kt, B_kt, qt, B_qt, vv, B_vv = pair_in[p]
                    z, B_z, L, B_L = s["z"], s["B_z"], s["L"], s["B_L"]
                    for hh in range(2):
                        sc.op("pe", lambda e, z=z, hh=hh, c0=c0, L=L: e.matmul(
                            z[:, hh, c0:512], lhsT=ntri, rhs=L[:, hh, c0:512], start=False, stop=False,
                            skip_group_check=True), reads=[B_L, B_cst], writes=[B_z])
                    if not s["first"]:
                        Rb, B_Rb = state["Rb"]
                        for hh in range(2):
                            sc.op("pe", lambda e, z=z, hh=hh, c0=c0, Rb=Rb: e.matmul(
                                z[:, hh, c0:512], lhsT=nones, rhs=Rb[:, hh, c0:512], start=False, stop=False,
                                skip_group_check=True), reads=[B_Rb, B_cst], writes=[B_z])
                    A, B_A = Ap.next()
                    sc.op("act", lambda e, A=A, z=z, c0=c0: e.activation(out=A[:, :, c0:512], in_=z[:, :, c0:512], func=AF.Exp),
                          reads=[B_z], writes=[B_A])
                    if s["first"]:
                        state["o"] = op_.next()
                    o, B_o = state["o"]
                    for hh in range(2):
                        sc.op("pe", lambda e, o=o, hh=hh, c0=c0, A=A, vv=vv, kj=kj, first=s["first"]: e.matmul(
                            o[hh * 64:(hh + 1) * 64, c0:512], lhsT=vv[:, kj, hh * 64:(hh + 1) * 64],
                            rhs=A[:, hh, c0:512], start=first, stop=False, skip_group_check=True),
                            reads=[B_A, B_vv], writes=[B_o])
                    if not s["last"]:
                        if s["first"]:
                            sc.op("pool", lambda e: e.memset(R32[:], 0.0), writes=[B_R32])
                        sc.op("pool", lambda e, L=L, c0=c0: e.tensor_tensor(
                            out=R32[:, :, c0:512], in0=R32[:, :, c0:512], in1=L[:, :, c0:512], op=ALU.add),
                            reads=[B_L, B_R32], writes=[B_R32])
                        Rb, B_Rb = Rbp.next()
                        sc.op("dve", lambda e, Rb=Rb: e.tensor_copy(out=Rb[:], in_=R32[:]), reads=[B_R32], writes=[B_Rb])
                        state["Rb"] = (Rb, B_Rb)
                    else:
                        ob, B_ob = oev.next()
                        sc.op("dve", lambda e, ob=ob, o=o: e.tensor_copy(out=ob[:], in_=o[:]), reads=[B_o], writes=[B_ob])
                        sc.dma("sp", [(otb_d[p * 128:(p + 1) * 128, i * 512:(i + 1) * 512], ob[:])], reads=[B_ob],
                               writes=[B_otb[i]], key=("oev", id(B_ob)))

                n = len(steps)
                for t in range(n + 2):
                    if t < n:
                        stage0(steps[t])
                    if 0 <= t - 1 < n:
                        stage1(steps[t - 1])
                    if 0 <= t - 2 < n:
                        stage2(steps[t - 2])
            sc.barrier()

        def phase_sb2(K=8):
            with ExitStack() as pes:
                per_pair = sum(4 * i + 4 for i in range(NT))
                nsets = 2 if per_pair >= 4 * K else 4
                ktp = Pool(nc, pes, "ktb", nsets, [128, S], BF16)
                qtp = Pool(nc, pes, "qtb", nsets, [128, S], BF16)
                vvp = Pool(nc, pes, "vvb", nsets, [128, NB, 128], BF16)
                zp = Pool(nc, pes, "zp", 3, [128, 2, 512], F32, psum=True)
                op_ = Pool(nc, pes, "op", 2, [128, 512], F32, psum=True)
                Lp = Pool(nc, pes, "L", 2 * K + 2, [128, 2, 512], BF16)
                Rbp = Pool(nc, pes, "Rb", 2 * K + 2, [128, 2, 512], BF16)
                Ap = Pool(nc, pes, "A", K + 3, [128, 2, 512], BF16)
                R32 = sbt(pes, "R32", [128, 2, 512], F32)
                B_R32 = Buf("R32")
                oev = Pool(nc, pes, "oev", 2, [128, 512], BF16)
                steps = []
                pair_in = {}
                for p in range(4):
                    for i in range(NT):
                        nk = 4 * i + 4
                        for si, kj in enumerate(range(nk - 1, -1, -1)):
                            steps.append(dict(p=p, i=i, kj=kj, first=(si == 0), last=(kj == 0),
                                              c0=max(0, (kj - 4 * i)) * 128, diag=(kj >= 4 * i)))
                for a, b in zip(steps[:-1], steps[1:]):
                    b["prev"] = a

                def load_pair(p):
                    kt, B_kt = ktp.next()
                    qt, B_qt = qtp.next()
                    vv, B_vv = vvp.next()
                    sc.dma("sp", [(kt[:], ktb_d[p * 128:(p + 1) * 128, :])], reads=B_qkv, writes=[B_kt], key=("sbk", id(B_kt)))
                    sc.dma("sp", [(qt[:], qtb_d[p * 128:(p + 1) * 128, :])], reads=B_qkv, writes=[B_qt], key=("sbq", id(B_qt)))
                    vbr = vb_d[:, p * 128:(p + 1) * 128].rearrange("(n p) c -> p n c", p=128)
                    sc.dma("sp", [(vv[:, n0:min(NB, n0 + 8), :], vbr[:, n0:min(NB, n0 + 8), :]) for n0 in range(0, NB, 8)],
                           reads=B_qkv, writes=[B_vv], key=("sbv", id(B_vv)))
                    pair_in[p] = (kt, B_kt, qt, B_qt, vv, B_vv)

                for p_ in range(nsets):
                    load_pair(p_)
                state = {}

                def zmm(s_, z, B_z):
                    p, i, kj, c0 = s_["p"], s_["i"], s_["kj"], s_["c0"]
                    kt, B_kt, qt, B_qt, vv, B_vv = pair_in[p]
                    for hh in range(2):
                        sc.op("pe", lambda e, z=z, hh=hh, kj=kj, i=i, c0=c0, kt=kt, qt=qt: e.matmul(
                            z[:, hh, c0:512], lhsT=kt[hh * 64:(hh + 1) * 64, kj * 128:(kj + 1) * 128],
                            rhs=qt[hh * 64:(hh + 1) * 64, i * 512 + c0:(i + 1) * 512],
                            start=True, stop=False, skip_group_check=True),
                            reads=[B_kt, B_qt], writes=[B_z])
                    if s_["diag"]:
                        for hh in range(2):
                            sc.op("pe", lambda e, z=z, hh=hh, c0=c0: e.matmul(
                                z[:, hh, c0:c0 + 128], lhsT=ident, rhs=dmask, start=False, stop=False,
                                skip_group_check=True), reads=[B_cst], writes=[B_z])

                def Xstep(s_):
                    p, i, c0 = s_["p"], s_["i"], s_["c0"]
                    z, B_z = zp.next()
                    zmm(s_, z, B_z)
                    L, B_L = Lp.next()
                    s_["L"], s_["B_L"] = L, B_L
                    sc.op("act", lambda e, L=L, z=z, c0=c0: e.activation(
                        out=L[:, :, c0:512], in_=z[:, :, c0:512], func=AF.Softplus), reads=[B_z], writes=[B_L])
                    if not s_["last"]:
                        if s_["first"]:
                            sc.op("pool", lambda e: e.memset(R32[:], 0.0), writes=[B_R32])
                        sc.op("dve", lambda e, L=L, c0=c0: e.tensor_tensor(
                            out=R32[:, :, c0:512], in0=R32[:, :, c0:512], in1=L[:, :, c0:512], op=ALU.add),
                            reads=[B_L, B_R32], writes=[B_R32])
                        Rb, B_Rb = Rbp.next()
                        sc.op("dve", lambda e, Rb=Rb: e.tensor_copy(out=Rb[:], in_=R32[:]), reads=[B_R32], writes=[B_Rb])
                        s_["Rb"] = (Rb, B_Rb)

                def Ystep(s_):
                    c0 = s_["c0"]
                    L, B_L = s_["L"], s_["B_L"]
                    z, B_z = zp.next()
                    zmm(s_, z, B_z)
                    for hh in range(2):
                        sc.op("pe", lambda e, z=z, hh=hh, c0=c0, L=L: e.matmul(
                            z[:, hh, c0:512], lhsT=ntri, rhs=L[:, hh, c0:512], start=False, stop=False,
                            skip_group_check=True), reads=[B_L, B_cst], writes=[B_z])
                    if not s_["first"]:
                        Rb, B_Rb = s_["prev"]["Rb"]
                        for hh in range(2):
                            sc.op("pe", lambda e, z=z, hh=hh, c0=c0, Rb=Rb: e.matmul(
                                z[:, hh, c0:512], lhsT=nones, rhs=Rb[:, hh, c0:512], start=False, stop=False,
                                skip_group_check=True), reads=[B_Rb, B_cst], writes=[B_z])
                    A, B_A = Ap.next()
                    s_["A"] = (A, B_A)
                    sc.op("act", lambda e, A=A, z=z, c0=c0: e.activation(out=A[:, :, c0:512], in_=z[:, :, c0:512], func=AF.Exp),
                          reads=[B_z], writes=[B_A])

                def AVstep(s_):
                    p, i, kj, c0 = s_["p"], s_["i"], s_["kj"], s_["c0"]
                    kt, B_kt, qt, B_qt, vv, B_vv = pair_in[p]
                    A, B_A = s_["A"]
                    if s_["first"]:
                        state["o"] = op_.next()
                    o, B_o = state["o"]
                    for hh in range(2):
                        sc.op("pe", lambda e, o=o, hh=hh, c0=c0, A=A, vv=vv, kj=kj, first=s_["first"]: e.matmul(
                            o[hh * 64:(hh + 1) * 64, c0:512], lhsT=vv[:, kj, hh * 64:(hh + 1) * 64],
                            rhs=A[:, hh, c0:512], start=first, stop=False, skip_group_check=True),
                            reads=[B_A, B_vv], writes=[B_o])
                    if s_["last"]:
                        ob, B_ob = oev.next()
                        sc.op("dve", lambda e, ob=ob, o=o: e.tensor_copy(out=ob[:], in_=o[:]), reads=[B_o], writes=[B_ob])
                        sc.dma("sp", [(otb_d[p * 128:(p + 1) * 128, i * 512:(i + 1) * 512], ob[:])], reads=[B_ob],
                               writes=[B_otb[i]], key=("oev", id(B_ob)))
                        if i == NT - 1 and p + 2 < 4 and nsets == 2:
                            load_pair(p + 2)

                batches = [steps[a:a + K] for a in range(0, len(steps), K)]
                nbt = len(batches)
                for b in range(-1, nbt + 1):
                    xs = batches[b + 1] if 0 <= b + 1 < nbt else []
                    avs = batches[b - 1] if 0 <= b - 1 < nbt else []
                    for idx in range(max(len(xs), len(avs))):
                        if idx < len(xs):
                            Xstep(xs[idx])
                        if idx < len(avs):
                            AVstep(avs[idx])
                    if 0 <= b < nbt:
                        for s_ in batches[b]:
                            Ystep(s_)
            sc.barrier()

        def phase_mix():
            with ExitStack() as pes:
                wg = sbt(pes, "wg", [128, 8, 2048], BF16)
                wba = sbt(pes, "wba", [128, 4, D], BF16)
                wbb = sbt(pes, "wbb", [128, 4, D], BF16)
                wo = sbt(pes, "wo", [128, 8, D], BF16)
                B_wg = [Buf("wg%d" % b) for b in range(4)]
                B_wba = [Buf("wba%d" % k) for k in range(4)]
                B_wbb = [Buf("wbb%d" % k) for k in range(4)]
                B_wo = [Buf("wo%d" % k) for k in range(8)]
                nt = NormT(pes, nht=2, nxin=4)
                nxt = nt.run(x1_d, 0, B_x1[0])
                stgp = Pool(nc, pes, "stg", 8, [128, STGW], F32)
                load_cast_cols(stgp, wg, B_wg, win_d[:, C_GA:INW], 8, 2048, gain_i=1)
                for k in range(4):
                    load_cast(stgp, wba, lambda a, b, k=k: wba[:, k, a:b], B_wba[k], wba_d[k * 128:(k + 1) * 128, :], D)
                    load_cast(stgp, wbb, lambda a, b, k=k: wbb[:, k, a:b], B_wbb[k], wbb_d[k * 128:(k + 1) * 128, :], D)
                for k in range(8):
                    load_cast(stgp, wo, lambda a, b, k=k: wo[:, k, a:b], B_wo[k], wout_d[k * 128:(k + 1) * 128, :], D)
                ps = Pool(nc, pes, "ps", 6, [128, 512], F32, psum=True)
                otap = Pool(nc, pes, "ota", 2, [128, 4, TT], BF16)
                otbp = Pool(nc, pes, "otb", 2, [128, 4, TT], BF16)
                MT = sbt(pes, "MT", [128, 8, TT], BF16)
                B_MT = [Buf("MT%d" % c) for c in range(8)]
                sgp = Pool(nc, pes, "sgm", 4, [128, TT], F32)
                mp = Pool(nc, pes, "mm", 4, [128, TT], F32)
                xres = Pool(nc, pes, "xres", 4, [128, D], F32)

                def load_o(tj):
                    oa, B_oa = otap.next()
                    ob, B_ob = otbp.next()
                    sc.dma("sp", [(oa[:], ota_d.rearrange("(c p) s -> p c s", p=128)[:, :, tj * TT:(tj + 1) * TT])],
                           reads=[B_ota[tj]], writes=[B_oa], key=("ota", id(B_oa)))
                    sc.dma("sp", [(ob[:], otb_d.rearrange("(c p) s -> p c s", p=128)[:, :, tj * TT:(tj + 1) * TT])],
                           reads=[B_otb[tj]], writes=[B_ob], key=("otb", id(B_ob)))
                    return oa, B_oa, ob, B_ob

                for ti in range(NT):
                    ht, B_ht = nxt
                    pcs = []
                    if ti + 1 < NT:
                        J = nt.begin(x1_d, ti + 1, B_x1[ti + 1])
                        nxt = (J["ht"], J["B_ht"])
                        pcs = nt.pieces(J)
                    if ti == 0:
                        nxt_o = load_o(0)
                    oa, B_oa, ob, B_ob = nxt_o
                    if ti + 1 < NT:
                        nxt_o = load_o(ti + 1)
                    xrs = []
                    for j in range(4):
                        xr, B_xr = xres.next()
                        r0 = ti * TT + j * 128
                        sc.dma("sp", [(xr[:], x1_d[r0:r0 + 128, :])], reads=[B_x1[ti]], writes=[B_xr], key=("xres", id(B_xr)))
                        xrs.append((xr, B_xr))
                    for cc in range(8):
                        ms = []
                        for br, (wbr, B_wbr, ot, B_ot, goff) in enumerate(((wba, B_wba, oa, B_oa, 0), (wbb, B_wbb, ob, B_ob, 1024))):
                            pg, B_pg = ps.next()
                            pb, B_pb = ps.next()
                            for k in range(8):
                                sc.op("pe", lambda e, pg=pg, k=k, cc=cc, goff=goff, ht=ht: e.matmul(
                                    pg[:], lhsT=wg[:, k, goff + cc * 128:goff + (cc + 1) * 128], rhs=ht[:, k, :],
                                    start=(k == 0), stop=(k == 7)), reads=wtoks(B_wg, goff + cc * 128, 128) + B_ht, writes=[B_pg])
                            for k in range(4):
                                sc.op("pe", lambda e, pb=pb, k=k, cc=cc, wbr=wbr, ot=ot: e.matmul(
                                    pb[:], lhsT=wbr[:, k, cc * 128:(cc + 1) * 128], rhs=ot[:, k, :],
                                    start=(k == 0), stop=(k == 3)), reads=[B_wbr[k], B_ot], writes=[B_pb])
                            sg, B_sg = sgp.next()
                            sc.op("act", lambda e, sg=sg, pg=pg: e.activation(out=sg[:], in_=pg[:], func=AF.Sigmoid),
                                  reads=[B_pg], writes=[B_sg])
                            m, B_m = mp.next()
                            sc.op("dve", lambda e, m=m, sg=sg, pb=pb: e.tensor_tensor(out=m[:], in0=sg[:], in1=pb[:], op=ALU.mult),
                                  reads=[B_sg, B_pb], writes=[B_m])
                            ms.append((m, B_m))
                            if pcs:
                                pcs.pop(0)()
                        sc.op("pool", lambda e, cc=cc, ms=ms: e.tensor_tensor(
                            out=MT[:, cc, :], in0=ms[0][0][:], in1=ms[1][0][:], op=ALU.add),
                            reads=[ms[0][1], ms[1][1]], writes=[B_MT[cc]])
                    while pcs:
                        pcs.pop(0)()
                    for j in range(4):
                        xr, B_xr = xrs[j]
                        r0 = ti * TT + j * 128
                        for half in range(2):
                            po, B_po = ps.next()
                            for cc in range(8):
                                sc.op("pe", lambda e, po=po, cc=cc, j=j, half=half: e.matmul(
                                    po[:], lhsT=MT[:, cc, j * 128:(j + 1) * 128], rhs=wo[:, cc, half * 512:(half + 1) * 512],
                                    start=(cc == 0), stop=(cc == 7)), reads=[B_MT[cc], B_wo[cc]], writes=[B_po])
                            sc.op("dve", lambda e, po=po, xr=xr, half=half: e.tensor_tensor(
                                out=xr[:, half * 512:(half + 1) * 512], in0=po[:], in1=xr[:, half * 512:(half + 1) * 512], op=ALU.add),
                                reads=[B_po, B_xr], writes=[B_xr])
                        sc.dma("sp", [(x2_d[r0:r0 + 128, :], xr[:])], reads=[B_xr], writes=[B_x2[ti]], key=("xres_st", id(B_xr)))
            sc.barrier()

        sc.barrier()
        if "ffn1" in phases:
            phase_ffn(0, x_d, None, x1_d, B_x1, final=False)
        if "proj" in phases:
            phase_proj()
        if "swa" in phases:
            phase_swa()
        if "sb" in phases:
            phase_sb2()
        if "sb_old" in phases:
            phase_sb()
        if "mix" in phases:
            phase_mix()
        if "ffn2" in phases:
            phase_ffn(1, x2_d, B_x2, out_d, None, final=True)
        sc.emit()
    return nc


def _rel_bucket_np(dist):
    max_exact = 16
    d = np.maximum(dist, 1).astype(np.float32)
    large = max_exact + (np.log(d / max_exact) / np.float32(np.log(128 / max_exact)) * (32 - max_exact)).astype(np.int32)
    large = np.minimum(large, 31)
    return np.where(dist < max_exact, dist, large)


def _bucket_table():
    import jax
    import jax.numpy as jnp
    qi = np.arange(128)[:, None] + 128
    kj = np.arange(256)[None, :]
    dist = qi - kj
    band = (dist >= 0) & (dist < 128)
    with jax.default_device(jax.devices("cpu")[0]):
        dj = jnp.maximum(jnp.asarray(dist), 0)
        max_exact = 16
        d = jnp.maximum(dj, 1).astype(jnp.float32)
        large = max_exact + (jnp.log(d / max_exact) / np.log(128 / max_exact) * (32 - max_exact)).astype(jnp.int32)
        large = jnp.minimum(large, 31)
        bucket = np.asarray(jnp.where(dj < max_exact, dj, large))
    return bucket, band


def _consts():
    c = np.zeros((128, 512), np.float32)
    c[:, 0:128] = np.eye(128)
    j = np.arange(128)[:, None]
    s = np.arange(128)[None, :]
    c[:, 128:256] = np.where(j >= s, -1.0, 0.0)
    c[:, 256:384] = -1.0
    c[:, 384:512] = np.where(j < s, 0.0, MASKB)
    return c.astype(ml_dtypes.bfloat16)


_PROG_CACHE = {}


def _prepare_shared(inp, S):
    f = lambda a: np.ascontiguousarray(np.asarray(a, dtype=np.float32))
    bucket, band = _bucket_table()
    rb = f(inp["rel_bias"])
    bias = rb[bucket]
    order = [g + 4 * kv for g in range(4) for kv in range(2)]
    bias = np.ascontiguousarray(bias.transpose(0, 2, 1)[:, order, :])
    mask = np.where(band, 0.0, NEG).astype(np.float32)
    w_in = f(inp["w_in"])[0]
    qcols = []
    for g in range(4):
        for kv in range(2):
            h = g + 4 * kv
            qcols.extend(range(h * 64, (h + 1) * 64))
    w_in = np.ascontiguousarray(np.concatenate([w_in[:, qcols], w_in[:, 512:]], axis=1))
    sinks = f(inp["swa_sinks"])[0].reshape(2, 4)
    shared = {
        "ffn1_w1": f(inp["ffn1_w1"])[0], "ffn1_w3": f(inp["ffn1_w3"])[0], "ffn1_w2": f(inp["ffn1_w2"])[0],
        "ffn2_w1": f(inp["ffn2_w1"])[0], "ffn2_w3": f(inp["ffn2_w3"])[0], "ffn2_w2": f(inp["ffn2_w2"])[0],
        "gains": np.ascontiguousarray(np.stack([f(inp["norm_ffn1"])[0].reshape(8, 128).T,
                                                f(inp["norm_mix"])[0].reshape(8, 128).T,
                                                f(inp["norm_ffn2"])[0].reshape(8, 128).T], axis=1)),
        "norm_final": f(inp["norm_final"]),
        "w_in": w_in, "swa_sinks": np.ascontiguousarray(sinks), "swa_bias": bias, "swa_mask": mask,
        "w_branch_swa": f(inp["w_branch_swa"])[0], "w_branch_sb": f(inp["w_branch_sb"])[0],
        "w_out": f(inp["w_out"])[0], "consts": _consts(),
    }
    return shared


def kernel(**inputs):
    x = np.asarray(inputs["x"], dtype=np.float32)
    B, S, _ = x.shape
    if S not in _PROG_CACHE:
        _PROG_CACHE[S] = build_program(S)
    nc = _PROG_CACHE[S]
    shared = _prepare_shared(inputs, S)
    in_maps = []
    for b in range(B):
        m = dict(shared)
        m["x"] = np.ascontiguousarray(x[b])
        in_maps.append(m)
    res = run_bass_kernel_spmd(nc, in_maps, core_ids=list(range(B)))
    return np.stack([np.asarray(r["out"], dtype=np.float32) for r in res.results], axis=0)
```
